# Optimizing a Trainium2 kernel written in Bass

```python
import math
import jax, jax.numpy as jnp
from jax import lax
import numpy as np

D_MODEL = 2048
BATCH = 4
SEQ = 2048
DEPTH = 2
DEC_BATCH = 8
DEC_SEQ = 4
PAST_LEN = 16384
PAGE_SIZE = 128

N_META = 16
N_EVEN = (DEPTH + 1) // 2
N_ODD = DEPTH // 2
D_ATT = D_MODEL // 2
N_DH = 8
DK = D_ATT // N_DH // 2
DV = 2 * DK
D_CONV = D_MODEL - D_ATT
CONV_W = 31
Q_BLOCK = 128
D_SSM = D_MODEL
SSM_GROUP = 16
N_SSM_GROUPS = D_SSM // SSM_GROUP
SSM_P = 64
D_FF = ((8 * D_MODEL // 3 + 255) // 256) * 256
EPS = 1e-6

kernel_name = 'hybrid_diffattn_conformer_s5_step'


def rms_norm(x, g):
    xf = x.astype(jnp.float32)
    y = xf * lax.rsqrt(jnp.mean(xf * xf, axis=-1, keepdims=True) + EPS)
    return (y * g.astype(jnp.float32)).astype(x.dtype)


def layer_norm(x, g, b):
    xf = x.astype(jnp.float32)
    mu = jnp.mean(xf, axis=-1, keepdims=True)
    var = jnp.mean(jnp.square(xf - mu), axis=-1, keepdims=True)
    y = (xf - mu) * lax.rsqrt(var + EPS)
    return (y * g.astype(jnp.float32) + b.astype(jnp.float32)).astype(x.dtype)


def alibi_slopes(n):
    return jnp.array([2.0 ** (-8.0 * (i + 1) / n) for i in range(n)], dtype=jnp.float32)


def diff_attn_core(q, qpos, segs, lam):
    slopes = alibi_slopes(N_DH)
    qf = q.astype(jnp.float32) * (DK ** -0.5)
    scores = []
    for k, v, kpos in segs:
        s = jnp.einsum('bqhmd,bshmd->bhmqs', qf, k.astype(jnp.float32))
        dist = (qpos[:, None] - kpos[None, :]).astype(jnp.float32)
        s = s - slopes[None, :, None, None, None] * jnp.abs(dist)
        scores.append(jnp.where(dist >= 0, s, -jnp.inf))
    p = jax.nn.softmax(jnp.concatenate(scores, axis=-1), axis=-1)
    out = 0.0
    off = 0
    for k, v, kpos in segs:
        n = kpos.shape[0]
        out = out + jnp.einsum('bhmqs,bshe->bqhme', p[..., off:off + n], v.astype(jnp.float32))
        off += n
    return out[:, :, :, 0] - lam * out[:, :, :, 1]


def prompt_diff_attention(q, k, v, lam):
    B, T = q.shape[0], q.shape[1]
    n_blk = -(-T // Q_BLOCK)
    t_pad = n_blk * Q_BLOCK
    qb = jnp.pad(q, ((0, 0), (0, t_pad - T), (0, 0), (0, 0), (0, 0)))
    qb = qb.reshape(B, n_blk, Q_BLOCK, N_DH, 2, DK).swapaxes(0, 1)
    kpos = jnp.arange(T, dtype=jnp.int32)

    def one(args):
        q_blk, start = args
        qpos = start + jnp.arange(Q_BLOCK, dtype=jnp.int32)
        return diff_attn_core(q_blk, qpos, [(k, v, kpos)], lam)

    o = lax.map(one, (qb, jnp.arange(n_blk, dtype=jnp.int32) * Q_BLOCK))
    return o.swapaxes(0, 1).reshape(B, t_pad, N_DH, DV)[:, :T]


def causal_dwconv(g_ext, w, b):
    out = lax.conv_general_dilated(g_ext, w.astype(g_ext.dtype)[:, None, :], window_strides=(1,),
                                   padding='VALID', dimension_numbers=('NWC', 'WIO', 'NWC'),
                                   feature_group_count=g_ext.shape[-1])
    return out + b.astype(g_ext.dtype)


def even_mixer(h, e, prm, past, conv_prev):
    B, T, _ = h.shape
    proj = h @ prm['w_in_even'][e]
    q, k, v, cv, cg = jnp.split(proj, [D_ATT, 2 * D_ATT, 3 * D_ATT, 3 * D_ATT + D_CONV], axis=-1)
    q = q.reshape(B, T, N_DH, 2, DK)
    k = k.reshape(B, T, N_DH, 2, DK)
    v = v.reshape(B, T, N_DH, DV)
    lam_init = 0.8 - 0.6 * math.exp(-0.3 * (2 * e))
    lq = prm['lambda_q'][e].astype(jnp.float32)
    lk = prm['lambda_k'][e].astype(jnp.float32)
    lam = jnp.exp(jnp.sum(lq[0] * lk[0])) - jnp.exp(jnp.sum(lq[1] * lk[1])) + lam_init
    if past is None:
        o = prompt_diff_attention(q, k, v, lam)
    else:
        cache_k, cache_v, page_table = past
        past_len = page_table.shape[1] * PAGE_SIZE
        k_past = cache_k[e, page_table].reshape(B, past_len, N_DH, 2, DK)
        v_past = cache_v[e, page_table].reshape(B, past_len, N_DH, DV)
        qpos = past_len + jnp.arange(T, dtype=jnp.int32)
        o = diff_attn_core(q, qpos, [(k_past, v_past, jnp.arange(past_len, dtype=jnp.int32)), (k, v, qpos)], lam)
    o = (rms_norm(o, prm['subln_g'][e]) * (1.0 - lam_init)).reshape(B, T, D_ATT).astype(h.dtype)
    g = cv * jax.nn.sigmoid(cg)
    if conv_prev is None:
        conv_prev = jnp.zeros((B, CONV_W - 1, D_CONV), g.dtype)
    g_ext = jnp.concatenate([conv_prev.astype(g.dtype), g], axis=1)
    c = causal_dwconv(g_ext, prm['conv_w'][e], prm['conv_b'][e])
    c = jax.nn.silu(layer_norm(c, prm['conv_ln_g'][e], prm['conv_ln_b'][e]))
    y = jnp.concatenate([o, c], axis=-1) @ prm['w_out_even'][e]
    return y, k, v, g_ext[:, -(CONV_W - 1):]


def odd_mixer(h, o, prm, h0_re, h0_im):
    B, T, _ = h.shape
    f32 = jnp.float32
    u = (h @ prm['w_in_odd'][o]).reshape(B, T, N_SSM_GROUPS, SSM_GROUP).astype(f32)
    lam_c = lax.complex(prm['ssm_a_re'][o].astype(f32), prm['ssm_a_im'][o].astype(f32))
    dt = jnp.exp(prm['ssm_log_dt'][o].astype(f32))[:, None]
    abar = jnp.exp(lam_c * dt)
    coef = (abar - 1.0) / lam_c
    b_re = prm['ssm_b_re'][o].astype(f32)
    b_im = prm['ssm_b_im'][o].astype(f32)
    bb_re = coef.real[..., None] * b_re - coef.imag[..., None] * b_im
    bb_im = coef.real[..., None] * b_im + coef.imag[..., None] * b_re
    bu = lax.complex(jnp.einsum('gpc,btgc->btgp', bb_re, u), jnp.einsum('gpc,btgc->btgp', bb_im, u))
    h0 = lax.complex(h0_re.astype(f32), h0_im.astype(f32))
    bu = bu.at[:, 0].add(abar * h0)
    a_seq = jnp.broadcast_to(abar, (1, T, N_SSM_GROUPS, SSM_P))

    def comb(l, r):
        a1, b1 = l
        a2, b2 = r
        return (a1 * a2, a2 * b1 + b2)

    _, states = lax.associative_scan(comb, (a_seq, bu), axis=1)
    y = (jnp.einsum('gcp,btgp->btgc', prm['ssm_c_re'][o].astype(f32), states.real)
         - jnp.einsum('gcp,btgp->btgc', prm['ssm_c_im'][o].astype(f32), states.imag)
         + prm['ssm_d'][o].astype(f32) * u)
    y = jax.nn.gelu(y.reshape(B, T, D_SSM)).astype(h.dtype)
    y = y * jax.nn.sigmoid(y @ prm['w_glu'][o])
    last = states[:, -1]
    return y @ prm['w_out_odd'][o], last.real, last.imag


def swiglu(h, wg, wu, wd):
    return (jax.nn.silu(h @ wg) * (h @ wu)) @ wd


def run_trunk(x, prm, past, conv_prev, ssm_prev_re, ssm_prev_im):
    B = x.shape[0]
    ks, vs, convs, sres, sims = [], [], [], [], []
    for l in range(DEPTH):
        h = rms_norm(x, prm['norm_mix_pre'][l])
        if l % 2 == 0:
            e = l // 2
            cp = None if conv_prev is None else conv_prev[e]
            m, k, v, c = even_mixer(h, e, prm, past, cp)
            ks.append(k)
            vs.append(v)
            convs.append(c)
        else:
            o = l // 2
            if ssm_prev_re is None:
                h0r = jnp.zeros((B, N_SSM_GROUPS, SSM_P), jnp.float32)
                h0i = h0r
            else:
                h0r, h0i = ssm_prev_re[o], ssm_prev_im[o]
            m, sr, si = odd_mixer(h, o, prm, h0r, h0i)
            sres.append(sr.astype(x.dtype))
            sims.append(si.astype(x.dtype))
        x = x + rms_norm(m, prm['norm_mix_post'][l])
        h = rms_norm(x, prm['norm_ffn_pre'][l])
        f = swiglu(h, prm['w_ffn_gate'][l], prm['w_ffn_up'][l], prm['w_ffn_down'][l])
        x = x + rms_norm(f, prm['norm_ffn_post'][l])
    return x, jnp.stack(ks), jnp.stack(vs), jnp.stack(convs), jnp.stack(sres), jnp.stack(sims)


def setup_inputs(seed: int = 0) -> dict:
    key = jax.random.key(seed)
    keys = iter(jax.random.split(key, 48))
    f32 = jnp.float32

    def nrm(shape, scale):
        return jax.random.normal(next(keys), shape, f32) * scale

    n_pages = PAST_LEN // PAGE_SIZE
    n_used = DEC_BATCH * n_pages
    n_pool = (5 * n_used + 3) // 4
    G, P = N_SSM_GROUPS, SSM_P
    inp = {}
    inp['x_prompt'] = nrm((BATCH, SEQ, D_MODEL), 1.0)
    inp['x_sample'] = nrm((DEC_BATCH, DEC_SEQ, D_MODEL), 1.0)
    inp['cache_k'] = nrm((N_EVEN, n_pool, PAGE_SIZE, N_DH, 2, DK), 1.0)
    inp['cache_v'] = nrm((N_EVEN, n_pool, PAGE_SIZE, N_DH, DV), 1.0)
    inp['state_conv'] = nrm((N_EVEN, DEC_BATCH, CONV_W - 1, D_CONV), 0.5)
    inp['state_ssm_re'] = nrm((N_ODD, DEC_BATCH, G, P), 0.1)
    inp['state_ssm_im'] = nrm((N_ODD, DEC_BATCH, G, P), 0.1)
    perm = jax.random.permutation(next(keys), n_pool)[:n_used]
    inp['page_table'] = perm.reshape(DEC_BATCH, n_pages).astype(jnp.int32)
    inp['meta_tokens'] = nrm((N_META, D_MODEL), 1.0)
    inp['norm_mix_pre'] = 1.0 + nrm((DEPTH, D_MODEL), 0.02)
    inp['norm_mix_post'] = 1.0 + nrm((DEPTH, D_MODEL), 0.02)
    inp['norm_ffn_pre'] = 1.0 + nrm((DEPTH, D_MODEL), 0.02)
    inp['norm_ffn_post'] = 1.0 + nrm((DEPTH, D_MODEL), 0.02)
    inp['w_in_even'] = nrm((N_EVEN, D_MODEL, 3 * D_ATT + 2 * D_CONV), D_MODEL ** -0.5)
    inp['lambda_q'] = nrm((N_EVEN, 2, DK), 0.1)
    inp['lambda_k'] = nrm((N_EVEN, 2, DK), 0.1)
    inp['subln_g'] = 1.0 + nrm((N_EVEN, DV), 0.02)
    inp['conv_w'] = nrm((N_EVEN, CONV_W, D_CONV), CONV_W ** -0.5)
    inp['conv_b'] = nrm((N_EVEN, D_CONV), 0.01)
    inp['conv_ln_g'] = 1.0 + nrm((N_EVEN, D_CONV), 0.02)
    inp['conv_ln_b'] = nrm((N_EVEN, D_CONV), 0.01)
    inp['w_out_even'] = nrm((N_EVEN, D_ATT + D_CONV, D_MODEL), (D_ATT + D_CONV) ** -0.5)
    inp['w_in_odd'] = nrm((N_ODD, D_MODEL, D_SSM), D_MODEL ** -0.5)
    inp['ssm_a_re'] = -0.5 * jnp.exp(nrm((N_ODD, G, P), 0.1))
    inp['ssm_a_im'] = math.pi * jnp.arange(P, dtype=f32)[None, None, :] + nrm((N_ODD, G, P), 0.01)
    inp['ssm_b_re'] = nrm((N_ODD, G, P, SSM_GROUP), (2.0 * SSM_GROUP) ** -0.5)
    inp['ssm_b_im'] = nrm((N_ODD, G, P, SSM_GROUP), (2.0 * SSM_GROUP) ** -0.5)
    inp['ssm_c_re'] = nrm((N_ODD, G, SSM_GROUP, P), (2.0 * P) ** -0.5)
    inp['ssm_c_im'] = nrm((N_ODD, G, SSM_GROUP, P), (2.0 * P) ** -0.5)
    inp['ssm_d'] = nrm((N_ODD, G, SSM_GROUP), 1.0)
    inp['ssm_log_dt'] = jax.random.uniform(next(keys), (N_ODD, G), f32, math.log(1e-3), math.log(1e-1))
    inp['w_glu'] = nrm((N_ODD, D_SSM, D_SSM), D_SSM ** -0.5)
    inp['w_out_odd'] = nrm((N_ODD, D_SSM, D_MODEL), D_SSM ** -0.5)
    inp['w_ffn_gate'] = nrm((DEPTH, D_MODEL, D_FF), D_MODEL ** -0.5)
    inp['w_ffn_up'] = nrm((DEPTH, D_MODEL, D_FF), D_MODEL ** -0.5)
    inp['w_ffn_down'] = nrm((DEPTH, D_FF, D_MODEL), D_FF ** -0.5)
    return inp


def reference(x_prompt, x_sample, cache_k, cache_v, state_conv, state_ssm_re, state_ssm_im, page_table,
              meta_tokens, norm_mix_pre, norm_mix_post, norm_ffn_pre, norm_ffn_post,
              w_in_even, lambda_q, lambda_k, subln_g, conv_w, conv_b, conv_ln_g, conv_ln_b, w_out_even,
              w_in_odd, ssm_a_re, ssm_a_im, ssm_b_re, ssm_b_im, ssm_c_re, ssm_c_im, ssm_d, ssm_log_dt,
              w_glu, w_out_odd, w_ffn_gate, w_ffn_up, w_ffn_down):
    prm = dict(norm_mix_pre=norm_mix_pre, norm_mix_post=norm_mix_post, norm_ffn_pre=norm_ffn_pre,
               norm_ffn_post=norm_ffn_post, w_in_even=w_in_even, lambda_q=lambda_q, lambda_k=lambda_k,
               subln_g=subln_g, conv_w=conv_w, conv_b=conv_b, conv_ln_g=conv_ln_g, conv_ln_b=conv_ln_b,
               w_out_even=w_out_even, w_in_odd=w_in_odd, ssm_a_re=ssm_a_re, ssm_a_im=ssm_a_im,
               ssm_b_re=ssm_b_re, ssm_b_im=ssm_b_im, ssm_c_re=ssm_c_re, ssm_c_im=ssm_c_im, ssm_d=ssm_d,
               ssm_log_dt=ssm_log_dt, w_glu=w_glu, w_out_odd=w_out_odd, w_ffn_gate=w_ffn_gate,
               w_ffn_up=w_ffn_up, w_ffn_down=w_ffn_down)
    B = x_prompt.shape[0]
    meta = jnp.broadcast_to(meta_tokens.astype(x_prompt.dtype)[None], (B, N_META, x_prompt.shape[-1]))
    xp = jnp.concatenate([meta, x_prompt], axis=1)
    yp, k_prompt, v_prompt, conv_prompt, ssm_re_prompt, ssm_im_prompt = run_trunk(xp, prm, None, None, None, None)
    y_prompt = yp[:, N_META:]
    y_sample, k_sample, v_sample, conv_sample, ssm_re_sample, ssm_im_sample = run_trunk(
        x_sample, prm, (cache_k, cache_v, page_table), state_conv, state_ssm_re, state_ssm_im)
    return (y_prompt, y_sample, k_prompt, v_prompt, k_sample, v_sample, conv_prompt, conv_sample,
            ssm_re_prompt, ssm_im_prompt, ssm_re_sample, ssm_im_sample)
```

```python
import numpy as np
from contextlib import ExitStack
import concourse.bass as bass
import concourse.mybir as mybir

F32 = mybir.dt.float32
F32R = mybir.dt.float32r
I32 = mybir.dt.int32
AF = mybir.ActivationFunctionType
ALU = mybir.AluOpType
AX = mybir.AxisListType

PH = 4096


class Buf:
    __slots__ = ("name", "writers", "readers", "dcount", "semi")

    def __init__(self, name):
        self.name = name
        self.writers = []
        self.readers = []
        self.dcount = 0
        self.semi = None


class Prog:
    ENG = ("pe", "act", "dve", "pool", "sp")

    def __init__(self, nc):
        self.nc = nc
        self.ops = {e: [] for e in self.ENG}
        self.cnt = {e: 0 for e in self.ENG}
        self.waited = {e: {} for e in self.ENG}
        self.dbufs = []
        self.stack = ExitStack()
        self.retype = None

    def sb(self, name, shape, dtype=F32):
        return self.stack.enter_context(self.nc.sbuf_tensor("sb_" + name, list(shape), dtype))

    def ps(self, name, shape, dtype=F32):
        return self.stack.enter_context(self.nc.psum_tensor(name, list(shape), dtype))

    def _need(self, eng, toks):
        best = {}
        for t in toks:
            if t[0] == "E":
                _, e2, idx = t
                if e2 == eng and eng in ("pe", "sp"):
                    continue
                key = ("E", e2, (idx - 1) // PH)
                val = (idx - 1) % PH + 1
            else:
                _, b, val = t
                key = ("D", id(b))
                if b.semi is None:
                    b.semi = len(self.dbufs)
                    self.dbufs.append(b)
            if best.get(key, (0,))[0] < val:
                best[key] = (val, t)
        out = []
        w = self.waited[eng]
        for key, (val, t) in best.items():
            if w.get(key, 0) >= val:
                continue
            w[key] = val
            out.append((key, val, t))
        return out

    def _deps(self, reads, writes):
        toks = []
        for b in reads:
            toks += b.writers
        for b in writes:
            toks += b.writers
            toks += b.readers
        return toks

    @staticmethod
    def _dedupe(lst):
        best = {}
        for t in lst:
            if t[0] == "E":
                key = ("E", t[1])
                v = t[2]
            else:
                key = ("D", id(t[1]))
                v = t[2]
            if key not in best or best[key][2] < v:
                best[key] = t
        return list(best.values())

    def _commit(self, tok, reads, writes, partial):
        for b in reads:
            b.readers = self._dedupe(b.readers + [tok])
        for b in writes:
            if partial:
                b.writers = self._dedupe(b.writers + [tok])
            else:
                b.writers = [tok]
                b.readers = []

    def op(self, eng, meth, *args, reads=(), writes=(), partial=False, **kw):
        if self.retype is not None and args:
            args = (self.retype(args[0]),) + tuple(args[1:])
        fn = lambda e, meth=meth, args=args, kw=kw: getattr(e, meth)(*args, **kw)
        waits = self._need(eng, self._deps(reads, writes))
        self.cnt[eng] += 1
        idx = self.cnt[eng]
        tok = ("E", eng, idx)
        self.ops[eng].append((waits, fn, ("E", eng, idx)))
        self._commit(tok, reads, writes, partial)
        return tok

    def dma(self, q, out, in_, reads=(), writes=(), sem=None, partial=False, **kw):
        assert sem is not None
        if self.retype is not None:
            out = self.retype(out)
            if out.dtype == F32R and in_.dtype == F32:
                in_ = in_.bitcast(F32R)
        waits = self._need(q, self._deps(reads, writes))
        if sem.semi is None:
            sem.semi = len(self.dbufs)
            self.dbufs.append(sem)
        sem.dcount += 1
        tok = ("D", sem, 16 * sem.dcount)
        fn = lambda e, out=out, in_=in_, kw=kw: e.dma_start(out=out, in_=in_, **kw)
        self.ops[q].append((waits, fn, ("D", sem)))
        self._commit(tok, reads, writes, partial)
        return tok

    def dma_custom(self, q, meth, reads=(), writes=(), sem=None, partial=False, **kw):
        fn = lambda e, meth=meth, kw=kw: getattr(e, meth)(**kw)
        waits = self._need(q, self._deps(reads, writes))
        if sem.semi is None:
            sem.semi = len(self.dbufs)
            self.dbufs.append(sem)
        sem.dcount += 1
        tok = ("D", sem, 16 * sem.dcount)
        self.ops[q].append((waits, fn, ("D", sem)))
        self._commit(tok, reads, writes, partial)
        return tok

    def barrier(self):
        toks = [("E", e, self.cnt[e]) for e in self.ENG if self.cnt[e] > 0]
        toks += [("D", b, 16 * b.dcount) for b in self.dbufs if b.dcount > 0]
        for e in self.ENG:
            waits = self._need(e, toks)
            if waits:
                self.ops[e].append((waits, None, None))

    def emit(self):
        nc = self.nc
        nsem = {e: (self.cnt[e] + PH - 1) // PH for e in self.ENG}
        esem = {e: [self.stack.enter_context(nc.semaphore(f"s_{e}{i}")) for i in range(nsem[e])]
                for e in self.ENG}
        dsem = [self.stack.enter_context(nc.semaphore(f"d_{i}")) for i in range(len(self.dbufs))]
        idmap = {id(b): i for i, b in enumerate(self.dbufs)}

        def semof(key):
            if key[0] == "E":
                return esem[key[1]][key[2]]
            return dsem[idmap[key[1]]]

        def run(engname, eng):
            for waits, fn, inc in self.ops[engname]:
                for key, val, _ in waits:
                    eng.wait_ge(semof(key), val)
                if fn is None:
                    continue
                ins = fn(eng)
                if inc[0] == "E":
                    idx = inc[2]
                    ins.then_inc(esem[engname][(idx - 1) // PH], 1)
                else:
                    ins.then_inc(dsem[idmap[id(inc[1])]], 16)

        with nc.Block() as block:
            @block.tensor
            def _(e):
                run("pe", e)

            @block.scalar
            def _(e):
                run("act", e)

            @block.vector
            def _(e):
                run("dve", e)

            @block.gpsimd
            def _(e):
                run("pool", e)

            @block.sync
            def _(e):
                run("sp", e)
        self.stack.close()

import math
from concourse.bass_utils import run_bass_kernel_spmd

D = 2048
NT = 2068
NPR = 2064
CHUNKS = [(0, 416), (416, 832), (832, 1248), (1248, 1664), (1664, 2068)]
W = 416
DFF = 5632
EPS = 1e-6
LAM_INIT = 0.8 - 0.6 * math.exp(-0.3 * 0)
SLOPES = [2.0 ** (-8.0 * (i + 1) / 8) for i in range(8)]
PAST = 16384
NEG = -30000.0
STOP_AFTER = None


def build_program(stop_after=None):
    nc = bass.Bass("TRN2", target_bir_lowering=False)
    nc.dge_precook = False
    P = Prog(nc)

    def din(name, shape, dt=F32):
        return nc.dram_tensor(name, list(shape), dt, kind="ExternalInput").ap()

    def dout(name, shape, dt=F32):
        return nc.dram_tensor(name, list(shape), dt, kind="ExternalOutput").ap()

    def dscr(name, shape, dt=F32):
        return nc.dram_tensor(name, list(shape), dt).ap()

    xin = din("xin", [NT, D])
    w_in_even = din("w_in_even", [D, 5120]); w_out_even = din("w_out_even", [D, D])
    w_in_odd = din("w_in_odd", [D, D]); w_glu = din("w_glu", [D, D]); w_out_odd = din("w_out_odd", [D, D])
    w_gate = din("w_gate", [2, D, DFF]); w_up = din("w_up", [2, D, DFF]); w_down = din("w_down", [2, DFF, D])
    gains_d = din("gains", [128, 8, 16])
    cw_d = din("cw", [128, 8, 31]); cvec_d = din("cvec", [128, 3, 8])
    subg_d = din("subg", [128, 1]); lqk_d = din("lqk", [1, 256])
    a_re_d = din("a_re", [128, 64]); a_im_d = din("a_im", [128, 64]); ldt_d = din("ldt", [128, 1])
    b_re_d = din("b_re", [128, 64, 16]); b_im_d = din("b_im", [128, 64, 16])
    ccat_d = din("ccat", [128, 128, 16]); dfm_d = din("dfm", [128, 16])
    ck_d = din("cache_k", [1280 * 128, 1024]); cv_d = din("cache_v", [1280 * 128, 1024])
    pt_d = din("ptab", [1, 128], I32); iota_d = din("iota", [128, 1], I32)
    sconv_d = din("sconv", [30, 1024]); sssm_d = din("sssm", [128, 128])
    ident_d = din("ident", [128, 128]); ones_d = din("ones", [128, 128]); d0_d = din("d0tab", [128, W])
    stab_d = din("stab", [64, 8 + 128])
    slopecol_d = din("slopecol", [64, 1])

    yout = dout("yout", [NT, D]); kout = dout("kout", [NT, 1024]); vout = dout("vout", [NT, 1024])
    convp_o = dout("convp", [30, 1024]); convs_o = dout("convs", [30, 1024])
    ssmp_o = dout("ssmp", [128, 128]); ssms_o = dout("ssms", [128, 128])

    dbg_o = dout("dbg", [6, 128, 16 * W]) if stop_after is not None else None
    b_dbg = Buf("dbg")

    def dump(slot, ap3, bufs, ci):
        if ci != 0 or dbg_o is None:
            return
        P.barrier()
        P.dma("pool", dbg_o[slot].rearrange("p (a b) -> p a b", a=16), ap3, reads=list(bufs), writes=[b_dbg], sem=b_dbg, partial=True)
        P.barrier()

    KTscr = dscr("KTscr", [8, 128, NPR]); Vscr = dscr("Vscr", [NT, 1024])
    BDscr = dscr("BDscr", [16, 128, 1024])
    BuScr = dscr("BuScr", [W, 128, 128]); HScr = dscr("HScr", [W, 128, 128])

    ident = P.sb("ident", [128, 128]); b_ident = Buf("ident")
    ones_r = P.sb("ones_r", [128, 128], F32R); b_ones = Buf("ones")
    gains = P.sb("gains", [128, 8, 16]); cw = P.sb("cw", [128, 8, 31]); cvec = P.sb("cvec", [128, 3, 8])
    subg = P.sb("subg", [128, 1]); lamt = P.sb("lamt", [128, 8]); b_par = Buf("params")
    d0tab = P.sb("d0tab", [128, W]); stab = P.sb("stab", [64, 136]); slopecol = P.sb("slopecol", [64, 1])
    dfm = P.sb("dfm", [128, 16])
    zeros = P.sb("zeros", [128, 512]); b_zero = Buf("zeros")
    gtail = P.sb("gtail", [128, 8, 30]); b_gt = Buf("gtail")
    ccat = P.sb("ccat", [128, 128, 16], F32R); b_ccat = Buf("ccat")
    XY = P.sb("XY", [128, 256]); b_xy = Buf("xy")
    S_p = P.sb("S_p", [128, 128]); S_s = P.sb("S_s", [128, 128]); b_Sp = Buf("Sp"); b_Ss = Buf("Ss")
    x_fm = P.sb("x_fm", [128, 16, W]); b_x = [Buf(f"x{i}") for i in range(16)]
    NWB = 3
    wbuf = [P.sb(f"wbuf{i}", [128, 16, 128], F32R) for i in range(NWB)]; b_w = [Buf(f"w{i}") for i in range(NWB)]
    ARENA = 16 * W + 16928
    arenaR = P.sb("arena", [128, ARENA], F32R)
    arena = arenaR[:].bitcast(F32)
    AFSZ = 7424
    arenaF = P.sb("arenaF", [128, AFSZ])
    small = P.sb("small", [128, 8 * W]); b_small = [Buf(f"sm{i}") for i in range(8)]
    smallr = P.sb("smallr", [128, 2 * W], F32R); b_smr = [Buf("smr0"), Buf("smr1")]
    ptb_t = P.sb("ptb", [128, 128], I32); idx_t = P.sb("idx", [128, 128], I32); iot_t = P.sb("iot", [128, 1], I32)
    PS = [P.ps(f"ps{i}", [128, 512]) for i in range(8)]; b_ps = [Buf(f"ps{i}") for i in range(8)]

    def sm(i, n=W):
        return small[:, i * W:i * W + n]

    def smr(i, n=W):
        return smallr[:, i * W:i * W + n]

    def av(off, a, b):
        return arena[:, off:off + a * b].rearrange("p (a b) -> p a b", a=a)

    R = lambda ap: ap.bitcast(F32R)

    def _rt(ap):
        try:
            if ap.name == "sb_arena" and ap.dtype == F32:
                return ap.bitcast(F32R)
        except Exception:
            pass
        return ap
    P.retype = _rt
    b_AF = [Buf(f"AF{i}") for i in range(8)]

    state = {"wi": 0, "mm": 0, "evac": 0}

    def ev_eng():
        state["evac"] += 1
        return "act" if state["evac"] % 2 else "dve"

    def copy_op(eng, out, in_, reads, writes, partial=False):
        if eng == "act":
            return P.op("act", "copy", out, in_, reads=reads, writes=writes, partial=partial)
        return P.op(eng, "tensor_copy", out, in_, reads=reads, writes=writes, partial=partial)

    P.dma("sp", ident[:], ident_d, writes=[b_ident], sem=b_ident)
    P.dma("sp", ones_r[:], R(ones_d), writes=[b_ones], sem=b_ones)
    P.op("dve", "memset", zeros[:], 0.0, writes=[b_zero])
    for dst, src in [(gains, gains_d), (cw, cw_d), (cvec, cvec_d), (subg, subg_d), (d0tab, d0_d), (stab, stab_d),
                     (slopecol, slopecol_d), (dfm, dfm_d)]:
        P.dma("sp", dst[:], src, writes=[b_par], sem=b_par, partial=True)
    lq = sm(0, 256)
    P.dma("pool", lq, lqk_d.partition_broadcast(128), writes=[b_small[0]], sem=b_small[0])
    P.op("dve", "tensor_tensor", sm(1, 128), lq[:, 0:128], lq[:, 128:256], op=ALU.mult, reads=[b_small[0]], writes=[b_small[1]])
    P.op("dve", "tensor_reduce", lamt[:, 0:2], sm(1, 128).rearrange("p (a b) -> p a b", a=2), axis=AX.X, op=ALU.add,
         reads=[b_small[1]], writes=[b_par], partial=True)
    P.op("act", "activation", lamt[:, 2:4], lamt[:, 0:2], AF.Exp, reads=[b_par], writes=[b_par], partial=True)
    P.op("dve", "scalar_tensor_tensor", lamt[:, 4:5], lamt[:, 3:4], -LAM_INIT, lamt[:, 2:3], op0=ALU.add, op1=ALU.subtract,
         reads=[b_par], writes=[b_par], partial=True)
    P.op("dve", "tensor_scalar", lamt[:, 5:6], subg[:], 1.0 - LAM_INIT, None, op0=ALU.mult, reads=[b_par], writes=[b_par], partial=True)
    neglam = lamt[:, 4:5]; subg2 = lamt[:, 5:6]

    def gain(kind, layer, ft):
        return gains[:, kind * 2 + layer, ft:ft + 1]

    def ssm_prep():
        o = 0
        def T(n):
            nonlocal o
            v = arenaF[:, o:o + n]; o += n
            return v
        b_t = Buf("ssmprep")
        lr, li, dt_, zr, zi, er, cs, sn, mk, t1, t2, cr, ci, den = [T(64) for _ in range(14)]
        bre = T(1024); bim = T(1024); bbT = T(2048); zero = T(1024)
        dtc = T(1)
        rw = dict(reads=[b_t], writes=[b_t], partial=True)
        P.dma("sp", lr, a_re_d, writes=[b_t], sem=b_t, partial=True)
        P.dma("sp", li, a_im_d, writes=[b_t], sem=b_t, partial=True)
        P.dma("sp", dtc, ldt_d, writes=[b_t], sem=b_t, partial=True)
        P.dma("sp", bre, b_re_d.rearrange("g p c -> g (p c)"), writes=[b_t], sem=b_t, partial=True)
        P.dma("sp", bim, b_im_d.rearrange("g p c -> g (p c)"), writes=[b_t], sem=b_t, partial=True)
        P.dma("sp", ccat[:], R(ccat_d), writes=[b_ccat], sem=b_ccat)
        P.op("act", "activation", ccat[64:128], ccat[64:128].bitcast(F32), AF.Copy, scale=-1.0, reads=[b_ccat], writes=[b_ccat])
        P.op("act", "activation", dtc, dtc, AF.Exp, **rw)
        P.op("dve", "tensor_scalar", zr, lr, dtc, None, op0=ALU.mult, **rw)
        P.op("dve", "tensor_scalar", zi, li, dtc, None, op0=ALU.mult, **rw)
        P.op("act", "activation", er, zr, AF.Exp, **rw)
        P.op("dve", "tensor_copy", sn, zi, **rw)
        P.op("dve", "tensor_scalar", cs, zi, math.pi / 2, None, op0=ALU.add, **rw)
        for tgt in (sn, cs):
            for _ in range(7):
                P.op("dve", "tensor_scalar", mk, tgt, math.pi, None, op0=ALU.is_gt, **rw)
                P.op("dve", "scalar_tensor_tensor", tgt, mk, -2 * math.pi, tgt, op0=ALU.mult, op1=ALU.add, **rw)
            P.op("act", "activation", tgt, tgt, AF.Sin, **rw)
        P.op("dve", "tensor_tensor", XY[:, 0:64], er, cs, op=ALU.mult, reads=[b_t], writes=[b_xy], partial=True)
        P.op("dve", "tensor_copy", XY[:, 64:128], XY[:, 0:64], reads=[b_xy], writes=[b_xy], partial=True)
        P.op("dve", "tensor_tensor", XY[:, 192:256], er, sn, op=ALU.mult, reads=[b_t], writes=[b_xy], partial=True)
        P.op("dve", "tensor_scalar", XY[:, 128:192], XY[:, 192:256], -1.0, None, op0=ALU.mult, reads=[b_xy], writes=[b_xy], partial=True)
        ar = XY[:, 0:64]; ai = XY[:, 192:256]
        rx = dict(reads=[b_t, b_xy], writes=[b_t], partial=True)
        P.op("dve", "tensor_scalar", t1, ar, -1.0, None, op0=ALU.add, **rx)
        P.op("dve", "tensor_tensor", den, lr, lr, op=ALU.mult, **rw)
        P.op("dve", "tensor_tensor", t2, li, li, op=ALU.mult, **rw)
        P.op("dve", "tensor_tensor", den, den, t2, op=ALU.add, **rw)
        P.op("dve", "reciprocal", den, den, **rw)
        P.op("dve", "tensor_tensor", cr, t1, lr, op=ALU.mult, **rw)
        P.op("dve", "tensor_tensor", t2, ai, li, op=ALU.mult, **rx)
        P.op("dve", "tensor_tensor", cr, cr, t2, op=ALU.add, **rw)
        P.op("dve", "tensor_tensor", cr, cr, den, op=ALU.mult, **rw)
        P.op("dve", "tensor_tensor", ci, ai, lr, op=ALU.mult, **rx)
        P.op("dve", "tensor_tensor", t2, t1, li, op=ALU.mult, **rw)
        P.op("dve", "tensor_tensor", ci, ci, t2, op=ALU.subtract, **rw)
        P.op("dve", "tensor_tensor", ci, ci, den, op=ALU.mult, **rw)
        bre3 = bre.rearrange("g (p c) -> g p c", c=16); bim3 = bim.rearrange("g (p c) -> g p c", c=16)
        bb3 = bbT.rearrange("g (c s) -> g c s", c=16)
        for c in range(16):
            br = bre3[:, :, c]; bi = bim3[:, :, c]
            o_re = bb3[:, c, 0:64]; o_im = bb3[:, c, 64:128]
            P.op("dve", "tensor_tensor", o_re, cr, br, op=ALU.mult, **rw)
            P.op("dve", "tensor_tensor", t2, ci, bi, op=ALU.mult, **rw)
            P.op("dve", "tensor_tensor", o_re, o_re, t2, op=ALU.subtract, **rw)
            P.op("dve", "tensor_tensor", o_im, cr, bi, op=ALU.mult, **rw)
            P.op("dve", "tensor_tensor", t2, ci, br, op=ALU.mult, **rw)
            P.op("dve", "tensor_tensor", o_im, o_im, t2, op=ALU.add, **rw)
        P.op("dve", "memset", zero, 0.0, **rw)
        b_bd = Buf("bdscr")
        for f in range(16):
            P.dma("pool", BDscr[f], zero, reads=[b_t], writes=[b_bd], sem=b_t, partial=True)
        P.barrier()
        for g in range(128):
            f, gl = divmod(g, 8)
            P.dma("pool", BDscr[f, gl * 16:(gl + 1) * 16, gl * 128:(gl + 1) * 128].unsqueeze(0),
                  bb3[g:g + 1, :, :], reads=[b_t], writes=[b_bd], sem=b_t, partial=True)
        P.op("dve", "memset", S_p[:], 0.0, writes=[b_Sp])
        P.dma("sp", S_s[:], sssm_d, writes=[b_Ss], sem=b_Ss)
        P.barrier()
        return b_bd

    b_bd = ssm_prep()
    b_ktscr = Buf("ktscr"); b_vscr = Buf("vscr"); b_buscr = Buf("buscr"); b_hscr = Buf("hscr")
    b_out = Buf("outs")

    def rmsnorm_stats(src_fn, nt_tiles, n, denom, srcbufs):
        ssb = b_ps[4]
        for i in range(nt_tiles):
            sl = i % 2
            P.op("act", "activation", smr(sl, n), src_fn(i), AF.Square, reads=[srcbufs[i]], writes=[b_smr[sl]])
            P.op("pe", "matmul", PS[4][:, :n], lhsT=ones_r[:], rhs=smr(sl, n), start=(i == 0), stop=(i == nt_tiles - 1),
                 reads=[b_smr[sl], b_ones], writes=[ssb], partial=(i > 0))
        P.op("act", "activation", sm(7, n), PS[4][:, :n], AF.Sqrt, bias=EPS, scale=1.0 / denom, reads=[ssb], writes=[b_small[7]])
        P.op("dve", "reciprocal", sm(7, n), sm(7, n), reads=[b_small[7]], writes=[b_small[7]])
        return sm(7, n), b_small[7]

    def linear(Wap, KT, blocks, rhs_fn, rhs_bufs, n, evac):
        W3 = Wap.rearrange("(kt p) c -> p kt c", p=128)
        for c0 in blocks:
            wi = state["wi"] % NWB; state["wi"] += 1
            wb = wbuf[wi]
            P.dma("sp", wb[:, 0:KT, :], R(W3[:, :, c0:c0 + 128]), writes=[b_w[wi]], sem=b_w[wi])
            pi = state["mm"] % 2; state["mm"] += 1
            for kt in range(KT):
                P.op("pe", "matmul", PS[pi][:, :n], lhsT=wb[:, kt, :], rhs=rhs_fn(kt), start=(kt == 0), stop=(kt == KT - 1),
                     reads=[b_w[wi], rhs_bufs[kt]], writes=[b_ps[pi]], partial=(kt > 0))
            evac(c0, PS[pi][:, :n], b_ps[pi])

    def blocks_range(c_lo, c_hi):
        return list(range(c_lo, c_hi, 128))

    def post_norm_residual(m3, b_m, kind, layer, n):
        rstd, b_r = rmsnorm_stats(lambda i: m3[:, i, :n], 16, n, float(D), b_m)
        for ft in range(16):
            eng = "dve"
            P.op(eng, "scalar_tensor_tensor", m3[:, ft, :n], m3[:, ft, :n], gain(kind, layer, ft), rstd, op0=ALU.mult, op1=ALU.mult,
                 reads=[b_m[ft], b_r, b_par], writes=[b_m[ft]])
            P.op(eng, "tensor_tensor", x_fm[:, ft, :n], x_fm[:, ft, :n], m3[:, ft, :n], op=ALU.add,
                 reads=[b_m[ft], b_x[ft]], writes=[b_x[ft]])

    def pre_norm(hn3, b_hn, kind, layer, n):
        rstd, b_r = rmsnorm_stats(lambda i: x_fm[:, i, :n], 16, n, float(D), b_x)
        for ft in range(16):
            eng = "dve"
            P.op(eng, "scalar_tensor_tensor", R(hn3[:, ft, :n]), x_fm[:, ft, :n], gain(kind, layer, ft), rstd, op0=ALU.mult, op1=ALU.mult,
                 reads=[b_x[ft], b_r, b_par], writes=[b_hn[ft]])

    A0 = 0
    A1 = 16 * W
    SZ16 = 16 * W
    b_A0 = [Buf(f"A0_{i}") for i in range(16)]
    b_A1 = [Buf(f"A1_{i}") for i in range(40)]


    def sample_attention(q3, b_q, ksamp, b_ks, oc3, b_oc, npr, AT):
        P.barrier()
        o = AT
        def T(nel):
            nonlocal o
            v = arena[:, o:o + nel]; o += nel
            return v
        kpg = [T(1024) for _ in range(2)]; vpg = [R(T(1024)) for _ in range(2)]
        ktp = [R(T(1024)).rearrange("p (a b) -> p a b", a=8) for _ in range(2)]
        qblk = T(512); ptb = ptb_t[:]; idx = idx_t[:]; iot = iot_t[:]
        ptp = [R(T(64)) for _ in range(2)]; vst = R(T(1024))
        psb = arenaF[:, 0:128]; tms = arenaF[:, 128:256]; oms = arenaF[:, 256:320]; ods = arenaF[:, 320:352]; rss = arenaF[:, 352:384]
        sqs = smr(1, 32)
        assert o <= ARENA
        bk = [Buf("kpg0"), Buf("kpg1")]; bv = [Buf("vpg0"), Buf("vpg1")]; bkt = [Buf("ktp0"), Buf("ktp1")]
        bq = Buf("qblk"); bi = Buf("idx"); bp = Buf("psb"); bpt = [Buf("ptp0"), Buf("ptp1")]; bt = Buf("tms"); bvs = Buf("vst"); bm = Buf("misc")
        P.dma("pool", ptb, pt_d.partition_broadcast(128), writes=[bi], sem=bi)
        P.dma("sp", iot, iota_d, writes=[bi], sem=bi, partial=True)
        P.op("dve", "tensor_scalar", idx, ptb, 128, iot[:, 0:1], op0=ALU.mult, op1=ALU.add, reads=[bi], writes=[bi], partial=True)
        P.dma("sp", vst[:4, :], R(Vscr[NPR:NPR + 4, :]), reads=[b_vscr], writes=[bvs], sem=bvs)
        q4 = qblk.rearrange("p (h c) -> p h c", h=8)
        P.op("dve", "tensor_copy", R(qblk), zeros[:, 0:512], reads=[b_zero], writes=[bq])
        for h in range(8):
            for m in range(2):
                P.op("dve", "tensor_copy", R(q4[64 * m:64 * m + 64, h, h * 8 + m * 4:h * 8 + m * 4 + 4]), q3[64 * m:64 * m + 64, h, npr:npr + 4],
                     reads=[b_q[h]], writes=[bq], partial=True)
        qb = R(q4)
        o_ps, d_ps, s_ps, t_ps = PS[2], PS[5], PS[7], PS[4]
        for i in range(128):
            s = i % 2
            P.dma_custom("pool", "indirect_dma_start", reads=[bi], writes=[bk[s]], sem=bk[s],
                         out=R(kpg[s]), out_offset=None, in_=R(ck_d), in_offset=bass.IndirectOffsetOnAxis(ap=idx[:, i:i + 1], axis=0))
            P.dma_custom("pool", "indirect_dma_start", reads=[bi], writes=[bv[s]], sem=bv[s],
                         out=vpg[s], out_offset=None, in_=R(cv_d), in_offset=bass.IndirectOffsetOnAxis(ap=idx[:, i:i + 1], axis=0))
            for h in range(8):
                P.op("pe", "transpose", PS[h // 4][:, (h % 4) * 128:(h % 4 + 1) * 128], kpg[s][:, h * 128:(h + 1) * 128], ident[:, :],
                     reads=[bk[s], b_ident], writes=[b_ps[h // 4]], partial=(h % 4 > 0))
            copy_op("act", ktp[s][:, 0:4, :], PS[0][:, :].rearrange("p (a b) -> p a b", a=4), reads=[b_ps[0]], writes=[bkt[s]])
            copy_op("dve", ktp[s][:, 4:8, :], PS[1][:, :].rearrange("p (a b) -> p a b", a=4), reads=[b_ps[1]], writes=[bkt[s]], partial=True)
            for h in range(8):
                P.op("pe", "matmul", s_ps[:64, 0:128], lhsT=qb[:, h, :], rhs=ktp[s][:, h, :], start=(h == 0), stop=(h == 7),
                     reads=[bq, bkt[s]], writes=[b_ps[7]], partial=(h > 0))
            P.op("pool", "tensor_scalar", tms[:64, :], stab[:, 8:136], float(128 * i), slopecol[:, 0:1], op0=ALU.add, op1=ALU.mult,
                 reads=[b_par], writes=[bt])
            P.op("dve", "tensor_tensor", tms[:64, :], tms[:64, :], s_ps[:64, 0:128], op=ALU.add, reads=[bt, b_ps[7]], writes=[bt])
            P.op("act", "activation", psb[:64, :], tms[:64, :], AF.Exp, reads=[bt], writes=[bp])
            P.op("pe", "transpose", t_ps[:, 0:64], psb[:64, :], ident[:64, :64], reads=[bp, b_ident], writes=[b_ps[4]])
            copy_op("dve", ptp[s], t_ps[:, 0:64], reads=[b_ps[4]], writes=[bpt[s]])
            for h in range(8):
                P.op("pe", "matmul", o_ps[:, h * 8:(h + 1) * 8], lhsT=vpg[s][:, h * 128:(h + 1) * 128], rhs=ptp[s][:, h * 8:(h + 1) * 8],
                     start=(i == 0), stop=False, reads=[bv[s], bpt[s]], writes=[b_ps[2]], partial=True)
            P.op("pe", "matmul", d_ps[:, 0:64], lhsT=ones_r[:], rhs=ptp[s], start=(i == 0), stop=False,
                 reads=[b_ones, bpt[s]], writes=[b_ps[5]], partial=True)
        for h in range(8):
            P.op("pe", "matmul", s_ps[:64, 0:4], lhsT=qb[:, h, :], rhs=R(ksamp[:, h, :]), start=(h == 0), stop=(h == 7),
                 reads=[bq, b_ks], writes=[b_ps[7]], partial=(h > 0))
        P.op("dve", "scalar_tensor_tensor", tms[:64, 0:4], stab[:, 0:4], slopecol[:, 0:1], stab[:, 4:8], op0=ALU.mult, op1=ALU.add,
             reads=[b_par], writes=[bt])
        P.op("dve", "tensor_tensor", tms[:64, 0:4], tms[:64, 0:4], s_ps[:64, 0:4], op=ALU.add, reads=[bt, b_ps[7]], writes=[bt])
        P.op("act", "activation", psb[:64, 0:4], tms[:64, 0:4], AF.Exp, reads=[bt], writes=[bp])
        P.op("pe", "transpose", t_ps[:4, 0:64], psb[:64, 0:4], ident[:64, :64], reads=[bp, b_ident], writes=[b_ps[4]])
        copy_op("dve", ptp[0][:4, :], t_ps[:4, 0:64], reads=[b_ps[4]], writes=[bpt[0]])
        for h in range(8):
            P.op("pe", "matmul", o_ps[:, h * 8:(h + 1) * 8], lhsT=vst[:4, h * 128:(h + 1) * 128], rhs=ptp[0][:4, h * 8:(h + 1) * 8],
                 start=False, stop=True, reads=[bvs, bpt[0]], writes=[b_ps[2]], partial=True)
        P.op("pe", "matmul", d_ps[:, 0:64], lhsT=ones_r[:4, :], rhs=ptp[0][:4, :], start=False, stop=True,
             reads=[b_ones, bpt[0]], writes=[b_ps[5]], partial=True)
        P.op("dve", "reciprocal", oms, d_ps[:, 0:64], reads=[b_ps[5]], writes=[bm])
        P.op("dve", "tensor_tensor", oms, oms, o_ps[:, 0:64], op=ALU.mult, reads=[bm, b_ps[2]], writes=[bm])
        om4 = oms.rearrange("p (h m q) -> p h m q", h=8, m=2)
        od3 = ods.rearrange("p (h q) -> p h q", h=8)
        P.op("dve", "scalar_tensor_tensor", od3, om4[:, :, 1, :], neglam, om4[:, :, 0, :], op0=ALU.mult, op1=ALU.add, reads=[bm, b_par], writes=[bm], partial=True)
        P.op("act", "activation", sqs, ods, AF.Square, reads=[bm], writes=[b_smr[1]])
        P.op("pe", "matmul", PS[4][:, 0:32], lhsT=ones_r[:], rhs=sqs, start=True, stop=True, reads=[b_smr[1], b_ones], writes=[b_ps[4]])
        P.op("act", "activation", rss, PS[4][:, 0:32], AF.Sqrt, bias=EPS, scale=1.0 / 128, reads=[b_ps[4]], writes=[bm], partial=True)
        P.op("dve", "reciprocal", rss, rss, reads=[bm], writes=[bm], partial=True)
        P.op("dve", "scalar_tensor_tensor", R(oc3[:, 0:8, npr:npr + 4]), od3, subg2, rss.rearrange("p (h q) -> p h q", h=8), op0=ALU.mult, op1=ALU.mult,
             reads=[bm, b_par], writes=list(b_oc[0:8]), partial=True)
        P.barrier()

    def ssm(u3, b_u, y3, b_y, npr, ns):
        SB = A1 + 16 * W
        o = SB
        def T(nel):
            nonlocal o
            v = arena[:, o:o + nel]; o += nel
            return v
        TSZ = npr // 4
        TS = TSZ // 4
        scan = arenaF[:, 0:3328].rearrange("g (t s) -> g t s", s=128)
        bd = [R(T(1024)) for _ in range(2)]; bus = [arenaF[:, 3328:4352]] * 2; htm = [arenaF[:, 4352:5376]] * 2
        hT = [R(T(1024)).rearrange("p (a b) -> p a b", a=8) for _ in range(2)]; ytm = arenaF[:, 5376:7424]
        assert o <= ARENA, o
        b_scan = Buf("scan"); b_bdb = [Buf("bd0"), Buf("bd1")]; b_bus = [Buf("bus0")] * 2; b_htm = [Buf("htm0")] * 2
        b_hT = [Buf("hT0"), Buf("hT1")]; b_ytm = Buf("ytm")
        b_t1 = Buf("t1"); b_t2 = Buf("t2")
        tiles = [(i * TSZ, TSZ, False) for i in range(4)] + ([(npr, ns, True)] if ns else [])
        k = 0
        for f in range(16):
            sb_ = f % 2
            P.dma("sp", bd[sb_], R(BDscr[f]), reads=[b_bd], writes=[b_bdb[sb_]], sem=b_bdb[sb_])
            for (t0, nt, _) in tiles:
                s = k % 2; k += 1
                for hf in range(2):
                    P.op("pe", "matmul", PS[hf][:nt, :], lhsT=R(u3[:, f, t0:t0 + nt]), rhs=bd[sb_][:, hf * 512:(hf + 1) * 512], start=True, stop=True,
                         reads=[b_u[f], b_bdb[sb_]], writes=[b_ps[hf]])
                    copy_op("act" if hf == 0 else "dve", bus[s][:nt, hf * 512:(hf + 1) * 512], PS[hf][:nt, :], reads=[b_ps[hf]], writes=[b_bus[s]], partial=(hf > 0))
                P.dma("pool", BuScr[t0:t0 + nt, f * 8:(f + 1) * 8, :], bus[s][:nt, :].rearrange("t (g s) -> t g s", g=8),
                      reads=[b_bus[s]], writes=[b_buscr], sem=b_bus[s], partial=True)
        X = XY[:, 0:128]; Yn = XY[:, 128:192]; Yp = XY[:, 192:256]
        for (t0, nt, is_s) in tiles:
            St, b_St = (S_s, b_Ss) if is_s else (S_p, b_Sp)
            subs = [(t0, nt)] if is_s else [(t0 + i * TS, TS) for i in range(4)]
            for (ts0, tn) in subs:
                P.dma("pool", scan[:, 0:tn, :], BuScr[ts0:ts0 + tn].rearrange("t g s -> g t s"), reads=[b_buscr], writes=[b_scan], sem=b_scan)
                for t in range(tn):
                    prev = St[:] if t == 0 else scan[:, t - 1, :]
                    pb = [b_St] if t == 0 else [b_scan]
                    cur = scan[:, t, :]
                    t1 = sm(0, 128); t2 = sm(1, 128)
                    P.op("dve", "tensor_tensor", t1, prev, X, op=ALU.mult, reads=pb + [b_xy], writes=[b_t1])
                    P.op("dve", "tensor_tensor", t2[:, 0:64], prev[:, 64:128], Yn, op=ALU.mult, reads=pb + [b_xy], writes=[b_t2])
                    P.op("dve", "tensor_tensor", t2[:, 64:128], prev[:, 0:64], Yp, op=ALU.mult, reads=pb + [b_xy], writes=[b_t2], partial=True)
                    P.op("dve", "tensor_tensor", cur, cur, t1, op=ALU.add, reads=[b_t1, b_scan], writes=[b_scan], partial=True)
                    P.op("dve", "tensor_tensor", cur, cur, t2, op=ALU.add, reads=[b_t2, b_scan], writes=[b_scan], partial=True)
                P.op("dve", "tensor_copy", St[:], scan[:, tn - 1, :], reads=[b_scan], writes=[b_St])
                P.dma("pool", HScr[ts0:ts0 + tn].rearrange("t g s -> g t s"), scan[:, 0:tn, :], reads=[b_scan], writes=[b_hscr], sem=b_scan, partial=True)
        k = 0
        for (t0, nt, _) in tiles:
            for f in range(16):
                s = k % 2; k += 1
                P.dma("sp", htm[s][:nt, :], HScr[t0:t0 + nt, f * 8:(f + 1) * 8, :].rearrange("t g s -> t (g s)"), reads=[b_hscr], writes=[b_htm[s]], sem=b_htm[s])
                for gl in range(8):
                    P.op("pe", "transpose", PS[2 + gl // 4][:, (gl % 4) * 128:(gl % 4) * 128 + nt], htm[s][:nt, gl * 128:(gl + 1) * 128], ident[:nt, :nt],
                         reads=[b_htm[s], b_ident], writes=[b_ps[2 + gl // 4]], partial=(gl % 4 > 0))
                copy_op("act", hT[s][:, 0:4, :nt], PS[2][:, :].rearrange("p (a b) -> p a b", a=4)[:, :, :nt], reads=[b_ps[2]], writes=[b_hT[s]])
                copy_op("dve", hT[s][:, 4:8, :nt], PS[3][:, :].rearrange("p (a b) -> p a b", a=4)[:, :, :nt], reads=[b_ps[3]], writes=[b_hT[s]], partial=True)
                for gl in range(8):
                    P.op("pe", "matmul", PS[4 + f // 4][:nt, (f % 4) * 128 + gl * 16:(f % 4) * 128 + gl * 16 + 16], lhsT=hT[s][:, gl, :nt], rhs=ccat[:, f * 8 + gl, :],
                         start=True, stop=True, reads=[b_hT[s], b_ccat], writes=[b_ps[4 + f // 4]], partial=True)
            for bq_ in range(4):
                copy_op("act" if bq_ % 2 == 0 else "dve", ytm[:nt, bq_ * 512:(bq_ + 1) * 512], PS[4 + bq_][:nt, :], reads=[b_ps[4 + bq_]], writes=[b_ytm], partial=(bq_ > 0))
            for fg in range(4):
                pb_ = fg % 2
                for kk in range(4):
                    f = fg * 4 + kk
                    P.op("pe", "transpose", PS[pb_][:, kk * 128:kk * 128 + nt], ytm[:nt, f * 128:(f + 1) * 128], ident[:nt, :nt],
                         reads=[b_ytm, b_ident], writes=[b_ps[pb_]], partial=(kk > 0))
                for kk in range(4):
                    f = fg * 4 + kk
                    P.op("dve", "scalar_tensor_tensor", y3[:, f, t0:t0 + nt], u3[:, f, t0:t0 + nt], dfm[:, f:f + 1], PS[pb_][:, kk * 128:kk * 128 + nt],
                         op0=ALU.mult, op1=ALU.add, reads=[b_u[f], b_ps[pb_], b_par], writes=[b_y[f]], partial=True)


    for ci, (c0, c1) in enumerate(CHUNKS):
        n = c1 - c0
        npr = min(c1, NPR) - c0
        ns = n - npr
        ttiles = [(t0, min(128, n - t0)) for t0 in range(0, n, 128)]

        xst = [arenaF[:, 0:2048], arenaF[:, 0:2048]]
        b_xst = [b_AF[0], b_AF[0]]
        for ti, (t0, nt) in enumerate(ttiles):
            s = ti % 2
            P.dma("sp", xst[s][:nt, :], xin[c0 + t0:c0 + t0 + nt, :], writes=[b_xst[s]], sem=b_xst[s])
            for fg in range(4):
                pb = 2 + (fg % 2)
                for k in range(4):
                    ft = fg * 4 + k
                    P.op("pe", "transpose", PS[pb][:, k * 128:k * 128 + nt], xst[s][:nt, ft * 128:(ft + 1) * 128], ident[:nt, :nt],
                         reads=[b_xst[s], b_ident], writes=[b_ps[pb]], partial=(k > 0))
                eng = ev_eng()
                copy_op(eng, x_fm[:, fg * 4:fg * 4 + 4, t0:t0 + nt], PS[pb][:].rearrange("p (a b) -> p a b", a=4)[:, :, :nt],
                        reads=[b_ps[pb]], writes=[b_x[fg * 4 + k] for k in range(4)], partial=True)
        P.barrier()

        hn3 = av(A0, 16, W); b_hn = b_A0
        pre_norm(hn3, b_hn, 0, 0, n)
        q3 = av(A1, 8, W); b_q = b_A1[0:8]
        GW = 30 + W
        gext = av(A1 + 8 * W, 8, GW); b_g = b_A1[8:16]
        conv3 = arenaF[:, 3072:3072 + 8 * W].rearrange("p (a b) -> p a b", a=8); b_cv = b_A1[16:24]
        OFFX = A1 + 8 * W + 8 * GW
        gs_ext = av(OFFX, 8, 34); b_gs = b_A1[24]
        ksamp = av(OFFX + 272, 8, 4); b_ks = b_A1[25]
        OFFX2 = OFFX + 272 + 32
        for j in range(8):
            if ci == 0:
                P.op("dve", "tensor_copy", gext[:, j, 0:30], zeros[:, 0:30], reads=[b_zero], writes=[b_g[j]])
            else:
                P.op("dve", "tensor_copy", gext[:, j, 0:30], gtail[:, j, :], reads=[b_gt], writes=[b_g[j]])
        if ns:
            scst = arenaF[:, 2048:3072]; b_sc = b_AF[1]
            P.dma("sp", scst[:30, :], sconv_d, writes=[b_sc], sem=b_sc)
            for j in range(8):
                P.op("pe", "transpose", PS[2][:, j * 32:j * 32 + 30], scst[:30, j * 128:(j + 1) * 128], ident[:30, :30],
                     reads=[b_sc, b_ident], writes=[b_ps[2]], partial=(j > 0))
            copy_op("dve", gs_ext[:, :, 0:30], PS[2][:, 0:256].rearrange("p (a b) -> p a b", a=8)[:, :, 0:30], reads=[b_ps[2]], writes=[b_gs], partial=True)

        def kv_out(which, h, src, b_src):
            dst = kout if which == "k" else vout
            for ti, (t0, nt) in enumerate(ttiles):
                P.op("pe", "transpose", PS[3][:nt, ti * 128:(ti + 1) * 128], src[:, t0:t0 + nt], ident[:, :],
                     reads=[b_src, b_ident], writes=[b_ps[3]], partial=(ti > 0))
            sl = 2 + (state["evac"] % 2)
            stg = arenaF[:, 3072 + (sl - 2) * 512: 3072 + (sl - 1) * 512]
            copy_op(ev_eng(), stg, PS[3][:, :], reads=[b_ps[3]], writes=[b_AF[sl]])
            for ti, (t0, nt) in enumerate(ttiles):
                P.dma("pool", dst[c0 + t0:c0 + t0 + nt, h * 128:(h + 1) * 128], stg[:nt, ti * 128:(ti + 1) * 128],
                      reads=[b_AF[sl]], writes=[b_out], sem=b_AF[sl], partial=True)
                if which == "v":
                    P.dma("pool", Vscr[c0 + t0:c0 + t0 + nt, h * 128:(h + 1) * 128], stg[:nt, ti * 128:(ti + 1) * 128],
                          reads=[b_AF[sl]], writes=[b_vscr], sem=b_AF[sl], partial=True)

        def evac_in_even(col0, ps, b_p):
            t = col0 // 128
            if t < 8:
                P.op("act", "activation", R(q3[:, t, :n]), ps, AF.Copy, scale=0.125, reads=[b_p], writes=[b_q[t]])
            elif t < 24:
                which = "k" if t < 16 else "v"
                h = t % 8
                sl = state["evac"] % 2
                tmp = sm(sl, n)
                copy_op(ev_eng(), tmp, ps, reads=[b_p], writes=[b_small[sl]])
                if which == "k":
                    P.dma("pool", KTscr[h, :, c0:c0 + npr], tmp[:, :npr], reads=[b_small[sl]], writes=[b_ktscr], sem=b_small[sl], partial=True)
                    if ns:
                        P.op("dve", "tensor_copy", R(ksamp[:, h, :]), tmp[:, npr:n], reads=[b_small[sl]], writes=[b_ks], partial=True)
                kv_out(which, h, tmp, b_small[sl])
            elif t < 32:
                P.op("act", "copy", sm(4, n), ps, reads=[b_p], writes=[b_small[4]])
            else:
                j = t - 32
                P.op("act", "activation", sm(5, n), ps, AF.Sigmoid, reads=[b_p], writes=[b_small[5]])
                P.op("dve", "tensor_tensor", gext[:, j, 30:30 + npr], sm(4, n)[:, :npr], sm(5, n)[:, :npr], op=ALU.mult,
                     reads=[b_small[4], b_small[5]], writes=[b_g[j]], partial=True)
                if ns:
                    P.op("dve", "tensor_tensor", gs_ext[:, j, 30:34], sm(4, n)[:, npr:n], sm(5, n)[:, npr:n], op=ALU.mult,
                         reads=[b_small[4], b_small[5]], writes=[b_gs], partial=True)

        blocks = blocks_range(0, 3072) + [c for j in range(8) for c in (3072 + 128 * j, 4096 + 128 * j)]
        linear(w_in_even, 16, blocks, lambda kt: R(hn3[:, kt, :n]), b_hn, n, evac_in_even)
        P.barrier()
        if stop_after == "inproj":
            break

        oc3 = av(A0, 16, W); b_oc = b_A0
        for j in range(8):
            eng = "dve"
            segs = [(gext, b_g[j], 0, npr)] + ([(gs_ext, b_gs, npr, ns)] if ns else [])
            for (gsrc, b_src, o0, ln) in segs:
                P.op(eng, "tensor_scalar", conv3[:, j, o0:o0 + ln], gsrc[:, j, 0:ln], cw[:, j, 0:1], cvec[:, 0, j:j + 1], op0=ALU.mult, op1=ALU.add,
                     reads=[b_src, b_par], writes=[b_cv[j]], partial=True)
                for w in range(1, 31):
                    P.op(eng, "scalar_tensor_tensor", conv3[:, j, o0:o0 + ln], gsrc[:, j, w:w + ln], cw[:, j, w:w + 1], conv3[:, j, o0:o0 + ln], op0=ALU.mult, op1=ALU.add,
                         reads=[b_src, b_par, b_cv[j]], writes=[b_cv[j]], partial=True)
        if ns:
            for (gsrc, b_src, lo, dsto) in [(gext, None, npr, convp_o), (gs_ext, b_gs, 4, convs_o)]:
                for j in range(8):
                    bs = b_g[j] if b_src is None else b_src
                    P.op("pe", "transpose", PS[2 + j // 4][:30, (j % 4) * 128:(j % 4) * 128 + 128], gsrc[:, j, lo:lo + 30], ident[:, :],
                         reads=[bs, b_ident], writes=[b_ps[2 + j // 4]], partial=(j % 4 > 0))
                stg = arenaF[:, 2048:3072]
                copy_op("act", stg[:30, 0:512], PS[2][:30, :], reads=[b_ps[2]], writes=[b_AF[1]])
                copy_op("dve", stg[:30, 512:1024], PS[3][:30, :], reads=[b_ps[3]], writes=[b_AF[1]], partial=True)
                P.dma("pool", dsto, stg[:30, :], reads=[b_AF[1]], writes=[b_out], sem=b_AF[1], partial=True)
        else:
            pass
        for j in range(8):
            sl = j % 2
            P.op("act", "copy", smr(sl, n), conv3[:, j, :n], reads=[b_cv[j]], writes=[b_smr[sl]])
            P.op("pe", "matmul", PS[4][:, :n], lhsT=ones_r[:], rhs=smr(sl, n), start=(j == 0), stop=(j == 7),
                 reads=[b_smr[sl], b_ones], writes=[b_ps[4]], partial=(j > 0))
        for j in range(8):
            sl = j % 2
            P.op("act", "activation", smr(sl, n), conv3[:, j, :n], AF.Square, reads=[b_cv[j]], writes=[b_smr[sl]])
            P.op("pe", "matmul", PS[5][:, :n], lhsT=ones_r[:], rhs=smr(sl, n), start=(j == 0), stop=(j == 7),
                 reads=[b_smr[sl], b_ones], writes=[b_ps[5]], partial=(j > 0))
        mean = sm(0, n); rstd = sm(1, n); msq = sm(2, n)
        P.op("dve", "tensor_scalar", mean, PS[4][:, :n], 1.0 / 1024, None, op0=ALU.mult, reads=[b_ps[4]], writes=[b_small[0]])
        P.op("dve", "tensor_tensor", msq, mean, mean, op=ALU.mult, reads=[b_small[0]], writes=[b_small[2]])
        P.op("dve", "scalar_tensor_tensor", rstd, PS[5][:, :n], 1.0 / 1024, msq, op0=ALU.mult, op1=ALU.subtract, reads=[b_ps[5], b_small[2]], writes=[b_small[1]])
        P.op("act", "activation", rstd, rstd, AF.Sqrt, bias=EPS, scale=1.0, reads=[b_small[1]], writes=[b_small[1]])
        P.op("dve", "reciprocal", rstd, rstd, reads=[b_small[1]], writes=[b_small[1]])
        for j in range(8):
            eng = "dve" if j % 2 == 0 else "pool"
            P.op(eng, "tensor_tensor", conv3[:, j, :n], conv3[:, j, :n], mean, op=ALU.subtract, reads=[b_cv[j], b_small[0]], writes=[b_cv[j]])
            P.op(eng, "tensor_tensor", conv3[:, j, :n], conv3[:, j, :n], rstd, op=ALU.mult, reads=[b_cv[j], b_small[1]], writes=[b_cv[j]])
            P.op("act", "activation", R(oc3[:, 8 + j, :n]), conv3[:, j, :n], AF.Silu, bias=cvec[:, 2, j:j + 1], scale=cvec[:, 1, j:j + 1],
                 reads=[b_cv[j], b_par], writes=[b_oc[8 + j]])
        if not ns:
            for j in range(8):
                P.op("pool", "tensor_copy", gtail[:, j, :], gext[:, j, npr:npr + 30], reads=[b_g[j]], writes=[b_gt], partial=(j > 0))
        if stop_after == "conv":
            break

        AT = OFFX2
        kend = c0 + npr
        nkt = (kend + 127) // 128
        ktb = [arena[:, AT + i * NPR: AT + (i + 1) * NPR].bitcast(F32R) for i in range(2)]; b_ktb = b_A1[27:29]
        VB0 = AT + 2 * NPR
        vb = [arena[:, VB0 + i * 17 * 128: VB0 + (i + 1) * 17 * 128].bitcast(F32R).rearrange("p (a b) -> p a b", a=17) for i in range(2)]; b_vb = b_A1[29:31]
        PB0 = VB0 + 2 * 17 * 128
        pex = [arena[:, PB0 + i * W: PB0 + (i + 1) * W] for i in range(3)]; b_pex = b_A1[31:34]
        tmpb = [arenaF[:, 6400 + i * W: 6400 + (i + 1) * W] for i in range(2)]; b_tmp = b_A1[34:36]
        ENDAT = PB0 + 3 * W
        assert ENDAT <= ARENA, ENDAT
        cnt = 0
        for h in range(8):
            s = h % 2
            P.dma("sp", ktb[s][:, 0:kend], R(KTscr[h, :, 0:kend]), reads=[b_ktscr], writes=[b_ktb[s]], sem=b_ktb[s])
            nfull = kend // 128
            if nfull:
                P.dma("sp", vb[s][:, 0:nfull, :], R(Vscr[0:nfull * 128, h * 128:(h + 1) * 128].rearrange("(a p) d -> p a d", p=128)),
                      reads=[b_vscr], writes=[b_vb[s]], sem=b_vb[s])
            if kend % 128:
                P.dma("sp", vb[s][:kend % 128, nfull, :], R(Vscr[nfull * 128:kend, h * 128:(h + 1) * 128]),
                      reads=[b_vscr], writes=[b_vb[s]], sem=b_vb[s], partial=True)
            for m in range(2):
                o_ps, d_ps = PS[2 + m], PS[5 + m]
                b_o, b_d = b_ps[2 + m], b_ps[5 + m]
                for kt in range(nkt):
                    k0 = kt * 128
                    kn = min(128, kend - k0)
                    sc = cnt % 2; pe_i = cnt % 3; tb = cnt % 2; cnt += 1
                    P.op("pe", "matmul", PS[sc][:kn, :npr], lhsT=ktb[s][64 * m:64 * m + 64, k0:k0 + kn],
                                                                                   rhs=R(q3[64 * m:64 * m + 64, h, :npr]), start=True, stop=True,
                         reads=[b_ktb[s], b_q[h]], writes=[b_ps[sc]])
                    P.op("dve", "scalar_tensor_tensor", tmpb[tb][:kn, :npr], d0tab[:kn, :npr], SLOPES[h], PS[sc][:kn, :npr], op0=ALU.mult, op1=ALU.add,
                         reads=[b_ps[sc], b_par], writes=[b_tmp[tb]])
                    if k0 + kn - 1 > c0:
                        P.op("dve", "tensor_scalar", sm(6, npr)[:kn, :], d0tab[:kn, :npr], float(k0 - c0), 0.0, op0=ALU.add, op1=ALU.is_gt,
                             reads=[b_par], writes=[b_small[6]])
                        P.op("dve", "scalar_tensor_tensor", tmpb[tb][:kn, :npr], sm(6, npr)[:kn, :], NEG, tmpb[tb][:kn, :npr], op0=ALU.mult, op1=ALU.add,
                             reads=[b_small[6], b_tmp[tb]], writes=[b_tmp[tb]])
                    P.op("act", "activation", R(pex[pe_i][:kn, :npr]), tmpb[tb][:kn, :npr], AF.Exp, bias=float(SLOPES[h] * (k0 - c0)), scale=1.0,
                         reads=[b_tmp[tb]], writes=[b_pex[pe_i]])
                    P.op("pe", "matmul", o_ps[:, :npr], lhsT=vb[s][:kn, kt, :], rhs=R(pex[pe_i][:kn, :npr]), start=(kt == 0), stop=(kt == nkt - 1),
                         reads=[b_vb[s], b_pex[pe_i]], writes=[b_o], partial=(kt > 0))
                    P.op("pe", "matmul", d_ps[:, :npr], lhsT=ones_r[:kn, :], rhs=R(pex[pe_i][:kn, :npr]), start=(kt == 0), stop=(kt == nkt - 1),
                         reads=[b_ones, b_pex[pe_i]], writes=[b_d], partial=(kt > 0))
                P.op("dve", "reciprocal", sm(7, npr), d_ps[:, :npr], reads=[b_d], writes=[b_small[7]])
                P.op("dve", "tensor_tensor", sm(2 + m, npr), o_ps[:, :npr], sm(7, npr), op=ALU.mult, reads=[b_o, b_small[7]], writes=[b_small[2 + m]])
            P.op("dve", "scalar_tensor_tensor", sm(2, npr), sm(3, npr), neglam, sm(2, npr), op0=ALU.mult, op1=ALU.add,
                 reads=[b_small[2], b_small[3], b_par], writes=[b_small[2]])
            P.op("act", "activation", smr(0, npr), sm(2, npr), AF.Square, reads=[b_small[2]], writes=[b_smr[0]])
            P.op("pe", "matmul", PS[4][:, :npr], lhsT=ones_r[:], rhs=smr(0, npr), start=True, stop=True, reads=[b_smr[0], b_ones], writes=[b_ps[4]])
            P.op("act", "activation", sm(7, npr), PS[4][:, :npr], AF.Sqrt, bias=EPS, scale=1.0 / 128, reads=[b_ps[4]], writes=[b_small[7]])
            P.op("dve", "reciprocal", sm(7, npr), sm(7, npr), reads=[b_small[7]], writes=[b_small[7]])
            P.op("dve", "scalar_tensor_tensor", R(oc3[:, h, :npr]), sm(2, npr), subg2, sm(7, npr), op0=ALU.mult, op1=ALU.mult,
                 reads=[b_small[2], b_small[7], b_par], writes=[b_oc[h]], partial=True)

        if ns:
            sample_attention(q3, b_q, ksamp, b_ks, oc3, b_oc, npr, AT)
        if stop_after == "attn":
            dump(0, oc3[:, :, :], b_oc, ci)
            break
        P.barrier()
        dump(0, oc3[:, :, :], b_oc, ci)

        m3 = av(A1, 16, W); b_m = b_A1[0:16]

        def evac_m(col0, ps, b_p, m3=m3, b_m=b_m):
            t = col0 // 128
            copy_op(ev_eng(), m3[:, t, :n], ps, reads=[b_p], writes=[b_m[t]])
        linear(w_out_even, 16, blocks_range(0, D), lambda kt: R(oc3[:, kt, :n]), b_oc, n, evac_m)
        post_norm_residual(m3, b_m, 1, 0, n)
        P.barrier()
        dump(1, x_fm[:, :, :], b_x, ci)
        if stop_after == "mix0":
            break

        for layer in range(2):
            if layer == 1:
                pre_norm(hn3, b_hn, 0, 1, n)
                u3 = av(A1, 16, W); b_u = b_A1[0:16]

                def evac_u(col0, ps, b_p):
                    t = col0 // 128
                    copy_op(ev_eng(), R(u3[:, t, :n]), ps, reads=[b_p], writes=[b_u[t]])
                linear(w_in_odd, 16, blocks_range(0, D), lambda kt: R(hn3[:, kt, :n]), b_hn, n, evac_u)
                P.barrier()
                y3 = av(A0, 16, W); b_y = b_A0
                ssm(u3, b_u, y3, b_y, npr, ns)
                P.barrier()
                dump(3, y3[:, :, :], b_y, ci)
                for ft in range(16):
                    eng = "dve" if ft % 2 == 0 else "pool"
                    sl = ft % 2
                    yv = y3[:, ft, :n]
                    P.op(eng, "tensor_tensor", sm(sl, n), yv, yv, op=ALU.mult, reads=[b_y[ft]], writes=[b_small[sl]])
                    P.op(eng, "tensor_scalar", sm(sl, n), sm(sl, n), 0.044715, 1.0, op0=ALU.mult, op1=ALU.add, reads=[b_small[sl]], writes=[b_small[sl]])
                    P.op(eng, "tensor_tensor", sm(sl, n), sm(sl, n), yv, op=ALU.mult, reads=[b_small[sl], b_y[ft]], writes=[b_small[sl]])
                    P.op("act", "activation", sm(sl, n), sm(sl, n), AF.Sigmoid, scale=1.5957691216057308, reads=[b_small[sl]], writes=[b_small[sl]])
                    P.op(eng, "tensor_tensor", R(yv), yv, sm(sl, n), op=ALU.mult, reads=[b_small[sl], b_y[ft]], writes=[b_y[ft]])
                z3 = av(A1, 16, W); b_z = b_A1[0:16]

                def evac_z(col0, ps, b_p):
                    t = col0 // 128
                    sl = 2 + t % 2
                    P.op("act", "activation", sm(sl, n), ps, AF.Sigmoid, reads=[b_p], writes=[b_small[sl]])
                    P.op("dve", "tensor_tensor", R(z3[:, t, :n]), y3[:, t, :n], sm(sl, n), op=ALU.mult, reads=[b_small[sl], b_y[t]], writes=[b_z[t]])
                linear(w_glu, 16, blocks_range(0, D), lambda kt: R(y3[:, kt, :n]), b_y, n, evac_z)
                P.barrier()
                m3 = av(A0, 16, W); b_m = b_A0

                def evac_m1(col0, ps, b_p):
                    t = col0 // 128
                    copy_op(ev_eng(), m3[:, t, :n], ps, reads=[b_p], writes=[b_m[t]])
                linear(w_out_odd, 16, blocks_range(0, D), lambda kt: R(z3[:, kt, :n]), b_z, n, evac_m1)
                post_norm_residual(m3, b_m, 1, 1, n)
                P.barrier()
                dump(4, x_fm[:, :, :], b_x, ci)
                if stop_after == "mix1":
                    break
            pre_norm(hn3, b_hn, 2, layer, n)
            act3 = av(A1, 11, W); b_act = b_A1[0:11]
            yacc = av(A1 + 11 * W, 16, W); b_ya = b_A1[11:27]
            for qd in range(4):
                lo = 1408 * qd

                def evac_gate(col0, ps, b_p, lo=lo):
                    jj = (col0 - lo) // 128
                    P.op("act", "activation", act3[:, jj, :n], ps, AF.Silu, reads=[b_p], writes=[b_act[jj]])

                def evac_up(col0, ps, b_p, lo=lo):
                    jj = (col0 - lo) // 128
                    P.op("dve", "tensor_tensor", R(act3[:, jj, :n]), act3[:, jj, :n], ps, op=ALU.mult, reads=[b_p, b_act[jj]], writes=[b_act[jj]])
                linear(w_gate[layer], 16, blocks_range(lo, lo + 1408), lambda kt: R(hn3[:, kt, :n]), b_hn, n, evac_gate)
                linear(w_up[layer], 16, blocks_range(lo, lo + 1408), lambda kt: R(hn3[:, kt, :n]), b_hn, n, evac_up)

                def evac_down(col0, ps, b_p, qd=qd):
                    t = col0 // 128
                    if qd == 0:
                        copy_op(ev_eng(), yacc[:, t, :n], ps, reads=[b_p], writes=[b_ya[t]])
                    else:
                        P.op("dve", "tensor_tensor", yacc[:, t, :n], yacc[:, t, :n], ps, op=ALU.add, reads=[b_p, b_ya[t]], writes=[b_ya[t]])
                linear(w_down[layer][lo:lo + 1408, :], 11, blocks_range(0, D), lambda kt: R(act3[:, kt, :n]), b_act, n, evac_down)
            post_norm_residual(yacc, b_ya, 3, layer, n)
            P.barrier()
            dump(2 if layer == 0 else 5, x_fm[:, :, :], b_x, ci)
            if stop_after == f"ffn{layer}":
                break
        if stop_after is not None and stop_after != "chunk0":
            break

        yst = [arenaF[:, 0:2048], arenaF[:, 0:2048]]
        b_yst = [b_AF[0], b_AF[0]]
        for ti, (t0, nt) in enumerate(ttiles):
            s = ti % 2
            for fg in range(4):
                pb = 2 + (fg % 2)
                for k in range(4):
                    ft = fg * 4 + k
                    P.op("pe", "transpose", PS[pb][:nt, k * 128:(k + 1) * 128], x_fm[:, ft, t0:t0 + nt], ident[:, :],
                         reads=[b_x[ft], b_ident], writes=[b_ps[pb]], partial=(k > 0))
                copy_op(ev_eng(), yst[s][:nt, fg * 512:(fg + 1) * 512], PS[pb][:nt, :], reads=[b_ps[pb]], writes=[b_yst[s]], partial=(fg > 0))
            P.dma("pool", yout[c0 + t0:c0 + t0 + nt, :], yst[s][:nt, :], reads=[b_yst[s]], writes=[b_out], sem=b_yst[s], partial=True)
        P.barrier()
        if stop_after == "chunk0":
            break

    if stop_after is None:
        P.dma("pool", ssmp_o, S_p[:], reads=[b_Sp], writes=[b_out], sem=b_Sp, partial=True)
        P.dma("pool", ssms_o, S_s[:], reads=[b_Ss], writes=[b_out], sem=b_Ss, partial=True)
    P.barrier()
    P.emit()
    return nc


_PROG_CACHE = {}


def _host_inputs(inp):
    f32 = np.float32
    c = lambda a: np.ascontiguousarray(a, dtype=a.dtype)
    shared = {}
    shared["w_in_even"] = c(inp["w_in_even"][0]); shared["w_out_even"] = c(inp["w_out_even"][0])
    shared["w_in_odd"] = c(inp["w_in_odd"][0]); shared["w_glu"] = c(inp["w_glu"][0]); shared["w_out_odd"] = c(inp["w_out_odd"][0])
    shared["w_gate"] = c(inp["w_ffn_gate"]); shared["w_up"] = c(inp["w_ffn_up"]); shared["w_down"] = c(inp["w_ffn_down"])
    g = np.zeros((8, 16, 128), f32)
    for kind, nm in enumerate(["norm_mix_pre", "norm_mix_post", "norm_ffn_pre", "norm_ffn_post"]):
        for layer in range(2):
            g[kind * 2 + layer] = np.asarray(inp[nm][layer]).reshape(16, 128)
    shared["gains"] = c(g.transpose(2, 0, 1))
    shared["cw"] = c(np.asarray(inp["conv_w"][0]).reshape(31, 8, 128).transpose(2, 1, 0))
    cv = np.stack([np.asarray(inp[k][0]).reshape(8, 128) for k in ("conv_b", "conv_ln_g", "conv_ln_b")])
    shared["cvec"] = c(cv.transpose(2, 0, 1))
    shared["subg"] = c(np.asarray(inp["subln_g"][0]).reshape(128, 1))
    shared["lqk"] = c(np.concatenate([np.asarray(inp["lambda_q"][0]).ravel(), np.asarray(inp["lambda_k"][0]).ravel()]).reshape(1, 256))
    shared["a_re"] = c(inp["ssm_a_re"][0]); shared["a_im"] = c(inp["ssm_a_im"][0])
    shared["ldt"] = c(np.asarray(inp["ssm_log_dt"][0]).reshape(128, 1))
    shared["b_re"] = c(inp["ssm_b_re"][0]); shared["b_im"] = c(inp["ssm_b_im"][0])
    cre = np.asarray(inp["ssm_c_re"][0]).transpose(2, 0, 1); cim = np.asarray(inp["ssm_c_im"][0]).transpose(2, 0, 1)
    shared["ccat"] = c(np.concatenate([cre, cim], axis=0))
    shared["dfm"] = c(np.asarray(inp["ssm_d"][0]).reshape(16, 128).T)
    shared["cache_k"] = c(np.asarray(inp["cache_k"][0]).reshape(1280 * 128, 1024))
    shared["cache_v"] = c(np.asarray(inp["cache_v"][0]).reshape(1280 * 128, 1024))
    shared["iota"] = np.arange(128, dtype=np.int32).reshape(128, 1)
    shared["ident"] = np.eye(128, dtype=f32)
    shared["ones"] = np.ones((128, 128), f32)
    shared["d0tab"] = (np.arange(128, dtype=f32)[:, None] - np.arange(W, dtype=f32)[None, :]).astype(f32)
    stab = np.zeros((64, 136), f32); slopecol = np.zeros((64, 1), f32)
    for h in range(8):
        for m in range(2):
            for q in range(4):
                r = h * 8 + m * 4 + q
                stab[r, 0:4] = np.arange(4)
                stab[r, 4:8] = np.where(np.arange(4) > q, NEG, 0.0)
                stab[r, 8:136] = np.arange(128) - PAST
                slopecol[r, 0] = SLOPES[h]
    shared["stab"] = stab; shared["slopecol"] = slopecol
    maps = []
    meta = np.asarray(inp["meta_tokens"], f32)
    for i in range(8):
        b = i % 4
        d = dict(shared)
        d["xin"] = c(np.concatenate([meta, np.asarray(inp["x_prompt"][b]), np.asarray(inp["x_sample"][i])], axis=0))
        d["ptab"] = c(np.asarray(inp["page_table"][i], dtype=np.int32).reshape(1, 128))
        d["sconv"] = c(inp["state_conv"][0, i])
        d["sssm"] = c(np.concatenate([np.asarray(inp["state_ssm_re"][0, i]), np.asarray(inp["state_ssm_im"][0, i])], axis=1))
        maps.append(d)
    return maps


def _assemble(res):
    f32 = np.float32
    R_ = [r for r in res]
    y_prompt = np.stack([R_[b]["yout"][16:NPR] for b in range(4)]).astype(f32)
    y_sample = np.stack([R_[i]["yout"][NPR:NT] for i in range(8)]).astype(f32)
    k_prompt = np.stack([R_[b]["kout"][0:NPR].reshape(NPR, 8, 2, 64) for b in range(4)])[None].astype(f32)
    v_prompt = np.stack([R_[b]["vout"][0:NPR].reshape(NPR, 8, 128) for b in range(4)])[None].astype(f32)
    k_sample = np.stack([R_[i]["kout"][NPR:NT].reshape(4, 8, 2, 64) for i in range(8)])[None].astype(f32)
    v_sample = np.stack([R_[i]["vout"][NPR:NT].reshape(4, 8, 128) for i in range(8)])[None].astype(f32)
    conv_prompt = np.stack([R_[b]["convp"] for b in range(4)])[None].astype(f32)
    conv_sample = np.stack([R_[i]["convs"] for i in range(8)])[None].astype(f32)
    srp = np.stack([R_[b]["ssmp"][:, 0:64] for b in range(4)])[None].astype(f32)
    sip = np.stack([R_[b]["ssmp"][:, 64:128] for b in range(4)])[None].astype(f32)
    srs = np.stack([R_[i]["ssms"][:, 0:64] for i in range(8)])[None].astype(f32)
    sis = np.stack([R_[i]["ssms"][:, 64:128] for i in range(8)])[None].astype(f32)
    return (y_prompt, y_sample, k_prompt, v_prompt, k_sample, v_sample, conv_prompt, conv_sample, srp, sip, srs, sis)


def kernel(_stop_after=None, **inputs):
    maps = _host_inputs(inputs)
    nc = build_program(_stop_after)
    res = run_bass_kernel_spmd(nc, maps, core_ids=list(range(8)))
    return _assemble(res.results)
```

```python
import numpy as np
from contextlib import ExitStack
import concourse.bass as bass
import concourse.mybir as mybir

F32 = mybir.dt.float32
F32R = mybir.dt.float32r
I32 = mybir.dt.int32
AF = mybir.ActivationFunctionType
ALU = mybir.AluOpType
AX = mybir.AxisListType

PH = 4096


class Buf:
    __slots__ = ("name", "writers", "readers", "dcount", "semi", "_sw")

    def __init__(self, name):
        self.name = name
        self.writers = []
        self.readers = []
        self.dcount = 0
        self.semi = None
        self._sw = None

    def sw(self):
        if self._sw is None:
            self._sw = Buf(self.name + "_sw")
        return self._sw


class Prog:
    ENG = ("pe", "act", "dve", "pool", "sp")

    def __init__(self, nc):
        self.nc = nc
        self.ops = {e: [] for e in self.ENG}
        self.cnt = {e: 0 for e in self.ENG}
        self.waited = {e: {} for e in self.ENG}
        self.dbufs = []
        self.stack = ExitStack()
        self.retype = None
        self.NR = 20
        self.ring = {q: [Buf(f"ring_{q}{i}") for i in range(self.NR)] for q in ("sp", "pool", "act")}
        self.ringpos = {q: 0 for q in ("sp", "pool", "act")}
        self.stage = "init"
        self.labels = {e: [] for e in self.ENG}

    def sb(self, name, shape, dtype=F32):
        return self.stack.enter_context(self.nc.sbuf_tensor("sb_" + name, list(shape), dtype))

    def ps(self, name, shape, dtype=F32):
        return self.stack.enter_context(self.nc.psum_tensor(name, list(shape), dtype))

    def _need(self, eng, toks):
        best = {}
        cur = self.cnt[eng] + 1
        for t in toks:
            if t[0] == "E":
                _, e2, idx = t
                if e2 == eng and eng in ("pe", "sp"):
                    continue
                if e2 == eng and eng == "dve" and cur - idx >= 2:
                    continue
                key = ("E", e2, (idx - 1) // PH)
                val = (idx - 1) % PH + 1
            else:
                _, b, val = t
                key = ("D", id(b))
                if b.semi is None:
                    b.semi = len(self.dbufs)
                    self.dbufs.append(b)
            if best.get(key, (0,))[0] < val:
                best[key] = (val, t)
        out = []
        w = self.waited[eng]
        for key, (val, t) in best.items():
            if w.get(key, 0) >= val:
                continue
            w[key] = val
            out.append((key, val, t))
        return out

    def _deps(self, reads, writes):
        toks = []
        for b in reads:
            toks += b.writers
        for b in writes:
            toks += b.writers
            toks += b.readers
        return toks

    @staticmethod
    def _dedupe(lst):
        best = {}
        for t in lst:
            if t[0] == "E":
                key = ("E", t[1])
                v = t[2]
            else:
                key = ("D", id(t[1]))
                v = t[2]
            if key not in best or best[key][2] < v:
                best[key] = t
        return list(best.values())

    def _commit(self, tok, reads, writes, partial):
        for b in reads:
            b.readers = self._dedupe(b.readers + [tok])
        for b in writes:
            if partial:
                b.writers = self._dedupe(b.writers + [tok])
            else:
                b.writers = [tok]
                b.readers = []

    def op(self, eng, meth, *args, reads=(), writes=(), partial=False, **kw):
        if self.retype is not None and args:
            args = (self.retype(args[0]),) + tuple(args[1:])
        fn = lambda e, meth=meth, args=args, kw=kw: getattr(e, meth)(*args, **kw)
        waits = self._need(eng, self._deps(reads, writes))
        self.cnt[eng] += 1
        idx = self.cnt[eng]
        tok = ("E", eng, idx)
        self.ops[eng].append((waits, fn, ("E", eng, idx)))
        self.labels[eng].append(self.stage)
        self._commit(tok, reads, writes, partial)
        return tok

    def _ring_sem(self, q):
        rb = self.ring[q][self.ringpos[q] % self.NR]
        self.ringpos[q] += 1
        pre = [("D", rb, 16 * rb.dcount)] if rb.dcount > 0 else []
        return rb, pre

    def dma(self, q, out, in_, reads=(), writes=(), sem=None, partial=False, **kw):
        rb, pre = self._ring_sem(q)
        if self.retype is not None:
            out = self.retype(out)
            if out.dtype == F32R and in_.dtype == F32:
                in_ = in_.bitcast(F32R)
        waits = self._need(q, self._deps(reads, writes) + pre)
        rb.dcount += 1
        tok = ("D", rb, 16 * rb.dcount)
        fn = lambda e, out=out, in_=in_, kw=kw: e.dma_start(out=out, in_=in_, **kw)
        self.ops[q].append((waits, fn, ("D", rb)))
        self._commit(tok, reads, writes, partial)
        return tok

    def dma_custom(self, q, meth, reads=(), writes=(), sem=None, partial=False, **kw):
        fn = lambda e, meth=meth, kw=kw: getattr(e, meth)(**kw)
        rb, pre = self._ring_sem(q)
        waits = self._need(q, self._deps(reads, writes) + pre)
        rb.dcount += 1
        tok = ("D", rb, 16 * rb.dcount)
        self.ops[q].append((waits, fn, ("D", rb)))
        self._commit(tok, reads, writes, partial)
        return tok

    def barrier(self):
        toks = [("E", e, self.cnt[e]) for e in self.ENG if self.cnt[e] > 0]
        toks += [("D", b, 16 * b.dcount) for q in self.ring for b in self.ring[q] if b.dcount > 0]
        for e in self.ENG:
            waits = self._need(e, toks)
            if waits:
                self.ops[e].append((waits, None, None))

    def emit(self):
        nc = self.nc
        import bisect
        mset = {e: set() for e in self.ENG}
        for e in self.ENG:
            for waits, fn, inc in self.ops[e]:
                for key, val, tok in waits:
                    if tok[0] == "E":
                        mset[tok[1]].add(tok[2])
        mlist = {e: sorted(mset[e]) for e in self.ENG}
        nsem = {e: (len(mlist[e]) + PH - 1) // PH for e in self.ENG}
        esem = {e: [self.stack.enter_context(nc.semaphore(f"s_{e}{i}")) for i in range(nsem[e])]
                for e in self.ENG}
        dsem = [self.stack.enter_context(nc.semaphore(f"d_{i}")) for i in range(len(self.dbufs))]
        idmap = {id(b): i for i, b in enumerate(self.dbufs)}

        def rank(e, idx):
            r = bisect.bisect_left(mlist[e], idx)
            assert mlist[e][r] == idx
            return r

        def run(engname, eng):
            for waits, fn, inc in self.ops[engname]:
                for key, val, tok in waits:
                    if tok[0] == "E":
                        r = rank(tok[1], tok[2])
                        eng.wait_ge(esem[tok[1]][r // PH], r % PH + 1)
                    else:
                        eng.wait_ge(dsem[idmap[id(tok[1])]], val)
                if fn is None:
                    continue
                ins = fn(eng)
                if inc[0] == "E":
                    idx = inc[2]
                    if idx in mset[engname]:
                        r = rank(engname, idx)
                        ins.then_inc(esem[engname][r // PH], 1)
                else:
                    ins.then_inc(dsem[idmap[id(inc[1])]], 16)

        with nc.Block() as block:
            @block.tensor
            def _(e):
                run("pe", e)

            @block.scalar
            def _(e):
                run("act", e)

            @block.vector
            def _(e):
                run("dve", e)

            @block.gpsimd
            def _(e):
                run("pool", e)

            @block.sync
            def _(e):
                run("sp", e)
        self.stack.close()

import math
from concourse.bass_utils import run_bass_kernel_spmd

D = 2048
NT = 2068
NPR = 2064
CHUNKS = [(0, 416), (416, 832), (832, 1248), (1248, 1664), (1664, 2068)]
W = 416
DFF = 5632
EPS = 1e-6
LAM_INIT = 0.8 - 0.6 * math.exp(-0.3 * 0)
SLOPES = [2.0 ** (-8.0 * (i + 1) / 8) for i in range(8)]
PAST = 16384
NEG = -30000.0
STOP_AFTER = None


def build_program(stop_after=None):
    nc = bass.Bass("TRN2", target_bir_lowering=False)
    nc.dge_precook = False
    P = Prog(nc)

    def din(name, shape, dt=F32):
        return nc.dram_tensor(name, list(shape), dt, kind="ExternalInput").ap()

    def dout(name, shape, dt=F32):
        return nc.dram_tensor(name, list(shape), dt, kind="ExternalOutput").ap()

    def dscr(name, shape, dt=F32):
        return nc.dram_tensor(name, list(shape), dt).ap()

    xin = din("xin", [NT, D])
    w_in_even = din("w_in_even", [D, 5120]); w_out_even = din("w_out_even", [D, D])
    w_in_odd = din("w_in_odd", [D, D]); w_glu = din("w_glu", [D, D]); w_out_odd = din("w_out_odd", [D, D])
    w_gate = din("w_gate", [2, D, DFF]); w_up = din("w_up", [2, D, DFF]); w_down = din("w_down", [2, DFF, D])
    gains_d = din("gains", [128, 8, 16])
    cw_d = din("cw", [128, 8, 31]); cvec_d = din("cvec", [128, 3, 8])
    subg_d = din("subg", [128, 1]); lqk_d = din("lqk", [1, 256])
    a_re_d = din("a_re", [128, 64]); a_im_d = din("a_im", [128, 64]); ldt_d = din("ldt", [128, 1])
    b_re_d = din("b_re", [128, 64, 16]); b_im_d = din("b_im", [128, 64, 16])
    ccat_d = din("ccat", [128, 128, 16]); dfm_d = din("dfm", [128, 16])
    ck_d = din("cache_k", [1280 * 128, 1024]); cv_d = din("cache_v", [1280 * 128, 1024])
    pt_d = din("ptab", [1, 128], I32); iota_d = din("iota", [128, 1], I32)
    sconv_d = din("sconv", [30, 1024]); sssm_d = din("sssm", [128, 128])
    ident_d = din("ident", [128, 128]); ones_d = din("ones", [128, 128]); d0_d = din("d0tab", [128, W])
    stab_d = din("stab", [64, 8 + 128])
    slopecol_d = din("slopecol", [64, 1])

    yout = dout("yout", [NT, D]); kout = dout("kout", [NT, 1024]); vout = dout("vout", [NT, 1024])
    convp_o = dout("convp", [30, 1024]); convs_o = dout("convs", [30, 1024])
    ssmp_o = dout("ssmp", [128, 128]); ssms_o = dout("ssms", [128, 128])

    dbg_o = dout("dbg", [6, 128, 16 * W]) if stop_after is not None else None
    b_dbg = Buf("dbg")

    def dump(slot, ap3, bufs, ci):
        if ci != 0 or dbg_o is None:
            return
        P.barrier()
        P.dma("pool", dbg_o[slot].rearrange("p (a b) -> p a b", a=16), ap3, reads=list(bufs), writes=[b_dbg], sem=b_dbg, partial=True)
        P.barrier()

    KTscr = dscr("KTscr", [8, 128, NPR]); Vscr = dscr("Vscr", [NT, 1024])
    BDscr = dscr("BDscr", [16, 128, 1024])
    BuScr = dscr("BuScr", [W, 128, 128]); HScr = dscr("HScr", [W, 128, 128])

    ident = P.sb("ident", [128, 128]); b_ident = Buf("ident")
    ones_r = P.sb("ones_r", [128, 128], F32R); b_ones = Buf("ones")
    gains = P.sb("gains", [128, 8, 16]); cw = P.sb("cw", [128, 8, 31]); cvec = P.sb("cvec", [128, 3, 8])
    subg = P.sb("subg", [128, 1]); lamt = P.sb("lamt", [128, 8]); b_par = Buf("params")
    stab = P.sb("stab", [64, 136]); slopecol = P.sb("slopecol", [64, 1])
    dfm = P.sb("dfm", [128, 16])
    zeros = P.sb("zeros", [128, 128]); b_zero = Buf("zeros")
    b_sid = Buf("d0r")
    d0r = P.sb("d0r", [128, W], F32R)
    gtail = P.sb("gtail", [128, 8, 30]); b_gt = Buf("gtail")
    ccat = P.sb("ccat", [128, 128, 16], F32R); b_ccat = Buf("ccat")
    XY = P.sb("XY", [128, 256]); b_xy = Buf("xy")
    S_p = P.sb("S_p", [128, 128]); S_s = P.sb("S_s", [128, 128]); b_Sp = Buf("Sp"); b_Ss = Buf("Ss")
    x_fm = P.sb("x_fm", [128, 16, W]); b_x = [Buf(f"x{i}") for i in range(16)]
    NWB = 3
    wbuf = [P.sb(f"wbuf{i}", [128, 16, 128], F32R) for i in range(NWB)]; b_w = [Buf(f"w{i}") for i in range(NWB)]
    ARENA = 16 * W + 16928
    arenaR = P.sb("arena", [128, ARENA], F32R)
    arena = arenaR[:].bitcast(F32)
    AFSZ = 7424
    arenaF = P.sb("arenaF", [128, AFSZ])
    small = P.sb("small", [128, 7 * W]); b_small = [Buf(f"sm{i}") for i in range(7)]
    smallr = P.sb("smallr", [128, 2 * W], F32R); b_smr = [Buf("smr0"), Buf("smr1")]
    ptb_t = P.sb("ptb", [128, 128], I32); idx_t = P.sb("idx", [128, 128], I32); iot_t = P.sb("iot", [128, 1], I32)
    PS = [P.ps(f"ps{i}", [128, 512]) for i in range(8)]; b_ps = [Buf(f"ps{i}") for i in range(8)]

    def sm(i, n=W):
        return small[:, i * W:i * W + n]

    def smr(i, n=W):
        return smallr[:, i * W:i * W + n]

    def av(off, a, b):
        return arena[:, off:off + a * b].rearrange("p (a b) -> p a b", a=a)

    R = lambda ap: ap.bitcast(F32R)

    def _rt(ap):
        try:
            if ap.name == "sb_arena" and ap.dtype == F32:
                return ap.bitcast(F32R)
        except Exception:
            pass
        return ap
    P.retype = _rt
    b_AF = [Buf(f"AF{i}") for i in range(8)]

    state = {"wi": 0, "mm": 0, "evac": 0}

    def ev_eng():
        state["evac"] += 1
        return "act" if state["evac"] % 2 else "dve"

    def copy_op(eng, out, in_, reads, writes, partial=False):
        if eng == "act":
            return P.op("act", "copy", out, in_, reads=reads, writes=writes, partial=partial)
        return P.op(eng, "tensor_copy", out, in_, reads=reads, writes=writes, partial=partial)

    P.dma("sp", ident[:], ident_d, writes=[b_ident], sem=b_ident)
    P.dma("sp", ones_r[:], R(ones_d), writes=[b_ones], sem=b_ones)
    P.op("dve", "memset", zeros[:], 0.0, writes=[b_zero])
    P.dma("sp", d0r[:], R(d0_d), writes=[b_sid], sem=b_sid)
    d0tab = d0r[:].bitcast(F32)
    for dst, src in [(gains, gains_d), (cw, cw_d), (cvec, cvec_d), (subg, subg_d), (stab, stab_d),
                     (slopecol, slopecol_d), (dfm, dfm_d)]:
        P.dma("sp", dst[:], src, writes=[b_par], sem=b_par, partial=True)
    lq = sm(0, 256)
    P.dma("pool", lq, lqk_d.partition_broadcast(128), writes=[b_small[0]], sem=b_small[0])
    P.op("dve", "tensor_tensor", sm(1, 128), lq[:, 0:128], lq[:, 128:256], op=ALU.mult, reads=[b_small[0]], writes=[b_small[1]])
    P.op("dve", "tensor_reduce", lamt[:, 0:2], sm(1, 128).rearrange("p (a b) -> p a b", a=2), axis=AX.X, op=ALU.add,
         reads=[b_small[1]], writes=[b_par], partial=True)
    P.op("act", "activation", lamt[:, 2:4], lamt[:, 0:2], AF.Exp, reads=[b_par], writes=[b_par], partial=True)
    P.op("dve", "scalar_tensor_tensor", lamt[:, 4:5], lamt[:, 3:4], -LAM_INIT, lamt[:, 2:3], op0=ALU.add, op1=ALU.subtract,
         reads=[b_par], writes=[b_par], partial=True)
    P.op("dve", "tensor_scalar", lamt[:, 5:6], subg[:], 1.0 - LAM_INIT, None, op0=ALU.mult, reads=[b_par], writes=[b_par], partial=True)
    neglam = lamt[:, 4:5]; subg2 = lamt[:, 5:6]

    def gain(kind, layer, ft):
        return gains[:, kind * 2 + layer, ft:ft + 1]

    def ssm_prep():
        o = 0
        def T(n):
            nonlocal o
            v = arenaF[:, o:o + n]; o += n
            return v
        b_t = Buf("ssmprep")
        lr, li, dt_, zr, zi, er, cs, sn, mk, t1, t2, cr, ci, den = [T(64) for _ in range(14)]
        bre = T(1024); bim = T(1024); bbT = T(2048); zero = T(1024)
        dtc = T(1)
        rw = dict(reads=[b_t], writes=[b_t], partial=True)
        P.dma("sp", lr, a_re_d, writes=[b_t], sem=b_t, partial=True)
        P.dma("sp", li, a_im_d, writes=[b_t], sem=b_t, partial=True)
        P.dma("sp", dtc, ldt_d, writes=[b_t], sem=b_t, partial=True)
        P.dma("sp", bre, b_re_d.rearrange("g p c -> g (p c)"), writes=[b_t], sem=b_t, partial=True)
        P.dma("sp", bim, b_im_d.rearrange("g p c -> g (p c)"), writes=[b_t], sem=b_t, partial=True)
        P.dma("sp", ccat[:], R(ccat_d), writes=[b_ccat], sem=b_ccat)
        P.op("act", "activation", ccat[64:128], ccat[64:128].bitcast(F32), AF.Copy, scale=-1.0, reads=[b_ccat], writes=[b_ccat])
        P.op("act", "activation", dtc, dtc, AF.Exp, **rw)
        P.op("dve", "tensor_scalar", zr, lr, dtc, None, op0=ALU.mult, **rw)
        P.op("dve", "tensor_scalar", zi, li, dtc, None, op0=ALU.mult, **rw)
        P.op("act", "activation", er, zr, AF.Exp, **rw)
        P.op("dve", "tensor_copy", sn, zi, **rw)
        P.op("dve", "tensor_scalar", cs, zi, math.pi / 2, None, op0=ALU.add, **rw)
        for tgt in (sn, cs):
            for _ in range(7):
                P.op("dve", "tensor_scalar", mk, tgt, math.pi, None, op0=ALU.is_gt, **rw)
                P.op("dve", "scalar_tensor_tensor", tgt, mk, -2 * math.pi, tgt, op0=ALU.mult, op1=ALU.add, **rw)
            P.op("act", "activation", tgt, tgt, AF.Sin, **rw)
        P.op("dve", "tensor_tensor", XY[:, 0:64], er, cs, op=ALU.mult, reads=[b_t], writes=[b_xy], partial=True)
        P.op("dve", "tensor_copy", XY[:, 64:128], XY[:, 0:64], reads=[b_xy], writes=[b_xy], partial=True)
        P.op("dve", "tensor_tensor", XY[:, 192:256], er, sn, op=ALU.mult, reads=[b_t], writes=[b_xy], partial=True)
        P.op("dve", "tensor_scalar", XY[:, 128:192], XY[:, 192:256], -1.0, None, op0=ALU.mult, reads=[b_xy], writes=[b_xy], partial=True)
        ar = XY[:, 0:64]; ai = XY[:, 192:256]
        rx = dict(reads=[b_t, b_xy], writes=[b_t], partial=True)
        P.op("dve", "tensor_scalar", t1, ar, -1.0, None, op0=ALU.add, **rx)
        P.op("dve", "tensor_tensor", den, lr, lr, op=ALU.mult, **rw)
        P.op("dve", "tensor_tensor", t2, li, li, op=ALU.mult, **rw)
        P.op("dve", "tensor_tensor", den, den, t2, op=ALU.add, **rw)
        P.op("dve", "reciprocal", den, den, **rw)
        P.op("dve", "tensor_tensor", cr, t1, lr, op=ALU.mult, **rw)
        P.op("dve", "tensor_tensor", t2, ai, li, op=ALU.mult, **rx)
        P.op("dve", "tensor_tensor", cr, cr, t2, op=ALU.add, **rw)
        P.op("dve", "tensor_tensor", cr, cr, den, op=ALU.mult, **rw)
        P.op("dve", "tensor_tensor", ci, ai, lr, op=ALU.mult, **rx)
        P.op("dve", "tensor_tensor", t2, t1, li, op=ALU.mult, **rw)
        P.op("dve", "tensor_tensor", ci, ci, t2, op=ALU.subtract, **rw)
        P.op("dve", "tensor_tensor", ci, ci, den, op=ALU.mult, **rw)
        bre3 = bre.rearrange("g (p c) -> g p c", c=16); bim3 = bim.rearrange("g (p c) -> g p c", c=16)
        bb3 = bbT.rearrange("g (c s) -> g c s", c=16)
        for c in range(16):
            br = bre3[:, :, c]; bi = bim3[:, :, c]
            o_re = bb3[:, c, 0:64]; o_im = bb3[:, c, 64:128]
            P.op("dve", "tensor_tensor", o_re, cr, br, op=ALU.mult, **rw)
            P.op("dve", "tensor_tensor", t2, ci, bi, op=ALU.mult, **rw)
            P.op("dve", "tensor_tensor", o_re, o_re, t2, op=ALU.subtract, **rw)
            P.op("dve", "tensor_tensor", o_im, cr, bi, op=ALU.mult, **rw)
            P.op("dve", "tensor_tensor", t2, ci, br, op=ALU.mult, **rw)
            P.op("dve", "tensor_tensor", o_im, o_im, t2, op=ALU.add, **rw)
        P.op("dve", "memset", zero, 0.0, **rw)
        b_bd = Buf("bdscr")
        for f in range(16):
            P.dma("pool", BDscr[f], zero, reads=[b_t], writes=[b_bd], sem=b_t, partial=True)
        P.barrier()
        for g in range(128):
            f, gl = divmod(g, 8)
            P.dma("pool", BDscr[f, gl * 16:(gl + 1) * 16, gl * 128:(gl + 1) * 128].unsqueeze(0),
                  bb3[g:g + 1, :, :], reads=[b_t], writes=[b_bd], sem=b_t, partial=True)
        P.op("dve", "memset", S_p[:], 0.0, writes=[b_Sp])
        P.dma("sp", S_s[:], sssm_d, writes=[b_Ss], sem=b_Ss)
        P.barrier()
        return b_bd

    b_bd = ssm_prep()
    b_ktscr = Buf("ktscr"); b_vscr = Buf("vscr"); b_buscr = Buf("buscr"); b_hscr = Buf("hscr")
    b_out = Buf("outs")

    def rmsnorm_stats(src_fn, nt_tiles, n, denom, srcbufs):
        ssb = b_ps[4]
        for i in range(nt_tiles):
            sl = i % 2
            P.op("act", "activation", smr(sl, n), src_fn(i), AF.Square, reads=[srcbufs[i]], writes=[b_smr[sl]])
            P.op("pe", "matmul", PS[4][:, :n], lhsT=ones_r[:], rhs=smr(sl, n), start=(i == 0), stop=(i == nt_tiles - 1),
                 reads=[b_smr[sl], b_ones], writes=[ssb], partial=(i > 0))
        P.op("act", "activation", sm(6, n), PS[4][:, :n], AF.Sqrt, bias=EPS, scale=1.0 / denom, reads=[ssb], writes=[b_small[6]])
        P.op("dve", "reciprocal", sm(6, n), sm(6, n), reads=[b_small[6]], writes=[b_small[6]])
        return sm(6, n), b_small[6]

    def linear(Wap, KT, blocks, rhs_fn, rhs_bufs, n, evac):
        W3 = Wap.rearrange("(kt p) c -> p kt c", p=128)
        for c0 in blocks:
            wi = state["wi"] % NWB; state["wi"] += 1
            wb = wbuf[wi]
            P.dma("sp", wb[:, 0:KT, :], R(W3[:, :, c0:c0 + 128]), writes=[b_w[wi]], sem=b_w[wi])
            pi = state["mm"] % 2; state["mm"] += 1
            for kt in range(KT):
                P.op("pe", "matmul", PS[pi][:, :n], lhsT=wb[:, kt, :], rhs=rhs_fn(kt), start=(kt == 0), stop=(kt == KT - 1),
                     reads=[b_w[wi], rhs_bufs[kt]], writes=[b_ps[pi]], partial=(kt > 0))
            evac(c0, PS[pi][:, :n], b_ps[pi])

    def blocks_range(c_lo, c_hi):
        return list(range(c_lo, c_hi, 128))

    def post_norm_residual(m3, b_m, kind, layer, n):
        rstd, b_r = rmsnorm_stats(lambda i: m3[:, i, :n], 16, n, float(D), b_m)
        for ft in range(16):
            eng = "dve"
            P.op(eng, "scalar_tensor_tensor", m3[:, ft, :n], m3[:, ft, :n], gain(kind, layer, ft), rstd, op0=ALU.mult, op1=ALU.mult,
                 reads=[b_m[ft], b_r, b_par], writes=[b_m[ft]])
            P.op(eng, "tensor_tensor", x_fm[:, ft, :n], x_fm[:, ft, :n], m3[:, ft, :n], op=ALU.add,
                 reads=[b_m[ft], b_x[ft]], writes=[b_x[ft]])

    def pre_norm(hn3, b_hn, kind, layer, n):
        rstd, b_r = rmsnorm_stats(lambda i: x_fm[:, i, :n], 16, n, float(D), b_x)
        for ft in range(16):
            eng = "dve"
            P.op(eng, "scalar_tensor_tensor", R(hn3[:, ft, :n]), x_fm[:, ft, :n], gain(kind, layer, ft), rstd, op0=ALU.mult, op1=ALU.mult,
                 reads=[b_x[ft], b_r, b_par], writes=[b_hn[ft]])

    A0 = 0
    A1 = 16 * W
    SZ16 = 16 * W
    b_A0 = [Buf(f"A0_{i}") for i in range(16)]
    b_A1 = [Buf(f"A1_{i}") for i in range(40)]


    def sample_attention(q3, b_q, ksamp, b_ks, oc3, b_oc, npr, AT):
        P.barrier()
        o = AT
        def T(nel):
            nonlocal o
            v = arena[:, o:o + nel]; o += nel
            return v
        kpg = [T(1024) for _ in range(2)]; vpg = [R(T(1024)) for _ in range(2)]
        ktp = [R(T(1024)).rearrange("p (a b) -> p a b", a=8) for _ in range(2)]
        qblk = T(512); ptb = ptb_t[:]; idx = idx_t[:]; iot = iot_t[:]
        ptp = [R(T(64)) for _ in range(2)]; vst = R(T(1024))
        psb = arenaF[:, 0:128]; tms = arenaF[:, 128:256]; oms = arenaF[:, 256:320]; ods = arenaF[:, 320:352]; rss = arenaF[:, 352:384]
        sqs = smr(1, 32)
        assert o <= ARENA
        bk = [Buf("kpg0"), Buf("kpg1")]; bv = [Buf("vpg0"), Buf("vpg1")]; bkt = [Buf("ktp0"), Buf("ktp1")]
        bq = Buf("qblk"); bi = Buf("idx"); bp = Buf("psb"); bpt = [Buf("ptp0"), Buf("ptp1")]; bt = Buf("tms"); bvs = Buf("vst"); bm = Buf("misc")
        P.dma("pool", ptb, pt_d.partition_broadcast(128), writes=[bi], sem=bi)
        P.dma("sp", iot, iota_d, writes=[bi], sem=bi, partial=True)
        P.op("dve", "tensor_scalar", idx, ptb, 128, iot[:, 0:1], op0=ALU.mult, op1=ALU.add, reads=[bi], writes=[bi], partial=True)
        P.dma("sp", vst[:4, :], R(Vscr[NPR:NPR + 4, :]), reads=[b_vscr], writes=[bvs], sem=bvs)
        q4 = qblk.rearrange("p (h c) -> p h c", h=8)
        for qi in range(4):
            P.op("dve", "tensor_copy", R(qblk[:, qi * 128:(qi + 1) * 128]), zeros[:, 0:128], reads=[b_zero], writes=[bq], partial=(qi > 0))
        for h in range(8):
            for m in range(2):
                P.op("dve", "tensor_copy", R(q4[64 * m:64 * m + 64, h, h * 8 + m * 4:h * 8 + m * 4 + 4]), q3[64 * m:64 * m + 64, h, npr:npr + 4],
                     reads=[b_q[h]], writes=[bq], partial=True)
        qb = R(q4)
        o_ps, d_ps, s_ps, t_ps = PS[2], PS[5], PS[7], PS[4]
        for i in range(128):
            s = i % 2
            P.dma_custom("pool", "indirect_dma_start", reads=[bi], writes=[bk[s]], sem=bk[s],
                         out=R(kpg[s]), out_offset=None, in_=R(ck_d), in_offset=bass.IndirectOffsetOnAxis(ap=idx[:, i:i + 1], axis=0))
            P.dma_custom("pool", "indirect_dma_start", reads=[bi], writes=[bv[s]], sem=bv[s],
                         out=vpg[s], out_offset=None, in_=R(cv_d), in_offset=bass.IndirectOffsetOnAxis(ap=idx[:, i:i + 1], axis=0))
            for h in range(8):
                P.op("pe", "transpose", PS[h // 4][:, (h % 4) * 128:(h % 4 + 1) * 128], kpg[s][:, h * 128:(h + 1) * 128], ident[:, :],
                     reads=[bk[s], b_ident], writes=[b_ps[h // 4]], partial=(h % 4 > 0))
            copy_op("act", ktp[s][:, 0:4, :], PS[0][:, :].rearrange("p (a b) -> p a b", a=4), reads=[b_ps[0]], writes=[bkt[s]])
            copy_op("dve", ktp[s][:, 4:8, :], PS[1][:, :].rearrange("p (a b) -> p a b", a=4), reads=[b_ps[1]], writes=[bkt[s]], partial=True)
            for h in range(8):
                P.op("pe", "matmul", s_ps[:64, 0:128], lhsT=qb[:, h, :], rhs=ktp[s][:, h, :], start=(h == 0), stop=(h == 7),
                     reads=[bq, bkt[s]], writes=[b_ps[7]], partial=(h > 0))
            P.op("pool", "tensor_scalar", tms[:64, :], stab[:, 8:136], float(128 * i), slopecol[:, 0:1], op0=ALU.add, op1=ALU.mult,
                 reads=[b_par], writes=[bt])
            P.op("dve", "tensor_tensor", tms[:64, :], tms[:64, :], s_ps[:64, 0:128], op=ALU.add, reads=[bt, b_ps[7]], writes=[bt])
            P.op("act", "activation", psb[:64, :], tms[:64, :], AF.Exp, reads=[bt], writes=[bp])
            P.op("pe", "transpose", t_ps[:, 0:64], psb[:64, :], ident[:64, :64], reads=[bp, b_ident], writes=[b_ps[4]])
            copy_op("dve", ptp[s], t_ps[:, 0:64], reads=[b_ps[4]], writes=[bpt[s]])
            for h in range(8):
                P.op("pe", "matmul", o_ps[:, h * 8:(h + 1) * 8], lhsT=vpg[s][:, h * 128:(h + 1) * 128], rhs=ptp[s][:, h * 8:(h + 1) * 8],
                     start=(i == 0), stop=False, reads=[bv[s], bpt[s]], writes=[b_ps[2]], partial=True)
            P.op("pe", "matmul", d_ps[:, 0:64], lhsT=ones_r[:], rhs=ptp[s], start=(i == 0), stop=False,
                 reads=[b_ones, bpt[s]], writes=[b_ps[5]], partial=True)
        for h in range(8):
            P.op("pe", "matmul", s_ps[:64, 0:4], lhsT=qb[:, h, :], rhs=R(ksamp[:, h, :]), start=(h == 0), stop=(h == 7),
                 reads=[bq, b_ks], writes=[b_ps[7]], partial=(h > 0))
        P.op("dve", "scalar_tensor_tensor", tms[:64, 0:4], stab[:, 0:4], slopecol[:, 0:1], stab[:, 4:8], op0=ALU.mult, op1=ALU.add,
             reads=[b_par], writes=[bt])
        P.op("dve", "tensor_tensor", tms[:64, 0:4], tms[:64, 0:4], s_ps[:64, 0:4], op=ALU.add, reads=[bt, b_ps[7]], writes=[bt])
        P.op("act", "activation", psb[:64, 0:4], tms[:64, 0:4], AF.Exp, reads=[bt], writes=[bp])
        P.op("pe", "transpose", t_ps[:4, 0:64], psb[:64, 0:4], ident[:64, :64], reads=[bp, b_ident], writes=[b_ps[4]])
        copy_op("dve", ptp[0][:4, :], t_ps[:4, 0:64], reads=[b_ps[4]], writes=[bpt[0]])
        for h in range(8):
            P.op("pe", "matmul", o_ps[:, h * 8:(h + 1) * 8], lhsT=vst[:4, h * 128:(h + 1) * 128], rhs=ptp[0][:4, h * 8:(h + 1) * 8],
                 start=False, stop=True, reads=[bvs, bpt[0]], writes=[b_ps[2]], partial=True)
        P.op("pe", "matmul", d_ps[:, 0:64], lhsT=ones_r[:4, :], rhs=ptp[0][:4, :], start=False, stop=True,
             reads=[b_ones, bpt[0]], writes=[b_ps[5]], partial=True)
        P.op("dve", "reciprocal", oms, d_ps[:, 0:64], reads=[b_ps[5]], writes=[bm])
        P.op("dve", "tensor_tensor", oms, oms, o_ps[:, 0:64], op=ALU.mult, reads=[bm, b_ps[2]], writes=[bm])
        om4 = oms.rearrange("p (h m q) -> p h m q", h=8, m=2)
        od3 = ods.rearrange("p (h q) -> p h q", h=8)
        P.op("dve", "scalar_tensor_tensor", od3, om4[:, :, 1, :], neglam, om4[:, :, 0, :], op0=ALU.mult, op1=ALU.add, reads=[bm, b_par], writes=[bm], partial=True)
        P.op("act", "activation", sqs, ods, AF.Square, reads=[bm], writes=[b_smr[1]])
        P.op("pe", "matmul", PS[4][:, 0:32], lhsT=ones_r[:], rhs=sqs, start=True, stop=True, reads=[b_smr[1], b_ones], writes=[b_ps[4]])
        P.op("act", "activation", rss, PS[4][:, 0:32], AF.Sqrt, bias=EPS, scale=1.0 / 128, reads=[b_ps[4]], writes=[bm], partial=True)
        P.op("dve", "reciprocal", rss, rss, reads=[bm], writes=[bm], partial=True)
        P.op("dve", "scalar_tensor_tensor", R(oc3[:, 0:8, npr:npr + 4]), od3, subg2, rss.rearrange("p (h q) -> p h q", h=8), op0=ALU.mult, op1=ALU.mult,
             reads=[bm, b_par], writes=list(b_oc[0:8]), partial=True)
        P.barrier()

    rgn = [Buf(f"ssm_r{i}") for i in range(4)]

    def ssm(u3, b_u, y3, b_y, npr, ns):
        SB = A1 + 16 * W
        o = SB
        def T(nel):
            nonlocal o
            v = arena[:, o:o + nel]; o += nel
            return v
        TSZ = npr // 4
        TS = TSZ // 4
        scan = arenaF[:, 0:3328].rearrange("g (t s) -> g t s", s=128)
        bd = [R(T(1024)) for _ in range(2)]
        bus = [arenaF[:, 3328 + i * 1024: 4352 + i * 1024] for i in range(3)]; b_bus = rgn[0:3]
        htm = [arenaF[:, 3328 + i * 1024: 4352 + i * 1024] for i in range(2)]; b_htm = rgn[0:2]
        ytm = arenaF[:, 5376:7424]; b_ytm = [rgn[2], rgn[3]]
        hT = [R(T(1024)).rearrange("p (a b) -> p a b", a=8) for _ in range(2)]
        assert o <= ARENA, o
        b_scan = Buf("scan"); b_bdb = [Buf("bd0"), Buf("bd1")]
        b_hT = [Buf("hT0"), Buf("hT1")]
        b_t1 = [Buf("t1a"), Buf("t1b")]; b_t2 = [Buf("t2a"), Buf("t2b")]; b_s1 = [Buf("s1a"), Buf("s1b")]
        tiles = [(i * TSZ, TSZ, False) for i in range(4)] + ([(npr, ns, True)] if ns else [])
        P.stage = "ssm_A"
        k = 0
        for f in range(16):
            sb_ = f % 2
            P.dma("sp", bd[sb_], R(BDscr[f]), reads=[b_bd], writes=[b_bdb[sb_]], sem=b_bdb[sb_])
            for (t0, nt, _) in tiles:
                s = k % 3; k += 1
                for hf in range(2):
                    P.op("pe", "matmul", PS[hf][:nt, :], lhsT=R(u3[:, f, t0:t0 + nt]), rhs=bd[sb_][:, hf * 512:(hf + 1) * 512], start=True, stop=True,
                         reads=[b_u[f], b_bdb[sb_]], writes=[b_ps[hf]])
                    copy_op("act", bus[s][:nt, hf * 512:(hf + 1) * 512], PS[hf][:nt, :], reads=[b_ps[hf]], writes=[b_bus[s]], partial=(hf > 0))
                P.dma("pool", BuScr[t0:t0 + nt, f * 8:(f + 1) * 8, :], bus[s][:nt, :].rearrange("t (g s) -> t g s", g=8),
                      reads=[b_bus[s]], writes=[b_buscr], sem=b_bus[s], partial=True)
        X3 = XY[:, 0:128].rearrange("g (r p) -> g r p", r=2); Yn = XY[:, 128:192]; Yp = XY[:, 192:256]
        t1s = sm(0, 128).rearrange("g (r p) -> g r p", r=2); t2s = sm(1, 128).rearrange("g (r p) -> g r p", r=2)
        s1s = sm(2, 128).rearrange("g (r p) -> g r p", r=2)

        def scan_tile(t0, nt, is_s):
            P.stage = "ssm_scan"
            St, b_St = (S_s, b_Ss) if is_s else (S_p, b_Sp)
            subs = [(t0, nt)] if is_s else [(t0 + i * TS, TS) for i in range(4)]
            for (ts0, tn) in subs:
                P.dma("pool", scan[:, 0:tn, :], BuScr[ts0:ts0 + tn].rearrange("t g s -> g t s"), reads=[b_buscr], writes=[b_scan], sem=b_scan)
                for t in range(tn):
                    prev = St[:] if t == 0 else scan[:, t - 1, :]
                    pb = [b_St] if t == 0 else [b_scan]
                    pv = prev.rearrange("g (r p) -> g r p", r=2)
                    cv_ = scan[:, t, :].rearrange("g (r p) -> g r p", r=2)
                    hs = [(0, 32), (32, 64)]
                    for si, (a, b_) in enumerate(hs):
                        P.op("dve", "tensor_tensor", t1s[:, :, a:b_], pv[:, :, a:b_], X3[:, :, a:b_], op=ALU.mult, reads=pb + [b_xy], writes=[b_t1[si]])
                    for si, (a, b_) in enumerate(hs):
                        P.op("dve", "tensor_tensor", t2s[:, 0, a:b_], pv[:, 1, a:b_], Yn[:, a:b_], op=ALU.mult, reads=pb + [b_xy], writes=[b_t2[si]])
                    for si, (a, b_) in enumerate(hs):
                        P.op("dve", "tensor_tensor", t2s[:, 1, a:b_], pv[:, 0, a:b_], Yp[:, a:b_], op=ALU.mult, reads=pb + [b_xy], writes=[b_t2[si]], partial=True)
                    for si, (a, b_) in enumerate(hs):
                        P.op("dve", "tensor_tensor", s1s[:, :, a:b_], cv_[:, :, a:b_], t1s[:, :, a:b_], op=ALU.add, reads=[b_t1[si], b_scan], writes=[b_s1[si]])
                    for si, (a, b_) in enumerate(hs):
                        P.op("dve", "tensor_tensor", cv_[:, :, a:b_], s1s[:, :, a:b_], t2s[:, :, a:b_], op=ALU.add, reads=[b_t2[si], b_s1[si]], writes=[b_scan], partial=True)
                P.op("dve", "tensor_copy", St[:], scan[:, tn - 1, :], reads=[b_scan], writes=[b_St])
                P.dma("pool", HScr[ts0:ts0 + tn].rearrange("t g s -> g t s"), scan[:, 0:tn, :], reads=[b_scan], writes=[b_hscr], sem=b_scan, partial=True)

        kc = {"k": 0}

        def c_tile(t0, nt):
            P.stage = "ssm_C"
            for f in range(16):
                s = kc["k"] % 2; kc["k"] += 1
                P.dma("sp", htm[s][:nt, :], HScr[t0:t0 + nt, f * 8:(f + 1) * 8, :].rearrange("t g s -> t (g s)"), reads=[b_hscr], writes=[b_htm[s]], sem=b_htm[s])
                for gl in range(8):
                    P.op("pe", "transpose", PS[2 + gl // 4][:, (gl % 4) * 128:(gl % 4) * 128 + nt], htm[s][:nt, gl * 128:(gl + 1) * 128], ident[:nt, :nt],
                         reads=[b_htm[s], b_ident], writes=[b_ps[2 + gl // 4]], partial=(gl % 4 > 0))
                copy_op("act", hT[s][:, 0:4, :nt], PS[2][:, :].rearrange("p (a b) -> p a b", a=4)[:, :, :nt], reads=[b_ps[2]], writes=[b_hT[s]])
                copy_op("act", hT[s][:, 4:8, :nt], PS[3][:, :].rearrange("p (a b) -> p a b", a=4)[:, :, :nt], reads=[b_ps[3]], writes=[b_hT[s]], partial=True)
                for gl in range(8):
                    P.op("pe", "matmul", PS[4 + f // 4][:nt, (f % 4) * 128 + gl * 16:(f % 4) * 128 + gl * 16 + 16], lhsT=hT[s][:, gl, :nt], rhs=ccat[:, f * 8 + gl, :],
                         start=True, stop=True, reads=[b_hT[s], b_ccat], writes=[b_ps[4 + f // 4]], partial=True)
            for bq_ in range(4):
                copy_op("act", ytm[:nt, bq_ * 512:(bq_ + 1) * 512], PS[4 + bq_][:nt, :], reads=[b_ps[4 + bq_]], writes=b_ytm, partial=(bq_ > 0))
            for fg in range(4):
                pb_ = fg % 2
                for kk in range(4):
                    f = fg * 4 + kk
                    P.op("pe", "transpose", PS[pb_][:, kk * 128:kk * 128 + nt], ytm[:nt, f * 128:(f + 1) * 128], ident[:nt, :nt],
                         reads=b_ytm + [b_ident], writes=[b_ps[pb_]], partial=(kk > 0))
                for kk in range(4):
                    f = fg * 4 + kk
                    P.op("dve", "scalar_tensor_tensor", y3[:, f, t0:t0 + nt], u3[:, f, t0:t0 + nt], dfm[:, f:f + 1], PS[pb_][:, kk * 128:kk * 128 + nt],
                         op0=ALU.mult, op1=ALU.add, reads=[b_u[f], b_ps[pb_], b_par], writes=[b_y[f]], partial=True)

        prev_tile = None
        for (t0, nt, is_s) in tiles:
            scan_tile(t0, nt, is_s)
            if prev_tile is not None:
                c_tile(*prev_tile)
            prev_tile = (t0, nt)
        c_tile(*prev_tile)

    for ci, (c0, c1) in enumerate(CHUNKS):
        n = c1 - c0
        npr = min(c1, NPR) - c0
        ns = n - npr
        ttiles = [(t0, min(128, n - t0)) for t0 in range(0, n, 128)]

        P.stage = "xload"
        xst = [arenaF[:, 0:2048], arenaF[:, 0:2048]]
        b_xst = [b_AF[0], b_AF[0]]
        for ti, (t0, nt) in enumerate(ttiles):
            s = ti % 2
            P.dma("sp", xst[s][:nt, :], xin[c0 + t0:c0 + t0 + nt, :], writes=[b_xst[s]], sem=b_xst[s])
            for fg in range(4):
                pb = 2 + (fg % 2)
                for k in range(4):
                    ft = fg * 4 + k
                    P.op("pe", "transpose", PS[pb][:, k * 128:k * 128 + nt], xst[s][:nt, ft * 128:(ft + 1) * 128], ident[:nt, :nt],
                         reads=[b_xst[s], b_ident], writes=[b_ps[pb]], partial=(k > 0))
                eng = ev_eng()
                copy_op(eng, x_fm[:, fg * 4:fg * 4 + 4, t0:t0 + nt], PS[pb][:].rearrange("p (a b) -> p a b", a=4)[:, :, :nt],
                        reads=[b_ps[pb]], writes=[b_x[fg * 4 + k] for k in range(4)], partial=True)
        P.barrier()

        P.stage = "l0_inproj"
        hn3 = av(A0, 16, W); b_hn = b_A0
        pre_norm(hn3, b_hn, 0, 0, n)
        q3 = av(A1, 8, W); b_q = b_A1[0:8]
        GW = 30 + W
        gext = av(A1 + 8 * W, 8, GW); b_g = b_A1[8:16]
        conv3 = arenaF[:, 3072:3072 + 8 * W].rearrange("p (a b) -> p a b", a=8); b_cv = b_A1[16:24]
        OFFX = A1 + 8 * W + 8 * GW
        gs_ext = av(OFFX, 8, 34); b_gs = b_A1[24]
        ksamp = av(OFFX + 272, 8, 4); b_ks = b_A1[25]
        OFFX2 = OFFX + 272 + 32
        for j in range(8):
            if ci == 0:
                P.op("dve", "tensor_copy", gext[:, j, 0:30], zeros[:, 0:30], reads=[b_zero], writes=[b_g[j]])
            else:
                P.op("dve", "tensor_copy", gext[:, j, 0:30], gtail[:, j, :], reads=[b_gt], writes=[b_g[j]])
        if ns:
            scst = arenaF[:, 2048:3072]; b_sc = b_AF[1]
            P.dma("sp", scst[:30, :], sconv_d, writes=[b_sc], sem=b_sc)
            for j in range(8):
                P.op("pe", "transpose", PS[2][:, j * 32:j * 32 + 30], scst[:30, j * 128:(j + 1) * 128], ident[:30, :30],
                     reads=[b_sc, b_ident], writes=[b_ps[2]], partial=(j > 0))
            copy_op("dve", gs_ext[:, :, 0:30], PS[2][:, 0:256].rearrange("p (a b) -> p a b", a=8)[:, :, 0:30], reads=[b_ps[2]], writes=[b_gs], partial=True)

        def kv_out(which, h, src, b_src):
            dst = kout if which == "k" else vout
            for ti, (t0, nt) in enumerate(ttiles):
                P.op("pe", "transpose", PS[3][:nt, ti * 128:(ti + 1) * 128], src[:, t0:t0 + nt], ident[:, :],
                     reads=[b_src, b_ident], writes=[b_ps[3]], partial=(ti > 0))
            sl = 2 + (state["evac"] % 2)
            stg = arenaF[:, 3072 + (sl - 2) * 512: 3072 + (sl - 1) * 512]
            copy_op(ev_eng(), stg, PS[3][:, :], reads=[b_ps[3]], writes=[b_AF[sl]])
            for ti, (t0, nt) in enumerate(ttiles):
                P.dma("pool", dst[c0 + t0:c0 + t0 + nt, h * 128:(h + 1) * 128], stg[:nt, ti * 128:(ti + 1) * 128],
                      reads=[b_AF[sl]], writes=[b_out], sem=b_AF[sl], partial=True)
                if which == "v":
                    P.dma("pool", Vscr[c0 + t0:c0 + t0 + nt, h * 128:(h + 1) * 128], stg[:nt, ti * 128:(ti + 1) * 128],
                          reads=[b_AF[sl]], writes=[b_vscr], sem=b_AF[sl], partial=True)

        def evac_in_even(col0, ps, b_p):
            t = col0 // 128
            if t < 8:
                P.op("act", "activation", R(q3[:, t, :n]), ps, AF.Copy, scale=0.125, reads=[b_p], writes=[b_q[t]])
            elif t < 24:
                which = "k" if t < 16 else "v"
                h = t % 8
                sl = state["evac"] % 2
                tmp = sm(sl, n)
                copy_op(ev_eng(), tmp, ps, reads=[b_p], writes=[b_small[sl]])
                if which == "k":
                    P.dma("pool", KTscr[h, :, c0:c0 + npr], tmp[:, :npr], reads=[b_small[sl]], writes=[b_ktscr], sem=b_small[sl], partial=True)
                    if ns:
                        P.op("dve", "tensor_copy", R(ksamp[:, h, :]), tmp[:, npr:n], reads=[b_small[sl]], writes=[b_ks], partial=True)
                kv_out(which, h, tmp, b_small[sl])
            elif t < 32:
                P.op("act", "copy", sm(4, n), ps, reads=[b_p], writes=[b_small[4]])
            else:
                j = t - 32
                P.op("act", "activation", sm(5, n), ps, AF.Sigmoid, reads=[b_p], writes=[b_small[5]])
                P.op("dve", "tensor_tensor", gext[:, j, 30:30 + npr], sm(4, n)[:, :npr], sm(5, n)[:, :npr], op=ALU.mult,
                     reads=[b_small[4], b_small[5]], writes=[b_g[j]], partial=True)
                if ns:
                    P.op("dve", "tensor_tensor", gs_ext[:, j, 30:34], sm(4, n)[:, npr:n], sm(5, n)[:, npr:n], op=ALU.mult,
                         reads=[b_small[4], b_small[5]], writes=[b_gs], partial=True)

        blocks = blocks_range(0, 3072) + [c for j in range(8) for c in (3072 + 128 * j, 4096 + 128 * j)]
        linear(w_in_even, 16, blocks, lambda kt: R(hn3[:, kt, :n]), b_hn, n, evac_in_even)
        P.barrier()
        if stop_after == "inproj":
            break

        P.stage = "conv"
        oc3 = av(A0, 16, W); b_oc = b_A0
        for w in range(31):
            for j in range(8):
                segs = [(gext, b_g[j], 0, npr)] + ([(gs_ext, b_gs, npr, ns)] if ns else [])
                for (gsrc, b_src, o0, ln) in segs:
                    if w == 0:
                        P.op("dve", "tensor_scalar", conv3[:, j, o0:o0 + ln], gsrc[:, j, 0:ln], cw[:, j, 0:1], cvec[:, 0, j:j + 1], op0=ALU.mult, op1=ALU.add,
                             reads=[b_src, b_par], writes=[b_cv[j]], partial=True)
                    else:
                        P.op("dve", "scalar_tensor_tensor", conv3[:, j, o0:o0 + ln], gsrc[:, j, w:w + ln], cw[:, j, w:w + 1], conv3[:, j, o0:o0 + ln], op0=ALU.mult, op1=ALU.add,
                             reads=[b_src, b_par, b_cv[j]], writes=[b_cv[j]], partial=True)
        if ns:
            for (gsrc, b_src, lo, dsto) in [(gext, None, npr, convp_o), (gs_ext, b_gs, 4, convs_o)]:
                for j in range(8):
                    bs = b_g[j] if b_src is None else b_src
                    P.op("pe", "transpose", PS[2 + j // 4][:30, (j % 4) * 128:(j % 4) * 128 + 128], gsrc[:, j, lo:lo + 30], ident[:, :],
                         reads=[bs, b_ident], writes=[b_ps[2 + j // 4]], partial=(j % 4 > 0))
                stg = arenaF[:, 2048:3072]
                copy_op("act", stg[:30, 0:512], PS[2][:30, :], reads=[b_ps[2]], writes=[b_AF[1]])
                copy_op("dve", stg[:30, 512:1024], PS[3][:30, :], reads=[b_ps[3]], writes=[b_AF[1]], partial=True)
                P.dma("pool", dsto, stg[:30, :], reads=[b_AF[1]], writes=[b_out], sem=b_AF[1], partial=True)
        else:
            pass
        for j in range(8):
            sl = j % 2
            P.op("act", "copy", smr(sl, n), conv3[:, j, :n], reads=[b_cv[j]], writes=[b_smr[sl]])
            P.op("pe", "matmul", PS[4][:, :n], lhsT=ones_r[:], rhs=smr(sl, n), start=(j == 0), stop=(j == 7),
                 reads=[b_smr[sl], b_ones], writes=[b_ps[4]], partial=(j > 0))
        for j in range(8):
            sl = j % 2
            P.op("act", "activation", smr(sl, n), conv3[:, j, :n], AF.Square, reads=[b_cv[j]], writes=[b_smr[sl]])
            P.op("pe", "matmul", PS[5][:, :n], lhsT=ones_r[:], rhs=smr(sl, n), start=(j == 0), stop=(j == 7),
                 reads=[b_smr[sl], b_ones], writes=[b_ps[5]], partial=(j > 0))
        mean = sm(0, n); rstd = sm(1, n); msq = sm(2, n)
        P.op("dve", "tensor_scalar", mean, PS[4][:, :n], 1.0 / 1024, None, op0=ALU.mult, reads=[b_ps[4]], writes=[b_small[0]])
        P.op("dve", "tensor_tensor", msq, mean, mean, op=ALU.mult, reads=[b_small[0]], writes=[b_small[2]])
        P.op("dve", "scalar_tensor_tensor", rstd, PS[5][:, :n], 1.0 / 1024, msq, op0=ALU.mult, op1=ALU.subtract, reads=[b_ps[5], b_small[2]], writes=[b_small[1]])
        P.op("act", "activation", rstd, rstd, AF.Sqrt, bias=EPS, scale=1.0, reads=[b_small[1]], writes=[b_small[1]])
        P.op("dve", "reciprocal", rstd, rstd, reads=[b_small[1]], writes=[b_small[1]])
        for j in range(8):
            eng = "dve" if j % 2 == 0 else "pool"
            P.op(eng, "tensor_tensor", conv3[:, j, :n], conv3[:, j, :n], mean, op=ALU.subtract, reads=[b_cv[j], b_small[0]], writes=[b_cv[j]])
            P.op(eng, "tensor_tensor", conv3[:, j, :n], conv3[:, j, :n], rstd, op=ALU.mult, reads=[b_cv[j], b_small[1]], writes=[b_cv[j]])
            P.op("act", "activation", R(oc3[:, 8 + j, :n]), conv3[:, j, :n], AF.Silu, bias=cvec[:, 2, j:j + 1], scale=cvec[:, 1, j:j + 1],
                 reads=[b_cv[j], b_par], writes=[b_oc[8 + j]])
        if not ns:
            for j in range(8):
                P.op("pool", "tensor_copy", gtail[:, j, :], gext[:, j, npr:npr + 30], reads=[b_g[j]], writes=[b_gt], partial=(j > 0))
        if stop_after == "conv":
            break

        P.stage = "attn"
        AT = OFFX2
        kend = c0 + npr
        nkt = (kend + 127) // 128
        ktb = [arena[:, AT + i * NPR: AT + (i + 1) * NPR].bitcast(F32R) for i in range(2)]; b_ktb = b_A1[27:29]
        VB0 = AT + 2 * NPR
        vb = [arena[:, VB0 + i * 17 * 128: VB0 + (i + 1) * 17 * 128].bitcast(F32R).rearrange("p (a b) -> p a b", a=17) for i in range(2)]; b_vb = b_A1[29:31]
        PB0 = VB0 + 2 * 17 * 128
        pex = [arena[:, PB0 + i * W: PB0 + (i + 1) * W] for i in range(3)]; b_pex = b_A1[31:34]
        tmpb = [arenaF[:, i * W: (i + 1) * W] for i in range(3)]; b_tmp = b_A1[34:37]
        mtmp = arenaF[:, 3 * W: 4 * W]; b_mt = b_A1[37]
        ENDAT = PB0 + 3 * W
        assert ENDAT <= ARENA, ENDAT
        SCB = [0, 1, 7]
        for h in range(8):
            s = h % 2
            P.dma("sp", ktb[s][:, 0:kend], R(KTscr[h, :, 0:kend]), reads=[b_ktscr], writes=[b_ktb[s]], sem=b_ktb[s])
            nfull = kend // 128
            if nfull:
                P.dma("sp", vb[s][:, 0:nfull, :], R(Vscr[0:nfull * 128, h * 128:(h + 1) * 128].rearrange("(a p) d -> p a d", p=128)),
                      reads=[b_vscr], writes=[b_vb[s]], sem=b_vb[s])
            if kend % 128:
                P.dma("sp", vb[s][:kend % 128, nfull, :], R(Vscr[nfull * 128:kend, h * 128:(h + 1) * 128]),
                      reads=[b_vscr], writes=[b_vb[s]], sem=b_vb[s], partial=True)
            work = [(m, kt) for m in range(2) for kt in range(nkt)]

            def front(i):
                m, kt = work[i]
                k0 = kt * 128; kn = min(128, kend - k0)
                sc = SCB[i % 3]; tb = i % 3; pe_i = i % 3
                P.op("pe", "matmul", PS[sc][:kn, :npr], lhsT=ktb[s][64 * m:64 * m + 64, k0:k0 + kn],
                     rhs=R(q3[64 * m:64 * m + 64, h, :npr]), start=True, stop=True,
                     reads=[b_ktb[s], b_q[h]], writes=[b_ps[sc]])
                P.op("dve", "scalar_tensor_tensor", tmpb[tb][:kn, :npr], d0tab[:kn, :npr], SLOPES[h], PS[sc][:kn, :npr], op0=ALU.mult, op1=ALU.add,
                     reads=[b_ps[sc], b_sid], writes=[b_tmp[tb]])
                if k0 + kn - 1 > c0:
                    P.op("dve", "tensor_scalar", mtmp[:kn, :npr], d0tab[:kn, :npr], float(k0 - c0), 0.0, op0=ALU.add, op1=ALU.is_gt,
                         reads=[b_sid], writes=[b_mt])
                    P.op("dve", "scalar_tensor_tensor", tmpb[tb][:kn, :npr], mtmp[:kn, :npr], NEG, tmpb[tb][:kn, :npr], op0=ALU.mult, op1=ALU.add,
                         reads=[b_mt, b_tmp[tb]], writes=[b_tmp[tb]])
                P.op("act", "activation", R(pex[pe_i][:kn, :npr]), tmpb[tb][:kn, :npr], AF.Exp, bias=float(SLOPES[h] * (k0 - c0)), scale=1.0,
                     reads=[b_tmp[tb]], writes=[b_pex[pe_i]])

            def back(i):
                m, kt = work[i]
                k0 = kt * 128; kn = min(128, kend - k0)
                pe_i = i % 3
                o_ps, d_ps = PS[2 + m], PS[5 + m]
                b_o, b_d = b_ps[2 + m], b_ps[5 + m]
                P.op("pe", "matmul", o_ps[:, :npr], lhsT=vb[s][:kn, kt, :], rhs=R(pex[pe_i][:kn, :npr]), start=(kt == 0), stop=(kt == nkt - 1),
                     reads=[b_vb[s], b_pex[pe_i]], writes=[b_o], partial=(kt > 0))
                P.op("pe", "matmul", d_ps[:, :npr], lhsT=ones_r[:kn, :], rhs=R(pex[pe_i][:kn, :npr]), start=(kt == 0), stop=(kt == nkt - 1),
                     reads=[b_ones, b_pex[pe_i]], writes=[b_d], partial=(kt > 0))
                if kt == nkt - 1:
                    P.op("dve", "reciprocal", sm(6, npr), d_ps[:, :npr], reads=[b_d], writes=[b_small[6]])
                    P.op("dve", "tensor_tensor", sm(2 + m, npr), o_ps[:, :npr], sm(6, npr), op=ALU.mult, reads=[b_o, b_small[6]], writes=[b_small[2 + m]])

            LA = 2
            for i in range(min(LA, len(work))):
                front(i)
            for i in range(len(work)):
                if i + LA < len(work):
                    front(i + LA)
                back(i)
            if True:
                if True:
                    pass
            P.op("dve", "scalar_tensor_tensor", sm(2, npr), sm(3, npr), neglam, sm(2, npr), op0=ALU.mult, op1=ALU.add,
                 reads=[b_small[2], b_small[3], b_par], writes=[b_small[2]])
            P.op("act", "activation", smr(0, npr), sm(2, npr), AF.Square, reads=[b_small[2]], writes=[b_smr[0]])
            P.op("pe", "matmul", PS[4][:, :npr], lhsT=ones_r[:], rhs=smr(0, npr), start=True, stop=True, reads=[b_smr[0], b_ones], writes=[b_ps[4]])
            P.op("act", "activation", sm(6, npr), PS[4][:, :npr], AF.Sqrt, bias=EPS, scale=1.0 / 128, reads=[b_ps[4]], writes=[b_small[6]])
            P.op("dve", "reciprocal", sm(6, npr), sm(6, npr), reads=[b_small[6]], writes=[b_small[6]])
            P.op("dve", "scalar_tensor_tensor", R(oc3[:, h, :npr]), sm(2, npr), subg2, sm(6, npr), op0=ALU.mult, op1=ALU.mult,
                 reads=[b_small[2], b_small[6], b_par], writes=[b_oc[h]], partial=True)

        if ns:
            P.stage = "sattn"
            sample_attention(q3, b_q, ksamp, b_ks, oc3, b_oc, npr, AT)
        if stop_after == "attn":
            dump(0, oc3[:, :, :], b_oc, ci)
            break
        P.barrier()
        dump(0, oc3[:, :, :], b_oc, ci)

        P.stage = "outproj0"
        m3 = av(A1, 16, W); b_m = b_A1[0:16]

        def evac_m(col0, ps, b_p, m3=m3, b_m=b_m):
            t = col0 // 128
            copy_op(ev_eng(), m3[:, t, :n], ps, reads=[b_p], writes=[b_m[t]])
        linear(w_out_even, 16, blocks_range(0, D), lambda kt: R(oc3[:, kt, :n]), b_oc, n, evac_m)
        post_norm_residual(m3, b_m, 1, 0, n)
        P.barrier()
        dump(1, x_fm[:, :, :], b_x, ci)
        if stop_after == "mix0":
            break

        for layer in range(2):
            if layer == 1:
                P.stage = "l1_inproj"
                pre_norm(hn3, b_hn, 0, 1, n)
                u3 = av(A1, 16, W); b_u = b_A1[0:16]

                def evac_u(col0, ps, b_p):
                    t = col0 // 128
                    copy_op(ev_eng(), R(u3[:, t, :n]), ps, reads=[b_p], writes=[b_u[t]])
                linear(w_in_odd, 16, blocks_range(0, D), lambda kt: R(hn3[:, kt, :n]), b_hn, n, evac_u)
                P.barrier()
                y3 = av(A0, 16, W); b_y = b_A0
                P.stage = "ssm"
                ssm(u3, b_u, y3, b_y, npr, ns)
                P.stage = "l1_glu_out"
                P.barrier()
                dump(3, y3[:, :, :], b_y, ci)
                for ft in range(16):
                    eng = "dve" if ft % 2 == 0 else "pool"
                    sl = ft % 2
                    yv = y3[:, ft, :n]
                    P.op(eng, "tensor_tensor", sm(sl, n), yv, yv, op=ALU.mult, reads=[b_y[ft]], writes=[b_small[sl]])
                    P.op(eng, "tensor_scalar", sm(sl, n), sm(sl, n), 0.044715, 1.0, op0=ALU.mult, op1=ALU.add, reads=[b_small[sl]], writes=[b_small[sl]])
                    P.op(eng, "tensor_tensor", sm(sl, n), sm(sl, n), yv, op=ALU.mult, reads=[b_small[sl], b_y[ft]], writes=[b_small[sl]])
                    P.op("act", "activation", sm(sl, n), sm(sl, n), AF.Sigmoid, scale=1.5957691216057308, reads=[b_small[sl]], writes=[b_small[sl]])
                    P.op(eng, "tensor_tensor", R(yv), yv, sm(sl, n), op=ALU.mult, reads=[b_small[sl], b_y[ft]], writes=[b_y[ft]])
                z3 = av(A1, 16, W); b_z = b_A1[0:16]

                def evac_z(col0, ps, b_p):
                    t = col0 // 128
                    sl = 2 + t % 2
                    P.op("act", "activation", sm(sl, n), ps, AF.Sigmoid, reads=[b_p], writes=[b_small[sl]])
                    P.op("dve", "tensor_tensor", R(z3[:, t, :n]), y3[:, t, :n], sm(sl, n), op=ALU.mult, reads=[b_small[sl], b_y[t]], writes=[b_z[t]])
                linear(w_glu, 16, blocks_range(0, D), lambda kt: R(y3[:, kt, :n]), b_y, n, evac_z)
                P.barrier()
                m3 = av(A0, 16, W); b_m = b_A0

                def evac_m1(col0, ps, b_p):
                    t = col0 // 128
                    copy_op(ev_eng(), m3[:, t, :n], ps, reads=[b_p], writes=[b_m[t]])
                linear(w_out_odd, 16, blocks_range(0, D), lambda kt: R(z3[:, kt, :n]), b_z, n, evac_m1)
                post_norm_residual(m3, b_m, 1, 1, n)
                P.barrier()
                dump(4, x_fm[:, :, :], b_x, ci)
                if stop_after == "mix1":
                    break
            P.stage = "ffn"
            pre_norm(hn3, b_hn, 2, layer, n)
            act3 = av(A1, 11, W); b_act = b_A1[0:11]
            yacc = av(A1 + 11 * W, 16, W); b_ya = b_A1[11:27]
            for qd in range(4):
                lo = 1408 * qd

                def evac_gate(col0, ps, b_p, lo=lo):
                    jj = (col0 - lo) // 128
                    P.op("act", "activation", act3[:, jj, :n], ps, AF.Silu, reads=[b_p], writes=[b_act[jj]])

                def evac_up(col0, ps, b_p, lo=lo):
                    jj = (col0 - lo) // 128
                    P.op("dve", "tensor_tensor", R(act3[:, jj, :n]), act3[:, jj, :n], ps, op=ALU.mult, reads=[b_p, b_act[jj]], writes=[b_act[jj]])
                linear(w_gate[layer], 16, blocks_range(lo, lo + 1408), lambda kt: R(hn3[:, kt, :n]), b_hn, n, evac_gate)
                linear(w_up[layer], 16, blocks_range(lo, lo + 1408), lambda kt: R(hn3[:, kt, :n]), b_hn, n, evac_up)

                def evac_down(col0, ps, b_p, qd=qd):
                    t = col0 // 128
                    if qd == 0:
                        copy_op(ev_eng(), yacc[:, t, :n], ps, reads=[b_p], writes=[b_ya[t]])
                    else:
                        P.op("dve", "tensor_tensor", yacc[:, t, :n], yacc[:, t, :n], ps, op=ALU.add, reads=[b_p, b_ya[t]], writes=[b_ya[t]])
                linear(w_down[layer][lo:lo + 1408, :], 11, blocks_range(0, D), lambda kt: R(act3[:, kt, :n]), b_act, n, evac_down)
            post_norm_residual(yacc, b_ya, 3, layer, n)
            P.barrier()
            dump(2 if layer == 0 else 5, x_fm[:, :, :], b_x, ci)
            if stop_after == f"ffn{layer}":
                break
        if stop_after is not None and stop_after != "chunk0":
            break

        P.stage = "ystore"
        yst = [arenaF[:, 0:2048], arenaF[:, 0:2048]]
        b_yst = [b_AF[0], b_AF[0]]
        for ti, (t0, nt) in enumerate(ttiles):
            s = ti % 2
            for fg in range(4):
                pb = 2 + (fg % 2)
                for k in range(4):
                    ft = fg * 4 + k
                    P.op("pe", "transpose", PS[pb][:nt, k * 128:(k + 1) * 128], x_fm[:, ft, t0:t0 + nt], ident[:, :],
                         reads=[b_x[ft], b_ident], writes=[b_ps[pb]], partial=(k > 0))
                copy_op(ev_eng(), yst[s][:nt, fg * 512:(fg + 1) * 512], PS[pb][:nt, :], reads=[b_ps[pb]], writes=[b_yst[s]], partial=(fg > 0))
            P.dma("pool", yout[c0 + t0:c0 + t0 + nt, :], yst[s][:nt, :], reads=[b_yst[s]], writes=[b_out], sem=b_yst[s], partial=True)
        P.barrier()
        if stop_after == "chunk0":
            break

    if stop_after is None:
        P.dma("pool", ssmp_o, S_p[:], reads=[b_Sp], writes=[b_out], sem=b_Sp, partial=True)
        P.dma("pool", ssms_o, S_s[:], reads=[b_Ss], writes=[b_out], sem=b_Ss, partial=True)
    P.barrier()
    P.emit()
    return nc


_PROG_CACHE = {}


def _host_inputs(inp):
    f32 = np.float32
    c = lambda a: np.ascontiguousarray(a, dtype=a.dtype)
    shared = {}
    shared["w_in_even"] = c(inp["w_in_even"][0]); shared["w_out_even"] = c(inp["w_out_even"][0])
    shared["w_in_odd"] = c(inp["w_in_odd"][0]); shared["w_glu"] = c(inp["w_glu"][0]); shared["w_out_odd"] = c(inp["w_out_odd"][0])
    shared["w_gate"] = c(inp["w_ffn_gate"]); shared["w_up"] = c(inp["w_ffn_up"]); shared["w_down"] = c(inp["w_ffn_down"])
    g = np.zeros((8, 16, 128), f32)
    for kind, nm in enumerate(["norm_mix_pre", "norm_mix_post", "norm_ffn_pre", "norm_ffn_post"]):
        for layer in range(2):
            g[kind * 2 + layer] = np.asarray(inp[nm][layer]).reshape(16, 128)
    shared["gains"] = c(g.transpose(2, 0, 1))
    shared["cw"] = c(np.asarray(inp["conv_w"][0]).reshape(31, 8, 128).transpose(2, 1, 0))
    cv = np.stack([np.asarray(inp[k][0]).reshape(8, 128) for k in ("conv_b", "conv_ln_g", "conv_ln_b")])
    shared["cvec"] = c(cv.transpose(2, 0, 1))
    shared["subg"] = c(np.asarray(inp["subln_g"][0]).reshape(128, 1))
    shared["lqk"] = c(np.concatenate([np.asarray(inp["lambda_q"][0]).ravel(), np.asarray(inp["lambda_k"][0]).ravel()]).reshape(1, 256))
    shared["a_re"] = c(inp["ssm_a_re"][0]); shared["a_im"] = c(inp["ssm_a_im"][0])
    shared["ldt"] = c(np.asarray(inp["ssm_log_dt"][0]).reshape(128, 1))
    shared["b_re"] = c(inp["ssm_b_re"][0]); shared["b_im"] = c(inp["ssm_b_im"][0])
    cre = np.asarray(inp["ssm_c_re"][0]).transpose(2, 0, 1); cim = np.asarray(inp["ssm_c_im"][0]).transpose(2, 0, 1)
    shared["ccat"] = c(np.concatenate([cre, cim], axis=0))
    shared["dfm"] = c(np.asarray(inp["ssm_d"][0]).reshape(16, 128).T)
    shared["cache_k"] = c(np.asarray(inp["cache_k"][0]).reshape(1280 * 128, 1024))
    shared["cache_v"] = c(np.asarray(inp["cache_v"][0]).reshape(1280 * 128, 1024))
    shared["iota"] = np.arange(128, dtype=np.int32).reshape(128, 1)
    shared["ident"] = np.eye(128, dtype=f32)
    shared["ones"] = np.ones((128, 128), f32)
    shared["d0tab"] = (np.arange(128, dtype=f32)[:, None] - np.arange(W, dtype=f32)[None, :]).astype(f32)
    stab = np.zeros((64, 136), f32); slopecol = np.zeros((64, 1), f32)
    for h in range(8):
        for m in range(2):
            for q in range(4):
                r = h * 8 + m * 4 + q
                stab[r, 0:4] = np.arange(4)
                stab[r, 4:8] = np.where(np.arange(4) > q, NEG, 0.0)
                stab[r, 8:136] = np.arange(128) - PAST
                slopecol[r, 0] = SLOPES[h]
    shared["stab"] = stab; shared["slopecol"] = slopecol
    maps = []
    meta = np.asarray(inp["meta_tokens"], f32)
    for i in range(8):
        b = i % 4
        d = dict(shared)
        d["xin"] = c(np.concatenate([meta, np.asarray(inp["x_prompt"][b]), np.asarray(inp["x_sample"][i])], axis=0))
        d["ptab"] = c(np.asarray(inp["page_table"][i], dtype=np.int32).reshape(1, 128))
        d["sconv"] = c(inp["state_conv"][0, i])
        d["sssm"] = c(np.concatenate([np.asarray(inp["state_ssm_re"][0, i]), np.asarray(inp["state_ssm_im"][0, i])], axis=1))
        maps.append(d)
    return maps


def _assemble(res):
    f32 = np.float32
    R_ = [r for r in res]
    y_prompt = np.stack([R_[b]["yout"][16:NPR] for b in range(4)]).astype(f32)
    y_sample = np.stack([R_[i]["yout"][NPR:NT] for i in range(8)]).astype(f32)
    k_prompt = np.stack([R_[b]["kout"][0:NPR].reshape(NPR, 8, 2, 64) for b in range(4)])[None].astype(f32)
    v_prompt = np.stack([R_[b]["vout"][0:NPR].reshape(NPR, 8, 128) for b in range(4)])[None].astype(f32)
    k_sample = np.stack([R_[i]["kout"][NPR:NT].reshape(4, 8, 2, 64) for i in range(8)])[None].astype(f32)
    v_sample = np.stack([R_[i]["vout"][NPR:NT].reshape(4, 8, 128) for i in range(8)])[None].astype(f32)
    conv_prompt = np.stack([R_[b]["convp"] for b in range(4)])[None].astype(f32)
    conv_sample = np.stack([R_[i]["convs"] for i in range(8)])[None].astype(f32)
    srp = np.stack([R_[b]["ssmp"][:, 0:64] for b in range(4)])[None].astype(f32)
    sip = np.stack([R_[b]["ssmp"][:, 64:128] for b in range(4)])[None].astype(f32)
    srs = np.stack([R_[i]["ssms"][:, 0:64] for i in range(8)])[None].astype(f32)
    sis = np.stack([R_[i]["ssms"][:, 64:128] for i in range(8)])[None].astype(f32)
    return (y_prompt, y_sample, k_prompt, v_prompt, k_sample, v_sample, conv_prompt, conv_sample, srp, sip, srs, sis)


def kernel(_stop_after=None, **inputs):
    maps = _host_inputs(inputs)
    nc = build_program(_stop_after)
    res = run_bass_kernel_spmd(nc, maps, core_ids=list(range(8)))
    return _assemble(res.results)
```

```python
import numpy as np
from contextlib import ExitStack
import concourse.bass as bass
import concourse.mybir as mybir

F32 = mybir.dt.float32
F32R = mybir.dt.float32r
I32 = mybir.dt.int32
AF = mybir.ActivationFunctionType
ALU = mybir.AluOpType
AX = mybir.AxisListType

PH = 4096


class Buf:
    __slots__ = ("name", "writers", "readers", "dcount", "semi", "_sw")

    def __init__(self, name):
        self.name = name
        self.writers = []
        self.readers = []
        self.dcount = 0
        self.semi = None
        self._sw = None

    def sw(self):
        if self._sw is None:
            self._sw = Buf(self.name + "_sw")
        return self._sw


class Prog:
    ENG = ("pe", "act", "dve", "pool", "sp")

    def __init__(self, nc):
        self.nc = nc
        self.ops = {e: [] for e in self.ENG}
        self.cnt = {e: 0 for e in self.ENG}
        self.waited = {e: {} for e in self.ENG}
        self.dbufs = []
        self.stack = ExitStack()
        self.retype = None
        self.NR = 20
        self.ring = {q: [Buf(f"ring_{q}{i}") for i in range(self.NR)] for q in ("sp", "pool", "act")}
        self.ringpos = {q: 0 for q in ("sp", "pool", "act")}
        self.stage = "init"
        self.labels = {e: [] for e in self.ENG}

    def sb(self, name, shape, dtype=F32):
        return self.stack.enter_context(self.nc.sbuf_tensor("sb_" + name, list(shape), dtype))

    def ps(self, name, shape, dtype=F32):
        return self.stack.enter_context(self.nc.psum_tensor(name, list(shape), dtype))

    def _need(self, eng, toks):
        best = {}
        cur = self.cnt[eng] + 1
        for t in toks:
            if t[0] == "E":
                _, e2, idx = t
                if e2 == eng and eng in ("pe", "sp"):
                    continue
                if e2 == eng and eng == "dve" and cur - idx >= 2:
                    continue
                key = ("E", e2, (idx - 1) // PH)
                val = (idx - 1) % PH + 1
            else:
                _, b, val = t
                key = ("D", id(b))
                if b.semi is None:
                    b.semi = len(self.dbufs)
                    self.dbufs.append(b)
            if best.get(key, (0,))[0] < val:
                best[key] = (val, t)
        out = []
        w = self.waited[eng]
        for key, (val, t) in best.items():
            if w.get(key, 0) >= val:
                continue
            w[key] = val
            out.append((key, val, t))
        return out

    def _deps(self, reads, writes):
        toks = []
        for b in reads:
            toks += b.writers
        for b in writes:
            toks += b.writers
            toks += b.readers
        return toks

    @staticmethod
    def _dedupe(lst):
        best = {}
        for t in lst:
            if t[0] == "E":
                key = ("E", t[1])
                v = t[2]
            else:
                key = ("D", id(t[1]))
                v = t[2]
            if key not in best or best[key][2] < v:
                best[key] = t
        return list(best.values())

    def _commit(self, tok, reads, writes, partial):
        for b in reads:
            b.readers = self._dedupe(b.readers + [tok])
        for b in writes:
            if partial:
                b.writers = self._dedupe(b.writers + [tok])
            else:
                b.writers = [tok]
                b.readers = []

    def op(self, eng, meth, *args, reads=(), writes=(), partial=False, **kw):
        if self.retype is not None and args:
            args = (self.retype(args[0]),) + tuple(args[1:])
        fn = lambda e, meth=meth, args=args, kw=kw: getattr(e, meth)(*args, **kw)
        waits = self._need(eng, self._deps(reads, writes))
        self.cnt[eng] += 1
        idx = self.cnt[eng]
        tok = ("E", eng, idx)
        self.ops[eng].append((waits, fn, ("E", eng, idx)))
        self.labels[eng].append(self.stage)
        self._commit(tok, reads, writes, partial)
        return tok

    def _ring_sem(self, q):
        rb = self.ring[q][self.ringpos[q] % self.NR]
        self.ringpos[q] += 1
        pre = [("D", rb, 16 * rb.dcount)] if rb.dcount > 0 else []
        return rb, pre

    def dma(self, q, out, in_, reads=(), writes=(), sem=None, partial=False, **kw):
        rb, pre = self._ring_sem(q)
        if self.retype is not None:
            out = self.retype(out)
            if out.dtype == F32R and in_.dtype == F32:
                in_ = in_.bitcast(F32R)
        waits = self._need(q, self._deps(reads, writes) + pre)
        rb.dcount += 1
        tok = ("D", rb, 16 * rb.dcount)
        fn = lambda e, out=out, in_=in_, kw=kw: e.dma_start(out=out, in_=in_, **kw)
        self.ops[q].append((waits, fn, ("D", rb)))
        self._commit(tok, reads, writes, partial)
        return tok

    def dma_custom(self, q, meth, reads=(), writes=(), sem=None, partial=False, **kw):
        fn = lambda e, meth=meth, kw=kw: getattr(e, meth)(**kw)
        rb, pre = self._ring_sem(q)
        waits = self._need(q, self._deps(reads, writes) + pre)
        rb.dcount += 1
        tok = ("D", rb, 16 * rb.dcount)
        self.ops[q].append((waits, fn, ("D", rb)))
        self._commit(tok, reads, writes, partial)
        return tok

    def barrier(self):
        toks = [("E", e, self.cnt[e]) for e in self.ENG if self.cnt[e] > 0]
        toks += [("D", b, 16 * b.dcount) for q in self.ring for b in self.ring[q] if b.dcount > 0]
        for e in self.ENG:
            waits = self._need(e, toks)
            if waits:
                self.ops[e].append((waits, None, None))

    def emit(self):
        nc = self.nc
        import bisect
        mset = {e: set() for e in self.ENG}
        for e in self.ENG:
            for waits, fn, inc in self.ops[e]:
                for key, val, tok in waits:
                    if tok[0] == "E":
                        mset[tok[1]].add(tok[2])
        mlist = {e: sorted(mset[e]) for e in self.ENG}
        nsem = {e: (len(mlist[e]) + PH - 1) // PH for e in self.ENG}
        esem = {e: [self.stack.enter_context(nc.semaphore(f"s_{e}{i}")) for i in range(nsem[e])]
                for e in self.ENG}
        dsem = [self.stack.enter_context(nc.semaphore(f"d_{i}")) for i in range(len(self.dbufs))]
        idmap = {id(b): i for i, b in enumerate(self.dbufs)}

        def rank(e, idx):
            r = bisect.bisect_left(mlist[e], idx)
            assert mlist[e][r] == idx
            return r

        def run(engname, eng):
            for waits, fn, inc in self.ops[engname]:
                for key, val, tok in waits:
                    if tok[0] == "E":
                        r = rank(tok[1], tok[2])
                        eng.wait_ge(esem[tok[1]][r // PH], r % PH + 1)
                    else:
                        eng.wait_ge(dsem[idmap[id(tok[1])]], val)
                if fn is None:
                    continue
                ins = fn(eng)
                if inc[0] == "E":
                    idx = inc[2]
                    if idx in mset[engname]:
                        r = rank(engname, idx)
                        ins.then_inc(esem[engname][r // PH], 1)
                else:
                    ins.then_inc(dsem[idmap[id(inc[1])]], 16)

        with nc.Block() as block:
            @block.tensor
            def _(e):
                run("pe", e)

            @block.scalar
            def _(e):
                run("act", e)

            @block.vector
            def _(e):
                run("dve", e)

            @block.gpsimd
            def _(e):
                run("pool", e)

            @block.sync
            def _(e):
                run("sp", e)
        self.stack.close()

import math
from concourse.bass_utils import run_bass_kernel_spmd

D = 2048
NT = 2068
NPR = 2064
CHUNKS = [(0, 416), (416, 832), (832, 1248), (1248, 1664), (1664, 2068)]
W = 416
DFF = 5632
EPS = 1e-6
LAM_INIT = 0.8 - 0.6 * math.exp(-0.3 * 0)
SLOPES = [2.0 ** (-8.0 * (i + 1) / 8) for i in range(8)]
PAST = 16384
NEG = -30000.0
STOP_AFTER = None


def build_program(stop_after=None):
    nc = bass.Bass("TRN2", target_bir_lowering=False)
    nc.dge_precook = False
    P = Prog(nc)

    def din(name, shape, dt=F32):
        return nc.dram_tensor(name, list(shape), dt, kind="ExternalInput").ap()

    def dout(name, shape, dt=F32):
        return nc.dram_tensor(name, list(shape), dt, kind="ExternalOutput").ap()

    def dscr(name, shape, dt=F32):
        return nc.dram_tensor(name, list(shape), dt).ap()

    xin = din("xin", [NT, D])
    w_in_even = din("w_in_even", [D, 5120]); w_out_even = din("w_out_even", [D, D])
    w_in_odd = din("w_in_odd", [D, D]); w_glu = din("w_glu", [D, D]); w_out_odd = din("w_out_odd", [D, D])
    w_gate = din("w_gate", [2, D, DFF]); w_up = din("w_up", [2, D, DFF]); w_down = din("w_down", [2, DFF, D])
    gains_d = din("gains", [128, 8, 16])
    cw_d = din("cw", [128, 8, 31]); cvec_d = din("cvec", [128, 3, 8])
    subg_d = din("subg", [128, 1]); lqk_d = din("lqk", [1, 256])
    a_re_d = din("a_re", [128, 64]); a_im_d = din("a_im", [128, 64]); ldt_d = din("ldt", [128, 1])
    b_re_d = din("b_re", [128, 64, 16]); b_im_d = din("b_im", [128, 64, 16])
    ccat_d = din("ccat", [128, 128, 16]); dfm_d = din("dfm", [128, 16])
    ck_d = din("cache_k", [1280 * 128, 1024]); cv_d = din("cache_v", [1280 * 128, 1024])
    pt_d = din("ptab", [1, 128], I32); iota_d = din("iota", [128, 1], I32)
    sconv_d = din("sconv", [30, 1024]); sssm_d = din("sssm", [128, 128])
    ident_d = din("ident", [128, 128]); ones_d = din("ones", [128, 128]); d0_d = din("d0tab", [128, W])
    stab_d = din("stab", [64, 8 + 128])
    slopecol_d = din("slopecol", [64, 1])

    yout = dout("yout", [NT, D]); kout = dout("kout", [NT, 1024]); vout = dout("vout", [NT, 1024])
    convp_o = dout("convp", [30, 1024]); convs_o = dout("convs", [30, 1024])
    ssmp_o = dout("ssmp", [128, 128]); ssms_o = dout("ssms", [128, 128])

    dbg_o = dout("dbg", [6, 128, 16 * W]) if stop_after is not None else None
    b_dbg = Buf("dbg")

    def dump(slot, ap3, bufs, ci):
        if ci != 0 or dbg_o is None:
            return
        P.barrier()
        P.dma("pool", dbg_o[slot].rearrange("p (a b) -> p a b", a=16), ap3, reads=list(bufs), writes=[b_dbg], sem=b_dbg, partial=True)
        P.barrier()

    KTscr = dscr("KTscr", [8, 128, NPR]); Vscr = dscr("Vscr", [NT, 1024])
    BDscr = dscr("BDscr", [16, 128, 1024])
    BuScr = dscr("BuScr", [W, 128, 128]); HScr = dscr("HScr", [W, 128, 128])

    ident = P.sb("ident", [128, 128]); b_ident = Buf("ident")
    ones_r = P.sb("ones_r", [128, 128], F32R); b_ones = Buf("ones")
    gains = P.sb("gains", [128, 8, 16]); cw = P.sb("cw", [128, 8, 31]); cvec = P.sb("cvec", [128, 3, 8])
    subg = P.sb("subg", [128, 1]); lamt = P.sb("lamt", [128, 8]); b_par = Buf("params")
    stab = P.sb("stab", [64, 136]); slopecol = P.sb("slopecol", [64, 1])
    dfm = P.sb("dfm", [128, 16])
    zeros = P.sb("zeros", [128, 128]); b_zero = Buf("zeros")
    b_sid = Buf("d0r")
    d0r = P.sb("d0r", [128, W], F32R)
    gtail = P.sb("gtail", [128, 8, 30]); b_gt = Buf("gtail")
    ccat = P.sb("ccat", [128, 128, 16], F32R); b_ccat = Buf("ccat")
    XY = P.sb("XY", [128, 256]); b_xy = Buf("xy")
    S_p = P.sb("S_p", [128, 2, 128]); S_s = P.sb("S_s", [128, 128]); b_Sp = Buf("Sp"); b_Ss = Buf("Ss")
    XY2 = P.sb("XY2", [128, 2, 256]); b_xy2 = Buf("xy2")
    u_last = P.sb("u_last", [128, 16]); b_ul = Buf("ulast")
    BDscr2 = dscr("BDscr2", [16, 128, 1024])
    x_fm = P.sb("x_fm", [128, 16, W]); b_x = [Buf(f"x{i}") for i in range(16)]
    NWB = 3
    wbuf = [P.sb(f"wbuf{i}", [128, 16, 128], F32R) for i in range(NWB)]; b_w = [Buf(f"w{i}") for i in range(NWB)]
    ARENA = 16 * W + 16928
    arenaR = P.sb("arena", [128, ARENA], F32R)
    arena = arenaR[:].bitcast(F32)
    AFSZ = 7424
    arenaF = P.sb("arenaF", [128, AFSZ])
    small = P.sb("small", [128, 7 * W]); b_small = [Buf(f"sm{i}") for i in range(7)]
    smallr = P.sb("smallr", [128, 2 * W], F32R); b_smr = [Buf("smr0"), Buf("smr1")]
    ptb_t = P.sb("ptb", [128, 128], I32); idx_t = P.sb("idx", [128, 128], I32); iot_t = P.sb("iot", [128, 1], I32)
    PS = [P.ps(f"ps{i}", [128, 512]) for i in range(8)]; b_ps = [Buf(f"ps{i}") for i in range(8)]

    def sm(i, n=W):
        return small[:, i * W:i * W + n]

    def smr(i, n=W):
        return smallr[:, i * W:i * W + n]

    def av(off, a, b):
        return arena[:, off:off + a * b].rearrange("p (a b) -> p a b", a=a)

    R = lambda ap: ap.bitcast(F32R)

    def _rt(ap):
        try:
            if ap.name == "sb_arena" and ap.dtype == F32:
                return ap.bitcast(F32R)
        except Exception:
            pass
        return ap
    P.retype = _rt
    b_AF = [Buf(f"AF{i}") for i in range(8)]

    state = {"wi": 0, "mm": 0, "evac": 0}

    def ev_eng():
        state["evac"] += 1
        return "act" if state["evac"] % 2 else "dve"

    def copy_op(eng, out, in_, reads, writes, partial=False):
        if eng == "act":
            return P.op("act", "copy", out, in_, reads=reads, writes=writes, partial=partial)
        return P.op(eng, "tensor_copy", out, in_, reads=reads, writes=writes, partial=partial)

    P.dma("sp", ident[:], ident_d, writes=[b_ident], sem=b_ident)
    P.dma("sp", ones_r[:], R(ones_d), writes=[b_ones], sem=b_ones)
    P.op("dve", "memset", zeros[:], 0.0, writes=[b_zero])
    P.dma("sp", d0r[:], R(d0_d), writes=[b_sid], sem=b_sid)
    d0tab = d0r[:].bitcast(F32)
    for dst, src in [(gains, gains_d), (cw, cw_d), (cvec, cvec_d), (subg, subg_d), (stab, stab_d),
                     (slopecol, slopecol_d), (dfm, dfm_d)]:
        P.dma("sp", dst[:], src, writes=[b_par], sem=b_par, partial=True)
    lq = sm(0, 256)
    P.dma("pool", lq, lqk_d.partition_broadcast(128), writes=[b_small[0]], sem=b_small[0])
    P.op("dve", "tensor_tensor", sm(1, 128), lq[:, 0:128], lq[:, 128:256], op=ALU.mult, reads=[b_small[0]], writes=[b_small[1]])
    P.op("dve", "tensor_reduce", lamt[:, 0:2], sm(1, 128).rearrange("p (a b) -> p a b", a=2), axis=AX.X, op=ALU.add,
         reads=[b_small[1]], writes=[b_par], partial=True)
    P.op("act", "activation", lamt[:, 2:4], lamt[:, 0:2], AF.Exp, reads=[b_par], writes=[b_par], partial=True)
    P.op("dve", "scalar_tensor_tensor", lamt[:, 4:5], lamt[:, 3:4], -LAM_INIT, lamt[:, 2:3], op0=ALU.add, op1=ALU.subtract,
         reads=[b_par], writes=[b_par], partial=True)
    P.op("dve", "tensor_scalar", lamt[:, 5:6], subg[:], 1.0 - LAM_INIT, None, op0=ALU.mult, reads=[b_par], writes=[b_par], partial=True)
    neglam = lamt[:, 4:5]; subg2 = lamt[:, 5:6]

    def gain(kind, layer, ft):
        return gains[:, kind * 2 + layer, ft:ft + 1]

    def ssm_prep():
        o = 0
        def T(n):
            nonlocal o
            v = arenaF[:, o:o + n]; o += n
            return v
        b_t = Buf("ssmprep")
        lr, li, dt_, zr, zi, er, cs, sn, mk, t1, t2, cr, ci, den = [T(64) for _ in range(14)]
        bre = T(1024); bim = T(1024); bbT = T(2048); zero = T(1024)
        dtc = T(1)
        rw = dict(reads=[b_t], writes=[b_t], partial=True)
        P.dma("sp", lr, a_re_d, writes=[b_t], sem=b_t, partial=True)
        P.dma("sp", li, a_im_d, writes=[b_t], sem=b_t, partial=True)
        P.dma("sp", dtc, ldt_d, writes=[b_t], sem=b_t, partial=True)
        P.dma("sp", bre, b_re_d.rearrange("g p c -> g (p c)"), writes=[b_t], sem=b_t, partial=True)
        P.dma("sp", bim, b_im_d.rearrange("g p c -> g (p c)"), writes=[b_t], sem=b_t, partial=True)
        P.dma("sp", ccat[:], R(ccat_d), writes=[b_ccat], sem=b_ccat)
        P.op("act", "activation", ccat[64:128], ccat[64:128].bitcast(F32), AF.Copy, scale=-1.0, reads=[b_ccat], writes=[b_ccat])
        P.op("act", "activation", dtc, dtc, AF.Exp, **rw)
        P.op("dve", "tensor_scalar", zr, lr, dtc, None, op0=ALU.mult, **rw)
        P.op("dve", "tensor_scalar", zi, li, dtc, None, op0=ALU.mult, **rw)
        P.op("act", "activation", er, zr, AF.Exp, **rw)
        P.op("dve", "tensor_copy", sn, zi, **rw)
        P.op("dve", "tensor_scalar", cs, zi, math.pi / 2, None, op0=ALU.add, **rw)
        for tgt in (sn, cs):
            for _ in range(7):
                P.op("dve", "tensor_scalar", mk, tgt, math.pi, None, op0=ALU.is_gt, **rw)
                P.op("dve", "scalar_tensor_tensor", tgt, mk, -2 * math.pi, tgt, op0=ALU.mult, op1=ALU.add, **rw)
            P.op("act", "activation", tgt, tgt, AF.Sin, **rw)
        P.op("dve", "tensor_tensor", XY[:, 0:64], er, cs, op=ALU.mult, reads=[b_t], writes=[b_xy], partial=True)
        P.op("dve", "tensor_copy", XY[:, 64:128], XY[:, 0:64], reads=[b_xy], writes=[b_xy], partial=True)
        P.op("dve", "tensor_tensor", XY[:, 192:256], er, sn, op=ALU.mult, reads=[b_t], writes=[b_xy], partial=True)
        P.op("dve", "tensor_scalar", XY[:, 128:192], XY[:, 192:256], -1.0, None, op0=ALU.mult, reads=[b_xy], writes=[b_xy], partial=True)
        ar = XY[:, 0:64]; ai = XY[:, 192:256]
        rx = dict(reads=[b_t, b_xy], writes=[b_t], partial=True)
        P.op("dve", "tensor_scalar", t1, ar, -1.0, None, op0=ALU.add, **rx)
        P.op("dve", "tensor_tensor", den, lr, lr, op=ALU.mult, **rw)
        P.op("dve", "tensor_tensor", t2, li, li, op=ALU.mult, **rw)
        P.op("dve", "tensor_tensor", den, den, t2, op=ALU.add, **rw)
        P.op("dve", "reciprocal", den, den, **rw)
        P.op("dve", "tensor_tensor", cr, t1, lr, op=ALU.mult, **rw)
        P.op("dve", "tensor_tensor", t2, ai, li, op=ALU.mult, **rx)
        P.op("dve", "tensor_tensor", cr, cr, t2, op=ALU.add, **rw)
        P.op("dve", "tensor_tensor", cr, cr, den, op=ALU.mult, **rw)
        P.op("dve", "tensor_tensor", ci, ai, lr, op=ALU.mult, **rx)
        P.op("dve", "tensor_tensor", t2, t1, li, op=ALU.mult, **rw)
        P.op("dve", "tensor_tensor", ci, ci, t2, op=ALU.subtract, **rw)
        P.op("dve", "tensor_tensor", ci, ci, den, op=ALU.mult, **rw)
        bre3 = bre.rearrange("g (p c) -> g p c", c=16); bim3 = bim.rearrange("g (p c) -> g p c", c=16)
        bb3 = bbT.rearrange("g (c s) -> g c s", c=16)
        for c in range(16):
            br = bre3[:, :, c]; bi = bim3[:, :, c]
            o_re = bb3[:, c, 0:64]; o_im = bb3[:, c, 64:128]
            P.op("dve", "tensor_tensor", o_re, cr, br, op=ALU.mult, **rw)
            P.op("dve", "tensor_tensor", t2, ci, bi, op=ALU.mult, **rw)
            P.op("dve", "tensor_tensor", o_re, o_re, t2, op=ALU.subtract, **rw)
            P.op("dve", "tensor_tensor", o_im, cr, bi, op=ALU.mult, **rw)
            P.op("dve", "tensor_tensor", t2, ci, br, op=ALU.mult, **rw)
            P.op("dve", "tensor_tensor", o_im, o_im, t2, op=ALU.add, **rw)
        for tt in range(2):
            P.op("dve", "tensor_tensor", XY2[:, tt, 0:64], ar, ar, op=ALU.mult, reads=[b_xy], writes=[b_xy2], partial=True)
            P.op("dve", "tensor_tensor", t2, ai, ai, op=ALU.mult, **rx)
            P.op("dve", "tensor_tensor", XY2[:, tt, 0:64], XY2[:, tt, 0:64], t2, op=ALU.subtract, reads=[b_xy2, b_t], writes=[b_xy2], partial=True)
            P.op("dve", "tensor_copy", XY2[:, tt, 64:128], XY2[:, tt, 0:64], reads=[b_xy2], writes=[b_xy2], partial=True)
            P.op("dve", "tensor_tensor", t2, ar, ai, op=ALU.mult, **rx)
            P.op("dve", "tensor_scalar", XY2[:, tt, 192:256], t2, 2.0, None, op0=ALU.mult, reads=[b_t], writes=[b_xy2], partial=True)
            P.op("dve", "tensor_scalar", XY2[:, tt, 128:192], t2, -2.0, None, op0=ALU.mult, reads=[b_t], writes=[b_xy2], partial=True)
        bbA3 = arenaF[:, 14 * 64: 14 * 64 + 2048].rearrange("g (c s) -> g c s", c=16)
        for c in range(16):
            s_re = bb3[:, c, 0:64]; s_im = bb3[:, c, 64:128]
            d_re = bbA3[:, c, 0:64]; d_im = bbA3[:, c, 64:128]
            P.op("dve", "tensor_tensor", d_re, ar, s_re, op=ALU.mult, **rx)
            P.op("dve", "tensor_tensor", t2, ai, s_im, op=ALU.mult, **rx)
            P.op("dve", "tensor_tensor", d_re, d_re, t2, op=ALU.subtract, **rw)
            P.op("dve", "tensor_tensor", d_im, ar, s_im, op=ALU.mult, **rx)
            P.op("dve", "tensor_tensor", t2, ai, s_re, op=ALU.mult, **rx)
            P.op("dve", "tensor_tensor", d_im, d_im, t2, op=ALU.add, **rw)
        P.op("dve", "memset", zero, 0.0, **rw)
        b_bd = Buf("bdscr")
        for f in range(16):
            P.dma("pool", BDscr[f], zero, reads=[b_t], writes=[b_bd], sem=b_t, partial=True)
            P.dma("sp", BDscr2[f], zero, reads=[b_t], writes=[b_bd], sem=b_t, partial=True)
        P.barrier()
        for g in range(128):
            f, gl = divmod(g, 8)
            P.dma("pool", BDscr[f, gl * 16:(gl + 1) * 16, gl * 128:(gl + 1) * 128].unsqueeze(0),
                  bb3[g:g + 1, :, :], reads=[b_t], writes=[b_bd], sem=b_t, partial=True)
            P.dma("sp", BDscr2[f, gl * 16:(gl + 1) * 16, gl * 128:(gl + 1) * 128].unsqueeze(0),
                  bbA3[g:g + 1, :, :], reads=[b_t], writes=[b_bd], sem=b_t, partial=True)
        P.op("dve", "memset", S_p[:], 0.0, writes=[b_Sp])
        P.op("dve", "memset", u_last[:], 0.0, writes=[b_ul])
        P.dma("sp", S_s[:], sssm_d, writes=[b_Ss], sem=b_Ss)
        P.barrier()
        return b_bd

    b_bd = ssm_prep()
    b_ktscr = Buf("ktscr"); b_vscr = Buf("vscr"); b_buscr = Buf("buscr"); b_hscr = Buf("hscr")
    b_out = Buf("outs")

    def rmsnorm_stats(src_fn, nt_tiles, n, denom, srcbufs):
        ssb = b_ps[4]
        for i in range(nt_tiles):
            sl = i % 2
            P.op("act", "activation", smr(sl, n), src_fn(i), AF.Square, reads=[srcbufs[i]], writes=[b_smr[sl]])
            P.op("pe", "matmul", PS[4][:, :n], lhsT=ones_r[:], rhs=smr(sl, n), start=(i == 0), stop=(i == nt_tiles - 1),
                 reads=[b_smr[sl], b_ones], writes=[ssb], partial=(i > 0))
        P.op("act", "activation", sm(6, n), PS[4][:, :n], AF.Sqrt, bias=EPS, scale=1.0 / denom, reads=[ssb], writes=[b_small[6]])
        P.op("dve", "reciprocal", sm(6, n), sm(6, n), reads=[b_small[6]], writes=[b_small[6]])
        return sm(6, n), b_small[6]

    def linear(Wap, KT, blocks, rhs_fn, rhs_bufs, n, evac):
        W3 = Wap.rearrange("(kt p) c -> p kt c", p=128)
        for c0 in blocks:
            wi = state["wi"] % NWB; state["wi"] += 1
            wb = wbuf[wi]
            P.dma("sp", wb[:, 0:KT, :], R(W3[:, :, c0:c0 + 128]), writes=[b_w[wi]], sem=b_w[wi])
            pi = state["mm"] % 2; state["mm"] += 1
            for kt in range(KT):
                P.op("pe", "matmul", PS[pi][:, :n], lhsT=wb[:, kt, :], rhs=rhs_fn(kt), start=(kt == 0), stop=(kt == KT - 1),
                     reads=[b_w[wi], rhs_bufs[kt]], writes=[b_ps[pi]], partial=(kt > 0))
            evac(c0, PS[pi][:, :n], b_ps[pi])

    def blocks_range(c_lo, c_hi):
        return list(range(c_lo, c_hi, 128))

    def post_norm_residual(m3, b_m, kind, layer, n):
        rstd, b_r = rmsnorm_stats(lambda i: m3[:, i, :n], 16, n, float(D), b_m)
        for ft in range(16):
            eng = "dve"
            P.op(eng, "scalar_tensor_tensor", m3[:, ft, :n], m3[:, ft, :n], gain(kind, layer, ft), rstd, op0=ALU.mult, op1=ALU.mult,
                 reads=[b_m[ft], b_r, b_par], writes=[b_m[ft]])
            P.op(eng, "tensor_tensor", x_fm[:, ft, :n], x_fm[:, ft, :n], m3[:, ft, :n], op=ALU.add,
                 reads=[b_m[ft], b_x[ft]], writes=[b_x[ft]])

    def pre_norm(hn3, b_hn, kind, layer, n):
        rstd, b_r = rmsnorm_stats(lambda i: x_fm[:, i, :n], 16, n, float(D), b_x)
        for ft in range(16):
            eng = "dve"
            P.op(eng, "scalar_tensor_tensor", R(hn3[:, ft, :n]), x_fm[:, ft, :n], gain(kind, layer, ft), rstd, op0=ALU.mult, op1=ALU.mult,
                 reads=[b_x[ft], b_r, b_par], writes=[b_hn[ft]])

    A0 = 0
    A1 = 16 * W
    SZ16 = 16 * W
    b_A0 = [Buf(f"A0_{i}") for i in range(16)]
    b_A1 = [Buf(f"A1_{i}") for i in range(40)]


    def sample_attention(q3, b_q, ksamp, b_ks, oc3, b_oc, npr, AT):
        P.barrier()
        o = AT
        def T(nel):
            nonlocal o
            v = arena[:, o:o + nel]; o += nel
            return v
        kpg = [T(1024) for _ in range(2)]; vpg = [R(T(1024)) for _ in range(2)]
        ktp = [R(T(1024)).rearrange("p (a b) -> p a b", a=8) for _ in range(2)]
        qblk = T(512); ptb = ptb_t[:]; idx = idx_t[:]; iot = iot_t[:]
        ptp = [R(T(64)) for _ in range(2)]; vst = R(T(1024))
        psb = arenaF[:, 0:128]; tms = arenaF[:, 128:256]; oms = arenaF[:, 256:320]; ods = arenaF[:, 320:352]; rss = arenaF[:, 352:384]
        sqs = smr(1, 32)
        assert o <= ARENA
        bk = [Buf("kpg0"), Buf("kpg1")]; bv = [Buf("vpg0"), Buf("vpg1")]; bkt = [Buf("ktp0"), Buf("ktp1")]
        bq = Buf("qblk"); bi = Buf("idx"); bp = Buf("psb"); bpt = [Buf("ptp0"), Buf("ptp1")]; bt = Buf("tms"); bvs = Buf("vst"); bm = Buf("misc")
        P.dma("pool", ptb, pt_d.partition_broadcast(128), writes=[bi], sem=bi)
        P.dma("sp", iot, iota_d, writes=[bi], sem=bi, partial=True)
        P.op("dve", "tensor_scalar", idx, ptb, 128, iot[:, 0:1], op0=ALU.mult, op1=ALU.add, reads=[bi], writes=[bi], partial=True)
        P.dma("sp", vst[:4, :], R(Vscr[NPR:NPR + 4, :]), reads=[b_vscr], writes=[bvs], sem=bvs)
        q4 = qblk.rearrange("p (h c) -> p h c", h=8)
        for qi in range(4):
            P.op("dve", "tensor_copy", R(qblk[:, qi * 128:(qi + 1) * 128]), zeros[:, 0:128], reads=[b_zero], writes=[bq], partial=(qi > 0))
        for h in range(8):
            for m in range(2):
                P.op("dve", "tensor_copy", R(q4[64 * m:64 * m + 64, h, h * 8 + m * 4:h * 8 + m * 4 + 4]), q3[64 * m:64 * m + 64, h, npr:npr + 4],
                     reads=[b_q[h]], writes=[bq], partial=True)
        qb = R(q4)
        o_ps, d_ps, s_ps, t_ps = PS[2], PS[5], PS[7], PS[4]
        for i in range(128):
            s = i % 2
            P.dma_custom("pool", "indirect_dma_start", reads=[bi], writes=[bk[s]], sem=bk[s],
                         out=R(kpg[s]), out_offset=None, in_=R(ck_d), in_offset=bass.IndirectOffsetOnAxis(ap=idx[:, i:i + 1], axis=0))
            P.dma_custom("pool", "indirect_dma_start", reads=[bi], writes=[bv[s]], sem=bv[s],
                         out=vpg[s], out_offset=None, in_=R(cv_d), in_offset=bass.IndirectOffsetOnAxis(ap=idx[:, i:i + 1], axis=0))
            for h in range(8):
                P.op("pe", "transpose", PS[h // 4][:, (h % 4) * 128:(h % 4 + 1) * 128], kpg[s][:, h * 128:(h + 1) * 128], ident[:, :],
                     reads=[bk[s], b_ident], writes=[b_ps[h // 4]], partial=(h % 4 > 0))
            copy_op("act", ktp[s][:, 0:4, :], PS[0][:, :].rearrange("p (a b) -> p a b", a=4), reads=[b_ps[0]], writes=[bkt[s]])
            copy_op("dve", ktp[s][:, 4:8, :], PS[1][:, :].rearrange("p (a b) -> p a b", a=4), reads=[b_ps[1]], writes=[bkt[s]], partial=True)
            for h in range(8):
                P.op("pe", "matmul", s_ps[:64, 0:128], lhsT=qb[:, h, :], rhs=ktp[s][:, h, :], start=(h == 0), stop=(h == 7),
                     reads=[bq, bkt[s]], writes=[b_ps[7]], partial=(h > 0))
            P.op("pool", "tensor_scalar", tms[:64, :], stab[:, 8:136], float(128 * i), slopecol[:, 0:1], op0=ALU.add, op1=ALU.mult,
                 reads=[b_par], writes=[bt])
            P.op("dve", "tensor_tensor", tms[:64, :], tms[:64, :], s_ps[:64, 0:128], op=ALU.add, reads=[bt, b_ps[7]], writes=[bt])
            P.op("act", "activation", psb[:64, :], tms[:64, :], AF.Exp, reads=[bt], writes=[bp])
            P.op("pe", "transpose", t_ps[:, 0:64], psb[:64, :], ident[:64, :64], reads=[bp, b_ident], writes=[b_ps[4]])
            copy_op("dve", ptp[s], t_ps[:, 0:64], reads=[b_ps[4]], writes=[bpt[s]])
            for h in range(8):
                P.op("pe", "matmul", o_ps[:, h * 8:(h + 1) * 8], lhsT=vpg[s][:, h * 128:(h + 1) * 128], rhs=ptp[s][:, h * 8:(h + 1) * 8],
                     start=(i == 0), stop=False, reads=[bv[s], bpt[s]], writes=[b_ps[2]], partial=True)
            P.op("pe", "matmul", d_ps[:, 0:64], lhsT=ones_r[:], rhs=ptp[s], start=(i == 0), stop=False,
                 reads=[b_ones, bpt[s]], writes=[b_ps[5]], partial=True)
        for h in range(8):
            P.op("pe", "matmul", s_ps[:64, 0:4], lhsT=qb[:, h, :], rhs=R(ksamp[:, h, :]), start=(h == 0), stop=(h == 7),
                 reads=[bq, b_ks], writes=[b_ps[7]], partial=(h > 0))
        P.op("dve", "scalar_tensor_tensor", tms[:64, 0:4], stab[:, 0:4], slopecol[:, 0:1], stab[:, 4:8], op0=ALU.mult, op1=ALU.add,
             reads=[b_par], writes=[bt])
        P.op("dve", "tensor_tensor", tms[:64, 0:4], tms[:64, 0:4], s_ps[:64, 0:4], op=ALU.add, reads=[bt, b_ps[7]], writes=[bt])
        P.op("act", "activation", psb[:64, 0:4], tms[:64, 0:4], AF.Exp, reads=[bt], writes=[bp])
        P.op("pe", "transpose", t_ps[:4, 0:64], psb[:64, 0:4], ident[:64, :64], reads=[bp, b_ident], writes=[b_ps[4]])
        copy_op("dve", ptp[0][:4, :], t_ps[:4, 0:64], reads=[b_ps[4]], writes=[bpt[0]])
        for h in range(8):
            P.op("pe", "matmul", o_ps[:, h * 8:(h + 1) * 8], lhsT=vst[:4, h * 128:(h + 1) * 128], rhs=ptp[0][:4, h * 8:(h + 1) * 8],
                 start=False, stop=True, reads=[bvs, bpt[0]], writes=[b_ps[2]], partial=True)
        P.op("pe", "matmul", d_ps[:, 0:64], lhsT=ones_r[:4, :], rhs=ptp[0][:4, :], start=False, stop=True,
             reads=[b_ones, bpt[0]], writes=[b_ps[5]], partial=True)
        P.op("dve", "reciprocal", oms, d_ps[:, 0:64], reads=[b_ps[5]], writes=[bm])
        P.op("dve", "tensor_tensor", oms, oms, o_ps[:, 0:64], op=ALU.mult, reads=[bm, b_ps[2]], writes=[bm])
        om4 = oms.rearrange("p (h m q) -> p h m q", h=8, m=2)
        od3 = ods.rearrange("p (h q) -> p h q", h=8)
        P.op("dve", "scalar_tensor_tensor", od3, om4[:, :, 1, :], neglam, om4[:, :, 0, :], op0=ALU.mult, op1=ALU.add, reads=[bm, b_par], writes=[bm], partial=True)
        P.op("act", "activation", sqs, ods, AF.Square, reads=[bm], writes=[b_smr[1]])
        P.op("pe", "matmul", PS[4][:, 0:32], lhsT=ones_r[:], rhs=sqs, start=True, stop=True, reads=[b_smr[1], b_ones], writes=[b_ps[4]])
        P.op("act", "activation", rss, PS[4][:, 0:32], AF.Sqrt, bias=EPS, scale=1.0 / 128, reads=[b_ps[4]], writes=[bm], partial=True)
        P.op("dve", "reciprocal", rss, rss, reads=[bm], writes=[bm], partial=True)
        P.op("dve", "scalar_tensor_tensor", R(oc3[:, 0:8, npr:npr + 4]), od3, subg2, rss.rearrange("p (h q) -> p h q", h=8), op0=ALU.mult, op1=ALU.mult,
             reads=[bm, b_par], writes=list(b_oc[0:8]), partial=True)
        P.barrier()

    rgn = [Buf(f"ssm_r{i}") for i in range(4)]

    def ssm(u3, u3e, b_u, y3, b_y, npr, ns):
        SB = A1 + 16 * (W + 1)
        o = SB
        def T(nel):
            nonlocal o
            v = arena[:, o:o + nel]; o += nel
            return v
        TSZ = npr // 4
        TS = TSZ // 4
        scan = arenaF[:, 0:3328].rearrange("g (t s) -> g t s", s=128)
        bd = [R(T(1024)) for _ in range(2)]; bd2 = [R(T(1024)) for _ in range(2)]
        bus = [arenaF[:, 3328 + i * 1024: 4352 + i * 1024] for i in range(3)]; b_bus = rgn[0:3]
        htm = [arenaF[:, 3328 + i * 1024: 4352 + i * 1024] for i in range(2)]; b_htm = rgn[0:2]
        ytm = arenaF[:, 5376:7424]; b_ytm = [rgn[2], rgn[3]]
        hT = [R(T(1024)).rearrange("p (a b) -> p a b", a=8) for _ in range(2)]
        assert o <= ARENA, o
        b_scan = Buf("scan"); b_bdb = [Buf("bd0"), Buf("bd1")]; b_bdb2 = [Buf("bd20"), Buf("bd21")]
        b_hT = [Buf("hT0"), Buf("hT1")]
        b_t1 = [Buf("t1a"), Buf("t1b")]; b_t2 = [Buf("t2a"), Buf("t2b")]; b_s1 = [Buf("s1a"), Buf("s1b")]
        tiles = [(i * TSZ, TSZ, False) for i in range(4)] + ([(npr, ns, True)] if ns else [])
        P.stage = "ssm_A"
        k = 0
        for f in range(16):
            sb_ = f % 2
            P.dma("sp", bd[sb_], R(BDscr[f]), reads=[b_bd], writes=[b_bdb[sb_]], sem=b_bdb[sb_])
            P.dma("sp", bd2[sb_], R(BDscr2[f]), reads=[b_bd], writes=[b_bdb2[sb_]], sem=b_bdb2[sb_])
            for (t0, nt, is_s) in tiles:
                s = k % 3; k += 1
                for hf in range(2):
                    P.op("pe", "matmul", PS[hf][:nt, :], lhsT=R(u3e[:, f, 1 + t0:1 + t0 + nt]), rhs=bd[sb_][:, hf * 512:(hf + 1) * 512], start=True, stop=is_s,
                         reads=[b_u[f], b_bdb[sb_]], writes=[b_ps[hf]])
                    if not is_s:
                        P.op("pe", "matmul", PS[hf][:nt, :], lhsT=R(u3e[:, f, t0:t0 + nt]), rhs=bd2[sb_][:, hf * 512:(hf + 1) * 512], start=False, stop=True,
                             reads=[b_u[f], b_bdb2[sb_]], writes=[b_ps[hf]], partial=True)
                    copy_op("act", bus[s][:nt, hf * 512:(hf + 1) * 512], PS[hf][:nt, :], reads=[b_ps[hf]], writes=[b_bus[s]], partial=(hf > 0))
                P.dma("pool", BuScr[t0:t0 + nt, f * 8:(f + 1) * 8, :], bus[s][:nt, :].rearrange("t (g s) -> t g s", g=8),
                      reads=[b_bus[s]], writes=[b_buscr], sem=b_bus[s], partial=True)
        X3 = XY[:, 0:128].rearrange("g (r p) -> g r p", r=2); Yn = XY[:, 128:192]; Yp = XY[:, 192:256]
        t1s = sm(0, 128).rearrange("g (r p) -> g r p", r=2); t2s = sm(1, 128).rearrange("g (r p) -> g r p", r=2)
        s1s = sm(2, 128).rearrange("g (r p) -> g r p", r=2)

        X2 = XY2[:, :, 0:128]; Y2n = XY2[:, :, 128:192]; Y2p = XY2[:, :, 192:256]
        p1 = sm(0, 256).rearrange("g (t s) -> g t s", t=2); p2 = sm(1, 256).rearrange("g (t r p) -> g t r p", t=2, r=2)
        p3 = sm(2, 256).rearrange("g (t s) -> g t s", t=2)
        b_p1 = Buf("p1"); b_p2 = Buf("p2"); b_p3 = Buf("p3")

        def scan_pairs(t0, nt):
            P.stage = "ssm_scan"
            sizes = [26, 26, 26, 26] if nt == 104 else [26, 26, 24, 24]
            assert sum(sizes) == nt
            ts0 = t0
            for tn in sizes:
                P.dma("pool", scan[:, 0:tn, :], BuScr[ts0:ts0 + tn].rearrange("t g s -> g t s"), reads=[b_buscr], writes=[b_scan], sem=b_scan)
                for i in range(tn // 2):
                    prev = S_p[:, :, :] if i == 0 else scan[:, 2 * i - 2:2 * i, :]
                    pb = [b_Sp] if i == 0 else [b_scan]
                    cur = scan[:, 2 * i:2 * i + 2, :]
                    pv4 = prev.rearrange("g t (r p) -> g t r p", r=2)
                    P.op("dve", "tensor_tensor", p1, prev, X2, op=ALU.mult, reads=pb + [b_xy2], writes=[b_p1])
                    P.op("dve", "tensor_tensor", p2[:, :, 0, :], pv4[:, :, 1, :], Y2n, op=ALU.mult, reads=pb + [b_xy2], writes=[b_p2])
                    P.op("dve", "tensor_tensor", p2[:, :, 1, :], pv4[:, :, 0, :], Y2p, op=ALU.mult, reads=pb + [b_xy2], writes=[b_p2], partial=True)
                    P.op("dve", "tensor_tensor", p3, cur, p1, op=ALU.add, reads=[b_p1, b_scan], writes=[b_p3])
                    P.op("dve", "tensor_tensor", cur, p3, p2.rearrange("g t r p -> g t (r p)"), op=ALU.add, reads=[b_p2, b_p3], writes=[b_scan], partial=True)
                P.op("dve", "tensor_copy", S_p[:, :, :], scan[:, tn - 2:tn, :], reads=[b_scan], writes=[b_Sp])
                P.dma("pool", HScr[ts0:ts0 + tn].rearrange("t g s -> g t s"), scan[:, 0:tn, :], reads=[b_scan], writes=[b_hscr], sem=b_scan, partial=True)
                ts0 += tn

        def scan_tile(t0, nt, is_s):
            if not is_s:
                return scan_pairs(t0, nt)
            P.stage = "ssm_scan"
            St, b_St = (S_s, b_Ss)
            subs = [(t0, nt)]
            for (ts0, tn) in subs:
                P.dma("pool", scan[:, 0:tn, :], BuScr[ts0:ts0 + tn].rearrange("t g s -> g t s"), reads=[b_buscr], writes=[b_scan], sem=b_scan)
                for t in range(tn):
                    prev = St[:] if t == 0 else scan[:, t - 1, :]
                    pb = [b_St] if t == 0 else [b_scan]
                    pv = prev.rearrange("g (r p) -> g r p", r=2)
                    cv_ = scan[:, t, :].rearrange("g (r p) -> g r p", r=2)
                    hs = [(0, 32), (32, 64)]
                    for si, (a, b_) in enumerate(hs):
                        P.op("dve", "tensor_tensor", t1s[:, :, a:b_], pv[:, :, a:b_], X3[:, :, a:b_], op=ALU.mult, reads=pb + [b_xy], writes=[b_t1[si]])
                    for si, (a, b_) in enumerate(hs):
                        P.op("dve", "tensor_tensor", t2s[:, 0, a:b_], pv[:, 1, a:b_], Yn[:, a:b_], op=ALU.mult, reads=pb + [b_xy], writes=[b_t2[si]])
                    for si, (a, b_) in enumerate(hs):
                        P.op("dve", "tensor_tensor", t2s[:, 1, a:b_], pv[:, 0, a:b_], Yp[:, a:b_], op=ALU.mult, reads=pb + [b_xy], writes=[b_t2[si]], partial=True)
                    for si, (a, b_) in enumerate(hs):
                        P.op("dve", "tensor_tensor", s1s[:, :, a:b_], cv_[:, :, a:b_], t1s[:, :, a:b_], op=ALU.add, reads=[b_t1[si], b_scan], writes=[b_s1[si]])
                    for si, (a, b_) in enumerate(hs):
                        P.op("dve", "tensor_tensor", cv_[:, :, a:b_], s1s[:, :, a:b_], t2s[:, :, a:b_], op=ALU.add, reads=[b_t2[si], b_s1[si]], writes=[b_scan], partial=True)
                P.op("dve", "tensor_copy", St[:], scan[:, tn - 1, :], reads=[b_scan], writes=[b_St])
                P.dma("pool", HScr[ts0:ts0 + tn].rearrange("t g s -> g t s"), scan[:, 0:tn, :], reads=[b_scan], writes=[b_hscr], sem=b_scan, partial=True)

        kc = {"k": 0}

        def c_tile(t0, nt):
            P.stage = "ssm_C"
            for f in range(16):
                s = kc["k"] % 2; kc["k"] += 1
                P.dma("sp", htm[s][:nt, :], HScr[t0:t0 + nt, f * 8:(f + 1) * 8, :].rearrange("t g s -> t (g s)"), reads=[b_hscr], writes=[b_htm[s]], sem=b_htm[s])
                for gl in range(8):
                    P.op("pe", "transpose", PS[2 + gl // 4][:, (gl % 4) * 128:(gl % 4) * 128 + nt], htm[s][:nt, gl * 128:(gl + 1) * 128], ident[:nt, :nt],
                         reads=[b_htm[s], b_ident], writes=[b_ps[2 + gl // 4]], partial=(gl % 4 > 0))
                copy_op("act", hT[s][:, 0:4, :nt], PS[2][:, :].rearrange("p (a b) -> p a b", a=4)[:, :, :nt], reads=[b_ps[2]], writes=[b_hT[s]])
                copy_op("act", hT[s][:, 4:8, :nt], PS[3][:, :].rearrange("p (a b) -> p a b", a=4)[:, :, :nt], reads=[b_ps[3]], writes=[b_hT[s]], partial=True)
                for gl in range(8):
                    P.op("pe", "matmul", PS[4 + f // 4][:nt, (f % 4) * 128 + gl * 16:(f % 4) * 128 + gl * 16 + 16], lhsT=hT[s][:, gl, :nt], rhs=ccat[:, f * 8 + gl, :],
                         start=True, stop=True, reads=[b_hT[s], b_ccat], writes=[b_ps[4 + f // 4]], partial=True)
            for bq_ in range(4):
                copy_op("act", ytm[:nt, bq_ * 512:(bq_ + 1) * 512], PS[4 + bq_][:nt, :], reads=[b_ps[4 + bq_]], writes=b_ytm, partial=(bq_ > 0))
            for fg in range(4):
                pb_ = fg % 2
                for kk in range(4):
                    f = fg * 4 + kk
                    P.op("pe", "transpose", PS[pb_][:, kk * 128:kk * 128 + nt], ytm[:nt, f * 128:(f + 1) * 128], ident[:nt, :nt],
                         reads=b_ytm + [b_ident], writes=[b_ps[pb_]], partial=(kk > 0))
                for kk in range(4):
                    f = fg * 4 + kk
                    P.op("dve", "scalar_tensor_tensor", y3[:, f, t0:t0 + nt], u3[:, f, t0:t0 + nt], dfm[:, f:f + 1], PS[pb_][:, kk * 128:kk * 128 + nt],
                         op0=ALU.mult, op1=ALU.add, reads=[b_u[f], b_ps[pb_], b_par], writes=[b_y[f]], partial=True)

        P.op("dve", "tensor_copy", u_last[:].unsqueeze(2), u3e[:, :, npr:npr + 1], reads=list(b_u), writes=[b_ul])
        prev_tile = None
        for (t0, nt, is_s) in tiles:
            scan_tile(t0, nt, is_s)
            if prev_tile is not None:
                c_tile(*prev_tile)
            prev_tile = (t0, nt)
        c_tile(*prev_tile)

    for ci, (c0, c1) in enumerate(CHUNKS):
        n = c1 - c0
        npr = min(c1, NPR) - c0
        ns = n - npr
        ttiles = [(t0, min(128, n - t0)) for t0 in range(0, n, 128)]

        P.stage = "xload"
        xst = [arenaF[:, 0:2048], arenaF[:, 0:2048]]
        b_xst = [b_AF[0], b_AF[0]]
        for ti, (t0, nt) in enumerate(ttiles):
            s = ti % 2
            P.dma("sp", xst[s][:nt, :], xin[c0 + t0:c0 + t0 + nt, :], writes=[b_xst[s]], sem=b_xst[s])
            for fg in range(4):
                pb = 2 + (fg % 2)
                for k in range(4):
                    ft = fg * 4 + k
                    P.op("pe", "transpose", PS[pb][:, k * 128:k * 128 + nt], xst[s][:nt, ft * 128:(ft + 1) * 128], ident[:nt, :nt],
                         reads=[b_xst[s], b_ident], writes=[b_ps[pb]], partial=(k > 0))
                eng = ev_eng()
                copy_op(eng, x_fm[:, fg * 4:fg * 4 + 4, t0:t0 + nt], PS[pb][:].rearrange("p (a b) -> p a b", a=4)[:, :, :nt],
                        reads=[b_ps[pb]], writes=[b_x[fg * 4 + k] for k in range(4)], partial=True)
        P.barrier()

        P.stage = "l0_inproj"
        hn3 = av(A0, 16, W); b_hn = b_A0
        pre_norm(hn3, b_hn, 0, 0, n)
        q3 = av(A1, 8, W); b_q = b_A1[0:8]
        GW = 30 + W
        gext = av(A1 + 8 * W, 8, GW); b_g = b_A1[8:16]
        conv3 = arenaF[:, 3072:3072 + 8 * W].rearrange("p (a b) -> p a b", a=8); b_cv = b_A1[16:24]
        OFFX = A1 + 8 * W + 8 * GW
        gs_ext = av(OFFX, 8, 34); b_gs = b_A1[24]
        ksamp = av(OFFX + 272, 8, 4); b_ks = b_A1[25]
        OFFX2 = OFFX + 272 + 32
        for j in range(8):
            if ci == 0:
                P.op("dve", "tensor_copy", gext[:, j, 0:30], zeros[:, 0:30], reads=[b_zero], writes=[b_g[j]])
            else:
                P.op("dve", "tensor_copy", gext[:, j, 0:30], gtail[:, j, :], reads=[b_gt], writes=[b_g[j]])
        if ns:
            scst = arenaF[:, 2048:3072]; b_sc = b_AF[1]
            P.dma("sp", scst[:30, :], sconv_d, writes=[b_sc], sem=b_sc)
            for j in range(8):
                P.op("pe", "transpose", PS[2][:, j * 32:j * 32 + 30], scst[:30, j * 128:(j + 1) * 128], ident[:30, :30],
                     reads=[b_sc, b_ident], writes=[b_ps[2]], partial=(j > 0))
            copy_op("dve", gs_ext[:, :, 0:30], PS[2][:, 0:256].rearrange("p (a b) -> p a b", a=8)[:, :, 0:30], reads=[b_ps[2]], writes=[b_gs], partial=True)

        def kv_out(which, h, src, b_src):
            dst = kout if which == "k" else vout
            for ti, (t0, nt) in enumerate(ttiles):
                P.op("pe", "transpose", PS[3][:nt, ti * 128:(ti + 1) * 128], src[:, t0:t0 + nt], ident[:, :],
                     reads=[b_src, b_ident], writes=[b_ps[3]], partial=(ti > 0))
            sl = 2 + (state["evac"] % 2)
            stg = arenaF[:, 3072 + (sl - 2) * 512: 3072 + (sl - 1) * 512]
            copy_op(ev_eng(), stg, PS[3][:, :], reads=[b_ps[3]], writes=[b_AF[sl]])
            for ti, (t0, nt) in enumerate(ttiles):
                P.dma("pool", dst[c0 + t0:c0 + t0 + nt, h * 128:(h + 1) * 128], stg[:nt, ti * 128:(ti + 1) * 128],
                      reads=[b_AF[sl]], writes=[b_out], sem=b_AF[sl], partial=True)
                if which == "v":
                    P.dma("pool", Vscr[c0 + t0:c0 + t0 + nt, h * 128:(h + 1) * 128], stg[:nt, ti * 128:(ti + 1) * 128],
                          reads=[b_AF[sl]], writes=[b_vscr], sem=b_AF[sl], partial=True)

        def evac_in_even(col0, ps, b_p):
            t = col0 // 128
            if t < 8:
                P.op("act", "activation", R(q3[:, t, :n]), ps, AF.Copy, scale=0.125, reads=[b_p], writes=[b_q[t]])
            elif t < 24:
                which = "k" if t < 16 else "v"
                h = t % 8
                sl = state["evac"] % 2
                tmp = sm(sl, n)
                copy_op(ev_eng(), tmp, ps, reads=[b_p], writes=[b_small[sl]])
                if which == "k":
                    P.dma("pool", KTscr[h, :, c0:c0 + npr], tmp[:, :npr], reads=[b_small[sl]], writes=[b_ktscr], sem=b_small[sl], partial=True)
                    if ns:
                        P.op("dve", "tensor_copy", R(ksamp[:, h, :]), tmp[:, npr:n], reads=[b_small[sl]], writes=[b_ks], partial=True)
                kv_out(which, h, tmp, b_small[sl])
            elif t < 32:
                P.op("act", "copy", sm(4, n), ps, reads=[b_p], writes=[b_small[4]])
            else:
                j = t - 32
                P.op("act", "activation", sm(5, n), ps, AF.Sigmoid, reads=[b_p], writes=[b_small[5]])
                P.op("dve", "tensor_tensor", gext[:, j, 30:30 + npr], sm(4, n)[:, :npr], sm(5, n)[:, :npr], op=ALU.mult,
                     reads=[b_small[4], b_small[5]], writes=[b_g[j]], partial=True)
                if ns:
                    P.op("dve", "tensor_tensor", gs_ext[:, j, 30:34], sm(4, n)[:, npr:n], sm(5, n)[:, npr:n], op=ALU.mult,
                         reads=[b_small[4], b_small[5]], writes=[b_gs], partial=True)

        blocks = blocks_range(0, 3072) + [c for j in range(8) for c in (3072 + 128 * j, 4096 + 128 * j)]
        linear(w_in_even, 16, blocks, lambda kt: R(hn3[:, kt, :n]), b_hn, n, evac_in_even)
        P.barrier()
        if stop_after == "inproj":
            break

        P.stage = "conv"
        oc3 = av(A0, 16, W); b_oc = b_A0
        for w in range(31):
            for j in range(8):
                segs = [(gext, b_g[j], 0, npr)] + ([(gs_ext, b_gs, npr, ns)] if ns else [])
                for (gsrc, b_src, o0, ln) in segs:
                    if w == 0:
                        P.op("dve", "tensor_scalar", conv3[:, j, o0:o0 + ln], gsrc[:, j, 0:ln], cw[:, j, 0:1], cvec[:, 0, j:j + 1], op0=ALU.mult, op1=ALU.add,
                             reads=[b_src, b_par], writes=[b_cv[j]], partial=True)
                    else:
                        P.op("dve", "scalar_tensor_tensor", conv3[:, j, o0:o0 + ln], gsrc[:, j, w:w + ln], cw[:, j, w:w + 1], conv3[:, j, o0:o0 + ln], op0=ALU.mult, op1=ALU.add,
                             reads=[b_src, b_par, b_cv[j]], writes=[b_cv[j]], partial=True)
        if ns:
            for (gsrc, b_src, lo, dsto) in [(gext, None, npr, convp_o), (gs_ext, b_gs, 4, convs_o)]:
                for j in range(8):
                    bs = b_g[j] if b_src is None else b_src
                    P.op("pe", "transpose", PS[2 + j // 4][:30, (j % 4) * 128:(j % 4) * 128 + 128], gsrc[:, j, lo:lo + 30], ident[:, :],
                         reads=[bs, b_ident], writes=[b_ps[2 + j // 4]], partial=(j % 4 > 0))
                stg = arenaF[:, 2048:3072]
                copy_op("act", stg[:30, 0:512], PS[2][:30, :], reads=[b_ps[2]], writes=[b_AF[1]])
                copy_op("dve", stg[:30, 512:1024], PS[3][:30, :], reads=[b_ps[3]], writes=[b_AF[1]], partial=True)
                P.dma("pool", dsto, stg[:30, :], reads=[b_AF[1]], writes=[b_out], sem=b_AF[1], partial=True)
        else:
            pass
        for j in range(8):
            sl = j % 2
            P.op("act", "copy", smr(sl, n), conv3[:, j, :n], reads=[b_cv[j]], writes=[b_smr[sl]])
            P.op("pe", "matmul", PS[4][:, :n], lhsT=ones_r[:], rhs=smr(sl, n), start=(j == 0), stop=(j == 7),
                 reads=[b_smr[sl], b_ones], writes=[b_ps[4]], partial=(j > 0))
        for j in range(8):
            sl = j % 2
            P.op("act", "activation", smr(sl, n), conv3[:, j, :n], AF.Square, reads=[b_cv[j]], writes=[b_smr[sl]])
            P.op("pe", "matmul", PS[5][:, :n], lhsT=ones_r[:], rhs=smr(sl, n), start=(j == 0), stop=(j == 7),
                 reads=[b_smr[sl], b_ones], writes=[b_ps[5]], partial=(j > 0))
        mean = sm(0, n); rstd = sm(1, n); msq = sm(2, n)
        P.op("dve", "tensor_scalar", mean, PS[4][:, :n], 1.0 / 1024, None, op0=ALU.mult, reads=[b_ps[4]], writes=[b_small[0]])
        P.op("dve", "tensor_tensor", msq, mean, mean, op=ALU.mult, reads=[b_small[0]], writes=[b_small[2]])
        P.op("dve", "scalar_tensor_tensor", rstd, PS[5][:, :n], 1.0 / 1024, msq, op0=ALU.mult, op1=ALU.subtract, reads=[b_ps[5], b_small[2]], writes=[b_small[1]])
        P.op("act", "activation", rstd, rstd, AF.Sqrt, bias=EPS, scale=1.0, reads=[b_small[1]], writes=[b_small[1]])
        P.op("dve", "reciprocal", rstd, rstd, reads=[b_small[1]], writes=[b_small[1]])
        for j in range(8):
            eng = "dve" if j % 2 == 0 else "pool"
            P.op(eng, "tensor_tensor", conv3[:, j, :n], conv3[:, j, :n], mean, op=ALU.subtract, reads=[b_cv[j], b_small[0]], writes=[b_cv[j]])
            P.op(eng, "tensor_tensor", conv3[:, j, :n], conv3[:, j, :n], rstd, op=ALU.mult, reads=[b_cv[j], b_small[1]], writes=[b_cv[j]])
            P.op("act", "activation", R(oc3[:, 8 + j, :n]), conv3[:, j, :n], AF.Silu, bias=cvec[:, 2, j:j + 1], scale=cvec[:, 1, j:j + 1],
                 reads=[b_cv[j], b_par], writes=[b_oc[8 + j]])
        if not ns:
            for j in range(8):
                P.op("pool", "tensor_copy", gtail[:, j, :], gext[:, j, npr:npr + 30], reads=[b_g[j]], writes=[b_gt], partial=(j > 0))
        if stop_after == "conv":
            break

        P.stage = "attn"
        AT = OFFX2
        kend = c0 + npr
        nkt = (kend + 127) // 128
        ktb = [arena[:, AT + i * NPR: AT + (i + 1) * NPR].bitcast(F32R) for i in range(2)]; b_ktb = b_A1[27:29]
        VB0 = AT + 2 * NPR
        vb = [arena[:, VB0 + i * 17 * 128: VB0 + (i + 1) * 17 * 128].bitcast(F32R).rearrange("p (a b) -> p a b", a=17) for i in range(2)]; b_vb = b_A1[29:31]
        PB0 = VB0 + 2 * 17 * 128
        pex = [arena[:, PB0 + i * W: PB0 + (i + 1) * W] for i in range(3)]; b_pex = b_A1[31:34]
        tmpb = [arenaF[:, i * W: (i + 1) * W] for i in range(3)]; b_tmp = b_A1[34:37]
        mtmp = arenaF[:, 3 * W: 4 * W]; b_mt = b_A1[37]
        ENDAT = PB0 + 3 * W
        assert ENDAT <= ARENA, ENDAT
        SCB = [0, 1, 7]
        for h in range(8):
            s = h % 2
            P.dma("sp", ktb[s][:, 0:kend], R(KTscr[h, :, 0:kend]), reads=[b_ktscr], writes=[b_ktb[s]], sem=b_ktb[s])
            nfull = kend // 128
            if nfull:
                P.dma("sp", vb[s][:, 0:nfull, :], R(Vscr[0:nfull * 128, h * 128:(h + 1) * 128].rearrange("(a p) d -> p a d", p=128)),
                      reads=[b_vscr], writes=[b_vb[s]], sem=b_vb[s])
            if kend % 128:
                P.dma("sp", vb[s][:kend % 128, nfull, :], R(Vscr[nfull * 128:kend, h * 128:(h + 1) * 128]),
                      reads=[b_vscr], writes=[b_vb[s]], sem=b_vb[s], partial=True)
            work = [(m, kt) for m in range(2) for kt in range(nkt)]

            def front(i):
                m, kt = work[i]
                k0 = kt * 128; kn = min(128, kend - k0)
                sc = SCB[i % 3]; tb = i % 3; pe_i = i % 3
                P.op("pe", "matmul", PS[sc][:kn, :npr], lhsT=ktb[s][64 * m:64 * m + 64, k0:k0 + kn],
                     rhs=R(q3[64 * m:64 * m + 64, h, :npr]), start=True, stop=True,
                     reads=[b_ktb[s], b_q[h]], writes=[b_ps[sc]])
                P.op("dve", "scalar_tensor_tensor", tmpb[tb][:kn, :npr], d0tab[:kn, :npr], SLOPES[h], PS[sc][:kn, :npr], op0=ALU.mult, op1=ALU.add,
                     reads=[b_ps[sc], b_sid], writes=[b_tmp[tb]])
                if k0 + kn - 1 > c0:
                    P.op("dve", "tensor_scalar", mtmp[:kn, :npr], d0tab[:kn, :npr], float(k0 - c0), 0.0, op0=ALU.add, op1=ALU.is_gt,
                         reads=[b_sid], writes=[b_mt])
                    P.op("dve", "scalar_tensor_tensor", tmpb[tb][:kn, :npr], mtmp[:kn, :npr], NEG, tmpb[tb][:kn, :npr], op0=ALU.mult, op1=ALU.add,
                         reads=[b_mt, b_tmp[tb]], writes=[b_tmp[tb]])
                P.op("act", "activation", R(pex[pe_i][:kn, :npr]), tmpb[tb][:kn, :npr], AF.Exp, bias=float(SLOPES[h] * (k0 - c0)), scale=1.0,
                     reads=[b_tmp[tb]], writes=[b_pex[pe_i]])

            def back(i):
                m, kt = work[i]
                k0 = kt * 128; kn = min(128, kend - k0)
                pe_i = i % 3
                o_ps, d_ps = PS[2 + m], PS[5 + m]
                b_o, b_d = b_ps[2 + m], b_ps[5 + m]
                P.op("pe", "matmul", o_ps[:, :npr], lhsT=vb[s][:kn, kt, :], rhs=R(pex[pe_i][:kn, :npr]), start=(kt == 0), stop=(kt == nkt - 1),
                     reads=[b_vb[s], b_pex[pe_i]], writes=[b_o], partial=(kt > 0))
                P.op("pe", "matmul", d_ps[:, :npr], lhsT=ones_r[:kn, :], rhs=R(pex[pe_i][:kn, :npr]), start=(kt == 0), stop=(kt == nkt - 1),
                     reads=[b_ones, b_pex[pe_i]], writes=[b_d], partial=(kt > 0))
                if kt == nkt - 1:
                    P.op("dve", "reciprocal", sm(6, npr), d_ps[:, :npr], reads=[b_d], writes=[b_small[6]])
                    P.op("dve", "tensor_tensor", sm(2 + m, npr), o_ps[:, :npr], sm(6, npr), op=ALU.mult, reads=[b_o, b_small[6]], writes=[b_small[2 + m]])

            LA = 2
            for i in range(min(LA, len(work))):
                front(i)
            for i in range(len(work)):
                if i + LA < len(work):
                    front(i + LA)
                back(i)
            if True:
                if True:
                    pass
            P.op("dve", "scalar_tensor_tensor", sm(2, npr), sm(3, npr), neglam, sm(2, npr), op0=ALU.mult, op1=ALU.add,
                 reads=[b_small[2], b_small[3], b_par], writes=[b_small[2]])
            P.op("act", "activation", smr(0, npr), sm(2, npr), AF.Square, reads=[b_small[2]], writes=[b_smr[0]])
            P.op("pe", "matmul", PS[4][:, :npr], lhsT=ones_r[:], rhs=smr(0, npr), start=True, stop=True, reads=[b_smr[0], b_ones], writes=[b_ps[4]])
            P.op("act", "activation", sm(6, npr), PS[4][:, :npr], AF.Sqrt, bias=EPS, scale=1.0 / 128, reads=[b_ps[4]], writes=[b_small[6]])
            P.op("dve", "reciprocal", sm(6, npr), sm(6, npr), reads=[b_small[6]], writes=[b_small[6]])
            P.op("dve", "scalar_tensor_tensor", R(oc3[:, h, :npr]), sm(2, npr), subg2, sm(6, npr), op0=ALU.mult, op1=ALU.mult,
                 reads=[b_small[2], b_small[6], b_par], writes=[b_oc[h]], partial=True)

        if ns:
            P.stage = "sattn"
            sample_attention(q3, b_q, ksamp, b_ks, oc3, b_oc, npr, AT)
        if stop_after == "attn":
            dump(0, oc3[:, :, :], b_oc, ci)
            break
        P.barrier()
        dump(0, oc3[:, :, :], b_oc, ci)

        P.stage = "outproj0"
        m3 = av(A1, 16, W); b_m = b_A1[0:16]

        def evac_m(col0, ps, b_p, m3=m3, b_m=b_m):
            t = col0 // 128
            copy_op(ev_eng(), m3[:, t, :n], ps, reads=[b_p], writes=[b_m[t]])
        linear(w_out_even, 16, blocks_range(0, D), lambda kt: R(oc3[:, kt, :n]), b_oc, n, evac_m)
        post_norm_residual(m3, b_m, 1, 0, n)
        P.barrier()
        dump(1, x_fm[:, :, :], b_x, ci)
        if stop_after == "mix0":
            break

        for layer in range(2):
            if layer == 1:
                P.stage = "l1_inproj"
                pre_norm(hn3, b_hn, 0, 1, n)
                u3e = av(A1, 16, W + 1); b_u = b_A1[0:16]
                u3 = u3e[:, :, 1:1 + W]
                P.op("dve", "tensor_copy", R(u3e[:, :, 0:1]), u_last[:].unsqueeze(2), reads=[b_ul], writes=list(b_u), partial=True)

                def evac_u(col0, ps, b_p):
                    t = col0 // 128
                    copy_op(ev_eng(), R(u3[:, t, :n]), ps, reads=[b_p], writes=[b_u[t]], partial=True)
                linear(w_in_odd, 16, blocks_range(0, D), lambda kt: R(hn3[:, kt, :n]), b_hn, n, evac_u)
                P.barrier()
                y3 = av(A0, 16, W); b_y = b_A0
                P.stage = "ssm"
                ssm(u3, u3e, b_u, y3, b_y, npr, ns)
                P.stage = "l1_glu_out"
                P.barrier()
                dump(3, y3[:, :, :], b_y, ci)
                for ft in range(16):
                    eng = "dve" if ft % 2 == 0 else "pool"
                    sl = ft % 2
                    yv = y3[:, ft, :n]
                    P.op(eng, "tensor_tensor", sm(sl, n), yv, yv, op=ALU.mult, reads=[b_y[ft]], writes=[b_small[sl]])
                    P.op(eng, "tensor_scalar", sm(sl, n), sm(sl, n), 0.044715, 1.0, op0=ALU.mult, op1=ALU.add, reads=[b_small[sl]], writes=[b_small[sl]])
                    P.op(eng, "tensor_tensor", sm(sl, n), sm(sl, n), yv, op=ALU.mult, reads=[b_small[sl], b_y[ft]], writes=[b_small[sl]])
                    P.op("act", "activation", sm(sl, n), sm(sl, n), AF.Sigmoid, scale=1.5957691216057308, reads=[b_small[sl]], writes=[b_small[sl]])
                    P.op(eng, "tensor_tensor", R(yv), yv, sm(sl, n), op=ALU.mult, reads=[b_small[sl], b_y[ft]], writes=[b_y[ft]])
                z3 = av(A1, 16, W); b_z = b_A1[0:16]

                def evac_z(col0, ps, b_p):
                    t = col0 // 128
                    sl = 2 + t % 2
                    P.op("act", "activation", sm(sl, n), ps, AF.Sigmoid, reads=[b_p], writes=[b_small[sl]])
                    P.op("dve", "tensor_tensor", R(z3[:, t, :n]), y3[:, t, :n], sm(sl, n), op=ALU.mult, reads=[b_small[sl], b_y[t]], writes=[b_z[t]])
                linear(w_glu, 16, blocks_range(0, D), lambda kt: R(y3[:, kt, :n]), b_y, n, evac_z)
                P.barrier()
                m3 = av(A0, 16, W); b_m = b_A0

                def evac_m1(col0, ps, b_p):
                    t = col0 // 128
                    copy_op(ev_eng(), m3[:, t, :n], ps, reads=[b_p], writes=[b_m[t]])
                linear(w_out_odd, 16, blocks_range(0, D), lambda kt: R(z3[:, kt, :n]), b_z, n, evac_m1)
                post_norm_residual(m3, b_m, 1, 1, n)
                P.barrier()
                dump(4, x_fm[:, :, :], b_x, ci)
                if stop_after == "mix1":
                    break
            P.stage = "ffn"
            pre_norm(hn3, b_hn, 2, layer, n)
            act3 = av(A1, 11, W); b_act = b_A1[0:11]
            yacc = av(A1 + 11 * W, 16, W); b_ya = b_A1[11:27]
            for qd in range(4):
                lo = 1408 * qd

                def evac_gate(col0, ps, b_p, lo=lo):
                    jj = (col0 - lo) // 128
                    P.op("act", "activation", act3[:, jj, :n], ps, AF.Silu, reads=[b_p], writes=[b_act[jj]])

                def evac_up(col0, ps, b_p, lo=lo):
                    jj = (col0 - lo) // 128
                    P.op("dve", "tensor_tensor", R(act3[:, jj, :n]), act3[:, jj, :n], ps, op=ALU.mult, reads=[b_p, b_act[jj]], writes=[b_act[jj]])
                linear(w_gate[layer], 16, blocks_range(lo, lo + 1408), lambda kt: R(hn3[:, kt, :n]), b_hn, n, evac_gate)
                linear(w_up[layer], 16, blocks_range(lo, lo + 1408), lambda kt: R(hn3[:, kt, :n]), b_hn, n, evac_up)

                def evac_down(col0, ps, b_p, qd=qd):
                    t = col0 // 128
                    if qd == 0:
                        copy_op(ev_eng(), yacc[:, t, :n], ps, reads=[b_p], writes=[b_ya[t]])
                    else:
                        P.op("dve", "tensor_tensor", yacc[:, t, :n], yacc[:, t, :n], ps, op=ALU.add, reads=[b_p, b_ya[t]], writes=[b_ya[t]])
                linear(w_down[layer][lo:lo + 1408, :], 11, blocks_range(0, D), lambda kt: R(act3[:, kt, :n]), b_act, n, evac_down)
            post_norm_residual(yacc, b_ya, 3, layer, n)
            P.barrier()
            dump(2 if layer == 0 else 5, x_fm[:, :, :], b_x, ci)
            if stop_after == f"ffn{layer}":
                break
        if stop_after is not None and stop_after != "chunk0":
            break

        P.stage = "ystore"
        yst = [arenaF[:, 0:2048], arenaF[:, 0:2048]]
        b_yst = [b_AF[0], b_AF[0]]
        for ti, (t0, nt) in enumerate(ttiles):
            s = ti % 2
            for fg in range(4):
                pb = 2 + (fg % 2)
                for k in range(4):
                    ft = fg * 4 + k
                    P.op("pe", "transpose", PS[pb][:nt, k * 128:(k + 1) * 128], x_fm[:, ft, t0:t0 + nt], ident[:, :],
                         reads=[b_x[ft], b_ident], writes=[b_ps[pb]], partial=(k > 0))
                copy_op(ev_eng(), yst[s][:nt, fg * 512:(fg + 1) * 512], PS[pb][:nt, :], reads=[b_ps[pb]], writes=[b_yst[s]], partial=(fg > 0))
            P.dma("pool", yout[c0 + t0:c0 + t0 + nt, :], yst[s][:nt, :], reads=[b_yst[s]], writes=[b_out], sem=b_yst[s], partial=True)
        P.barrier()
        if stop_after == "chunk0":
            break

    if stop_after is None:
        P.dma("pool", ssmp_o, S_p[:, 1, :], reads=[b_Sp], writes=[b_out], sem=b_Sp, partial=True)
        P.dma("pool", ssms_o, S_s[:], reads=[b_Ss], writes=[b_out], sem=b_Ss, partial=True)
    P.barrier()
    P.emit()
    return nc


_PROG_CACHE = {}


def _host_inputs(inp):
    f32 = np.float32
    c = lambda a: np.ascontiguousarray(a, dtype=a.dtype)
    shared = {}
    shared["w_in_even"] = c(inp["w_in_even"][0]); shared["w_out_even"] = c(inp["w_out_even"][0])
    shared["w_in_odd"] = c(inp["w_in_odd"][0]); shared["w_glu"] = c(inp["w_glu"][0]); shared["w_out_odd"] = c(inp["w_out_odd"][0])
    shared["w_gate"] = c(inp["w_ffn_gate"]); shared["w_up"] = c(inp["w_ffn_up"]); shared["w_down"] = c(inp["w_ffn_down"])
    g = np.zeros((8, 16, 128), f32)
    for kind, nm in enumerate(["norm_mix_pre", "norm_mix_post", "norm_ffn_pre", "norm_ffn_post"]):
        for layer in range(2):
            g[kind * 2 + layer] = np.asarray(inp[nm][layer]).reshape(16, 128)
    shared["gains"] = c(g.transpose(2, 0, 1))
    shared["cw"] = c(np.asarray(inp["conv_w"][0]).reshape(31, 8, 128).transpose(2, 1, 0))
    cv = np.stack([np.asarray(inp[k][0]).reshape(8, 128) for k in ("conv_b", "conv_ln_g", "conv_ln_b")])
    shared["cvec"] = c(cv.transpose(2, 0, 1))
    shared["subg"] = c(np.asarray(inp["subln_g"][0]).reshape(128, 1))
    shared["lqk"] = c(np.concatenate([np.asarray(inp["lambda_q"][0]).ravel(), np.asarray(inp["lambda_k"][0]).ravel()]).reshape(1, 256))
    shared["a_re"] = c(inp["ssm_a_re"][0]); shared["a_im"] = c(inp["ssm_a_im"][0])
    shared["ldt"] = c(np.asarray(inp["ssm_log_dt"][0]).reshape(128, 1))
    shared["b_re"] = c(inp["ssm_b_re"][0]); shared["b_im"] = c(inp["ssm_b_im"][0])
    cre = np.asarray(inp["ssm_c_re"][0]).transpose(2, 0, 1); cim = np.asarray(inp["ssm_c_im"][0]).transpose(2, 0, 1)
    shared["ccat"] = c(np.concatenate([cre, cim], axis=0))
    shared["dfm"] = c(np.asarray(inp["ssm_d"][0]).reshape(16, 128).T)
    shared["cache_k"] = c(np.asarray(inp["cache_k"][0]).reshape(1280 * 128, 1024))
    shared["cache_v"] = c(np.asarray(inp["cache_v"][0]).reshape(1280 * 128, 1024))
    shared["iota"] = np.arange(128, dtype=np.int32).reshape(128, 1)
    shared["ident"] = np.eye(128, dtype=f32)
    shared["ones"] = np.ones((128, 128), f32)
    shared["d0tab"] = (np.arange(128, dtype=f32)[:, None] - np.arange(W, dtype=f32)[None, :]).astype(f32)
    stab = np.zeros((64, 136), f32); slopecol = np.zeros((64, 1), f32)
    for h in range(8):
        for m in range(2):
            for q in range(4):
                r = h * 8 + m * 4 + q
                stab[r, 0:4] = np.arange(4)
                stab[r, 4:8] = np.where(np.arange(4) > q, NEG, 0.0)
                stab[r, 8:136] = np.arange(128) - PAST
                slopecol[r, 0] = SLOPES[h]
    shared["stab"] = stab; shared["slopecol"] = slopecol
    maps = []
    meta = np.asarray(inp["meta_tokens"], f32)
    for i in range(8):
        b = i % 4
        d = dict(shared)
        d["xin"] = c(np.concatenate([meta, np.asarray(inp["x_prompt"][b]), np.asarray(inp["x_sample"][i])], axis=0))
        d["ptab"] = c(np.asarray(inp["page_table"][i], dtype=np.int32).reshape(1, 128))
        d["sconv"] = c(inp["state_conv"][0, i])
        d["sssm"] = c(np.concatenate([np.asarray(inp["state_ssm_re"][0, i]), np.asarray(inp["state_ssm_im"][0, i])], axis=1))
        maps.append(d)
    return maps


def _assemble(res):
    f32 = np.float32
    R_ = [r for r in res]
    y_prompt = np.stack([R_[b]["yout"][16:NPR] for b in range(4)]).astype(f32)
    y_sample = np.stack([R_[i]["yout"][NPR:NT] for i in range(8)]).astype(f32)
    k_prompt = np.stack([R_[b]["kout"][0:NPR].reshape(NPR, 8, 2, 64) for b in range(4)])[None].astype(f32)
    v_prompt = np.stack([R_[b]["vout"][0:NPR].reshape(NPR, 8, 128) for b in range(4)])[None].astype(f32)
    k_sample = np.stack([R_[i]["kout"][NPR:NT].reshape(4, 8, 2, 64) for i in range(8)])[None].astype(f32)
    v_sample = np.stack([R_[i]["vout"][NPR:NT].reshape(4, 8, 128) for i in range(8)])[None].astype(f32)
    conv_prompt = np.stack([R_[b]["convp"] for b in range(4)])[None].astype(f32)
    conv_sample = np.stack([R_[i]["convs"] for i in range(8)])[None].astype(f32)
    srp = np.stack([R_[b]["ssmp"][:, 0:64] for b in range(4)])[None].astype(f32)
    sip = np.stack([R_[b]["ssmp"][:, 64:128] for b in range(4)])[None].astype(f32)
    srs = np.stack([R_[i]["ssms"][:, 0:64] for i in range(8)])[None].astype(f32)
    sis = np.stack([R_[i]["ssms"][:, 64:128] for i in range(8)])[None].astype(f32)
    return (y_prompt, y_sample, k_prompt, v_prompt, k_sample, v_sample, conv_prompt, conv_sample, srp, sip, srs, sis)


def kernel(_stop_after=None, **inputs):
    maps = _host_inputs(inputs)
    nc = build_program(_stop_after)
    res = run_bass_kernel_spmd(nc, maps, core_ids=list(range(8)))
    return _assemble(res.results)
```

```python
import numpy as np
from contextlib import ExitStack
import concourse.bass as bass
import concourse.mybir as mybir

F32 = mybir.dt.float32
F32R = mybir.dt.float32r
I32 = mybir.dt.int32
AF = mybir.ActivationFunctionType
ALU = mybir.AluOpType
AX = mybir.AxisListType

PH = 4096


class Buf:
    __slots__ = ("name", "writers", "readers", "dcount", "semi", "_sw")

    def __init__(self, name):
        self.name = name
        self.writers = []
        self.readers = []
        self.dcount = 0
        self.semi = None
        self._sw = None

    def sw(self):
        if self._sw is None:
            self._sw = Buf(self.name + "_sw")
        return self._sw


class Prog:
    ENG = ("pe", "act", "dve", "pool", "sp")

    def __init__(self, nc):
        self.nc = nc
        self.ops = {e: [] for e in self.ENG}
        self.cnt = {e: 0 for e in self.ENG}
        self.waited = {e: {} for e in self.ENG}
        self.dbufs = []
        self.stack = ExitStack()
        self.retype = None
        self.NR = 20
        self.ring = {q: [Buf(f"ring_{q}{i}") for i in range(self.NR)] for q in ("sp", "pool", "act")}
        self.ringpos = {q: 0 for q in ("sp", "pool", "act")}
        self.stage = "init"
        self.labels = {e: [] for e in self.ENG}

    def sb(self, name, shape, dtype=F32):
        return self.stack.enter_context(self.nc.sbuf_tensor("sb_" + name, list(shape), dtype))

    def ps(self, name, shape, dtype=F32):
        return self.stack.enter_context(self.nc.psum_tensor(name, list(shape), dtype))

    def _need(self, eng, toks):
        best = {}
        cur = self.cnt[eng] + 1
        for t in toks:
            if t[0] == "E":
                _, e2, idx = t
                if e2 == eng and eng in ("pe", "sp"):
                    continue
                if e2 == eng and eng == "dve" and cur - idx >= 2:
                    continue
                key = ("E", e2, (idx - 1) // PH)
                val = (idx - 1) % PH + 1
            else:
                _, b, val = t
                key = ("D", id(b))
                if b.semi is None:
                    b.semi = len(self.dbufs)
                    self.dbufs.append(b)
            if best.get(key, (0,))[0] < val:
                best[key] = (val, t)
        out = []
        w = self.waited[eng]
        for key, (val, t) in best.items():
            if w.get(key, 0) >= val:
                continue
            w[key] = val
            out.append((key, val, t))
        return out

    def _deps(self, reads, writes):
        toks = []
        for b in reads:
            toks += b.writers
        for b in writes:
            toks += b.writers
            toks += b.readers
        return toks

    @staticmethod
    def _dedupe(lst):
        best = {}
        for t in lst:
            if t[0] == "E":
                key = ("E", t[1])
                v = t[2]
            else:
                key = ("D", id(t[1]))
                v = t[2]
            if key not in best or best[key][2] < v:
                best[key] = t
        return list(best.values())

    def _commit(self, tok, reads, writes, partial):
        for b in reads:
            b.readers = self._dedupe(b.readers + [tok])
        for b in writes:
            if partial:
                b.writers = self._dedupe(b.writers + [tok])
            else:
                b.writers = [tok]
                b.readers = []

    def op(self, eng, meth, *args, reads=(), writes=(), partial=False, **kw):
        if self.retype is not None and args:
            args = (self.retype(args[0]),) + tuple(args[1:])
        fn = lambda e, meth=meth, args=args, kw=kw: getattr(e, meth)(*args, **kw)
        waits = self._need(eng, self._deps(reads, writes))
        self.cnt[eng] += 1
        idx = self.cnt[eng]
        tok = ("E", eng, idx)
        self.ops[eng].append((waits, fn, ("E", eng, idx)))
        self.labels[eng].append(self.stage)
        self._commit(tok, reads, writes, partial)
        return tok

    def _ring_sem(self, q):
        rb = self.ring[q][self.ringpos[q] % self.NR]
        self.ringpos[q] += 1
        pre = [("D", rb, 16 * rb.dcount)] if rb.dcount > 0 else []
        return rb, pre

    def dma(self, q, out, in_, reads=(), writes=(), sem=None, partial=False, **kw):
        rb, pre = self._ring_sem(q)
        if self.retype is not None:
            out = self.retype(out)
            if out.dtype == F32R and in_.dtype == F32:
                in_ = in_.bitcast(F32R)
        waits = self._need(q, self._deps(reads, writes) + pre)
        rb.dcount += 1
        tok = ("D", rb, 16 * rb.dcount)
        fn = lambda e, out=out, in_=in_, kw=kw: e.dma_start(out=out, in_=in_, **kw)
        self.ops[q].append((waits, fn, ("D", rb)))
        self._commit(tok, reads, writes, partial)
        return tok

    def dma_custom(self, q, meth, reads=(), writes=(), sem=None, partial=False, **kw):
        fn = lambda e, meth=meth, kw=kw: getattr(e, meth)(**kw)
        rb, pre = self._ring_sem(q)
        waits = self._need(q, self._deps(reads, writes) + pre)
        rb.dcount += 1
        tok = ("D", rb, 16 * rb.dcount)
        self.ops[q].append((waits, fn, ("D", rb)))
        self._commit(tok, reads, writes, partial)
        return tok

    def barrier(self):
        toks = [("E", e, self.cnt[e]) for e in self.ENG if self.cnt[e] > 0]
        toks += [("D", b, 16 * b.dcount) for q in self.ring for b in self.ring[q] if b.dcount > 0]
        for e in self.ENG:
            waits = self._need(e, toks)
            if waits:
                self.ops[e].append((waits, None, None))

    def emit(self):
        nc = self.nc
        import bisect
        mset = {e: set() for e in self.ENG}
        for e in self.ENG:
            for waits, fn, inc in self.ops[e]:
                for key, val, tok in waits:
                    if tok[0] == "E":
                        mset[tok[1]].add(tok[2])
        mlist = {e: sorted(mset[e]) for e in self.ENG}
        nsem = {e: (len(mlist[e]) + PH - 1) // PH for e in self.ENG}
        esem = {e: [self.stack.enter_context(nc.semaphore(f"s_{e}{i}")) for i in range(nsem[e])]
                for e in self.ENG}
        dsem = [self.stack.enter_context(nc.semaphore(f"d_{i}")) for i in range(len(self.dbufs))]
        idmap = {id(b): i for i, b in enumerate(self.dbufs)}

        def rank(e, idx):
            r = bisect.bisect_left(mlist[e], idx)
            assert mlist[e][r] == idx
            return r

        def run(engname, eng):
            for waits, fn, inc in self.ops[engname]:
                for key, val, tok in waits:
                    if tok[0] == "E":
                        r = rank(tok[1], tok[2])
                        eng.wait_ge(esem[tok[1]][r // PH], r % PH + 1)
                    else:
                        eng.wait_ge(dsem[idmap[id(tok[1])]], val)
                if fn is None:
                    continue
                ins = fn(eng)
                if inc[0] == "E":
                    idx = inc[2]
                    if idx in mset[engname]:
                        r = rank(engname, idx)
                        ins.then_inc(esem[engname][r // PH], 1)
                else:
                    ins.then_inc(dsem[idmap[id(inc[1])]], 16)

        with nc.Block() as block:
            @block.tensor
            def _(e):
                run("pe", e)

            @block.scalar
            def _(e):
                run("act", e)

            @block.vector
            def _(e):
                run("dve", e)

            @block.gpsimd
            def _(e):
                run("pool", e)

            @block.sync
            def _(e):
                run("sp", e)
        self.stack.close()

import math
from concourse.bass_utils import run_bass_kernel_spmd

D = 2048
NT = 2068
NPR = 2064
CHUNKS = [(0, 416), (416, 832), (832, 1248), (1248, 1664), (1664, 2068)]
W = 416
DFF = 5632
EPS = 1e-6
LAM_INIT = 0.8 - 0.6 * math.exp(-0.3 * 0)
SLOPES = [2.0 ** (-8.0 * (i + 1) / 8) for i in range(8)]
PAST = 16384
NEG = -30000.0
STOP_AFTER = None


def build_program(stop_after=None):
    nc = bass.Bass("TRN2", target_bir_lowering=False)
    nc.dge_precook = False
    P = Prog(nc)

    def din(name, shape, dt=F32):
        return nc.dram_tensor(name, list(shape), dt, kind="ExternalInput").ap()

    def dout(name, shape, dt=F32):
        return nc.dram_tensor(name, list(shape), dt, kind="ExternalOutput").ap()

    def dscr(name, shape, dt=F32):
        return nc.dram_tensor(name, list(shape), dt).ap()

    xin = din("xin", [NT, D])
    w_in_even = din("w_in_even", [D, 5120]); w_out_even = din("w_out_even", [D, D])
    w_in_odd = din("w_in_odd", [D, D]); w_glu = din("w_glu", [D, D]); w_out_odd = din("w_out_odd", [D, D])
    w_gate = din("w_gate", [2, D, DFF]); w_up = din("w_up", [2, D, DFF]); w_down = din("w_down", [2, DFF, D])
    gains_d = din("gains", [128, 8, 16])
    cw_d = din("cw", [128, 8, 31]); cvec_d = din("cvec", [128, 3, 8])
    subg_d = din("subg", [128, 1]); lqk_d = din("lqk", [1, 256])
    a_re_d = din("a_re", [128, 64]); a_im_d = din("a_im", [128, 64]); ldt_d = din("ldt", [128, 1])
    b_re_d = din("b_re", [128, 64, 16]); b_im_d = din("b_im", [128, 64, 16])
    ccat_d = din("ccat", [128, 128, 16]); dfm_d = din("dfm", [128, 16])
    ck_d = din("cache_k", [1280 * 128, 1024]); cv_d = din("cache_v", [1280 * 128, 1024])
    pt_d = din("ptab", [1, 128], I32); iota_d = din("iota", [128, 1], I32)
    sconv_d = din("sconv", [30, 1024]); sssm_d = din("sssm", [128, 128])
    ident_d = din("ident", [128, 128]); ones_d = din("ones", [128, 128]); d0_d = din("d0tab", [128, W])
    stab_d = din("stab", [64, 8 + 128])
    slopecol_d = din("slopecol", [64, 1])

    yout = dout("yout", [NT, D]); kout = dout("kout", [NT, 1024]); vout = dout("vout", [NT, 1024])
    convp_o = dout("convp", [30, 1024]); convs_o = dout("convs", [30, 1024])
    ssmp_o = dout("ssmp", [128, 128]); ssms_o = dout("ssms", [128, 128])

    dbg_o = dout("dbg", [6, 128, 16 * W]) if stop_after is not None else None
    b_dbg = Buf("dbg")

    def dump(slot, ap3, bufs, ci):
        if ci != 0 or dbg_o is None:
            return
        P.barrier()
        P.dma("pool", dbg_o[slot].rearrange("p (a b) -> p a b", a=16), ap3, reads=list(bufs), writes=[b_dbg], sem=b_dbg, partial=True)
        P.barrier()

    KTscr = dscr("KTscr", [8, 128, NPR]); Vscr = dscr("Vscr", [NT, 1024])
    BDscr = dscr("BDscr", [16, 128, 1024])
    BuScr = dscr("BuScr", [W, 128, 128]); HScr = dscr("HScr", [W, 128, 128])

    ident = P.sb("ident", [128, 128]); b_ident = Buf("ident")
    ones_r = P.sb("ones_r", [128, 128], F32R); b_ones = Buf("ones")
    gains = P.sb("gains", [128, 8, 16]); cw = P.sb("cw", [128, 8, 31]); cvec = P.sb("cvec", [128, 3, 8])
    subg = P.sb("subg", [128, 1]); lamt = P.sb("lamt", [128, 8]); b_par = Buf("params")
    stab = P.sb("stab", [64, 136]); slopecol = P.sb("slopecol", [64, 1])
    dfm = P.sb("dfm", [128, 16])
    zeros = P.sb("zeros", [128, 128]); b_zero = Buf("zeros")
    b_sid = Buf("d0r")
    d0r = P.sb("d0r", [128, W], F32R)
    gtail = P.sb("gtail", [128, 8, 30]); b_gt = Buf("gtail")
    ccat = P.sb("ccat", [128, 128, 16], F32R); b_ccat = Buf("ccat")
    XY = P.sb("XY", [128, 256]); b_xy = Buf("xy")
    S_p = P.sb("S_p", [128, 2, 128]); S_s = P.sb("S_s", [128, 128]); b_Sp = Buf("Sp"); b_Ss = Buf("Ss")
    XY2 = P.sb("XY2", [128, 2, 256]); b_xy2 = Buf("xy2")
    u_last = P.sb("u_last", [128, 16]); b_ul = Buf("ulast")
    BDscr2 = dscr("BDscr2", [16, 128, 1024])
    x_fm = P.sb("x_fm", [128, 16, W]); b_x = [Buf(f"x{i}") for i in range(16)]
    NWB = 3
    wbuf = [P.sb(f"wbuf{i}", [128, 16, 128], F32R) for i in range(NWB)]; b_w = [Buf(f"w{i}") for i in range(NWB)]
    ARENA = 16 * W + 16928
    arenaR = P.sb("arena", [128, ARENA], F32R)
    arena = arenaR[:].bitcast(F32)
    AFSZ = 7424
    arenaF = P.sb("arenaF", [128, AFSZ])
    small = P.sb("small", [128, 7 * W]); b_small = [Buf(f"sm{i}") for i in range(7)]
    smallr = P.sb("smallr", [128, 2 * W], F32R); b_smr = [Buf("smr0"), Buf("smr1")]
    for i in range(2):
        wbuf.append(arenaR[:, ARENA - (2 - i) * 2048: ARENA - (1 - i) * 2048].rearrange("p (a b) -> p a b", a=16))
        b_w.append(Buf(f"w{NWB + i}"))
    NWB += 2
    ptb_t = P.sb("ptb", [128, 128], I32); idx_t = P.sb("idx", [128, 128], I32); iot_t = P.sb("iot", [128, 1], I32)
    PS = [P.ps(f"ps{i}", [128, 512]) for i in range(8)]; b_ps = [Buf(f"ps{i}") for i in range(8)]

    def sm(i, n=W):
        return small[:, i * W:i * W + n]

    def smr(i, n=W):
        return smallr[:, i * W:i * W + n]

    def av(off, a, b):
        return arena[:, off:off + a * b].rearrange("p (a b) -> p a b", a=a)

    R = lambda ap: ap.bitcast(F32R)

    def _rt(ap):
        try:
            if ap.name == "sb_arena" and ap.dtype == F32:
                return ap.bitcast(F32R)
        except Exception:
            pass
        return ap
    P.retype = _rt
    b_AF = [Buf(f"AF{i}") for i in range(8)]

    state = {"wi": 0, "mm": 0, "evac": 0}

    def ev_eng():
        state["evac"] += 1
        return "act" if state["evac"] % 2 else "dve"

    def copy_op(eng, out, in_, reads, writes, partial=False):
        if eng == "act":
            return P.op("act", "copy", out, in_, reads=reads, writes=writes, partial=partial)
        return P.op(eng, "tensor_copy", out, in_, reads=reads, writes=writes, partial=partial)

    P.dma("sp", ident[:], ident_d, writes=[b_ident], sem=b_ident)
    P.dma("sp", ones_r[:], R(ones_d), writes=[b_ones], sem=b_ones)
    P.op("dve", "memset", zeros[:], 0.0, writes=[b_zero])
    P.dma("sp", d0r[:], R(d0_d), writes=[b_sid], sem=b_sid)
    d0tab = d0r[:].bitcast(F32)
    for dst, src in [(gains, gains_d), (cw, cw_d), (cvec, cvec_d), (subg, subg_d), (stab, stab_d),
                     (slopecol, slopecol_d), (dfm, dfm_d)]:
        P.dma("sp", dst[:], src, writes=[b_par], sem=b_par, partial=True)
    lq = sm(0, 256)
    P.dma("pool", lq, lqk_d.partition_broadcast(128), writes=[b_small[0]], sem=b_small[0])
    P.op("dve", "tensor_tensor", sm(1, 128), lq[:, 0:128], lq[:, 128:256], op=ALU.mult, reads=[b_small[0]], writes=[b_small[1]])
    P.op("dve", "tensor_reduce", lamt[:, 0:2], sm(1, 128).rearrange("p (a b) -> p a b", a=2), axis=AX.X, op=ALU.add,
         reads=[b_small[1]], writes=[b_par], partial=True)
    P.op("act", "activation", lamt[:, 2:4], lamt[:, 0:2], AF.Exp, reads=[b_par], writes=[b_par], partial=True)
    P.op("dve", "scalar_tensor_tensor", lamt[:, 4:5], lamt[:, 3:4], -LAM_INIT, lamt[:, 2:3], op0=ALU.add, op1=ALU.subtract,
         reads=[b_par], writes=[b_par], partial=True)
    P.op("dve", "tensor_scalar", lamt[:, 5:6], subg[:], 1.0 - LAM_INIT, None, op0=ALU.mult, reads=[b_par], writes=[b_par], partial=True)
    neglam = lamt[:, 4:5]; subg2 = lamt[:, 5:6]

    def gain(kind, layer, ft):
        return gains[:, kind * 2 + layer, ft:ft + 1]

    def ssm_prep():
        o = 0
        def T(n):
            nonlocal o
            v = arenaF[:, o:o + n]; o += n
            return v
        b_t = Buf("ssmprep")
        lr, li, dt_, zr, zi, er, cs, sn, mk, t1, t2, cr, ci, den = [T(64) for _ in range(14)]
        bre = T(1024); bim = T(1024); bbT = T(2048); zero = T(1024)
        dtc = T(1)
        rw = dict(reads=[b_t], writes=[b_t], partial=True)
        P.dma("sp", lr, a_re_d, writes=[b_t], sem=b_t, partial=True)
        P.dma("sp", li, a_im_d, writes=[b_t], sem=b_t, partial=True)
        P.dma("sp", dtc, ldt_d, writes=[b_t], sem=b_t, partial=True)
        P.dma("sp", bre, b_re_d.rearrange("g p c -> g (p c)"), writes=[b_t], sem=b_t, partial=True)
        P.dma("sp", bim, b_im_d.rearrange("g p c -> g (p c)"), writes=[b_t], sem=b_t, partial=True)
        P.dma("sp", ccat[:], R(ccat_d), writes=[b_ccat], sem=b_ccat)
        P.op("act", "activation", ccat[64:128], ccat[64:128].bitcast(F32), AF.Copy, scale=-1.0, reads=[b_ccat], writes=[b_ccat])
        P.op("act", "activation", dtc, dtc, AF.Exp, **rw)
        P.op("dve", "tensor_scalar", zr, lr, dtc, None, op0=ALU.mult, **rw)
        P.op("dve", "tensor_scalar", zi, li, dtc, None, op0=ALU.mult, **rw)
        P.op("act", "activation", er, zr, AF.Exp, **rw)
        P.op("dve", "tensor_copy", sn, zi, **rw)
        P.op("dve", "tensor_scalar", cs, zi, math.pi / 2, None, op0=ALU.add, **rw)
        for tgt in (sn, cs):
            for _ in range(7):
                P.op("dve", "tensor_scalar", mk, tgt, math.pi, None, op0=ALU.is_gt, **rw)
                P.op("dve", "scalar_tensor_tensor", tgt, mk, -2 * math.pi, tgt, op0=ALU.mult, op1=ALU.add, **rw)
            P.op("act", "activation", tgt, tgt, AF.Sin, **rw)
        P.op("dve", "tensor_tensor", XY[:, 0:64], er, cs, op=ALU.mult, reads=[b_t], writes=[b_xy], partial=True)
        P.op("dve", "tensor_copy", XY[:, 64:128], XY[:, 0:64], reads=[b_xy], writes=[b_xy], partial=True)
        P.op("dve", "tensor_tensor", XY[:, 192:256], er, sn, op=ALU.mult, reads=[b_t], writes=[b_xy], partial=True)
        P.op("dve", "tensor_scalar", XY[:, 128:192], XY[:, 192:256], -1.0, None, op0=ALU.mult, reads=[b_xy], writes=[b_xy], partial=True)
        ar = XY[:, 0:64]; ai = XY[:, 192:256]
        rx = dict(reads=[b_t, b_xy], writes=[b_t], partial=True)
        P.op("dve", "tensor_scalar", t1, ar, -1.0, None, op0=ALU.add, **rx)
        P.op("dve", "tensor_tensor", den, lr, lr, op=ALU.mult, **rw)
        P.op("dve", "tensor_tensor", t2, li, li, op=ALU.mult, **rw)
        P.op("dve", "tensor_tensor", den, den, t2, op=ALU.add, **rw)
        P.op("dve", "reciprocal", den, den, **rw)
        P.op("dve", "tensor_tensor", cr, t1, lr, op=ALU.mult, **rw)
        P.op("dve", "tensor_tensor", t2, ai, li, op=ALU.mult, **rx)
        P.op("dve", "tensor_tensor", cr, cr, t2, op=ALU.add, **rw)
        P.op("dve", "tensor_tensor", cr, cr, den, op=ALU.mult, **rw)
        P.op("dve", "tensor_tensor", ci, ai, lr, op=ALU.mult, **rx)
        P.op("dve", "tensor_tensor", t2, t1, li, op=ALU.mult, **rw)
        P.op("dve", "tensor_tensor", ci, ci, t2, op=ALU.subtract, **rw)
        P.op("dve", "tensor_tensor", ci, ci, den, op=ALU.mult, **rw)
        bre3 = bre.rearrange("g (p c) -> g p c", c=16); bim3 = bim.rearrange("g (p c) -> g p c", c=16)
        bb3 = bbT.rearrange("g (c s) -> g c s", c=16)
        for c in range(16):
            br = bre3[:, :, c]; bi = bim3[:, :, c]
            o_re = bb3[:, c, 0:64]; o_im = bb3[:, c, 64:128]
            P.op("dve", "tensor_tensor", o_re, cr, br, op=ALU.mult, **rw)
            P.op("dve", "tensor_tensor", t2, ci, bi, op=ALU.mult, **rw)
            P.op("dve", "tensor_tensor", o_re, o_re, t2, op=ALU.subtract, **rw)
            P.op("dve", "tensor_tensor", o_im, cr, bi, op=ALU.mult, **rw)
            P.op("dve", "tensor_tensor", t2, ci, br, op=ALU.mult, **rw)
            P.op("dve", "tensor_tensor", o_im, o_im, t2, op=ALU.add, **rw)
        for tt in range(2):
            P.op("dve", "tensor_tensor", XY2[:, tt, 0:64], ar, ar, op=ALU.mult, reads=[b_xy], writes=[b_xy2], partial=True)
            P.op("dve", "tensor_tensor", t2, ai, ai, op=ALU.mult, **rx)
            P.op("dve", "tensor_tensor", XY2[:, tt, 0:64], XY2[:, tt, 0:64], t2, op=ALU.subtract, reads=[b_xy2, b_t], writes=[b_xy2], partial=True)
            P.op("dve", "tensor_copy", XY2[:, tt, 64:128], XY2[:, tt, 0:64], reads=[b_xy2], writes=[b_xy2], partial=True)
            P.op("dve", "tensor_tensor", t2, ar, ai, op=ALU.mult, **rx)
            P.op("dve", "tensor_scalar", XY2[:, tt, 192:256], t2, 2.0, None, op0=ALU.mult, reads=[b_t], writes=[b_xy2], partial=True)
            P.op("dve", "tensor_scalar", XY2[:, tt, 128:192], t2, -2.0, None, op0=ALU.mult, reads=[b_t], writes=[b_xy2], partial=True)
        bbA3 = arenaF[:, 14 * 64: 14 * 64 + 2048].rearrange("g (c s) -> g c s", c=16)
        for c in range(16):
            s_re = bb3[:, c, 0:64]; s_im = bb3[:, c, 64:128]
            d_re = bbA3[:, c, 0:64]; d_im = bbA3[:, c, 64:128]
            P.op("dve", "tensor_tensor", d_re, ar, s_re, op=ALU.mult, **rx)
            P.op("dve", "tensor_tensor", t2, ai, s_im, op=ALU.mult, **rx)
            P.op("dve", "tensor_tensor", d_re, d_re, t2, op=ALU.subtract, **rw)
            P.op("dve", "tensor_tensor", d_im, ar, s_im, op=ALU.mult, **rx)
            P.op("dve", "tensor_tensor", t2, ai, s_re, op=ALU.mult, **rx)
            P.op("dve", "tensor_tensor", d_im, d_im, t2, op=ALU.add, **rw)
        P.op("dve", "memset", zero, 0.0, **rw)
        b_bd = Buf("bdscr")
        for f in range(16):
            P.dma("pool", BDscr[f], zero, reads=[b_t], writes=[b_bd], sem=b_t, partial=True)
            P.dma("sp", BDscr2[f], zero, reads=[b_t], writes=[b_bd], sem=b_t, partial=True)
        P.barrier()
        for g in range(128):
            f, gl = divmod(g, 8)
            P.dma("pool", BDscr[f, gl * 16:(gl + 1) * 16, gl * 128:(gl + 1) * 128].unsqueeze(0),
                  bb3[g:g + 1, :, :], reads=[b_t], writes=[b_bd], sem=b_t, partial=True)
            P.dma("sp", BDscr2[f, gl * 16:(gl + 1) * 16, gl * 128:(gl + 1) * 128].unsqueeze(0),
                  bbA3[g:g + 1, :, :], reads=[b_t], writes=[b_bd], sem=b_t, partial=True)
        P.op("dve", "memset", S_p[:], 0.0, writes=[b_Sp])
        P.op("dve", "memset", u_last[:], 0.0, writes=[b_ul])
        P.dma("sp", S_s[:], sssm_d, writes=[b_Ss], sem=b_Ss)
        P.barrier()
        return b_bd

    b_bd = ssm_prep()
    b_ktscr = Buf("ktscr"); b_vscr = Buf("vscr"); b_buscr = Buf("buscr"); b_hscr = Buf("hscr")
    b_out = Buf("outs")

    def rmsnorm_stats(src_fn, nt_tiles, n, denom, srcbufs):
        ssb = b_ps[4]
        for i in range(nt_tiles):
            sl = i % 2
            P.op("act", "activation", smr(sl, n), src_fn(i), AF.Square, reads=[srcbufs[i]], writes=[b_smr[sl]])
            P.op("pe", "matmul", PS[4][:, :n], lhsT=ones_r[:], rhs=smr(sl, n), start=(i == 0), stop=(i == nt_tiles - 1),
                 reads=[b_smr[sl], b_ones], writes=[ssb], partial=(i > 0))
        P.op("act", "activation", sm(6, n), PS[4][:, :n], AF.Sqrt, bias=EPS, scale=1.0 / denom, reads=[ssb], writes=[b_small[6]])
        P.op("dve", "reciprocal", sm(6, n), sm(6, n), reads=[b_small[6]], writes=[b_small[6]])
        return sm(6, n), b_small[6]

    def linear(Wap, KT, blocks, rhs_fn, rhs_bufs, n, evac):
        W3 = Wap.rearrange("(kt p) c -> p kt c", p=128)
        for c0 in blocks:
            wi = state["wi"] % NWB; state["wi"] += 1
            wb = wbuf[wi]
            P.dma("sp", wb[:, 0:KT, :], R(W3[:, :, c0:c0 + 128]), writes=[b_w[wi]], sem=b_w[wi])
            pi = state["mm"] % 2; state["mm"] += 1
            for kt in range(KT):
                P.op("pe", "matmul", PS[pi][:, :n], lhsT=wb[:, kt, :], rhs=rhs_fn(kt), start=(kt == 0), stop=(kt == KT - 1),
                     reads=[b_w[wi], rhs_bufs[kt]], writes=[b_ps[pi]], partial=(kt > 0))
            evac(c0, PS[pi][:, :n], b_ps[pi])

    def blocks_range(c_lo, c_hi):
        return list(range(c_lo, c_hi, 128))

    def post_norm_residual(m3, b_m, kind, layer, n):
        rstd, b_r = rmsnorm_stats(lambda i: m3[:, i, :n], 16, n, float(D), b_m)
        for ft in range(16):
            eng = "dve"
            P.op(eng, "scalar_tensor_tensor", m3[:, ft, :n], m3[:, ft, :n], gain(kind, layer, ft), rstd, op0=ALU.mult, op1=ALU.mult,
                 reads=[b_m[ft], b_r, b_par], writes=[b_m[ft]])
            P.op(eng, "tensor_tensor", x_fm[:, ft, :n], x_fm[:, ft, :n], m3[:, ft, :n], op=ALU.add,
                 reads=[b_m[ft], b_x[ft]], writes=[b_x[ft]])

    def pre_norm(hn3, b_hn, kind, layer, n):
        rstd, b_r = rmsnorm_stats(lambda i: x_fm[:, i, :n], 16, n, float(D), b_x)
        for ft in range(16):
            eng = "dve"
            P.op(eng, "scalar_tensor_tensor", R(hn3[:, ft, :n]), x_fm[:, ft, :n], gain(kind, layer, ft), rstd, op0=ALU.mult, op1=ALU.mult,
                 reads=[b_x[ft], b_r, b_par], writes=[b_hn[ft]])

    A0 = 0
    A1 = 16 * W
    SZ16 = 16 * W
    b_A0 = [Buf(f"A0_{i}") for i in range(16)]
    b_A1 = [Buf(f"A1_{i}") for i in range(40)]


    def sample_attention(q3, b_q, ksamp, b_ks, oc3, b_oc, npr, AT):
        P.barrier()
        o = AT
        def T(nel):
            nonlocal o
            v = arena[:, o:o + nel]; o += nel
            return v
        kpg = [T(1024) for _ in range(2)]; vpg = [R(T(1024)) for _ in range(2)]
        ktp = [R(T(1024)).rearrange("p (a b) -> p a b", a=8) for _ in range(2)]
        qblk = T(512); ptb = ptb_t[:]; idx = idx_t[:]; iot = iot_t[:]
        ptp = [R(T(64)) for _ in range(2)]; vst = R(T(1024))
        psb = arenaF[:, 0:128]; tms = arenaF[:, 128:256]; oms = arenaF[:, 256:320]; ods = arenaF[:, 320:352]; rss = arenaF[:, 352:384]
        sqs = smr(1, 32)
        assert o <= ARENA
        bk = [Buf("kpg0"), Buf("kpg1")]; bv = [Buf("vpg0"), Buf("vpg1")]; bkt = [Buf("ktp0"), Buf("ktp1")]
        bq = Buf("qblk"); bi = Buf("idx"); bp = Buf("psb"); bpt = [Buf("ptp0"), Buf("ptp1")]; bt = Buf("tms"); bvs = Buf("vst"); bm = Buf("misc")
        P.dma("pool", ptb, pt_d.partition_broadcast(128), writes=[bi], sem=bi)
        P.dma("sp", iot, iota_d, writes=[bi], sem=bi, partial=True)
        P.op("dve", "tensor_scalar", idx, ptb, 128, iot[:, 0:1], op0=ALU.mult, op1=ALU.add, reads=[bi], writes=[bi], partial=True)
        P.dma("sp", vst[:4, :], R(Vscr[NPR:NPR + 4, :]), reads=[b_vscr], writes=[bvs], sem=bvs)
        q4 = qblk.rearrange("p (h c) -> p h c", h=8)
        for qi in range(4):
            P.op("dve", "tensor_copy", R(qblk[:, qi * 128:(qi + 1) * 128]), zeros[:, 0:128], reads=[b_zero], writes=[bq], partial=(qi > 0))
        for h in range(8):
            for m in range(2):
                P.op("dve", "tensor_copy", R(q4[64 * m:64 * m + 64, h, h * 8 + m * 4:h * 8 + m * 4 + 4]), q3[64 * m:64 * m + 64, h, npr:npr + 4],
                     reads=[b_q[h]], writes=[bq], partial=True)
        qb = R(q4)
        o_ps, d_ps, s_ps, t_ps = PS[2], PS[5], PS[7], PS[4]
        for i in range(128):
            s = i % 2
            P.dma_custom("pool", "indirect_dma_start", reads=[bi], writes=[bk[s]], sem=bk[s],
                         out=R(kpg[s]), out_offset=None, in_=R(ck_d), in_offset=bass.IndirectOffsetOnAxis(ap=idx[:, i:i + 1], axis=0))
            P.dma_custom("pool", "indirect_dma_start", reads=[bi], writes=[bv[s]], sem=bv[s],
                         out=vpg[s], out_offset=None, in_=R(cv_d), in_offset=bass.IndirectOffsetOnAxis(ap=idx[:, i:i + 1], axis=0))
            for h in range(8):
                P.op("pe", "transpose", PS[h // 4][:, (h % 4) * 128:(h % 4 + 1) * 128], kpg[s][:, h * 128:(h + 1) * 128], ident[:, :],
                     reads=[bk[s], b_ident], writes=[b_ps[h // 4]], partial=(h % 4 > 0))
            copy_op("act", ktp[s][:, 0:4, :], PS[0][:, :].rearrange("p (a b) -> p a b", a=4), reads=[b_ps[0]], writes=[bkt[s]])
            copy_op("dve", ktp[s][:, 4:8, :], PS[1][:, :].rearrange("p (a b) -> p a b", a=4), reads=[b_ps[1]], writes=[bkt[s]], partial=True)
            for h in range(8):
                P.op("pe", "matmul", s_ps[:64, 0:128], lhsT=qb[:, h, :], rhs=ktp[s][:, h, :], start=(h == 0), stop=(h == 7),
                     reads=[bq, bkt[s]], writes=[b_ps[7]], partial=(h > 0))
            P.op("pool", "tensor_scalar", tms[:64, :], stab[:, 8:136], float(128 * i), slopecol[:, 0:1], op0=ALU.add, op1=ALU.mult,
                 reads=[b_par], writes=[bt])
            P.op("dve", "tensor_tensor", tms[:64, :], tms[:64, :], s_ps[:64, 0:128], op=ALU.add, reads=[bt, b_ps[7]], writes=[bt])
            P.op("act", "activation", psb[:64, :], tms[:64, :], AF.Exp, reads=[bt], writes=[bp])
            P.op("pe", "transpose", t_ps[:, 0:64], psb[:64, :], ident[:64, :64], reads=[bp, b_ident], writes=[b_ps[4]])
            copy_op("dve", ptp[s], t_ps[:, 0:64], reads=[b_ps[4]], writes=[bpt[s]])
            for h in range(8):
                P.op("pe", "matmul", o_ps[:, h * 8:(h + 1) * 8], lhsT=vpg[s][:, h * 128:(h + 1) * 128], rhs=ptp[s][:, h * 8:(h + 1) * 8],
                     start=(i == 0), stop=False, reads=[bv[s], bpt[s]], writes=[b_ps[2]], partial=True)
            P.op("pe", "matmul", d_ps[:, 0:64], lhsT=ones_r[:], rhs=ptp[s], start=(i == 0), stop=False,
                 reads=[b_ones, bpt[s]], writes=[b_ps[5]], partial=True)
        for h in range(8):
            P.op("pe", "matmul", s_ps[:64, 0:4], lhsT=qb[:, h, :], rhs=R(ksamp[:, h, :]), start=(h == 0), stop=(h == 7),
                 reads=[bq, b_ks], writes=[b_ps[7]], partial=(h > 0))
        P.op("dve", "scalar_tensor_tensor", tms[:64, 0:4], stab[:, 0:4], slopecol[:, 0:1], stab[:, 4:8], op0=ALU.mult, op1=ALU.add,
             reads=[b_par], writes=[bt])
        P.op("dve", "tensor_tensor", tms[:64, 0:4], tms[:64, 0:4], s_ps[:64, 0:4], op=ALU.add, reads=[bt, b_ps[7]], writes=[bt])
        P.op("act", "activation", psb[:64, 0:4], tms[:64, 0:4], AF.Exp, reads=[bt], writes=[bp])
        P.op("pe", "transpose", t_ps[:4, 0:64], psb[:64, 0:4], ident[:64, :64], reads=[bp, b_ident], writes=[b_ps[4]])
        copy_op("dve", ptp[0][:4, :], t_ps[:4, 0:64], reads=[b_ps[4]], writes=[bpt[0]])
        for h in range(8):
            P.op("pe", "matmul", o_ps[:, h * 8:(h + 1) * 8], lhsT=vst[:4, h * 128:(h + 1) * 128], rhs=ptp[0][:4, h * 8:(h + 1) * 8],
                 start=False, stop=True, reads=[bvs, bpt[0]], writes=[b_ps[2]], partial=True)
        P.op("pe", "matmul", d_ps[:, 0:64], lhsT=ones_r[:4, :], rhs=ptp[0][:4, :], start=False, stop=True,
             reads=[b_ones, bpt[0]], writes=[b_ps[5]], partial=True)
        P.op("dve", "reciprocal", oms, d_ps[:, 0:64], reads=[b_ps[5]], writes=[bm])
        P.op("dve", "tensor_tensor", oms, oms, o_ps[:, 0:64], op=ALU.mult, reads=[bm, b_ps[2]], writes=[bm])
        om4 = oms.rearrange("p (h m q) -> p h m q", h=8, m=2)
        od3 = ods.rearrange("p (h q) -> p h q", h=8)
        P.op("dve", "scalar_tensor_tensor", od3, om4[:, :, 1, :], neglam, om4[:, :, 0, :], op0=ALU.mult, op1=ALU.add, reads=[bm, b_par], writes=[bm], partial=True)
        P.op("act", "activation", sqs, ods, AF.Square, reads=[bm], writes=[b_smr[1]])
        P.op("pe", "matmul", PS[4][:, 0:32], lhsT=ones_r[:], rhs=sqs, start=True, stop=True, reads=[b_smr[1], b_ones], writes=[b_ps[4]])
        P.op("act", "activation", rss, PS[4][:, 0:32], AF.Sqrt, bias=EPS, scale=1.0 / 128, reads=[b_ps[4]], writes=[bm], partial=True)
        P.op("dve", "reciprocal", rss, rss, reads=[bm], writes=[bm], partial=True)
        P.op("dve", "scalar_tensor_tensor", R(oc3[:, 0:8, npr:npr + 4]), od3, subg2, rss.rearrange("p (h q) -> p h q", h=8), op0=ALU.mult, op1=ALU.mult,
             reads=[bm, b_par], writes=list(b_oc[0:8]), partial=True)
        P.barrier()

    rgn = [Buf(f"ssm_r{i}") for i in range(4)]

    def ssm(u3, u3e, b_u, y3, b_y, npr, ns):
        SB = A1 + 16 * (W + 1)
        o = SB
        def T(nel):
            nonlocal o
            v = arena[:, o:o + nel]; o += nel
            return v
        TSZ = npr // 4
        TS = TSZ // 4
        scan = arenaF[:, 0:3328].rearrange("g (t s) -> g t s", s=128)
        bd = [R(T(1024)) for _ in range(2)]; bd2 = [R(T(1024)) for _ in range(2)]
        bus = [arenaF[:, 3328 + i * 1024: 4352 + i * 1024] for i in range(3)]; b_bus = rgn[0:3]
        htm = [arenaF[:, 3328 + i * 1024: 4352 + i * 1024] for i in range(2)]; b_htm = rgn[0:2]
        ytm = arenaF[:, 5376:7424]; b_ytm = [rgn[2], rgn[3]]
        hT = [R(T(1024)).rearrange("p (a b) -> p a b", a=8) for _ in range(2)]
        assert o <= ARENA, o
        b_scan = Buf("scan"); b_bdb = [Buf("bd0"), Buf("bd1")]; b_bdb2 = [Buf("bd20"), Buf("bd21")]
        b_hT = [Buf("hT0"), Buf("hT1")]
        b_t1 = [Buf("t1a"), Buf("t1b")]; b_t2 = [Buf("t2a"), Buf("t2b")]; b_s1 = [Buf("s1a"), Buf("s1b")]
        tiles = [(i * TSZ, TSZ, False) for i in range(4)] + ([(npr, ns, True)] if ns else [])
        P.stage = "ssm_A"
        k = 0
        for f in range(16):
            sb_ = f % 2
            P.dma("sp", bd[sb_], R(BDscr[f]), reads=[b_bd], writes=[b_bdb[sb_]], sem=b_bdb[sb_])
            P.dma("sp", bd2[sb_], R(BDscr2[f]), reads=[b_bd], writes=[b_bdb2[sb_]], sem=b_bdb2[sb_])
            for (t0, nt, is_s) in tiles:
                s = k % 3; k += 1
                for hf in range(2):
                    P.op("pe", "matmul", PS[hf][:nt, :], lhsT=R(u3e[:, f, 1 + t0:1 + t0 + nt]), rhs=bd[sb_][:, hf * 512:(hf + 1) * 512], start=True, stop=is_s,
                         reads=[b_u[f], b_bdb[sb_]], writes=[b_ps[hf]])
                    if not is_s:
                        P.op("pe", "matmul", PS[hf][:nt, :], lhsT=R(u3e[:, f, t0:t0 + nt]), rhs=bd2[sb_][:, hf * 512:(hf + 1) * 512], start=False, stop=True,
                             reads=[b_u[f], b_bdb2[sb_]], writes=[b_ps[hf]], partial=True)
                    copy_op("act", bus[s][:nt, hf * 512:(hf + 1) * 512], PS[hf][:nt, :], reads=[b_ps[hf]], writes=[b_bus[s]], partial=(hf > 0))
                P.dma("sp", BuScr[t0:t0 + nt, f * 8:(f + 1) * 8, :].rearrange("t g s -> t (g s)"), bus[s][:nt, :],
                      reads=[b_bus[s]], writes=[b_buscr], sem=b_bus[s], partial=True)
        X3 = XY[:, 0:128].rearrange("g (r p) -> g r p", r=2); Yn = XY[:, 128:192]; Yp = XY[:, 192:256]
        t1s = sm(0, 128).rearrange("g (r p) -> g r p", r=2); t2s = sm(1, 128).rearrange("g (r p) -> g r p", r=2)
        s1s = sm(2, 128).rearrange("g (r p) -> g r p", r=2)

        X2 = XY2[:, :, 0:128]; Y2n = XY2[:, :, 128:192]; Y2p = XY2[:, :, 192:256]
        p1 = sm(0, 256).rearrange("g (t s) -> g t s", t=2); p2 = sm(1, 256).rearrange("g (t r p) -> g t r p", t=2, r=2)
        p3 = sm(2, 256).rearrange("g (t s) -> g t s", t=2)
        b_p1 = Buf("p1"); b_p2 = Buf("p2"); b_p3 = Buf("p3")
        spc = sm(3, 256); b_spc = Buf("spc"); b_spc2 = Buf("spc2")

        def scan_pairs(t0, nt):
            P.stage = "ssm_scan"
            sizes = [26, 26, 26, 26] if nt == 104 else [26, 26, 24, 24]
            assert sum(sizes) == nt
            ts0 = t0
            for tn in sizes:
                P.dma("pool", scan[:, 0:tn, :], BuScr[ts0:ts0 + tn].rearrange("t g s -> g t s"), reads=[b_buscr], writes=[b_scan], sem=b_scan)
                for i in range(tn // 2):
                    prev = S_p[:, :, :] if i == 0 else scan[:, 2 * i - 2:2 * i, :]
                    pb = [b_Sp] if i == 0 else [b_scan]
                    cur = scan[:, 2 * i:2 * i + 2, :]
                    pv4 = prev.rearrange("g t (r p) -> g t r p", r=2)
                    P.op("dve", "tensor_tensor", p1, prev, X2, op=ALU.mult, reads=pb + [b_xy2], writes=[b_p1])
                    P.op("dve", "tensor_tensor", p2[:, :, 0, :], pv4[:, :, 1, :], Y2n, op=ALU.mult, reads=pb + [b_xy2], writes=[b_p2])
                    P.op("dve", "tensor_tensor", p2[:, :, 1, :], pv4[:, :, 0, :], Y2p, op=ALU.mult, reads=pb + [b_xy2], writes=[b_p2], partial=True)
                    P.op("dve", "tensor_tensor", p3, cur, p1, op=ALU.add, reads=[b_p1, b_scan], writes=[b_p3])
                    P.op("dve", "tensor_copy", spc[:, 0:128], zeros[:, 0:128], reads=[b_zero], writes=[b_spc])
                    P.op("dve", "tensor_tensor", cur, p3, p2.rearrange("g t r p -> g t (r p)"), op=ALU.add, reads=[b_p2, b_p3], writes=[b_scan], partial=True)
                    P.op("dve", "tensor_copy", spc[:, 128:256], zeros[:, 0:128], reads=[b_zero], writes=[b_spc2])
                P.op("dve", "tensor_copy", S_p[:, :, :], scan[:, tn - 2:tn, :], reads=[b_scan], writes=[b_Sp])
                P.dma("pool", HScr[ts0:ts0 + tn].rearrange("t g s -> g t s"), scan[:, 0:tn, :], reads=[b_scan], writes=[b_hscr], sem=b_scan, partial=True)
                ts0 += tn

        def scan_tile(t0, nt, is_s):
            if not is_s:
                return scan_pairs(t0, nt)
            P.stage = "ssm_scan"
            St, b_St = (S_s, b_Ss)
            subs = [(t0, nt)]
            for (ts0, tn) in subs:
                P.dma("pool", scan[:, 0:tn, :], BuScr[ts0:ts0 + tn].rearrange("t g s -> g t s"), reads=[b_buscr], writes=[b_scan], sem=b_scan)
                for t in range(tn):
                    prev = St[:] if t == 0 else scan[:, t - 1, :]
                    pb = [b_St] if t == 0 else [b_scan]
                    pv = prev.rearrange("g (r p) -> g r p", r=2)
                    cv_ = scan[:, t, :].rearrange("g (r p) -> g r p", r=2)
                    hs = [(0, 32), (32, 64)]
                    for si, (a, b_) in enumerate(hs):
                        P.op("dve", "tensor_tensor", t1s[:, :, a:b_], pv[:, :, a:b_], X3[:, :, a:b_], op=ALU.mult, reads=pb + [b_xy], writes=[b_t1[si]])
                    for si, (a, b_) in enumerate(hs):
                        P.op("dve", "tensor_tensor", t2s[:, 0, a:b_], pv[:, 1, a:b_], Yn[:, a:b_], op=ALU.mult, reads=pb + [b_xy], writes=[b_t2[si]])
                    for si, (a, b_) in enumerate(hs):
                        P.op("dve", "tensor_tensor", t2s[:, 1, a:b_], pv[:, 0, a:b_], Yp[:, a:b_], op=ALU.mult, reads=pb + [b_xy], writes=[b_t2[si]], partial=True)
                    for si, (a, b_) in enumerate(hs):
                        P.op("dve", "tensor_tensor", s1s[:, :, a:b_], cv_[:, :, a:b_], t1s[:, :, a:b_], op=ALU.add, reads=[b_t1[si], b_scan], writes=[b_s1[si]])
                    for si, (a, b_) in enumerate(hs):
                        P.op("dve", "tensor_tensor", cv_[:, :, a:b_], s1s[:, :, a:b_], t2s[:, :, a:b_], op=ALU.add, reads=[b_t2[si], b_s1[si]], writes=[b_scan], partial=True)
                P.op("dve", "tensor_copy", St[:], scan[:, tn - 1, :], reads=[b_scan], writes=[b_St])
                P.dma("pool", HScr[ts0:ts0 + tn].rearrange("t g s -> g t s"), scan[:, 0:tn, :], reads=[b_scan], writes=[b_hscr], sem=b_scan, partial=True)

        kc = {"k": 0}

        def c_tile(t0, nt):
            P.stage = "ssm_C"
            for f in range(16):
                s = kc["k"] % 2; kc["k"] += 1
                P.dma("sp", htm[s][:nt, :], HScr[t0:t0 + nt, f * 8:(f + 1) * 8, :].rearrange("t g s -> t (g s)"), reads=[b_hscr], writes=[b_htm[s]], sem=b_htm[s])
                for gl in range(8):
                    P.op("pe", "transpose", PS[2 + gl // 4][:, (gl % 4) * 128:(gl % 4) * 128 + nt], htm[s][:nt, gl * 128:(gl + 1) * 128], ident[:nt, :nt],
                         reads=[b_htm[s], b_ident], writes=[b_ps[2 + gl // 4]], partial=(gl % 4 > 0))
                copy_op("act", hT[s][:, 0:4, :nt], PS[2][:, :].rearrange("p (a b) -> p a b", a=4)[:, :, :nt], reads=[b_ps[2]], writes=[b_hT[s]])
                copy_op("act", hT[s][:, 4:8, :nt], PS[3][:, :].rearrange("p (a b) -> p a b", a=4)[:, :, :nt], reads=[b_ps[3]], writes=[b_hT[s]], partial=True)
                for gl in range(8):
                    P.op("pe", "matmul", PS[4 + f // 4][:nt, (f % 4) * 128 + gl * 16:(f % 4) * 128 + gl * 16 + 16], lhsT=hT[s][:, gl, :nt], rhs=ccat[:, f * 8 + gl, :],
                         start=True, stop=True, reads=[b_hT[s], b_ccat], writes=[b_ps[4 + f // 4]], partial=True)
            for bq_ in range(4):
                copy_op("act", ytm[:nt, bq_ * 512:(bq_ + 1) * 512], PS[4 + bq_][:nt, :], reads=[b_ps[4 + bq_]], writes=b_ytm, partial=(bq_ > 0))
            for fg in range(4):
                pb_ = fg % 2
                for kk in range(4):
                    f = fg * 4 + kk
                    P.op("pe", "transpose", PS[pb_][:, kk * 128:kk * 128 + nt], ytm[:nt, f * 128:(f + 1) * 128], ident[:nt, :nt],
                         reads=b_ytm + [b_ident], writes=[b_ps[pb_]], partial=(kk > 0))
                for kk in range(4):
                    f = fg * 4 + kk
                    P.op("dve", "scalar_tensor_tensor", y3[:, f, t0:t0 + nt], u3[:, f, t0:t0 + nt], dfm[:, f:f + 1], PS[pb_][:, kk * 128:kk * 128 + nt],
                         op0=ALU.mult, op1=ALU.add, reads=[b_u[f], b_ps[pb_], b_par], writes=[b_y[f]], partial=True)

        P.op("dve", "tensor_copy", u_last[:].unsqueeze(2), u3e[:, :, npr:npr + 1], reads=list(b_u), writes=[b_ul])
        prev_tile = None
        for (t0, nt, is_s) in tiles:
            scan_tile(t0, nt, is_s)
            if prev_tile is not None:
                c_tile(*prev_tile)
            prev_tile = (t0, nt)
        c_tile(*prev_tile)

    for ci, (c0, c1) in enumerate(CHUNKS):
        n = c1 - c0
        npr = min(c1, NPR) - c0
        ns = n - npr
        ttiles = [(t0, min(128, n - t0)) for t0 in range(0, n, 128)]

        P.stage = "xload"
        xst = [arenaF[:, 0:2048], arenaF[:, 0:2048]]
        b_xst = [b_AF[0], b_AF[0]]
        for ti, (t0, nt) in enumerate(ttiles):
            s = ti % 2
            P.dma("sp", xst[s][:nt, :], xin[c0 + t0:c0 + t0 + nt, :], writes=[b_xst[s]], sem=b_xst[s])
            for fg in range(4):
                pb = 2 + (fg % 2)
                for k in range(4):
                    ft = fg * 4 + k
                    P.op("pe", "transpose", PS[pb][:, k * 128:k * 128 + nt], xst[s][:nt, ft * 128:(ft + 1) * 128], ident[:nt, :nt],
                         reads=[b_xst[s], b_ident], writes=[b_ps[pb]], partial=(k > 0))
                eng = ev_eng()
                copy_op(eng, x_fm[:, fg * 4:fg * 4 + 4, t0:t0 + nt], PS[pb][:].rearrange("p (a b) -> p a b", a=4)[:, :, :nt],
                        reads=[b_ps[pb]], writes=[b_x[fg * 4 + k] for k in range(4)], partial=True)
        P.barrier()

        P.stage = "l0_inproj"
        hn3 = av(A0, 16, W); b_hn = b_A0
        pre_norm(hn3, b_hn, 0, 0, n)
        q3 = av(A1, 8, W); b_q = b_A1[0:8]
        GW = 30 + W
        gext = av(A1 + 8 * W, 8, GW); b_g = b_A1[8:16]
        conv3 = arenaF[:, 3072:3072 + 8 * W].rearrange("p (a b) -> p a b", a=8); b_cv = b_A1[16:24]
        OFFX = A1 + 8 * W + 8 * GW
        gs_ext = av(OFFX, 8, 34); b_gs = b_A1[24]
        ksamp = av(OFFX + 272, 8, 4); b_ks = b_A1[25]
        OFFX2 = OFFX + 272 + 32
        for j in range(8):
            if ci == 0:
                P.op("dve", "tensor_copy", gext[:, j, 0:30], zeros[:, 0:30], reads=[b_zero], writes=[b_g[j]])
            else:
                P.op("dve", "tensor_copy", gext[:, j, 0:30], gtail[:, j, :], reads=[b_gt], writes=[b_g[j]])
        if ns:
            scst = arenaF[:, 2048:3072]; b_sc = b_AF[1]
            P.dma("sp", scst[:30, :], sconv_d, writes=[b_sc], sem=b_sc)
            for j in range(8):
                P.op("pe", "transpose", PS[2][:, j * 32:j * 32 + 30], scst[:30, j * 128:(j + 1) * 128], ident[:30, :30],
                     reads=[b_sc, b_ident], writes=[b_ps[2]], partial=(j > 0))
            copy_op("dve", gs_ext[:, :, 0:30], PS[2][:, 0:256].rearrange("p (a b) -> p a b", a=8)[:, :, 0:30], reads=[b_ps[2]], writes=[b_gs], partial=True)

        def kv_out(which, h, src, b_src):
            dst = kout if which == "k" else vout
            sl = 2 + (state["evac"] % 2)
            for ti, (t0, nt) in enumerate(ttiles):
                P.op("pe", "transpose", PS[sl][:nt, ti * 128:(ti + 1) * 128], src[:, t0:t0 + nt], ident[:, :],
                     reads=[b_src, b_ident], writes=[b_ps[sl]], partial=(ti > 0))
            stg = arenaF[:, 3072 + (sl - 2) * 512: 3072 + (sl - 1) * 512]
            copy_op(ev_eng(), stg, PS[sl][:, :], reads=[b_ps[sl]], writes=[b_AF[sl]])
            for ti, (t0, nt) in enumerate(ttiles):
                P.dma("pool", dst[c0 + t0:c0 + t0 + nt, h * 128:(h + 1) * 128], stg[:nt, ti * 128:(ti + 1) * 128],
                      reads=[b_AF[sl]], writes=[b_out], sem=b_AF[sl], partial=True)
                if which == "v":
                    P.dma("pool", Vscr[c0 + t0:c0 + t0 + nt, h * 128:(h + 1) * 128], stg[:nt, ti * 128:(ti + 1) * 128],
                          reads=[b_AF[sl]], writes=[b_vscr], sem=b_AF[sl], partial=True)

        def evac_in_even(col0, ps, b_p):
            t = col0 // 128
            if t < 8:
                P.op("act", "activation", R(q3[:, t, :n]), ps, AF.Copy, scale=0.125, reads=[b_p], writes=[b_q[t]])
            elif t < 24:
                which = "k" if t < 16 else "v"
                h = t % 8
                sl = state["evac"] % 2
                tmp = sm(sl, n)
                copy_op(ev_eng(), tmp, ps, reads=[b_p], writes=[b_small[sl]])
                if which == "k":
                    P.dma("pool", KTscr[h, :, c0:c0 + npr], tmp[:, :npr], reads=[b_small[sl]], writes=[b_ktscr], sem=b_small[sl], partial=True)
                    if ns:
                        P.op("dve", "tensor_copy", R(ksamp[:, h, :]), tmp[:, npr:n], reads=[b_small[sl]], writes=[b_ks], partial=True)
                kv_out(which, h, tmp, b_small[sl])
            elif t < 32:
                P.op("act", "copy", sm(4, n), ps, reads=[b_p], writes=[b_small[4]])
            else:
                j = t - 32
                P.op("act", "activation", sm(5, n), ps, AF.Sigmoid, reads=[b_p], writes=[b_small[5]])
                P.op("dve", "tensor_tensor", gext[:, j, 30:30 + npr], sm(4, n)[:, :npr], sm(5, n)[:, :npr], op=ALU.mult,
                     reads=[b_small[4], b_small[5]], writes=[b_g[j]], partial=True)
                if ns:
                    P.op("dve", "tensor_tensor", gs_ext[:, j, 30:34], sm(4, n)[:, npr:n], sm(5, n)[:, npr:n], op=ALU.mult,
                         reads=[b_small[4], b_small[5]], writes=[b_gs], partial=True)

        blocks = blocks_range(0, 3072) + [c for j in range(8) for c in (3072 + 128 * j, 4096 + 128 * j)]
        linear(w_in_even, 16, blocks, lambda kt: R(hn3[:, kt, :n]), b_hn, n, evac_in_even)
        P.barrier()
        if stop_after == "inproj":
            break

        P.stage = "conv"
        oc3 = av(A0, 16, W); b_oc = b_A0
        for w in range(31):
            for j in range(8):
                segs = [(gext, b_g[j], 0, npr)] + ([(gs_ext, b_gs, npr, ns)] if ns else [])
                for (gsrc, b_src, o0, ln) in segs:
                    if w == 0:
                        P.op("dve", "tensor_scalar", conv3[:, j, o0:o0 + ln], gsrc[:, j, 0:ln], cw[:, j, 0:1], cvec[:, 0, j:j + 1], op0=ALU.mult, op1=ALU.add,
                             reads=[b_src, b_par], writes=[b_cv[j]], partial=True)
                    else:
                        P.op("dve", "scalar_tensor_tensor", conv3[:, j, o0:o0 + ln], gsrc[:, j, w:w + ln], cw[:, j, w:w + 1], conv3[:, j, o0:o0 + ln], op0=ALU.mult, op1=ALU.add,
                             reads=[b_src, b_par, b_cv[j]], writes=[b_cv[j]], partial=True)
        if ns:
            for (gsrc, b_src, lo, dsto) in [(gext, None, npr, convp_o), (gs_ext, b_gs, 4, convs_o)]:
                for j in range(8):
                    bs = b_g[j] if b_src is None else b_src
                    P.op("pe", "transpose", PS[2 + j // 4][:30, (j % 4) * 128:(j % 4) * 128 + 128], gsrc[:, j, lo:lo + 30], ident[:, :],
                         reads=[bs, b_ident], writes=[b_ps[2 + j // 4]], partial=(j % 4 > 0))
                stg = arenaF[:, 2048:3072]
                copy_op("act", stg[:30, 0:512], PS[2][:30, :], reads=[b_ps[2]], writes=[b_AF[1]])
                copy_op("dve", stg[:30, 512:1024], PS[3][:30, :], reads=[b_ps[3]], writes=[b_AF[1]], partial=True)
                P.dma("pool", dsto, stg[:30, :], reads=[b_AF[1]], writes=[b_out], sem=b_AF[1], partial=True)
        else:
            pass
        for j in range(8):
            sl = j % 2
            P.op("act", "copy", smr(sl, n), conv3[:, j, :n], reads=[b_cv[j]], writes=[b_smr[sl]])
            P.op("pe", "matmul", PS[4][:, :n], lhsT=ones_r[:], rhs=smr(sl, n), start=(j == 0), stop=(j == 7),
                 reads=[b_smr[sl], b_ones], writes=[b_ps[4]], partial=(j > 0))
        for j in range(8):
            sl = j % 2
            P.op("act", "activation", smr(sl, n), conv3[:, j, :n], AF.Square, reads=[b_cv[j]], writes=[b_smr[sl]])
            P.op("pe", "matmul", PS[5][:, :n], lhsT=ones_r[:], rhs=smr(sl, n), start=(j == 0), stop=(j == 7),
                 reads=[b_smr[sl], b_ones], writes=[b_ps[5]], partial=(j > 0))
        mean = sm(0, n); rstd = sm(1, n); msq = sm(2, n)
        P.op("dve", "tensor_scalar", mean, PS[4][:, :n], 1.0 / 1024, None, op0=ALU.mult, reads=[b_ps[4]], writes=[b_small[0]])
        P.op("dve", "tensor_tensor", msq, mean, mean, op=ALU.mult, reads=[b_small[0]], writes=[b_small[2]])
        P.op("dve", "scalar_tensor_tensor", rstd, PS[5][:, :n], 1.0 / 1024, msq, op0=ALU.mult, op1=ALU.subtract, reads=[b_ps[5], b_small[2]], writes=[b_small[1]])
        P.op("act", "activation", rstd, rstd, AF.Sqrt, bias=EPS, scale=1.0, reads=[b_small[1]], writes=[b_small[1]])
        P.op("dve", "reciprocal", rstd, rstd, reads=[b_small[1]], writes=[b_small[1]])
        for j in range(8):
            eng = "dve" if j % 2 == 0 else "pool"
            P.op(eng, "tensor_tensor", conv3[:, j, :n], conv3[:, j, :n], mean, op=ALU.subtract, reads=[b_cv[j], b_small[0]], writes=[b_cv[j]])
            P.op(eng, "tensor_tensor", conv3[:, j, :n], conv3[:, j, :n], rstd, op=ALU.mult, reads=[b_cv[j], b_small[1]], writes=[b_cv[j]])
            P.op("act", "activation", R(oc3[:, 8 + j, :n]), conv3[:, j, :n], AF.Silu, bias=cvec[:, 2, j:j + 1], scale=cvec[:, 1, j:j + 1],
                 reads=[b_cv[j], b_par], writes=[b_oc[8 + j]])
        if not ns:
            for j in range(8):
                P.op("pool", "tensor_copy", gtail[:, j, :], gext[:, j, npr:npr + 30], reads=[b_g[j]], writes=[b_gt], partial=(j > 0))
        if stop_after == "conv":
            break

        P.stage = "attn"
        AT = OFFX2
        kend = c0 + npr
        nkt = (kend + 127) // 128
        ktb = [arena[:, AT + i * NPR: AT + (i + 1) * NPR].bitcast(F32R) for i in range(2)]; b_ktb = b_A1[27:29]
        VB0 = AT + 2 * NPR
        vb = [arena[:, VB0 + i * 17 * 128: VB0 + (i + 1) * 17 * 128].bitcast(F32R).rearrange("p (a b) -> p a b", a=17) for i in range(2)]; b_vb = b_A1[29:31]
        PB0 = VB0 + 2 * 17 * 128
        pex = [arena[:, PB0 + i * W: PB0 + (i + 1) * W] for i in range(3)]; b_pex = b_A1[31:34]
        tmpb = [arenaF[:, i * W: (i + 1) * W] for i in range(3)]; b_tmp = b_A1[34:37]
        mtmp = arenaF[:, 3 * W: 4 * W]; b_mt = b_A1[37]
        ENDAT = PB0 + 3 * W
        assert ENDAT <= ARENA, ENDAT
        SCB = [0, 1, 7]
        for h in range(8):
            s = h % 2
            P.dma("sp", ktb[s][:, 0:kend], R(KTscr[h, :, 0:kend]), reads=[b_ktscr], writes=[b_ktb[s]], sem=b_ktb[s])
            nfull = kend // 128
            if nfull:
                P.dma("sp", vb[s][:, 0:nfull, :], R(Vscr[0:nfull * 128, h * 128:(h + 1) * 128].rearrange("(a p) d -> p a d", p=128)),
                      reads=[b_vscr], writes=[b_vb[s]], sem=b_vb[s])
            if kend % 128:
                P.dma("sp", vb[s][:kend % 128, nfull, :], R(Vscr[nfull * 128:kend, h * 128:(h + 1) * 128]),
                      reads=[b_vscr], writes=[b_vb[s]], sem=b_vb[s], partial=True)
            work = [(m, kt) for m in range(2) for kt in range(nkt)]

            def front(i):
                m, kt = work[i]
                k0 = kt * 128; kn = min(128, kend - k0)
                sc = SCB[i % 3]; tb = i % 3; pe_i = i % 3
                P.op("pe", "matmul", PS[sc][:kn, :npr], lhsT=ktb[s][64 * m:64 * m + 64, k0:k0 + kn],
                     rhs=R(q3[64 * m:64 * m + 64, h, :npr]), start=True, stop=True,
                     reads=[b_ktb[s], b_q[h]], writes=[b_ps[sc]])
                P.op("dve", "scalar_tensor_tensor", tmpb[tb][:kn, :npr], d0tab[:kn, :npr], SLOPES[h], PS[sc][:kn, :npr], op0=ALU.mult, op1=ALU.add,
                     reads=[b_ps[sc], b_sid], writes=[b_tmp[tb]])
                if k0 + kn - 1 > c0:
                    P.op("dve", "tensor_scalar", mtmp[:kn, :npr], d0tab[:kn, :npr], float(k0 - c0), 0.0, op0=ALU.add, op1=ALU.is_gt,
                         reads=[b_sid], writes=[b_mt])
                    P.op("dve", "scalar_tensor_tensor", tmpb[tb][:kn, :npr], mtmp[:kn, :npr], NEG, tmpb[tb][:kn, :npr], op0=ALU.mult, op1=ALU.add,
                         reads=[b_mt, b_tmp[tb]], writes=[b_tmp[tb]])
                P.op("act", "activation", R(pex[pe_i][:kn, :npr]), tmpb[tb][:kn, :npr], AF.Exp, bias=float(SLOPES[h] * (k0 - c0)), scale=1.0,
                     reads=[b_tmp[tb]], writes=[b_pex[pe_i]])

            def back(i):
                m, kt = work[i]
                k0 = kt * 128; kn = min(128, kend - k0)
                pe_i = i % 3
                o_ps, d_ps = PS[2 + m], PS[5 + m]
                b_o, b_d = b_ps[2 + m], b_ps[5 + m]
                P.op("pe", "matmul", o_ps[:, :npr], lhsT=vb[s][:kn, kt, :], rhs=R(pex[pe_i][:kn, :npr]), start=(kt == 0), stop=(kt == nkt - 1),
                     reads=[b_vb[s], b_pex[pe_i]], writes=[b_o], partial=(kt > 0))
                P.op("pe", "matmul", d_ps[:, :npr], lhsT=ones_r[:kn, :], rhs=R(pex[pe_i][:kn, :npr]), start=(kt == 0), stop=(kt == nkt - 1),
                     reads=[b_ones, b_pex[pe_i]], writes=[b_d], partial=(kt > 0))
                if kt == nkt - 1:
                    P.op("dve", "reciprocal", sm(6, npr), d_ps[:, :npr], reads=[b_d], writes=[b_small[6]])
                    P.op("dve", "tensor_tensor", sm(2 + m, npr), o_ps[:, :npr], sm(6, npr), op=ALU.mult, reads=[b_o, b_small[6]], writes=[b_small[2 + m]])

            LA = 2
            for i in range(min(LA, len(work))):
                front(i)
            for i in range(len(work)):
                if i + LA < len(work):
                    front(i + LA)
                back(i)
            if True:
                if True:
                    pass
            P.op("dve", "scalar_tensor_tensor", sm(2, npr), sm(3, npr), neglam, sm(2, npr), op0=ALU.mult, op1=ALU.add,
                 reads=[b_small[2], b_small[3], b_par], writes=[b_small[2]])
            P.op("act", "activation", smr(0, npr), sm(2, npr), AF.Square, reads=[b_small[2]], writes=[b_smr[0]])
            P.op("pe", "matmul", PS[4][:, :npr], lhsT=ones_r[:], rhs=smr(0, npr), start=True, stop=True, reads=[b_smr[0], b_ones], writes=[b_ps[4]])
            P.op("act", "activation", sm(6, npr), PS[4][:, :npr], AF.Sqrt, bias=EPS, scale=1.0 / 128, reads=[b_ps[4]], writes=[b_small[6]])
            P.op("dve", "reciprocal", sm(6, npr), sm(6, npr), reads=[b_small[6]], writes=[b_small[6]])
            P.op("dve", "scalar_tensor_tensor", R(oc3[:, h, :npr]), sm(2, npr), subg2, sm(6, npr), op0=ALU.mult, op1=ALU.mult,
                 reads=[b_small[2], b_small[6], b_par], writes=[b_oc[h]], partial=True)

        if ns:
            P.stage = "sattn"
            sample_attention(q3, b_q, ksamp, b_ks, oc3, b_oc, npr, AT)
        if stop_after == "attn":
            dump(0, oc3[:, :, :], b_oc, ci)
            break
        P.barrier()
        dump(0, oc3[:, :, :], b_oc, ci)

        P.stage = "outproj0"
        m3 = av(A1, 16, W); b_m = b_A1[0:16]

        def evac_m(col0, ps, b_p, m3=m3, b_m=b_m):
            t = col0 // 128
            copy_op(ev_eng(), m3[:, t, :n], ps, reads=[b_p], writes=[b_m[t]])
        linear(w_out_even, 16, blocks_range(0, D), lambda kt: R(oc3[:, kt, :n]), b_oc, n, evac_m)
        post_norm_residual(m3, b_m, 1, 0, n)
        P.barrier()
        dump(1, x_fm[:, :, :], b_x, ci)
        if stop_after == "mix0":
            break

        for layer in range(2):
            if layer == 1:
                P.stage = "l1_inproj"
                pre_norm(hn3, b_hn, 0, 1, n)
                u3e = av(A1, 16, W + 1); b_u = b_A1[0:16]
                u3 = u3e[:, :, 1:1 + W]
                P.op("dve", "tensor_copy", R(u3e[:, :, 0:1]), u_last[:].unsqueeze(2), reads=[b_ul], writes=list(b_u), partial=True)

                def evac_u(col0, ps, b_p):
                    t = col0 // 128
                    copy_op(ev_eng(), R(u3[:, t, :n]), ps, reads=[b_p], writes=[b_u[t]], partial=True)
                linear(w_in_odd, 16, blocks_range(0, D), lambda kt: R(hn3[:, kt, :n]), b_hn, n, evac_u)
                P.barrier()
                y3 = av(A0, 16, W); b_y = b_A0
                P.stage = "ssm"
                ssm(u3, u3e, b_u, y3, b_y, npr, ns)
                P.stage = "l1_glu_out"
                P.barrier()
                dump(3, y3[:, :, :], b_y, ci)
                for ft in range(16):
                    eng = "dve" if ft % 2 == 0 else "pool"
                    sl = ft % 2
                    yv = y3[:, ft, :n]
                    P.op(eng, "tensor_tensor", sm(sl, n), yv, yv, op=ALU.mult, reads=[b_y[ft]], writes=[b_small[sl]])
                    P.op(eng, "tensor_scalar", sm(sl, n), sm(sl, n), 0.044715, 1.0, op0=ALU.mult, op1=ALU.add, reads=[b_small[sl]], writes=[b_small[sl]])
                    P.op(eng, "tensor_tensor", sm(sl, n), sm(sl, n), yv, op=ALU.mult, reads=[b_small[sl], b_y[ft]], writes=[b_small[sl]])
                    P.op("act", "activation", sm(sl, n), sm(sl, n), AF.Sigmoid, scale=1.5957691216057308, reads=[b_small[sl]], writes=[b_small[sl]])
                    P.op(eng, "tensor_tensor", R(yv), yv, sm(sl, n), op=ALU.mult, reads=[b_small[sl], b_y[ft]], writes=[b_y[ft]])
                z3 = av(A1, 16, W); b_z = b_A1[0:16]

                def evac_z(col0, ps, b_p):
                    t = col0 // 128
                    sl = 2 + t % 2
                    P.op("act", "activation", sm(sl, n), ps, AF.Sigmoid, reads=[b_p], writes=[b_small[sl]])
                    P.op("dve", "tensor_tensor", R(z3[:, t, :n]), y3[:, t, :n], sm(sl, n), op=ALU.mult, reads=[b_small[sl], b_y[t]], writes=[b_z[t]])
                linear(w_glu, 16, blocks_range(0, D), lambda kt: R(y3[:, kt, :n]), b_y, n, evac_z)
                P.barrier()
                m3 = av(A0, 16, W); b_m = b_A0

                def evac_m1(col0, ps, b_p):
                    t = col0 // 128
                    copy_op(ev_eng(), m3[:, t, :n], ps, reads=[b_p], writes=[b_m[t]])
                linear(w_out_odd, 16, blocks_range(0, D), lambda kt: R(z3[:, kt, :n]), b_z, n, evac_m1)
                post_norm_residual(m3, b_m, 1, 1, n)
                P.barrier()
                dump(4, x_fm[:, :, :], b_x, ci)
                if stop_after == "mix1":
                    break
            P.stage = "ffn"
            pre_norm(hn3, b_hn, 2, layer, n)
            act3 = av(A1, 11, W); b_act = b_A1[0:11]
            yacc = av(A1 + 11 * W, 16, W); b_ya = b_A1[11:27]
            for qd in range(4):
                lo = 1408 * qd

                def evac_gate(col0, ps, b_p, lo=lo):
                    jj = (col0 - lo) // 128
                    P.op("act", "activation", act3[:, jj, :n], ps, AF.Silu, reads=[b_p], writes=[b_act[jj]])

                def evac_up(col0, ps, b_p, lo=lo):
                    jj = (col0 - lo) // 128
                    P.op("dve", "tensor_tensor", R(act3[:, jj, :n]), act3[:, jj, :n], ps, op=ALU.mult, reads=[b_p, b_act[jj]], writes=[b_act[jj]])
                linear(w_gate[layer], 16, blocks_range(lo, lo + 1408), lambda kt: R(hn3[:, kt, :n]), b_hn, n, evac_gate)
                linear(w_up[layer], 16, blocks_range(lo, lo + 1408), lambda kt: R(hn3[:, kt, :n]), b_hn, n, evac_up)

                def evac_down(col0, ps, b_p, qd=qd):
                    t = col0 // 128
                    if qd == 0:
                        copy_op(ev_eng(), yacc[:, t, :n], ps, reads=[b_p], writes=[b_ya[t]])
                    else:
                        P.op("dve", "tensor_tensor", yacc[:, t, :n], yacc[:, t, :n], ps, op=ALU.add, reads=[b_p, b_ya[t]], writes=[b_ya[t]])
                linear(w_down[layer][lo:lo + 1408, :], 11, blocks_range(0, D), lambda kt: R(act3[:, kt, :n]), b_act, n, evac_down)
            post_norm_residual(yacc, b_ya, 3, layer, n)
            P.barrier()
            dump(2 if layer == 0 else 5, x_fm[:, :, :], b_x, ci)
            if stop_after == f"ffn{layer}":
                break
        if stop_after is not None and stop_after != "chunk0":
            break

        P.stage = "ystore"
        yst = [arenaF[:, 0:2048], arenaF[:, 0:2048]]
        b_yst = [b_AF[0], b_AF[0]]
        for ti, (t0, nt) in enumerate(ttiles):
            s = ti % 2
            for fg in range(4):
                pb = 2 + (fg % 2)
                for k in range(4):
                    ft = fg * 4 + k
                    P.op("pe", "transpose", PS[pb][:nt, k * 128:(k + 1) * 128], x_fm[:, ft, t0:t0 + nt], ident[:, :],
                         reads=[b_x[ft], b_ident], writes=[b_ps[pb]], partial=(k > 0))
                copy_op(ev_eng(), yst[s][:nt, fg * 512:(fg + 1) * 512], PS[pb][:nt, :], reads=[b_ps[pb]], writes=[b_yst[s]], partial=(fg > 0))
            P.dma("pool", yout[c0 + t0:c0 + t0 + nt, :], yst[s][:nt, :], reads=[b_yst[s]], writes=[b_out], sem=b_yst[s], partial=True)
        P.barrier()
        if stop_after == "chunk0":
            break

    if stop_after is None:
        P.dma("pool", ssmp_o, S_p[:, 1, :], reads=[b_Sp], writes=[b_out], sem=b_Sp, partial=True)
        P.dma("pool", ssms_o, S_s[:], reads=[b_Ss], writes=[b_out], sem=b_Ss, partial=True)
    P.barrier()
    P.emit()
    return nc


_PROG_CACHE = {}


def _host_inputs(inp):
    f32 = np.float32
    c = lambda a: np.ascontiguousarray(a, dtype=a.dtype)
    shared = {}
    shared["w_in_even"] = c(inp["w_in_even"][0]); shared["w_out_even"] = c(inp["w_out_even"][0])
    shared["w_in_odd"] = c(inp["w_in_odd"][0]); shared["w_glu"] = c(inp["w_glu"][0]); shared["w_out_odd"] = c(inp["w_out_odd"][0])
    shared["w_gate"] = c(inp["w_ffn_gate"]); shared["w_up"] = c(inp["w_ffn_up"]); shared["w_down"] = c(inp["w_ffn_down"])
    g = np.zeros((8, 16, 128), f32)
    for kind, nm in enumerate(["norm_mix_pre", "norm_mix_post", "norm_ffn_pre", "norm_ffn_post"]):
        for layer in range(2):
            g[kind * 2 + layer] = np.asarray(inp[nm][layer]).reshape(16, 128)
    shared["gains"] = c(g.transpose(2, 0, 1))
    shared["cw"] = c(np.asarray(inp["conv_w"][0]).reshape(31, 8, 128).transpose(2, 1, 0))
    cv = np.stack([np.asarray(inp[k][0]).reshape(8, 128) for k in ("conv_b", "conv_ln_g", "conv_ln_b")])
    shared["cvec"] = c(cv.transpose(2, 0, 1))
    shared["subg"] = c(np.asarray(inp["subln_g"][0]).reshape(128, 1))
    shared["lqk"] = c(np.concatenate([np.asarray(inp["lambda_q"][0]).ravel(), np.asarray(inp["lambda_k"][0]).ravel()]).reshape(1, 256))
    shared["a_re"] = c(inp["ssm_a_re"][0]); shared["a_im"] = c(inp["ssm_a_im"][0])
    shared["ldt"] = c(np.asarray(inp["ssm_log_dt"][0]).reshape(128, 1))
    shared["b_re"] = c(inp["ssm_b_re"][0]); shared["b_im"] = c(inp["ssm_b_im"][0])
    cre = np.asarray(inp["ssm_c_re"][0]).transpose(2, 0, 1); cim = np.asarray(inp["ssm_c_im"][0]).transpose(2, 0, 1)
    shared["ccat"] = c(np.concatenate([cre, cim], axis=0))
    shared["dfm"] = c(np.asarray(inp["ssm_d"][0]).reshape(16, 128).T)
    shared["cache_k"] = c(np.asarray(inp["cache_k"][0]).reshape(1280 * 128, 1024))
    shared["cache_v"] = c(np.asarray(inp["cache_v"][0]).reshape(1280 * 128, 1024))
    shared["iota"] = np.arange(128, dtype=np.int32).reshape(128, 1)
    shared["ident"] = np.eye(128, dtype=f32)
    shared["ones"] = np.ones((128, 128), f32)
    shared["d0tab"] = (np.arange(128, dtype=f32)[:, None] - np.arange(W, dtype=f32)[None, :]).astype(f32)
    stab = np.zeros((64, 136), f32); slopecol = np.zeros((64, 1), f32)
    for h in range(8):
        for m in range(2):
            for q in range(4):
                r = h * 8 + m * 4 + q
                stab[r, 0:4] = np.arange(4)
                stab[r, 4:8] = np.where(np.arange(4) > q, NEG, 0.0)
                stab[r, 8:136] = np.arange(128) - PAST
                slopecol[r, 0] = SLOPES[h]
    shared["stab"] = stab; shared["slopecol"] = slopecol
    maps = []
    meta = np.asarray(inp["meta_tokens"], f32)
    for i in range(8):
        b = i % 4
        d = dict(shared)
        d["xin"] = c(np.concatenate([meta, np.asarray(inp["x_prompt"][b]), np.asarray(inp["x_sample"][i])], axis=0))
        d["ptab"] = c(np.asarray(inp["page_table"][i], dtype=np.int32).reshape(1, 128))
        d["sconv"] = c(inp["state_conv"][0, i])
        d["sssm"] = c(np.concatenate([np.asarray(inp["state_ssm_re"][0, i]), np.asarray(inp["state_ssm_im"][0, i])], axis=1))
        maps.append(d)
    return maps


def _assemble(res):
    f32 = np.float32
    R_ = [r for r in res]
    y_prompt = np.stack([R_[b]["yout"][16:NPR] for b in range(4)]).astype(f32)
    y_sample = np.stack([R_[i]["yout"][NPR:NT] for i in range(8)]).astype(f32)
    k_prompt = np.stack([R_[b]["kout"][0:NPR].reshape(NPR, 8, 2, 64) for b in range(4)])[None].astype(f32)
    v_prompt = np.stack([R_[b]["vout"][0:NPR].reshape(NPR, 8, 128) for b in range(4)])[None].astype(f32)
    k_sample = np.stack([R_[i]["kout"][NPR:NT].reshape(4, 8, 2, 64) for i in range(8)])[None].astype(f32)
    v_sample = np.stack([R_[i]["vout"][NPR:NT].reshape(4, 8, 128) for i in range(8)])[None].astype(f32)
    conv_prompt = np.stack([R_[b]["convp"] for b in range(4)])[None].astype(f32)
    conv_sample = np.stack([R_[i]["convs"] for i in range(8)])[None].astype(f32)
    srp = np.stack([R_[b]["ssmp"][:, 0:64] for b in range(4)])[None].astype(f32)
    sip = np.stack([R_[b]["ssmp"][:, 64:128] for b in range(4)])[None].astype(f32)
    srs = np.stack([R_[i]["ssms"][:, 0:64] for i in range(8)])[None].astype(f32)
    sis = np.stack([R_[i]["ssms"][:, 64:128] for i in range(8)])[None].astype(f32)
    return (y_prompt, y_sample, k_prompt, v_prompt, k_sample, v_sample, conv_prompt, conv_sample, srp, sip, srs, sis)


def kernel(_stop_after=None, **inputs):
    maps = _host_inputs(inputs)
    nc = build_program(_stop_after)
    res = run_bass_kernel_spmd(nc, maps, core_ids=list(range(8)))
    return _assemble(res.results)
```

```python
import numpy as np
from contextlib import ExitStack
import concourse.bass as bass
import concourse.mybir as mybir

F32 = mybir.dt.float32
F32R = mybir.dt.float32r
I32 = mybir.dt.int32
AF = mybir.ActivationFunctionType
ALU = mybir.AluOpType
AX = mybir.AxisListType

PH = 4096


class Buf:
    __slots__ = ("name", "writers", "readers", "dcount", "semi", "_sw")

    def __init__(self, name):
        self.name = name
        self.writers = []
        self.readers = []
        self.dcount = 0
        self.semi = None
        self._sw = None

    def sw(self):
        if self._sw is None:
            self._sw = Buf(self.name + "_sw")
        return self._sw


class Prog:
    ENG = ("pe", "act", "dve", "pool", "sp")

    def __init__(self, nc):
        self.nc = nc
        self.ops = {e: [] for e in self.ENG}
        self.cnt = {e: 0 for e in self.ENG}
        self.waited = {e: {} for e in self.ENG}
        self.dbufs = []
        self.stack = ExitStack()
        self.retype = None
        self.NR = 20
        self.ring = {q: [Buf(f"ring_{q}{i}") for i in range(self.NR)] for q in ("sp", "pool", "act")}
        self.ringpos = {q: 0 for q in ("sp", "pool", "act")}
        self.stage = "init"
        self.labels = {e: [] for e in self.ENG}

    def sb(self, name, shape, dtype=F32):
        return self.stack.enter_context(self.nc.sbuf_tensor("sb_" + name, list(shape), dtype))

    def ps(self, name, shape, dtype=F32):
        return self.stack.enter_context(self.nc.psum_tensor(name, list(shape), dtype))

    def _need(self, eng, toks):
        best = {}
        cur = self.cnt[eng] + 1
        for t in toks:
            if t[0] == "E":
                _, e2, idx = t
                if e2 == eng and eng in ("pe", "sp"):
                    continue
                if e2 == eng and eng == "dve" and cur - idx >= 2:
                    continue
                key = ("E", e2, (idx - 1) // PH)
                val = (idx - 1) % PH + 1
            else:
                _, b, val = t
                key = ("D", id(b))
                if b.semi is None:
                    b.semi = len(self.dbufs)
                    self.dbufs.append(b)
            if best.get(key, (0,))[0] < val:
                best[key] = (val, t)
        out = []
        w = self.waited[eng]
        for key, (val, t) in best.items():
            if w.get(key, 0) >= val:
                continue
            w[key] = val
            out.append((key, val, t))
        return out

    def _deps(self, reads, writes):
        toks = []
        for b in reads:
            toks += b.writers
        for b in writes:
            toks += b.writers
            toks += b.readers
        return toks

    @staticmethod
    def _dedupe(lst):
        best = {}
        for t in lst:
            if t[0] == "E":
                key = ("E", t[1])
                v = t[2]
            else:
                key = ("D", id(t[1]))
                v = t[2]
            if key not in best or best[key][2] < v:
                best[key] = t
        return list(best.values())

    def _commit(self, tok, reads, writes, partial):
        for b in reads:
            b.readers = self._dedupe(b.readers + [tok])
        for b in writes:
            if partial:
                b.writers = self._dedupe(b.writers + [tok])
            else:
                b.writers = [tok]
                b.readers = []

    def op(self, eng, meth, *args, reads=(), writes=(), partial=False, **kw):
        if self.retype is not None and args:
            args = (self.retype(args[0]),) + tuple(args[1:])
        fn = lambda e, meth=meth, args=args, kw=kw: getattr(e, meth)(*args, **kw)
        waits = self._need(eng, self._deps(reads, writes))
        self.cnt[eng] += 1
        idx = self.cnt[eng]
        tok = ("E", eng, idx)
        self.ops[eng].append((waits, fn, ("E", eng, idx)))
        self.labels[eng].append(self.stage)
        self._commit(tok, reads, writes, partial)
        return tok

    def _ring_sem(self, q):
        rb = self.ring[q][self.ringpos[q] % self.NR]
        self.ringpos[q] += 1
        pre = [("D", rb, 16 * rb.dcount)] if rb.dcount > 0 else []
        return rb, pre

    def dma(self, q, out, in_, reads=(), writes=(), sem=None, partial=False, **kw):
        rb, pre = self._ring_sem(q)
        if self.retype is not None:
            out = self.retype(out)
            if out.dtype == F32R and in_.dtype == F32:
                in_ = in_.bitcast(F32R)
        waits = self._need(q, self._deps(reads, writes) + pre)
        rb.dcount += 1
        tok = ("D", rb, 16 * rb.dcount)
        fn = lambda e, out=out, in_=in_, kw=kw: e.dma_start(out=out, in_=in_, **kw)
        self.ops[q].append((waits, fn, ("D", rb)))
        self._commit(tok, reads, writes, partial)
        return tok

    def dma_custom(self, q, meth, reads=(), writes=(), sem=None, partial=False, **kw):
        fn = lambda e, meth=meth, kw=kw: getattr(e, meth)(**kw)
        rb, pre = self._ring_sem(q)
        waits = self._need(q, self._deps(reads, writes) + pre)
        rb.dcount += 1
        tok = ("D", rb, 16 * rb.dcount)
        self.ops[q].append((waits, fn, ("D", rb)))
        self._commit(tok, reads, writes, partial)
        return tok

    def barrier(self):
        toks = [("E", e, self.cnt[e]) for e in self.ENG if self.cnt[e] > 0]
        toks += [("D", b, 16 * b.dcount) for q in self.ring for b in self.ring[q] if b.dcount > 0]
        for e in self.ENG:
            waits = self._need(e, toks)
            if waits:
                self.ops[e].append((waits, None, None))

    def emit(self):
        nc = self.nc
        import bisect
        mset = {e: set() for e in self.ENG}
        for e in self.ENG:
            for waits, fn, inc in self.ops[e]:
                for key, val, tok in waits:
                    if tok[0] == "E":
                        mset[tok[1]].add(tok[2])
        mlist = {e: sorted(mset[e]) for e in self.ENG}
        nsem = {e: (len(mlist[e]) + PH - 1) // PH for e in self.ENG}
        esem = {e: [self.stack.enter_context(nc.semaphore(f"s_{e}{i}")) for i in range(nsem[e])]
                for e in self.ENG}
        dsem = [self.stack.enter_context(nc.semaphore(f"d_{i}")) for i in range(len(self.dbufs))]
        idmap = {id(b): i for i, b in enumerate(self.dbufs)}

        def rank(e, idx):
            r = bisect.bisect_left(mlist[e], idx)
            assert mlist[e][r] == idx
            return r

        def run(engname, eng):
            for waits, fn, inc in self.ops[engname]:
                for key, val, tok in waits:
                    if tok[0] == "E":
                        r = rank(tok[1], tok[2])
                        eng.wait_ge(esem[tok[1]][r // PH], r % PH + 1)
                    else:
                        eng.wait_ge(dsem[idmap[id(tok[1])]], val)
                if fn is None:
                    continue
                ins = fn(eng)
                if inc[0] == "E":
                    idx = inc[2]
                    if idx in mset[engname]:
                        r = rank(engname, idx)
                        ins.then_inc(esem[engname][r // PH], 1)
                else:
                    ins.then_inc(dsem[idmap[id(inc[1])]], 16)

        with nc.Block() as block:
            @block.tensor
            def _(e):
                run("pe", e)

            @block.scalar
            def _(e):
                run("act", e)

            @block.vector
            def _(e):
                run("dve", e)

            @block.gpsimd
            def _(e):
                run("pool", e)

            @block.sync
            def _(e):
                run("sp", e)
        self.stack.close()

import math
from concourse.bass_utils import run_bass_kernel_spmd

D = 2048
NT = 2068
NPR = 2064
CHUNKS = [(0, 416), (416, 832), (832, 1248), (1248, 1664), (1664, 2068)]
W = 416
DFF = 5632
EPS = 1e-6
LAM_INIT = 0.8 - 0.6 * math.exp(-0.3 * 0)
SLOPES = [2.0 ** (-8.0 * (i + 1) / 8) for i in range(8)]
PAST = 16384
NEG = -30000.0
STOP_AFTER = None


def build_program(stop_after=None):
    nc = bass.Bass("TRN2", target_bir_lowering=False)
    nc.dge_precook = False
    P = Prog(nc)

    def din(name, shape, dt=F32):
        return nc.dram_tensor(name, list(shape), dt, kind="ExternalInput").ap()

    def dout(name, shape, dt=F32):
        return nc.dram_tensor(name, list(shape), dt, kind="ExternalOutput").ap()

    def dscr(name, shape, dt=F32):
        return nc.dram_tensor(name, list(shape), dt).ap()

    xin = din("xin", [NT, D])
    w_in_even = din("w_in_even", [D, 5120]); w_out_even = din("w_out_even", [D, D])
    w_in_odd = din("w_in_odd", [D, D]); w_glu = din("w_glu", [D, D]); w_out_odd = din("w_out_odd", [D, D])
    w_gate = din("w_gate", [2, D, DFF]); w_up = din("w_up", [2, D, DFF]); w_down = din("w_down", [2, DFF, D])
    gains_d = din("gains", [128, 8, 16])
    cw_d = din("cw", [128, 8, 31]); cvec_d = din("cvec", [128, 3, 8])
    subg_d = din("subg", [128, 1]); lqk_d = din("lqk", [1, 256])
    a_re_d = din("a_re", [128, 64]); a_im_d = din("a_im", [128, 64]); ldt_d = din("ldt", [128, 1])
    b_re_d = din("b_re", [128, 64, 16]); b_im_d = din("b_im", [128, 64, 16])
    ccat_d = din("ccat", [128, 128, 16]); dfm_d = din("dfm", [128, 16])
    ck_d = din("cache_k", [1280 * 128, 1024]); cv_d = din("cache_v", [1280 * 128, 1024])
    pt_d = din("ptab", [1, 128], I32); iota_d = din("iota", [128, 1], I32)
    sconv_d = din("sconv", [30, 1024]); sssm_d = din("sssm", [128, 128])
    ident_d = din("ident", [128, 128]); ones_d = din("ones", [128, 128]); d0_d = din("d0tab", [128, W])
    stab_d = din("stab", [64, 8 + 128])
    slopecol_d = din("slopecol", [64, 1])

    yout = dout("yout", [NT, D]); kout = dout("kout", [NT, 1024]); vout = dout("vout", [NT, 1024])
    convp_o = dout("convp", [30, 1024]); convs_o = dout("convs", [30, 1024])
    ssmp_o = dout("ssmp", [128, 128]); ssms_o = dout("ssms", [128, 128])

    dbg_o = dout("dbg", [6, 128, 16 * W]) if stop_after is not None else None
    b_dbg = Buf("dbg")

    def dump(slot, ap3, bufs, ci):
        if ci != 0 or dbg_o is None:
            return
        P.barrier()
        P.dma("pool", dbg_o[slot].rearrange("p (a b) -> p a b", a=16), ap3, reads=list(bufs), writes=[b_dbg], sem=b_dbg, partial=True)
        P.barrier()

    KTscr = dscr("KTscr", [8, 128, NPR]); Vscr = dscr("Vscr", [NT, 1024])
    BDscr = dscr("BDscr", [16, 128, 1024])
    BuScr = dscr("BuScr", [W, 128, 128]); HScr = dscr("HScr", [W, 128, 128])

    ident = P.sb("ident", [128, 128]); b_ident = Buf("ident")
    ones_r = P.sb("ones_r", [128, 128], F32R); b_ones = Buf("ones")
    gains = P.sb("gains", [128, 8, 16]); cw = P.sb("cw", [128, 8, 31]); cvec = P.sb("cvec", [128, 3, 8])
    subg = P.sb("subg", [128, 1]); lamt = P.sb("lamt", [128, 8]); b_par = Buf("params")
    stab = P.sb("stab", [64, 136]); slopecol = P.sb("slopecol", [64, 1])
    dfm = P.sb("dfm", [128, 16])
    zeros = P.sb("zeros", [128, 128]); b_zero = Buf("zeros")
    b_sid = Buf("d0r")
    d0r = P.sb("d0r", [128, W], F32R)
    gtail = P.sb("gtail", [128, 8, 30]); b_gt = Buf("gtail")
    ccat = P.sb("ccat", [128, 128, 16], F32R); b_ccat = Buf("ccat")
    XY = P.sb("XY", [128, 256]); b_xy = Buf("xy")
    S_p = P.sb("S_p", [128, 2, 128]); S_s = P.sb("S_s", [128, 128]); b_Sp = Buf("Sp"); b_Ss = Buf("Ss")
    XY2 = P.sb("XY2", [128, 2, 256]); b_xy2 = Buf("xy2")
    u_last = P.sb("u_last", [128, 16]); b_ul = Buf("ulast")
    BDscr2 = dscr("BDscr2", [16, 128, 1024])
    x_fm = P.sb("x_fm", [128, 16, W]); b_x = [Buf(f"x{i}") for i in range(16)]
    NWB = 3
    wbuf = [P.sb(f"wbuf{i}", [128, 16, 128], F32R) for i in range(NWB)]; b_w = [Buf(f"w{i}") for i in range(NWB)]
    ARENA = 16 * W + 16928
    arenaR = P.sb("arena", [128, ARENA], F32R)
    arena = arenaR[:].bitcast(F32)
    AFSZ = 7680
    arenaF = P.sb("arenaF", [128, AFSZ])
    small = P.sb("small", [128, 7 * W]); b_small = [Buf(f"sm{i}") for i in range(7)]
    smallr = P.sb("smallr", [128, 2 * W], F32R); b_smr = [Buf("smr0"), Buf("smr1")]
    for i in range(2):
        wbuf.append(arenaR[:, ARENA - (2 - i) * 2048: ARENA - (1 - i) * 2048].rearrange("p (a b) -> p a b", a=16))
        b_w.append(Buf(f"w{NWB + i}"))
    NWB += 2
    ptb_t = P.sb("ptb", [128, 128], I32); idx_t = P.sb("idx", [128, 128], I32); iot_t = P.sb("iot", [128, 1], I32)
    PS = [P.ps(f"ps{i}", [128, 512]) for i in range(8)]; b_ps = [Buf(f"ps{i}") for i in range(8)]

    def sm(i, n=W):
        return small[:, i * W:i * W + n]

    def smr(i, n=W):
        return smallr[:, i * W:i * W + n]

    def av(off, a, b):
        return arena[:, off:off + a * b].rearrange("p (a b) -> p a b", a=a)

    R = lambda ap: ap.bitcast(F32R)

    def _rt(ap):
        try:
            if ap.name == "sb_arena" and ap.dtype == F32:
                return ap.bitcast(F32R)
        except Exception:
            pass
        return ap
    P.retype = _rt
    b_AF = [Buf(f"AF{i}") for i in range(8)]

    state = {"wi": 0, "mm": 0, "evac": 0}

    def ev_eng():
        state["evac"] += 1
        return "act" if state["evac"] % 2 else "dve"

    def copy_op(eng, out, in_, reads, writes, partial=False):
        if eng == "act":
            return P.op("act", "copy", out, in_, reads=reads, writes=writes, partial=partial)
        return P.op(eng, "tensor_copy", out, in_, reads=reads, writes=writes, partial=partial)

    P.dma("sp", ident[:], ident_d, writes=[b_ident], sem=b_ident)
    P.dma("sp", ones_r[:], R(ones_d), writes=[b_ones], sem=b_ones)
    P.op("dve", "memset", zeros[:], 0.0, writes=[b_zero])
    P.dma("sp", d0r[:], R(d0_d), writes=[b_sid], sem=b_sid)
    d0tab = d0r[:].bitcast(F32)
    for dst, src in [(gains, gains_d), (cw, cw_d), (cvec, cvec_d), (subg, subg_d), (stab, stab_d),
                     (slopecol, slopecol_d), (dfm, dfm_d)]:
        P.dma("sp", dst[:], src, writes=[b_par], sem=b_par, partial=True)
    lq = sm(0, 256)
    P.dma("pool", lq, lqk_d.partition_broadcast(128), writes=[b_small[0]], sem=b_small[0])
    P.op("dve", "tensor_tensor", sm(1, 128), lq[:, 0:128], lq[:, 128:256], op=ALU.mult, reads=[b_small[0]], writes=[b_small[1]])
    P.op("dve", "tensor_reduce", lamt[:, 0:2], sm(1, 128).rearrange("p (a b) -> p a b", a=2), axis=AX.X, op=ALU.add,
         reads=[b_small[1]], writes=[b_par], partial=True)
    P.op("act", "activation", lamt[:, 2:4], lamt[:, 0:2], AF.Exp, reads=[b_par], writes=[b_par], partial=True)
    P.op("dve", "scalar_tensor_tensor", lamt[:, 4:5], lamt[:, 3:4], -LAM_INIT, lamt[:, 2:3], op0=ALU.add, op1=ALU.subtract,
         reads=[b_par], writes=[b_par], partial=True)
    P.op("dve", "tensor_scalar", lamt[:, 5:6], subg[:], 1.0 - LAM_INIT, None, op0=ALU.mult, reads=[b_par], writes=[b_par], partial=True)
    neglam = lamt[:, 4:5]; subg2 = lamt[:, 5:6]

    def gain(kind, layer, ft):
        return gains[:, kind * 2 + layer, ft:ft + 1]

    def ssm_prep():
        o = 0
        def T(n):
            nonlocal o
            v = arenaF[:, o:o + n]; o += n
            return v
        b_t = Buf("ssmprep")
        lr, li, dt_, zr, zi, er, cs, sn, mk, t1, t2, cr, ci, den = [T(64) for _ in range(14)]
        bre = T(1024); bim = T(1024); bbT = T(2048); zero = T(1024)
        dtc = T(1)
        rw = dict(reads=[b_t], writes=[b_t], partial=True)
        P.dma("sp", lr, a_re_d, writes=[b_t], sem=b_t, partial=True)
        P.dma("sp", li, a_im_d, writes=[b_t], sem=b_t, partial=True)
        P.dma("sp", dtc, ldt_d, writes=[b_t], sem=b_t, partial=True)
        P.dma("sp", bre, b_re_d.rearrange("g p c -> g (p c)"), writes=[b_t], sem=b_t, partial=True)
        P.dma("sp", bim, b_im_d.rearrange("g p c -> g (p c)"), writes=[b_t], sem=b_t, partial=True)
        P.dma("sp", ccat[:], R(ccat_d), writes=[b_ccat], sem=b_ccat)
        P.op("act", "activation", ccat[64:128], ccat[64:128].bitcast(F32), AF.Copy, scale=-1.0, reads=[b_ccat], writes=[b_ccat])
        P.op("act", "activation", dtc, dtc, AF.Exp, **rw)
        P.op("dve", "tensor_scalar", zr, lr, dtc, None, op0=ALU.mult, **rw)
        P.op("dve", "tensor_scalar", zi, li, dtc, None, op0=ALU.mult, **rw)
        P.op("act", "activation", er, zr, AF.Exp, **rw)
        P.op("dve", "tensor_copy", sn, zi, **rw)
        P.op("dve", "tensor_scalar", cs, zi, math.pi / 2, None, op0=ALU.add, **rw)
        for tgt in (sn, cs):
            for _ in range(7):
                P.op("dve", "tensor_scalar", mk, tgt, math.pi, None, op0=ALU.is_gt, **rw)
                P.op("dve", "scalar_tensor_tensor", tgt, mk, -2 * math.pi, tgt, op0=ALU.mult, op1=ALU.add, **rw)
            P.op("act", "activation", tgt, tgt, AF.Sin, **rw)
        P.op("dve", "tensor_tensor", XY[:, 0:64], er, cs, op=ALU.mult, reads=[b_t], writes=[b_xy], partial=True)
        P.op("dve", "tensor_copy", XY[:, 64:128], XY[:, 0:64], reads=[b_xy], writes=[b_xy], partial=True)
        P.op("dve", "tensor_tensor", XY[:, 192:256], er, sn, op=ALU.mult, reads=[b_t], writes=[b_xy], partial=True)
        P.op("dve", "tensor_scalar", XY[:, 128:192], XY[:, 192:256], -1.0, None, op0=ALU.mult, reads=[b_xy], writes=[b_xy], partial=True)
        ar = XY[:, 0:64]; ai = XY[:, 192:256]
        rx = dict(reads=[b_t, b_xy], writes=[b_t], partial=True)
        P.op("dve", "tensor_scalar", t1, ar, -1.0, None, op0=ALU.add, **rx)
        P.op("dve", "tensor_tensor", den, lr, lr, op=ALU.mult, **rw)
        P.op("dve", "tensor_tensor", t2, li, li, op=ALU.mult, **rw)
        P.op("dve", "tensor_tensor", den, den, t2, op=ALU.add, **rw)
        P.op("dve", "reciprocal", den, den, **rw)
        P.op("dve", "tensor_tensor", cr, t1, lr, op=ALU.mult, **rw)
        P.op("dve", "tensor_tensor", t2, ai, li, op=ALU.mult, **rx)
        P.op("dve", "tensor_tensor", cr, cr, t2, op=ALU.add, **rw)
        P.op("dve", "tensor_tensor", cr, cr, den, op=ALU.mult, **rw)
        P.op("dve", "tensor_tensor", ci, ai, lr, op=ALU.mult, **rx)
        P.op("dve", "tensor_tensor", t2, t1, li, op=ALU.mult, **rw)
        P.op("dve", "tensor_tensor", ci, ci, t2, op=ALU.subtract, **rw)
        P.op("dve", "tensor_tensor", ci, ci, den, op=ALU.mult, **rw)
        bre3 = bre.rearrange("g (p c) -> g p c", c=16); bim3 = bim.rearrange("g (p c) -> g p c", c=16)
        bb3 = bbT.rearrange("g (c s) -> g c s", c=16)
        for c in range(16):
            br = bre3[:, :, c]; bi = bim3[:, :, c]
            o_re = bb3[:, c, 0:64]; o_im = bb3[:, c, 64:128]
            P.op("dve", "tensor_tensor", o_re, cr, br, op=ALU.mult, **rw)
            P.op("dve", "tensor_tensor", t2, ci, bi, op=ALU.mult, **rw)
            P.op("dve", "tensor_tensor", o_re, o_re, t2, op=ALU.subtract, **rw)
            P.op("dve", "tensor_tensor", o_im, cr, bi, op=ALU.mult, **rw)
            P.op("dve", "tensor_tensor", t2, ci, br, op=ALU.mult, **rw)
            P.op("dve", "tensor_tensor", o_im, o_im, t2, op=ALU.add, **rw)
        for tt in range(2):
            P.op("dve", "tensor_tensor", XY2[:, tt, 0:64], ar, ar, op=ALU.mult, reads=[b_xy], writes=[b_xy2], partial=True)
            P.op("dve", "tensor_tensor", t2, ai, ai, op=ALU.mult, **rx)
            P.op("dve", "tensor_tensor", XY2[:, tt, 0:64], XY2[:, tt, 0:64], t2, op=ALU.subtract, reads=[b_xy2, b_t], writes=[b_xy2], partial=True)
            P.op("dve", "tensor_copy", XY2[:, tt, 64:128], XY2[:, tt, 0:64], reads=[b_xy2], writes=[b_xy2], partial=True)
            P.op("dve", "tensor_tensor", t2, ar, ai, op=ALU.mult, **rx)
            P.op("dve", "tensor_scalar", XY2[:, tt, 192:256], t2, 2.0, None, op0=ALU.mult, reads=[b_t], writes=[b_xy2], partial=True)
            P.op("dve", "tensor_scalar", XY2[:, tt, 128:192], t2, -2.0, None, op0=ALU.mult, reads=[b_t], writes=[b_xy2], partial=True)
        bbA3 = arenaF[:, 14 * 64: 14 * 64 + 2048].rearrange("g (c s) -> g c s", c=16)
        for c in range(16):
            s_re = bb3[:, c, 0:64]; s_im = bb3[:, c, 64:128]
            d_re = bbA3[:, c, 0:64]; d_im = bbA3[:, c, 64:128]
            P.op("dve", "tensor_tensor", d_re, ar, s_re, op=ALU.mult, **rx)
            P.op("dve", "tensor_tensor", t2, ai, s_im, op=ALU.mult, **rx)
            P.op("dve", "tensor_tensor", d_re, d_re, t2, op=ALU.subtract, **rw)
            P.op("dve", "tensor_tensor", d_im, ar, s_im, op=ALU.mult, **rx)
            P.op("dve", "tensor_tensor", t2, ai, s_re, op=ALU.mult, **rx)
            P.op("dve", "tensor_tensor", d_im, d_im, t2, op=ALU.add, **rw)
        P.op("dve", "memset", zero, 0.0, **rw)
        b_bd = Buf("bdscr")
        for f in range(16):
            P.dma("pool", BDscr[f], zero, reads=[b_t], writes=[b_bd], sem=b_t, partial=True)
            P.dma("sp", BDscr2[f], zero, reads=[b_t], writes=[b_bd], sem=b_t, partial=True)
        P.barrier()
        for g in range(128):
            f, gl = divmod(g, 8)
            P.dma("pool", BDscr[f, gl * 16:(gl + 1) * 16, gl * 128:(gl + 1) * 128].unsqueeze(0),
                  bb3[g:g + 1, :, :], reads=[b_t], writes=[b_bd], sem=b_t, partial=True)
            P.dma("sp", BDscr2[f, gl * 16:(gl + 1) * 16, gl * 128:(gl + 1) * 128].unsqueeze(0),
                  bbA3[g:g + 1, :, :], reads=[b_t], writes=[b_bd], sem=b_t, partial=True)
        P.op("dve", "memset", S_p[:], 0.0, writes=[b_Sp])
        P.op("dve", "memset", u_last[:], 0.0, writes=[b_ul])
        P.dma("sp", S_s[:], sssm_d, writes=[b_Ss], sem=b_Ss)
        P.barrier()
        return b_bd

    b_bd = ssm_prep()
    b_ktscr = Buf("ktscr"); b_vscr = Buf("vscr"); b_buscr = Buf("buscr"); b_hscr = Buf("hscr")
    b_out = Buf("outs")

    def rmsnorm_stats(src_fn, nt_tiles, n, denom, srcbufs):
        ssb = b_ps[4]
        for i in range(nt_tiles):
            sl = i % 2
            P.op("act", "activation", smr(sl, n), src_fn(i), AF.Square, reads=[srcbufs[i]], writes=[b_smr[sl]])
            P.op("pe", "matmul", PS[4][:, :n], lhsT=ones_r[:], rhs=smr(sl, n), start=(i == 0), stop=(i == nt_tiles - 1),
                 reads=[b_smr[sl], b_ones], writes=[ssb], partial=(i > 0))
        P.op("act", "activation", sm(6, n), PS[4][:, :n], AF.Sqrt, bias=EPS, scale=1.0 / denom, reads=[ssb], writes=[b_small[6]])
        P.op("dve", "reciprocal", sm(6, n), sm(6, n), reads=[b_small[6]], writes=[b_small[6]])
        return sm(6, n), b_small[6]

    def linear(Wap, KT, blocks, rhs_fn, rhs_bufs, n, evac):
        W3 = Wap.rearrange("(kt p) c -> p kt c", p=128)
        for c0 in blocks:
            wi = state["wi"] % NWB; state["wi"] += 1
            wb = wbuf[wi]
            P.dma("sp", wb[:, 0:KT, :], R(W3[:, :, c0:c0 + 128]), writes=[b_w[wi]], sem=b_w[wi])
            pi = state["mm"] % 2; state["mm"] += 1
            for kt in range(KT):
                P.op("pe", "matmul", PS[pi][:, :n], lhsT=wb[:, kt, :], rhs=rhs_fn(kt), start=(kt == 0), stop=(kt == KT - 1),
                     reads=[b_w[wi], rhs_bufs[kt]], writes=[b_ps[pi]], partial=(kt > 0))
            evac(c0, PS[pi][:, :n], b_ps[pi])

    def blocks_range(c_lo, c_hi):
        return list(range(c_lo, c_hi, 128))

    def post_norm_residual(m3, b_m, kind, layer, n):
        rstd, b_r = rmsnorm_stats(lambda i: m3[:, i, :n], 16, n, float(D), b_m)
        for ft in range(16):
            eng = "dve"
            P.op(eng, "scalar_tensor_tensor", m3[:, ft, :n], m3[:, ft, :n], gain(kind, layer, ft), rstd, op0=ALU.mult, op1=ALU.mult,
                 reads=[b_m[ft], b_r, b_par], writes=[b_m[ft]])
            P.op(eng, "tensor_tensor", x_fm[:, ft, :n], x_fm[:, ft, :n], m3[:, ft, :n], op=ALU.add,
                 reads=[b_m[ft], b_x[ft]], writes=[b_x[ft]])

    def pre_norm(hn3, b_hn, kind, layer, n):
        rstd, b_r = rmsnorm_stats(lambda i: x_fm[:, i, :n], 16, n, float(D), b_x)
        for ft in range(16):
            eng = "dve"
            P.op(eng, "scalar_tensor_tensor", R(hn3[:, ft, :n]), x_fm[:, ft, :n], gain(kind, layer, ft), rstd, op0=ALU.mult, op1=ALU.mult,
                 reads=[b_x[ft], b_r, b_par], writes=[b_hn[ft]])

    A0 = 0
    A1 = 16 * W
    SZ16 = 16 * W
    b_A0 = [Buf(f"A0_{i}") for i in range(16)]
    b_A1 = [Buf(f"A1_{i}") for i in range(40)]


    def sample_attention(q3, b_q, ksamp, b_ks, oc3, b_oc, npr, AT):
        P.barrier()
        o = AT
        def T(nel):
            nonlocal o
            v = arena[:, o:o + nel]; o += nel
            return v
        kpg = [T(1024) for _ in range(2)]; vpg = [R(T(1024)) for _ in range(2)]
        ktp = [R(T(1024)).rearrange("p (a b) -> p a b", a=8) for _ in range(2)]
        qblk = T(512); ptb = ptb_t[:]; idx = idx_t[:]; iot = iot_t[:]
        ptp = [R(T(64)) for _ in range(2)]; vst = R(T(1024))
        psb = arenaF[:, 0:128]; tms = arenaF[:, 128:256]; oms = arenaF[:, 256:320]; ods = arenaF[:, 320:352]; rss = arenaF[:, 352:384]
        sqs = smr(1, 32)
        assert o <= ARENA
        bk = [Buf("kpg0"), Buf("kpg1")]; bv = [Buf("vpg0"), Buf("vpg1")]; bkt = [Buf("ktp0"), Buf("ktp1")]
        bq = Buf("qblk"); bi = Buf("idx"); bp = Buf("psb"); bpt = [Buf("ptp0"), Buf("ptp1")]; bt = Buf("tms"); bvs = Buf("vst"); bm = Buf("misc")
        P.dma("pool", ptb, pt_d.partition_broadcast(128), writes=[bi], sem=bi)
        P.dma("sp", iot, iota_d, writes=[bi], sem=bi, partial=True)
        P.op("dve", "tensor_scalar", idx, ptb, 128, iot[:, 0:1], op0=ALU.mult, op1=ALU.add, reads=[bi], writes=[bi], partial=True)
        P.dma("sp", vst[:4, :], R(Vscr[NPR:NPR + 4, :]), reads=[b_vscr], writes=[bvs], sem=bvs)
        q4 = qblk.rearrange("p (h c) -> p h c", h=8)
        for qi in range(4):
            P.op("dve", "tensor_copy", R(qblk[:, qi * 128:(qi + 1) * 128]), zeros[:, 0:128], reads=[b_zero], writes=[bq], partial=(qi > 0))
        for h in range(8):
            for m in range(2):
                P.op("dve", "tensor_copy", R(q4[64 * m:64 * m + 64, h, h * 8 + m * 4:h * 8 + m * 4 + 4]), q3[64 * m:64 * m + 64, h, npr:npr + 4],
                     reads=[b_q[h]], writes=[bq], partial=True)
        qb = R(q4)
        o_ps, d_ps, s_ps, t_ps = PS[2], PS[5], PS[7], PS[4]
        for i in range(128):
            s = i % 2
            P.dma_custom("pool", "indirect_dma_start", reads=[bi], writes=[bk[s]], sem=bk[s],
                         out=R(kpg[s]), out_offset=None, in_=R(ck_d), in_offset=bass.IndirectOffsetOnAxis(ap=idx[:, i:i + 1], axis=0))
            P.dma_custom("pool", "indirect_dma_start", reads=[bi], writes=[bv[s]], sem=bv[s],
                         out=vpg[s], out_offset=None, in_=R(cv_d), in_offset=bass.IndirectOffsetOnAxis(ap=idx[:, i:i + 1], axis=0))
            for h in range(8):
                P.op("pe", "transpose", PS[h // 4][:, (h % 4) * 128:(h % 4 + 1) * 128], kpg[s][:, h * 128:(h + 1) * 128], ident[:, :],
                     reads=[bk[s], b_ident], writes=[b_ps[h // 4]], partial=(h % 4 > 0))
            copy_op("act", ktp[s][:, 0:4, :], PS[0][:, :].rearrange("p (a b) -> p a b", a=4), reads=[b_ps[0]], writes=[bkt[s]])
            copy_op("dve", ktp[s][:, 4:8, :], PS[1][:, :].rearrange("p (a b) -> p a b", a=4), reads=[b_ps[1]], writes=[bkt[s]], partial=True)
            for h in range(8):
                P.op("pe", "matmul", s_ps[:64, 0:128], lhsT=qb[:, h, :], rhs=ktp[s][:, h, :], start=(h == 0), stop=(h == 7),
                     reads=[bq, bkt[s]], writes=[b_ps[7]], partial=(h > 0))
            P.op("pool", "tensor_scalar", tms[:64, :], stab[:, 8:136], float(128 * i), slopecol[:, 0:1], op0=ALU.add, op1=ALU.mult,
                 reads=[b_par], writes=[bt])
            P.op("dve", "tensor_tensor", tms[:64, :], tms[:64, :], s_ps[:64, 0:128], op=ALU.add, reads=[bt, b_ps[7]], writes=[bt])
            P.op("act", "activation", psb[:64, :], tms[:64, :], AF.Exp, reads=[bt], writes=[bp])
            P.op("pe", "transpose", t_ps[:, 0:64], psb[:64, :], ident[:64, :64], reads=[bp, b_ident], writes=[b_ps[4]])
            copy_op("dve", ptp[s], t_ps[:, 0:64], reads=[b_ps[4]], writes=[bpt[s]])
            for h in range(8):
                P.op("pe", "matmul", o_ps[:, h * 8:(h + 1) * 8], lhsT=vpg[s][:, h * 128:(h + 1) * 128], rhs=ptp[s][:, h * 8:(h + 1) * 8],
                     start=(i == 0), stop=False, reads=[bv[s], bpt[s]], writes=[b_ps[2]], partial=True)
            P.op("pe", "matmul", d_ps[:, 0:64], lhsT=ones_r[:], rhs=ptp[s], start=(i == 0), stop=False,
                 reads=[b_ones, bpt[s]], writes=[b_ps[5]], partial=True)
        for h in range(8):
            P.op("pe", "matmul", s_ps[:64, 0:4], lhsT=qb[:, h, :], rhs=R(ksamp[:, h, :]), start=(h == 0), stop=(h == 7),
                 reads=[bq, b_ks], writes=[b_ps[7]], partial=(h > 0))
        P.op("dve", "scalar_tensor_tensor", tms[:64, 0:4], stab[:, 0:4], slopecol[:, 0:1], stab[:, 4:8], op0=ALU.mult, op1=ALU.add,
             reads=[b_par], writes=[bt])
        P.op("dve", "tensor_tensor", tms[:64, 0:4], tms[:64, 0:4], s_ps[:64, 0:4], op=ALU.add, reads=[bt, b_ps[7]], writes=[bt])
        P.op("act", "activation", psb[:64, 0:4], tms[:64, 0:4], AF.Exp, reads=[bt], writes=[bp])
        P.op("pe", "transpose", t_ps[:4, 0:64], psb[:64, 0:4], ident[:64, :64], reads=[bp, b_ident], writes=[b_ps[4]])
        copy_op("dve", ptp[0][:4, :], t_ps[:4, 0:64], reads=[b_ps[4]], writes=[bpt[0]])
        for h in range(8):
            P.op("pe", "matmul", o_ps[:, h * 8:(h + 1) * 8], lhsT=vst[:4, h * 128:(h + 1) * 128], rhs=ptp[0][:4, h * 8:(h + 1) * 8],
                 start=False, stop=True, reads=[bvs, bpt[0]], writes=[b_ps[2]], partial=True)
        P.op("pe", "matmul", d_ps[:, 0:64], lhsT=ones_r[:4, :], rhs=ptp[0][:4, :], start=False, stop=True,
             reads=[b_ones, bpt[0]], writes=[b_ps[5]], partial=True)
        P.op("dve", "reciprocal", oms, d_ps[:, 0:64], reads=[b_ps[5]], writes=[bm])
        P.op("dve", "tensor_tensor", oms, oms, o_ps[:, 0:64], op=ALU.mult, reads=[bm, b_ps[2]], writes=[bm])
        om4 = oms.rearrange("p (h m q) -> p h m q", h=8, m=2)
        od3 = ods.rearrange("p (h q) -> p h q", h=8)
        P.op("dve", "scalar_tensor_tensor", od3, om4[:, :, 1, :], neglam, om4[:, :, 0, :], op0=ALU.mult, op1=ALU.add, reads=[bm, b_par], writes=[bm], partial=True)
        P.op("act", "activation", sqs, ods, AF.Square, reads=[bm], writes=[b_smr[1]])
        P.op("pe", "matmul", PS[4][:, 0:32], lhsT=ones_r[:], rhs=sqs, start=True, stop=True, reads=[b_smr[1], b_ones], writes=[b_ps[4]])
        P.op("act", "activation", rss, PS[4][:, 0:32], AF.Sqrt, bias=EPS, scale=1.0 / 128, reads=[b_ps[4]], writes=[bm], partial=True)
        P.op("dve", "reciprocal", rss, rss, reads=[bm], writes=[bm], partial=True)
        P.op("dve", "scalar_tensor_tensor", R(oc3[:, 0:8, npr:npr + 4]), od3, subg2, rss.rearrange("p (h q) -> p h q", h=8), op0=ALU.mult, op1=ALU.mult,
             reads=[bm, b_par], writes=list(b_oc[0:8]), partial=True)
        P.barrier()

    rgn = [Buf(f"ssm_r{i}") for i in range(4)]

    def ssm(u3, u3e, b_u, y3, b_y, npr, ns):
        SB = A1 + 16 * (W + 1)
        o = SB
        def T(nel):
            nonlocal o
            v = arena[:, o:o + nel]; o += nel
            return v
        TSZ = npr // 4
        TS = TSZ // 4
        scanb = [arenaF[:, i * 1792:(i + 1) * 1792].rearrange("g (t s) -> g t s", s=128) for i in range(2)]
        scan = arenaF[:, 0:3328].rearrange("g (t s) -> g t s", s=128)
        bd = [R(T(1024)) for _ in range(2)]; bd2 = [R(T(1024)) for _ in range(2)]
        bus = [arenaF[:, 3584 + i * 1024: 4608 + i * 1024] for i in range(3)]; b_bus = rgn[0:3]
        htm = [arenaF[:, 3584 + i * 1024: 4608 + i * 1024] for i in range(2)]; b_htm = rgn[0:2]
        ytm = arenaF[:, 5632:7680]; b_ytm = [rgn[2], rgn[3]]
        hT = [R(T(1024)).rearrange("p (a b) -> p a b", a=8) for _ in range(2)]
        assert o <= ARENA, o
        b_scan = Buf("scan"); b_scb = [b_scan, Buf("scan1")]; b_bdb = [Buf("bd0"), Buf("bd1")]; b_bdb2 = [Buf("bd20"), Buf("bd21")]
        b_hT = [Buf("hT0"), Buf("hT1")]
        b_t1 = [Buf("t1a"), Buf("t1b")]; b_t2 = [Buf("t2a"), Buf("t2b")]; b_s1 = [Buf("s1a"), Buf("s1b")]
        tiles = [(i * TSZ, TSZ, False) for i in range(4)] + ([(npr, ns, True)] if ns else [])
        P.stage = "ssm_A"
        k = 0
        for f in range(16):
            sb_ = f % 2
            P.dma("sp", bd[sb_], R(BDscr[f]), reads=[b_bd], writes=[b_bdb[sb_]], sem=b_bdb[sb_])
            P.dma("sp", bd2[sb_], R(BDscr2[f]), reads=[b_bd], writes=[b_bdb2[sb_]], sem=b_bdb2[sb_])
            for (t0, nt, is_s) in tiles:
                s = k % 3; k += 1
                for hf in range(2):
                    pb = (k % 2) * 2 + hf
                    P.op("pe", "matmul", PS[pb][:nt, :], lhsT=R(u3e[:, f, 1 + t0:1 + t0 + nt]), rhs=bd[sb_][:, hf * 512:(hf + 1) * 512], start=True, stop=is_s,
                         reads=[b_u[f], b_bdb[sb_]], writes=[b_ps[pb]])
                    if not is_s:
                        P.op("pe", "matmul", PS[pb][:nt, :], lhsT=R(u3e[:, f, t0:t0 + nt]), rhs=bd2[sb_][:, hf * 512:(hf + 1) * 512], start=False, stop=True,
                             reads=[b_u[f], b_bdb2[sb_]], writes=[b_ps[pb]], partial=True)
                    copy_op("act" if hf == 0 else "dve", bus[s][:nt, hf * 512:(hf + 1) * 512], PS[pb][:nt, :], reads=[b_ps[pb]], writes=[b_bus[s]], partial=(hf > 0))
                P.dma("sp", BuScr[t0:t0 + nt, f * 8:(f + 1) * 8, :].rearrange("t g s -> t (g s)"), bus[s][:nt, :],
                      reads=[b_bus[s]], writes=[b_buscr], sem=b_bus[s], partial=True)
        X3 = XY[:, 0:128].rearrange("g (r p) -> g r p", r=2); Yn = XY[:, 128:192]; Yp = XY[:, 192:256]
        t1s = sm(0, 128).rearrange("g (r p) -> g r p", r=2); t2s = sm(1, 128).rearrange("g (r p) -> g r p", r=2)
        s1s = sm(2, 128).rearrange("g (r p) -> g r p", r=2)

        X2 = XY2[:, :, 0:128]; Y2n = XY2[:, :, 128:192]; Y2p = XY2[:, :, 192:256]
        p1 = sm(0, 256).rearrange("g (t s) -> g t s", t=2); p2 = sm(1, 256).rearrange("g (t r p) -> g t r p", t=2, r=2)
        p3 = sm(2, 256).rearrange("g (t s) -> g t s", t=2)
        b_p1 = Buf("p1"); b_p2 = Buf("p2"); b_p3 = Buf("p3")
        spc = sm(3, 256); b_spc = Buf("spc"); b_spc2 = Buf("spc2")

        sub_i = {"k": 0}

        def scan_pairs(t0, nt):
            P.stage = "ssm_scan"
            sizes = [14, 12] * 4 if nt == 104 else [14, 12, 14, 12, 12, 12, 12, 12]
            assert sum(sizes) == nt
            starts = [t0 + sum(sizes[:i]) for i in range(len(sizes))]
            base = sub_i["k"]; sub_i["k"] += len(sizes)

            def load(j):
                bi = (base + j) % 2
                P.dma("sp", scanb[bi][:, 0:sizes[j], :], BuScr[starts[j]:starts[j] + sizes[j]].rearrange("t g s -> g t s"),
                      reads=[b_buscr], writes=[b_scb[bi]], sem=b_scb[bi])
            load(0)
            for j, tn in enumerate(sizes):
                if j + 1 < len(sizes):
                    load(j + 1)
                bi = (base + j) % 2
                sc_ = scanb[bi]; bsc = b_scb[bi]
                ts0 = starts[j]
                for i in range(tn // 2):
                    prev = S_p[:, :, :] if i == 0 else sc_[:, 2 * i - 2:2 * i, :]
                    pb = [b_Sp] if i == 0 else [bsc]
                    cur = sc_[:, 2 * i:2 * i + 2, :]
                    pv4 = prev.rearrange("g t (r p) -> g t r p", r=2)
                    P.op("dve", "tensor_tensor", p1, prev, X2, op=ALU.mult, reads=pb + [b_xy2], writes=[b_p1])
                    P.op("dve", "tensor_tensor", p2[:, :, 0, :], pv4[:, :, 1, :], Y2n, op=ALU.mult, reads=pb + [b_xy2], writes=[b_p2])
                    P.op("dve", "tensor_tensor", p2[:, :, 1, :], pv4[:, :, 0, :], Y2p, op=ALU.mult, reads=pb + [b_xy2], writes=[b_p2], partial=True)
                    P.op("dve", "tensor_tensor", p3, cur, p1, op=ALU.add, reads=[b_p1, bsc], writes=[b_p3])
                    P.op("dve", "tensor_copy", spc[:, 0:128], zeros[:, 0:128], reads=[b_zero], writes=[b_spc])
                    P.op("dve", "tensor_tensor", cur, p3, p2.rearrange("g t r p -> g t (r p)"), op=ALU.add, reads=[b_p2, b_p3], writes=[bsc], partial=True)
                    P.op("dve", "tensor_copy", spc[:, 128:256], zeros[:, 0:128], reads=[b_zero], writes=[b_spc2])
                P.op("dve", "tensor_copy", S_p[:, :, :], sc_[:, tn - 2:tn, :], reads=[bsc], writes=[b_Sp])
                P.dma("pool", HScr[ts0:ts0 + tn].rearrange("t g s -> g t s"), sc_[:, 0:tn, :], reads=[bsc], writes=[b_hscr], sem=bsc, partial=True)

        def scan_tile(t0, nt, is_s):
            if not is_s:
                return scan_pairs(t0, nt)
            P.stage = "ssm_scan"
            St, b_St = (S_s, b_Ss)
            subs = [(t0, nt)]
            for (ts0, tn) in subs:
                P.dma("pool", scan[:, 0:tn, :], BuScr[ts0:ts0 + tn].rearrange("t g s -> g t s"), reads=[b_buscr], writes=[b_scan], sem=b_scan)
                for t in range(tn):
                    prev = St[:] if t == 0 else scan[:, t - 1, :]
                    pb = [b_St] if t == 0 else [b_scan]
                    pv = prev.rearrange("g (r p) -> g r p", r=2)
                    cv_ = scan[:, t, :].rearrange("g (r p) -> g r p", r=2)
                    hs = [(0, 32), (32, 64)]
                    for si, (a, b_) in enumerate(hs):
                        P.op("dve", "tensor_tensor", t1s[:, :, a:b_], pv[:, :, a:b_], X3[:, :, a:b_], op=ALU.mult, reads=pb + [b_xy], writes=[b_t1[si]])
                    for si, (a, b_) in enumerate(hs):
                        P.op("dve", "tensor_tensor", t2s[:, 0, a:b_], pv[:, 1, a:b_], Yn[:, a:b_], op=ALU.mult, reads=pb + [b_xy], writes=[b_t2[si]])
                    for si, (a, b_) in enumerate(hs):
                        P.op("dve", "tensor_tensor", t2s[:, 1, a:b_], pv[:, 0, a:b_], Yp[:, a:b_], op=ALU.mult, reads=pb + [b_xy], writes=[b_t2[si]], partial=True)
                    for si, (a, b_) in enumerate(hs):
                        P.op("dve", "tensor_tensor", s1s[:, :, a:b_], cv_[:, :, a:b_], t1s[:, :, a:b_], op=ALU.add, reads=[b_t1[si], b_scan], writes=[b_s1[si]])
                    for si, (a, b_) in enumerate(hs):
                        P.op("dve", "tensor_tensor", cv_[:, :, a:b_], s1s[:, :, a:b_], t2s[:, :, a:b_], op=ALU.add, reads=[b_t2[si], b_s1[si]], writes=[b_scan], partial=True)
                P.op("dve", "tensor_copy", St[:], scan[:, tn - 1, :], reads=[b_scan], writes=[b_St])
                P.dma("pool", HScr[ts0:ts0 + tn].rearrange("t g s -> g t s"), scan[:, 0:tn, :], reads=[b_scan], writes=[b_hscr], sem=b_scan, partial=True)

        kc = {"k": 0}

        def c_tile(t0, nt):
            P.stage = "ssm_C"
            for f in range(16):
                s = kc["k"] % 2; kc["k"] += 1
                P.dma("sp", htm[s][:nt, :], HScr[t0:t0 + nt, f * 8:(f + 1) * 8, :].rearrange("t g s -> t (g s)"), reads=[b_hscr], writes=[b_htm[s]], sem=b_htm[s])
                for gl in range(8):
                    P.op("pe", "transpose", PS[2 + gl // 4][:, (gl % 4) * 128:(gl % 4) * 128 + nt], htm[s][:nt, gl * 128:(gl + 1) * 128], ident[:nt, :nt],
                         reads=[b_htm[s], b_ident], writes=[b_ps[2 + gl // 4]], partial=(gl % 4 > 0))
                copy_op("act", hT[s][:, 0:4, :nt], PS[2][:, :].rearrange("p (a b) -> p a b", a=4)[:, :, :nt], reads=[b_ps[2]], writes=[b_hT[s]])
                copy_op("act", hT[s][:, 4:8, :nt], PS[3][:, :].rearrange("p (a b) -> p a b", a=4)[:, :, :nt], reads=[b_ps[3]], writes=[b_hT[s]], partial=True)
                for gl in range(8):
                    P.op("pe", "matmul", PS[4 + f // 4][:nt, (f % 4) * 128 + gl * 16:(f % 4) * 128 + gl * 16 + 16], lhsT=hT[s][:, gl, :nt], rhs=ccat[:, f * 8 + gl, :],
                         start=True, stop=True, reads=[b_hT[s], b_ccat], writes=[b_ps[4 + f // 4]], partial=True)
            for bq_ in range(4):
                copy_op("act", ytm[:nt, bq_ * 512:(bq_ + 1) * 512], PS[4 + bq_][:nt, :], reads=[b_ps[4 + bq_]], writes=b_ytm, partial=(bq_ > 0))
            for fg in range(4):
                pb_ = fg % 2
                for kk in range(4):
                    f = fg * 4 + kk
                    P.op("pe", "transpose", PS[pb_][:, kk * 128:kk * 128 + nt], ytm[:nt, f * 128:(f + 1) * 128], ident[:nt, :nt],
                         reads=b_ytm + [b_ident], writes=[b_ps[pb_]], partial=(kk > 0))
                for kk in range(4):
                    f = fg * 4 + kk
                    P.op("dve", "scalar_tensor_tensor", y3[:, f, t0:t0 + nt], u3[:, f, t0:t0 + nt], dfm[:, f:f + 1], PS[pb_][:, kk * 128:kk * 128 + nt],
                         op0=ALU.mult, op1=ALU.add, reads=[b_u[f], b_ps[pb_], b_par], writes=[b_y[f]], partial=True)

        P.op("dve", "tensor_copy", u_last[:].unsqueeze(2), u3e[:, :, npr:npr + 1], reads=list(b_u), writes=[b_ul])
        prev_tile = None
        for (t0, nt, is_s) in tiles:
            scan_tile(t0, nt, is_s)
            if prev_tile is not None:
                c_tile(*prev_tile)
            prev_tile = (t0, nt)
        c_tile(*prev_tile)

    for ci, (c0, c1) in enumerate(CHUNKS):
        n = c1 - c0
        npr = min(c1, NPR) - c0
        ns = n - npr
        ttiles = [(t0, min(128, n - t0)) for t0 in range(0, n, 128)]

        P.stage = "xload"
        xst = [arenaF[:, 0:2048], arenaF[:, 0:2048]]
        b_xst = [b_AF[0], b_AF[0]]
        for ti, (t0, nt) in enumerate(ttiles):
            s = ti % 2
            P.dma("sp", xst[s][:nt, :], xin[c0 + t0:c0 + t0 + nt, :], writes=[b_xst[s]], sem=b_xst[s])
            for fg in range(4):
                pb = 2 + (fg % 2)
                for k in range(4):
                    ft = fg * 4 + k
                    P.op("pe", "transpose", PS[pb][:, k * 128:k * 128 + nt], xst[s][:nt, ft * 128:(ft + 1) * 128], ident[:nt, :nt],
                         reads=[b_xst[s], b_ident], writes=[b_ps[pb]], partial=(k > 0))
                eng = ev_eng()
                copy_op(eng, x_fm[:, fg * 4:fg * 4 + 4, t0:t0 + nt], PS[pb][:].rearrange("p (a b) -> p a b", a=4)[:, :, :nt],
                        reads=[b_ps[pb]], writes=[b_x[fg * 4 + k] for k in range(4)], partial=True)
        P.barrier()

        P.stage = "l0_inproj"
        hn3 = av(A0, 16, W); b_hn = b_A0
        pre_norm(hn3, b_hn, 0, 0, n)
        q3 = av(A1, 8, W); b_q = b_A1[0:8]
        GW = 30 + W
        gext = av(A1 + 8 * W, 8, GW); b_g = b_A1[8:16]
        conv3 = arenaF[:, 3072:3072 + 8 * W].rearrange("p (a b) -> p a b", a=8); b_cv = b_A1[16:24]
        OFFX = A1 + 8 * W + 8 * GW
        gs_ext = av(OFFX, 8, 34); b_gs = b_A1[24]
        ksamp = av(OFFX + 272, 8, 4); b_ks = b_A1[25]
        OFFX2 = OFFX + 272 + 32
        for j in range(8):
            if ci == 0:
                P.op("dve", "tensor_copy", gext[:, j, 0:30], zeros[:, 0:30], reads=[b_zero], writes=[b_g[j]])
            else:
                P.op("dve", "tensor_copy", gext[:, j, 0:30], gtail[:, j, :], reads=[b_gt], writes=[b_g[j]])
        if ns:
            scst = arenaF[:, 2048:3072]; b_sc = b_AF[1]
            P.dma("sp", scst[:30, :], sconv_d, writes=[b_sc], sem=b_sc)
            for j in range(8):
                P.op("pe", "transpose", PS[2][:, j * 32:j * 32 + 30], scst[:30, j * 128:(j + 1) * 128], ident[:30, :30],
                     reads=[b_sc, b_ident], writes=[b_ps[2]], partial=(j > 0))
            copy_op("dve", gs_ext[:, :, 0:30], PS[2][:, 0:256].rearrange("p (a b) -> p a b", a=8)[:, :, 0:30], reads=[b_ps[2]], writes=[b_gs], partial=True)

        def kv_out(which, h, src, b_src):
            dst = kout if which == "k" else vout
            sl = 2 + (state["evac"] % 2)
            for ti, (t0, nt) in enumerate(ttiles):
                P.op("pe", "transpose", PS[sl][:nt, ti * 128:(ti + 1) * 128], src[:, t0:t0 + nt], ident[:, :],
                     reads=[b_src, b_ident], writes=[b_ps[sl]], partial=(ti > 0))
            stg = arenaF[:, 3072 + (sl - 2) * 512: 3072 + (sl - 1) * 512]
            copy_op(ev_eng(), stg, PS[sl][:, :], reads=[b_ps[sl]], writes=[b_AF[sl]])
            for ti, (t0, nt) in enumerate(ttiles):
                P.dma("pool", dst[c0 + t0:c0 + t0 + nt, h * 128:(h + 1) * 128], stg[:nt, ti * 128:(ti + 1) * 128],
                      reads=[b_AF[sl]], writes=[b_out], sem=b_AF[sl], partial=True)
                if which == "v":
                    P.dma("pool", Vscr[c0 + t0:c0 + t0 + nt, h * 128:(h + 1) * 128], stg[:nt, ti * 128:(ti + 1) * 128],
                          reads=[b_AF[sl]], writes=[b_vscr], sem=b_AF[sl], partial=True)

        def evac_in_even(col0, ps, b_p):
            t = col0 // 128
            if t < 8:
                P.op("act", "activation", R(q3[:, t, :n]), ps, AF.Copy, scale=0.125, reads=[b_p], writes=[b_q[t]])
            elif t < 24:
                which = "k" if t < 16 else "v"
                h = t % 8
                sl = state["evac"] % 2
                tmp = sm(sl, n)
                copy_op(ev_eng(), tmp, ps, reads=[b_p], writes=[b_small[sl]])
                if which == "k":
                    P.dma("pool", KTscr[h, :, c0:c0 + npr], tmp[:, :npr], reads=[b_small[sl]], writes=[b_ktscr], sem=b_small[sl], partial=True)
                    if ns:
                        P.op("dve", "tensor_copy", R(ksamp[:, h, :]), tmp[:, npr:n], reads=[b_small[sl]], writes=[b_ks], partial=True)
                kv_out(which, h, tmp, b_small[sl])
            elif t < 32:
                P.op("act", "copy", sm(4, n), ps, reads=[b_p], writes=[b_small[4]])
            else:
                j = t - 32
                P.op("act", "activation", sm(5, n), ps, AF.Sigmoid, reads=[b_p], writes=[b_small[5]])
                P.op("dve", "tensor_tensor", gext[:, j, 30:30 + npr], sm(4, n)[:, :npr], sm(5, n)[:, :npr], op=ALU.mult,
                     reads=[b_small[4], b_small[5]], writes=[b_g[j]], partial=True)
                if ns:
                    P.op("dve", "tensor_tensor", gs_ext[:, j, 30:34], sm(4, n)[:, npr:n], sm(5, n)[:, npr:n], op=ALU.mult,
                         reads=[b_small[4], b_small[5]], writes=[b_gs], partial=True)

        blocks = blocks_range(0, 3072) + [c for j in range(8) for c in (3072 + 128 * j, 4096 + 128 * j)]
        linear(w_in_even, 16, blocks, lambda kt: R(hn3[:, kt, :n]), b_hn, n, evac_in_even)
        P.barrier()
        if stop_after == "inproj":
            break

        P.stage = "conv"
        oc3 = av(A0, 16, W); b_oc = b_A0
        for w in range(31):
            for j in range(8):
                segs = [(gext, b_g[j], 0, npr)] + ([(gs_ext, b_gs, npr, ns)] if ns else [])
                for (gsrc, b_src, o0, ln) in segs:
                    if w == 0:
                        P.op("dve", "tensor_scalar", conv3[:, j, o0:o0 + ln], gsrc[:, j, 0:ln], cw[:, j, 0:1], cvec[:, 0, j:j + 1], op0=ALU.mult, op1=ALU.add,
                             reads=[b_src, b_par], writes=[b_cv[j]], partial=True)
                    else:
                        P.op("dve", "scalar_tensor_tensor", conv3[:, j, o0:o0 + ln], gsrc[:, j, w:w + ln], cw[:, j, w:w + 1], conv3[:, j, o0:o0 + ln], op0=ALU.mult, op1=ALU.add,
                             reads=[b_src, b_par, b_cv[j]], writes=[b_cv[j]], partial=True)
        if ns:
            for (gsrc, b_src, lo, dsto) in [(gext, None, npr, convp_o), (gs_ext, b_gs, 4, convs_o)]:
                for j in range(8):
                    bs = b_g[j] if b_src is None else b_src
                    P.op("pe", "transpose", PS[2 + j // 4][:30, (j % 4) * 128:(j % 4) * 128 + 128], gsrc[:, j, lo:lo + 30], ident[:, :],
                         reads=[bs, b_ident], writes=[b_ps[2 + j // 4]], partial=(j % 4 > 0))
                stg = arenaF[:, 2048:3072]
                copy_op("act", stg[:30, 0:512], PS[2][:30, :], reads=[b_ps[2]], writes=[b_AF[1]])
                copy_op("dve", stg[:30, 512:1024], PS[3][:30, :], reads=[b_ps[3]], writes=[b_AF[1]], partial=True)
                P.dma("pool", dsto, stg[:30, :], reads=[b_AF[1]], writes=[b_out], sem=b_AF[1], partial=True)
        else:
            pass
        for j in range(8):
            sl = j % 2
            P.op("act", "copy", smr(sl, n), conv3[:, j, :n], reads=[b_cv[j]], writes=[b_smr[sl]])
            P.op("pe", "matmul", PS[4][:, :n], lhsT=ones_r[:], rhs=smr(sl, n), start=(j == 0), stop=(j == 7),
                 reads=[b_smr[sl], b_ones], writes=[b_ps[4]], partial=(j > 0))
        for j in range(8):
            sl = j % 2
            P.op("act", "activation", smr(sl, n), conv3[:, j, :n], AF.Square, reads=[b_cv[j]], writes=[b_smr[sl]])
            P.op("pe", "matmul", PS[5][:, :n], lhsT=ones_r[:], rhs=smr(sl, n), start=(j == 0), stop=(j == 7),
                 reads=[b_smr[sl], b_ones], writes=[b_ps[5]], partial=(j > 0))
        mean = sm(0, n); rstd = sm(1, n); msq = sm(2, n)
        P.op("dve", "tensor_scalar", mean, PS[4][:, :n], 1.0 / 1024, None, op0=ALU.mult, reads=[b_ps[4]], writes=[b_small[0]])
        P.op("dve", "tensor_tensor", msq, mean, mean, op=ALU.mult, reads=[b_small[0]], writes=[b_small[2]])
        P.op("dve", "scalar_tensor_tensor", rstd, PS[5][:, :n], 1.0 / 1024, msq, op0=ALU.mult, op1=ALU.subtract, reads=[b_ps[5], b_small[2]], writes=[b_small[1]])
        P.op("act", "activation", rstd, rstd, AF.Sqrt, bias=EPS, scale=1.0, reads=[b_small[1]], writes=[b_small[1]])
        P.op("dve", "reciprocal", rstd, rstd, reads=[b_small[1]], writes=[b_small[1]])
        for j in range(8):
            eng = "dve" if j % 2 == 0 else "pool"
            P.op(eng, "tensor_tensor", conv3[:, j, :n], conv3[:, j, :n], mean, op=ALU.subtract, reads=[b_cv[j], b_small[0]], writes=[b_cv[j]])
            P.op(eng, "tensor_tensor", conv3[:, j, :n], conv3[:, j, :n], rstd, op=ALU.mult, reads=[b_cv[j], b_small[1]], writes=[b_cv[j]])
            P.op("act", "activation", R(oc3[:, 8 + j, :n]), conv3[:, j, :n], AF.Silu, bias=cvec[:, 2, j:j + 1], scale=cvec[:, 1, j:j + 1],
                 reads=[b_cv[j], b_par], writes=[b_oc[8 + j]])
        if not ns:
            for j in range(8):
                P.op("pool", "tensor_copy", gtail[:, j, :], gext[:, j, npr:npr + 30], reads=[b_g[j]], writes=[b_gt], partial=(j > 0))
        if stop_after == "conv":
            break

        P.stage = "attn"
        AT = OFFX2
        kend = c0 + npr
        nkt = (kend + 127) // 128
        ktb = [arena[:, AT + i * NPR: AT + (i + 1) * NPR].bitcast(F32R) for i in range(2)]; b_ktb = b_A1[27:29]
        VB0 = AT + 2 * NPR
        vb = [arena[:, VB0 + i * 17 * 128: VB0 + (i + 1) * 17 * 128].bitcast(F32R).rearrange("p (a b) -> p a b", a=17) for i in range(2)]; b_vb = b_A1[29:31]
        PB0 = VB0 + 2 * 17 * 128
        pex = [arena[:, PB0 + i * W: PB0 + (i + 1) * W] for i in range(3)]; b_pex = b_A1[31:34]
        tmpb = [arenaF[:, i * W: (i + 1) * W] for i in range(3)]; b_tmp = b_A1[34:37]
        mtmp = arenaF[:, 3 * W: 4 * W]; b_mt = b_A1[37]
        ENDAT = PB0 + 3 * W
        assert ENDAT <= ARENA, ENDAT
        SCB = [0, 1, 7]
        for h in range(8):
            s = h % 2
            P.dma("sp", ktb[s][:, 0:kend], R(KTscr[h, :, 0:kend]), reads=[b_ktscr], writes=[b_ktb[s]], sem=b_ktb[s])
            nfull = kend // 128
            if nfull:
                P.dma("sp", vb[s][:, 0:nfull, :], R(Vscr[0:nfull * 128, h * 128:(h + 1) * 128].rearrange("(a p) d -> p a d", p=128)),
                      reads=[b_vscr], writes=[b_vb[s]], sem=b_vb[s])
            if kend % 128:
                P.dma("sp", vb[s][:kend % 128, nfull, :], R(Vscr[nfull * 128:kend, h * 128:(h + 1) * 128]),
                      reads=[b_vscr], writes=[b_vb[s]], sem=b_vb[s], partial=True)
            work = [(m, kt) for m in range(2) for kt in range(nkt)]

            def front(i):
                m, kt = work[i]
                k0 = kt * 128; kn = min(128, kend - k0)
                sc = SCB[i % 3]; tb = i % 3; pe_i = i % 3
                P.op("pe", "matmul", PS[sc][:kn, :npr], lhsT=ktb[s][64 * m:64 * m + 64, k0:k0 + kn],
                     rhs=R(q3[64 * m:64 * m + 64, h, :npr]), start=True, stop=True,
                     reads=[b_ktb[s], b_q[h]], writes=[b_ps[sc]])
                P.op("dve", "scalar_tensor_tensor", tmpb[tb][:kn, :npr], d0tab[:kn, :npr], SLOPES[h], PS[sc][:kn, :npr], op0=ALU.mult, op1=ALU.add,
                     reads=[b_ps[sc], b_sid], writes=[b_tmp[tb]])
                if k0 + kn - 1 > c0:
                    P.op("dve", "tensor_scalar", mtmp[:kn, :npr], d0tab[:kn, :npr], float(k0 - c0), 0.0, op0=ALU.add, op1=ALU.is_gt,
                         reads=[b_sid], writes=[b_mt])
                    P.op("dve", "scalar_tensor_tensor", tmpb[tb][:kn, :npr], mtmp[:kn, :npr], NEG, tmpb[tb][:kn, :npr], op0=ALU.mult, op1=ALU.add,
                         reads=[b_mt, b_tmp[tb]], writes=[b_tmp[tb]])
                P.op("act", "activation", R(pex[pe_i][:kn, :npr]), tmpb[tb][:kn, :npr], AF.Exp, bias=float(SLOPES[h] * (k0 - c0)), scale=1.0,
                     reads=[b_tmp[tb]], writes=[b_pex[pe_i]])

            def back(i):
                m, kt = work[i]
                k0 = kt * 128; kn = min(128, kend - k0)
                pe_i = i % 3
                o_ps, d_ps = PS[2 + m], PS[5 + m]
                b_o, b_d = b_ps[2 + m], b_ps[5 + m]
                P.op("pe", "matmul", o_ps[:, :npr], lhsT=vb[s][:kn, kt, :], rhs=R(pex[pe_i][:kn, :npr]), start=(kt == 0), stop=(kt == nkt - 1),
                     reads=[b_vb[s], b_pex[pe_i]], writes=[b_o], partial=(kt > 0))
                P.op("pe", "matmul", d_ps[:, :npr], lhsT=ones_r[:kn, :], rhs=R(pex[pe_i][:kn, :npr]), start=(kt == 0), stop=(kt == nkt - 1),
                     reads=[b_ones, b_pex[pe_i]], writes=[b_d], partial=(kt > 0))
                if kt == nkt - 1:
                    P.op("dve", "reciprocal", sm(6, npr), d_ps[:, :npr], reads=[b_d], writes=[b_small[6]])
                    P.op("dve", "tensor_tensor", sm(2 + m, npr), o_ps[:, :npr], sm(6, npr), op=ALU.mult, reads=[b_o, b_small[6]], writes=[b_small[2 + m]])

            LA = 2
            for i in range(min(LA, len(work))):
                front(i)
            for i in range(len(work)):
                if i + LA < len(work):
                    front(i + LA)
                back(i)
            if True:
                if True:
                    pass
            P.op("dve", "scalar_tensor_tensor", sm(2, npr), sm(3, npr), neglam, sm(2, npr), op0=ALU.mult, op1=ALU.add,
                 reads=[b_small[2], b_small[3], b_par], writes=[b_small[2]])
            P.op("act", "activation", smr(0, npr), sm(2, npr), AF.Square, reads=[b_small[2]], writes=[b_smr[0]])
            P.op("pe", "matmul", PS[4][:, :npr], lhsT=ones_r[:], rhs=smr(0, npr), start=True, stop=True, reads=[b_smr[0], b_ones], writes=[b_ps[4]])
            P.op("act", "activation", sm(6, npr), PS[4][:, :npr], AF.Sqrt, bias=EPS, scale=1.0 / 128, reads=[b_ps[4]], writes=[b_small[6]])
            P.op("dve", "reciprocal", sm(6, npr), sm(6, npr), reads=[b_small[6]], writes=[b_small[6]])
            P.op("dve", "scalar_tensor_tensor", R(oc3[:, h, :npr]), sm(2, npr), subg2, sm(6, npr), op0=ALU.mult, op1=ALU.mult,
                 reads=[b_small[2], b_small[6], b_par], writes=[b_oc[h]], partial=True)

        if ns:
            P.stage = "sattn"
            sample_attention(q3, b_q, ksamp, b_ks, oc3, b_oc, npr, AT)
        if stop_after == "attn":
            dump(0, oc3[:, :, :], b_oc, ci)
            break
        P.barrier()
        dump(0, oc3[:, :, :], b_oc, ci)

        P.stage = "outproj0"
        m3 = av(A1, 16, W); b_m = b_A1[0:16]

        def evac_m(col0, ps, b_p, m3=m3, b_m=b_m):
            t = col0 // 128
            copy_op(ev_eng(), m3[:, t, :n], ps, reads=[b_p], writes=[b_m[t]])
        linear(w_out_even, 16, blocks_range(0, D), lambda kt: R(oc3[:, kt, :n]), b_oc, n, evac_m)
        post_norm_residual(m3, b_m, 1, 0, n)
        P.barrier()
        dump(1, x_fm[:, :, :], b_x, ci)
        if stop_after == "mix0":
            break

        for layer in range(2):
            if layer == 1:
                P.stage = "l1_inproj"
                pre_norm(hn3, b_hn, 0, 1, n)
                u3e = av(A1, 16, W + 1); b_u = b_A1[0:16]
                u3 = u3e[:, :, 1:1 + W]
                P.op("dve", "tensor_copy", R(u3e[:, :, 0:1]), u_last[:].unsqueeze(2), reads=[b_ul], writes=list(b_u), partial=True)

                def evac_u(col0, ps, b_p):
                    t = col0 // 128
                    copy_op(ev_eng(), R(u3[:, t, :n]), ps, reads=[b_p], writes=[b_u[t]], partial=True)
                linear(w_in_odd, 16, blocks_range(0, D), lambda kt: R(hn3[:, kt, :n]), b_hn, n, evac_u)
                P.barrier()
                y3 = av(A0, 16, W); b_y = b_A0
                P.stage = "ssm"
                ssm(u3, u3e, b_u, y3, b_y, npr, ns)
                P.stage = "l1_glu_out"
                P.barrier()
                dump(3, y3[:, :, :], b_y, ci)
                for ft in range(16):
                    eng = "dve" if ft % 2 == 0 else "pool"
                    sl = ft % 2
                    yv = y3[:, ft, :n]
                    P.op(eng, "tensor_tensor", sm(sl, n), yv, yv, op=ALU.mult, reads=[b_y[ft]], writes=[b_small[sl]])
                    P.op(eng, "tensor_scalar", sm(sl, n), sm(sl, n), 0.044715, 1.0, op0=ALU.mult, op1=ALU.add, reads=[b_small[sl]], writes=[b_small[sl]])
                    P.op(eng, "tensor_tensor", sm(sl, n), sm(sl, n), yv, op=ALU.mult, reads=[b_small[sl], b_y[ft]], writes=[b_small[sl]])
                    P.op("act", "activation", sm(sl, n), sm(sl, n), AF.Sigmoid, scale=1.5957691216057308, reads=[b_small[sl]], writes=[b_small[sl]])
                    P.op(eng, "tensor_tensor", R(yv), yv, sm(sl, n), op=ALU.mult, reads=[b_small[sl], b_y[ft]], writes=[b_y[ft]])
                z3 = av(A1, 16, W); b_z = b_A1[0:16]

                def evac_z(col0, ps, b_p):
                    t = col0 // 128
                    sl = 2 + t % 2
                    P.op("act", "activation", sm(sl, n), ps, AF.Sigmoid, reads=[b_p], writes=[b_small[sl]])
                    P.op("dve", "tensor_tensor", R(z3[:, t, :n]), y3[:, t, :n], sm(sl, n), op=ALU.mult, reads=[b_small[sl], b_y[t]], writes=[b_z[t]])
                linear(w_glu, 16, blocks_range(0, D), lambda kt: R(y3[:, kt, :n]), b_y, n, evac_z)
                P.barrier()
                m3 = av(A0, 16, W); b_m = b_A0

                def evac_m1(col0, ps, b_p):
                    t = col0 // 128
                    copy_op(ev_eng(), m3[:, t, :n], ps, reads=[b_p], writes=[b_m[t]])
                linear(w_out_odd, 16, blocks_range(0, D), lambda kt: R(z3[:, kt, :n]), b_z, n, evac_m1)
                post_norm_residual(m3, b_m, 1, 1, n)
                P.barrier()
                dump(4, x_fm[:, :, :], b_x, ci)
                if stop_after == "mix1":
                    break
            P.stage = "ffn"
            pre_norm(hn3, b_hn, 2, layer, n)
            act3 = av(A1, 11, W); b_act = b_A1[0:11]
            yacc = av(A1 + 11 * W, 16, W); b_ya = b_A1[11:27]
            for qd in range(4):
                lo = 1408 * qd

                def evac_gate(col0, ps, b_p, lo=lo):
                    jj = (col0 - lo) // 128
                    P.op("act", "activation", act3[:, jj, :n], ps, AF.Silu, reads=[b_p], writes=[b_act[jj]])

                def evac_up(col0, ps, b_p, lo=lo):
                    jj = (col0 - lo) // 128
                    P.op("dve", "tensor_tensor", R(act3[:, jj, :n]), act3[:, jj, :n], ps, op=ALU.mult, reads=[b_p, b_act[jj]], writes=[b_act[jj]])
                linear(w_gate[layer], 16, blocks_range(lo, lo + 1408), lambda kt: R(hn3[:, kt, :n]), b_hn, n, evac_gate)
                linear(w_up[layer], 16, blocks_range(lo, lo + 1408), lambda kt: R(hn3[:, kt, :n]), b_hn, n, evac_up)

                def evac_down(col0, ps, b_p, qd=qd):
                    t = col0 // 128
                    if qd == 0:
                        copy_op(ev_eng(), yacc[:, t, :n], ps, reads=[b_p], writes=[b_ya[t]])
                    else:
                        P.op("dve", "tensor_tensor", yacc[:, t, :n], yacc[:, t, :n], ps, op=ALU.add, reads=[b_p, b_ya[t]], writes=[b_ya[t]])
                linear(w_down[layer][lo:lo + 1408, :], 11, blocks_range(0, D), lambda kt: R(act3[:, kt, :n]), b_act, n, evac_down)
            post_norm_residual(yacc, b_ya, 3, layer, n)
            P.barrier()
            dump(2 if layer == 0 else 5, x_fm[:, :, :], b_x, ci)
            if stop_after == f"ffn{layer}":
                break
        if stop_after is not None and stop_after != "chunk0":
            break

        P.stage = "ystore"
        yst = [arenaF[:, 0:2048], arenaF[:, 0:2048]]
        b_yst = [b_AF[0], b_AF[0]]
        for ti, (t0, nt) in enumerate(ttiles):
            s = ti % 2
            for fg in range(4):
                pb = 2 + (fg % 2)
                for k in range(4):
                    ft = fg * 4 + k
                    P.op("pe", "transpose", PS[pb][:nt, k * 128:(k + 1) * 128], x_fm[:, ft, t0:t0 + nt], ident[:, :],
                         reads=[b_x[ft], b_ident], writes=[b_ps[pb]], partial=(k > 0))
                copy_op(ev_eng(), yst[s][:nt, fg * 512:(fg + 1) * 512], PS[pb][:nt, :], reads=[b_ps[pb]], writes=[b_yst[s]], partial=(fg > 0))
            P.dma("pool", yout[c0 + t0:c0 + t0 + nt, :], yst[s][:nt, :], reads=[b_yst[s]], writes=[b_out], sem=b_yst[s], partial=True)
        P.barrier()
        if stop_after == "chunk0":
            break

    if stop_after is None:
        P.dma("pool", ssmp_o, S_p[:, 1, :], reads=[b_Sp], writes=[b_out], sem=b_Sp, partial=True)
        P.dma("pool", ssms_o, S_s[:], reads=[b_Ss], writes=[b_out], sem=b_Ss, partial=True)
    P.barrier()
    P.emit()
    return nc


_PROG_CACHE = {}


def _host_inputs(inp):
    f32 = np.float32
    c = lambda a: np.ascontiguousarray(a, dtype=a.dtype)
    shared = {}
    shared["w_in_even"] = c(inp["w_in_even"][0]); shared["w_out_even"] = c(inp["w_out_even"][0])
    shared["w_in_odd"] = c(inp["w_in_odd"][0]); shared["w_glu"] = c(inp["w_glu"][0]); shared["w_out_odd"] = c(inp["w_out_odd"][0])
    shared["w_gate"] = c(inp["w_ffn_gate"]); shared["w_up"] = c(inp["w_ffn_up"]); shared["w_down"] = c(inp["w_ffn_down"])
    g = np.zeros((8, 16, 128), f32)
    for kind, nm in enumerate(["norm_mix_pre", "norm_mix_post", "norm_ffn_pre", "norm_ffn_post"]):
        for layer in range(2):
            g[kind * 2 + layer] = np.asarray(inp[nm][layer]).reshape(16, 128)
    shared["gains"] = c(g.transpose(2, 0, 1))
    shared["cw"] = c(np.asarray(inp["conv_w"][0]).reshape(31, 8, 128).transpose(2, 1, 0))
    cv = np.stack([np.asarray(inp[k][0]).reshape(8, 128) for k in ("conv_b", "conv_ln_g", "conv_ln_b")])
    shared["cvec"] = c(cv.transpose(2, 0, 1))
    shared["subg"] = c(np.asarray(inp["subln_g"][0]).reshape(128, 1))
    shared["lqk"] = c(np.concatenate([np.asarray(inp["lambda_q"][0]).ravel(), np.asarray(inp["lambda_k"][0]).ravel()]).reshape(1, 256))
    shared["a_re"] = c(inp["ssm_a_re"][0]); shared["a_im"] = c(inp["ssm_a_im"][0])
    shared["ldt"] = c(np.asarray(inp["ssm_log_dt"][0]).reshape(128, 1))
    shared["b_re"] = c(inp["ssm_b_re"][0]); shared["b_im"] = c(inp["ssm_b_im"][0])
    cre = np.asarray(inp["ssm_c_re"][0]).transpose(2, 0, 1); cim = np.asarray(inp["ssm_c_im"][0]).transpose(2, 0, 1)
    shared["ccat"] = c(np.concatenate([cre, cim], axis=0))
    shared["dfm"] = c(np.asarray(inp["ssm_d"][0]).reshape(16, 128).T)
    shared["cache_k"] = c(np.asarray(inp["cache_k"][0]).reshape(1280 * 128, 1024))
    shared["cache_v"] = c(np.asarray(inp["cache_v"][0]).reshape(1280 * 128, 1024))
    shared["iota"] = np.arange(128, dtype=np.int32).reshape(128, 1)
    shared["ident"] = np.eye(128, dtype=f32)
    shared["ones"] = np.ones((128, 128), f32)
    shared["d0tab"] = (np.arange(128, dtype=f32)[:, None] - np.arange(W, dtype=f32)[None, :]).astype(f32)
    stab = np.zeros((64, 136), f32); slopecol = np.zeros((64, 1), f32)
    for h in range(8):
        for m in range(2):
            for q in range(4):
                r = h * 8 + m * 4 + q
                stab[r, 0:4] = np.arange(4)
                stab[r, 4:8] = np.where(np.arange(4) > q, NEG, 0.0)
                stab[r, 8:136] = np.arange(128) - PAST
                slopecol[r, 0] = SLOPES[h]
    shared["stab"] = stab; shared["slopecol"] = slopecol
    maps = []
    meta = np.asarray(inp["meta_tokens"], f32)
    for i in range(8):
        b = i % 4
        d = dict(shared)
        d["xin"] = c(np.concatenate([meta, np.asarray(inp["x_prompt"][b]), np.asarray(inp["x_sample"][i])], axis=0))
        d["ptab"] = c(np.asarray(inp["page_table"][i], dtype=np.int32).reshape(1, 128))
        d["sconv"] = c(inp["state_conv"][0, i])
        d["sssm"] = c(np.concatenate([np.asarray(inp["state_ssm_re"][0, i]), np.asarray(inp["state_ssm_im"][0, i])], axis=1))
        maps.append(d)
    return maps


def _assemble(res):
    f32 = np.float32
    R_ = [r for r in res]
    y_prompt = np.stack([R_[b]["yout"][16:NPR] for b in range(4)]).astype(f32)
    y_sample = np.stack([R_[i]["yout"][NPR:NT] for i in range(8)]).astype(f32)
    k_prompt = np.stack([R_[b]["kout"][0:NPR].reshape(NPR, 8, 2, 64) for b in range(4)])[None].astype(f32)
    v_prompt = np.stack([R_[b]["vout"][0:NPR].reshape(NPR, 8, 128) for b in range(4)])[None].astype(f32)
    k_sample = np.stack([R_[i]["kout"][NPR:NT].reshape(4, 8, 2, 64) for i in range(8)])[None].astype(f32)
    v_sample = np.stack([R_[i]["vout"][NPR:NT].reshape(4, 8, 128) for i in range(8)])[None].astype(f32)
    conv_prompt = np.stack([R_[b]["convp"] for b in range(4)])[None].astype(f32)
    conv_sample = np.stack([R_[i]["convs"] for i in range(8)])[None].astype(f32)
    srp = np.stack([R_[b]["ssmp"][:, 0:64] for b in range(4)])[None].astype(f32)
    sip = np.stack([R_[b]["ssmp"][:, 64:128] for b in range(4)])[None].astype(f32)
    srs = np.stack([R_[i]["ssms"][:, 0:64] for i in range(8)])[None].astype(f32)
    sis = np.stack([R_[i]["ssms"][:, 64:128] for i in range(8)])[None].astype(f32)
    return (y_prompt, y_sample, k_prompt, v_prompt, k_sample, v_sample, conv_prompt, conv_sample, srp, sip, srs, sis)


def kernel(_stop_after=None, **inputs):
    maps = _host_inputs(inputs)
    nc = build_program(_stop_after)
    res = run_bass_kernel_spmd(nc, maps, core_ids=list(range(8)))
    return _assemble(res.results)
```

```python
import numpy as np
from contextlib import ExitStack
import concourse.bass as bass
import concourse.mybir as mybir

F32 = mybir.dt.float32
F32R = mybir.dt.float32r
I32 = mybir.dt.int32
AF = mybir.ActivationFunctionType
ALU = mybir.AluOpType
AX = mybir.AxisListType

PH = 4096


class Buf:
    __slots__ = ("name", "writers", "readers", "dcount", "semi", "_sw")

    def __init__(self, name):
        self.name = name
        self.writers = []
        self.readers = []
        self.dcount = 0
        self.semi = None
        self._sw = None

    def sw(self):
        if self._sw is None:
            self._sw = Buf(self.name + "_sw")
        return self._sw


class Prog:
    ENG = ("pe", "act", "dve", "pool", "sp")

    def __init__(self, nc):
        self.nc = nc
        self.ops = {e: [] for e in self.ENG}
        self.cnt = {e: 0 for e in self.ENG}
        self.waited = {e: {} for e in self.ENG}
        self.dbufs = []
        self.stack = ExitStack()
        self.retype = None
        self.NR = 20
        self.ring = {q: [Buf(f"ring_{q}{i}") for i in range(self.NR)] for q in ("sp", "pool", "act")}
        self.ringpos = {q: 0 for q in ("sp", "pool", "act")}
        self.stage = "init"
        self.labels = {e: [] for e in self.ENG}

    def sb(self, name, shape, dtype=F32):
        return self.stack.enter_context(self.nc.sbuf_tensor("sb_" + name, list(shape), dtype))

    def ps(self, name, shape, dtype=F32):
        return self.stack.enter_context(self.nc.psum_tensor(name, list(shape), dtype))

    def _need(self, eng, toks):
        best = {}
        cur = self.cnt[eng] + 1
        for t in toks:
            if t[0] == "E":
                _, e2, idx = t
                if e2 == eng and eng in ("pe", "sp"):
                    continue
                if e2 == eng and eng == "dve" and cur - idx >= 2:
                    continue
                key = ("E", e2, (idx - 1) // PH)
                val = (idx - 1) % PH + 1
            else:
                _, b, val = t
                key = ("D", id(b))
                if b.semi is None:
                    b.semi = len(self.dbufs)
                    self.dbufs.append(b)
            if best.get(key, (0,))[0] < val:
                best[key] = (val, t)
        out = []
        w = self.waited[eng]
        for key, (val, t) in best.items():
            if w.get(key, 0) >= val:
                continue
            w[key] = val
            out.append((key, val, t))
        return out

    def _deps(self, reads, writes):
        toks = []
        for b in reads:
            toks += b.writers
        for b in writes:
            toks += b.writers
            toks += b.readers
        return toks

    @staticmethod
    def _dedupe(lst):
        best = {}
        for t in lst:
            if t[0] == "E":
                key = ("E", t[1])
                v = t[2]
            else:
                key = ("D", id(t[1]))
                v = t[2]
            if key not in best or best[key][2] < v:
                best[key] = t
        return list(best.values())

    def _commit(self, tok, reads, writes, partial):
        for b in reads:
            b.readers = self._dedupe(b.readers + [tok])
        for b in writes:
            if partial:
                b.writers = self._dedupe(b.writers + [tok])
            else:
                b.writers = [tok]
                b.readers = []

    def op(self, eng, meth, *args, reads=(), writes=(), partial=False, **kw):
        if self.retype is not None and args:
            args = (self.retype(args[0]),) + tuple(args[1:])
        fn = lambda e, meth=meth, args=args, kw=kw: getattr(e, meth)(*args, **kw)
        waits = self._need(eng, self._deps(reads, writes))
        self.cnt[eng] += 1
        idx = self.cnt[eng]
        tok = ("E", eng, idx)
        self.ops[eng].append((waits, fn, ("E", eng, idx)))
        self.labels[eng].append(self.stage)
        self._commit(tok, reads, writes, partial)
        return tok

    def _ring_sem(self, q):
        rb = self.ring[q][self.ringpos[q] % self.NR]
        self.ringpos[q] += 1
        pre = [("D", rb, 16 * rb.dcount)] if rb.dcount > 0 else []
        return rb, pre

    def dma(self, q, out, in_, reads=(), writes=(), sem=None, partial=False, **kw):
        rb, pre = self._ring_sem(q)
        if self.retype is not None:
            out = self.retype(out)
            if out.dtype == F32R and in_.dtype == F32:
                in_ = in_.bitcast(F32R)
        waits = self._need(q, self._deps(reads, writes) + pre)
        rb.dcount += 1
        tok = ("D", rb, 16 * rb.dcount)
        fn = lambda e, out=out, in_=in_, kw=kw: e.dma_start(out=out, in_=in_, **kw)
        self.ops[q].append((waits, fn, ("D", rb)))
        self._commit(tok, reads, writes, partial)
        return tok

    def dma_custom(self, q, meth, reads=(), writes=(), sem=None, partial=False, **kw):
        fn = lambda e, meth=meth, kw=kw: getattr(e, meth)(**kw)
        rb, pre = self._ring_sem(q)
        waits = self._need(q, self._deps(reads, writes) + pre)
        rb.dcount += 1
        tok = ("D", rb, 16 * rb.dcount)
        self.ops[q].append((waits, fn, ("D", rb)))
        self._commit(tok, reads, writes, partial)
        return tok

    def barrier(self):
        toks = [("E", e, self.cnt[e]) for e in self.ENG if self.cnt[e] > 0]
        toks += [("D", b, 16 * b.dcount) for q in self.ring for b in self.ring[q] if b.dcount > 0]
        for e in self.ENG:
            waits = self._need(e, toks)
            if waits:
                self.ops[e].append((waits, None, None))

    def emit(self):
        nc = self.nc
        import bisect
        mset = {e: set() for e in self.ENG}
        for e in self.ENG:
            for waits, fn, inc in self.ops[e]:
                for key, val, tok in waits:
                    if tok[0] == "E":
                        mset[tok[1]].add(tok[2])
        mlist = {e: sorted(mset[e]) for e in self.ENG}
        nsem = {e: (len(mlist[e]) + PH - 1) // PH for e in self.ENG}
        esem = {e: [self.stack.enter_context(nc.semaphore(f"s_{e}{i}")) for i in range(nsem[e])]
                for e in self.ENG}
        dsem = [self.stack.enter_context(nc.semaphore(f"d_{i}")) for i in range(len(self.dbufs))]
        idmap = {id(b): i for i, b in enumerate(self.dbufs)}

        def rank(e, idx):
            r = bisect.bisect_left(mlist[e], idx)
            assert mlist[e][r] == idx
            return r

        def run(engname, eng):
            for waits, fn, inc in self.ops[engname]:
                for key, val, tok in waits:
                    if tok[0] == "E":
                        r = rank(tok[1], tok[2])
                        eng.wait_ge(esem[tok[1]][r // PH], r % PH + 1)
                    else:
                        eng.wait_ge(dsem[idmap[id(tok[1])]], val)
                if fn is None:
                    continue
                ins = fn(eng)
                if inc[0] == "E":
                    idx = inc[2]
                    if idx in mset[engname]:
                        r = rank(engname, idx)
                        ins.then_inc(esem[engname][r // PH], 1)
                else:
                    ins.then_inc(dsem[idmap[id(inc[1])]], 16)

        with nc.Block() as block:
            @block.tensor
            def _(e):
                run("pe", e)

            @block.scalar
            def _(e):
                run("act", e)

            @block.vector
            def _(e):
                run("dve", e)

            @block.gpsimd
            def _(e):
                run("pool", e)

            @block.sync
            def _(e):
                run("sp", e)
        self.stack.close()

import math
from concourse.bass_utils import run_bass_kernel_spmd

D = 2048
NT = 2068
NPR = 2064
CHUNKS = [(0, 416), (416, 832), (832, 1248), (1248, 1664), (1664, 2068)]
W = 416
DFF = 5632
EPS = 1e-6
LAM_INIT = 0.8 - 0.6 * math.exp(-0.3 * 0)
SLOPES = [2.0 ** (-8.0 * (i + 1) / 8) for i in range(8)]
PAST = 16384
NEG = -30000.0
STOP_AFTER = None


def build_program(stop_after=None):
    nc = bass.Bass("TRN2", target_bir_lowering=False)
    nc.dge_precook = False
    P = Prog(nc)

    def din(name, shape, dt=F32):
        return nc.dram_tensor(name, list(shape), dt, kind="ExternalInput").ap()

    def dout(name, shape, dt=F32):
        return nc.dram_tensor(name, list(shape), dt, kind="ExternalOutput").ap()

    def dscr(name, shape, dt=F32):
        return nc.dram_tensor(name, list(shape), dt).ap()

    xin = din("xin", [NT, D])
    w_in_even = din("w_in_even", [D, 5120]); w_out_even = din("w_out_even", [D, D])
    w_in_odd = din("w_in_odd", [D, D]); w_glu = din("w_glu", [D, D]); w_out_odd = din("w_out_odd", [D, D])
    w_gate = din("w_gate", [2, D, DFF]); w_up = din("w_up", [2, D, DFF]); w_down = din("w_down", [2, DFF, D])
    gains_d = din("gains", [128, 8, 16])
    cw_d = din("cw", [128, 8, 31]); cvec_d = din("cvec", [128, 3, 8])
    subg_d = din("subg", [128, 1]); lqk_d = din("lqk", [1, 256])
    a_re_d = din("a_re", [128, 64]); a_im_d = din("a_im", [128, 64]); ldt_d = din("ldt", [128, 1])
    b_re_d = din("b_re", [128, 64, 16]); b_im_d = din("b_im", [128, 64, 16])
    ccat_d = din("ccat", [128, 128, 16]); dfm_d = din("dfm", [128, 16])
    ck_d = din("cache_k", [1280 * 128, 1024]); cv_d = din("cache_v", [1280 * 128, 1024])
    pt_d = din("ptab", [1, 128], I32); iota_d = din("iota", [128, 1], I32)
    sconv_d = din("sconv", [30, 1024]); sssm_d = din("sssm", [128, 128])
    ident_d = din("ident", [128, 128]); ones_d = din("ones", [128, 128]); d0_d = din("d0tab", [128, W])
    stab_d = din("stab", [64, 8 + 128])
    slopecol_d = din("slopecol", [64, 1])

    yout = dout("yout", [NT, D]); kout = dout("kout", [NT, 1024]); vout = dout("vout", [NT, 1024])
    convp_o = dout("convp", [30, 1024]); convs_o = dout("convs", [30, 1024])
    ssmp_o = dout("ssmp", [128, 128]); ssms_o = dout("ssms", [128, 128])

    dbg_o = dout("dbg", [6, 128, 16 * W]) if stop_after is not None else None
    b_dbg = Buf("dbg")

    def dump(slot, ap3, bufs, ci):
        if ci != 0 or dbg_o is None:
            return
        P.barrier()
        P.dma("pool", dbg_o[slot].rearrange("p (a b) -> p a b", a=16), ap3, reads=list(bufs), writes=[b_dbg], sem=b_dbg, partial=True)
        P.barrier()

    KTscr = dscr("KTscr", [8, 128, NPR]); Vscr = dscr("Vscr", [NT, 1024])
    BDscr = dscr("BDscr", [16, 128, 1024])
    BuScr = dscr("BuScr", [W, 128, 128]); HScr = dscr("HScr", [W, 128, 128])

    ident = P.sb("ident", [128, 128]); b_ident = Buf("ident")
    ones_r = P.sb("ones_r", [128, 128], F32R); b_ones = Buf("ones")
    gains = P.sb("gains", [128, 8, 16]); cw = P.sb("cw", [128, 8, 31]); cvec = P.sb("cvec", [128, 3, 8])
    subg = P.sb("subg", [128, 1]); lamt = P.sb("lamt", [128, 8]); b_par = Buf("params")
    stab = P.sb("stab", [64, 136]); slopecol = P.sb("slopecol", [64, 1])
    dfm = P.sb("dfm", [128, 16])
    zeros = P.sb("zeros", [128, 128]); b_zero = Buf("zeros")
    b_sid = Buf("d0r")
    d0r = P.sb("d0r", [128, W], F32R)
    gtail = P.sb("gtail", [128, 8, 30]); b_gt = Buf("gtail")
    ccat = P.sb("ccat", [128, 128, 16], F32R); b_ccat = Buf("ccat")
    XY = P.sb("XY", [128, 256]); b_xy = Buf("xy")
    S_p = P.sb("S_p", [128, 2, 128]); S_s = P.sb("S_s", [128, 128]); b_Sp = Buf("Sp"); b_Ss = Buf("Ss")
    XY2 = P.sb("XY2", [128, 2, 256]); b_xy2 = Buf("xy2")
    u_last = P.sb("u_last", [128, 16]); b_ul = Buf("ulast")
    BDscr2 = dscr("BDscr2", [16, 128, 1024])
    x_fm = P.sb("x_fm", [128, 16, W]); b_x = [Buf(f"x{i}") for i in range(16)]
    NWB = 3
    wbuf = [P.sb(f"wbuf{i}", [128, 16, 128], F32R) for i in range(NWB)]; b_w = [Buf(f"w{i}") for i in range(NWB)]
    ARENA = 16 * W + 16928
    arenaR = P.sb("arena", [128, ARENA], F32R)
    arena = arenaR[:].bitcast(F32)
    AFSZ = 7680
    arenaF = P.sb("arenaF", [128, AFSZ])
    small = P.sb("small", [128, 7 * W]); b_small = [Buf(f"sm{i}") for i in range(7)]
    smallr = P.sb("smallr", [128, 2 * W], F32R); b_smr = [Buf("smr0"), Buf("smr1")]
    for i in range(2):
        wbuf.append(arenaR[:, ARENA - (2 - i) * 2048: ARENA - (1 - i) * 2048].rearrange("p (a b) -> p a b", a=16))
        b_w.append(Buf(f"w{NWB + i}"))
    NWB += 2
    ptb_t = P.sb("ptb", [128, 128], I32); idx_t = P.sb("idx", [128, 128], I32); iot_t = P.sb("iot", [128, 1], I32)
    PS = [P.ps(f"ps{i}", [128, 512]) for i in range(8)]; b_ps = [Buf(f"ps{i}") for i in range(8)]

    def sm(i, n=W):
        return small[:, i * W:i * W + n]

    def smr(i, n=W):
        return smallr[:, i * W:i * W + n]

    def av(off, a, b):
        return arena[:, off:off + a * b].rearrange("p (a b) -> p a b", a=a)

    R = lambda ap: ap.bitcast(F32R)

    def _rt(ap):
        try:
            if ap.name == "sb_arena" and ap.dtype == F32:
                return ap.bitcast(F32R)
        except Exception:
            pass
        return ap
    P.retype = _rt
    b_AF = [Buf(f"AF{i}") for i in range(8)]

    state = {"wi": 0, "mm": 0, "evac": 0}

    def ev_eng():
        state["evac"] += 1
        return "act" if state["evac"] % 2 else "dve"

    def copy_op(eng, out, in_, reads, writes, partial=False):
        if eng == "act":
            return P.op("act", "copy", out, in_, reads=reads, writes=writes, partial=partial)
        return P.op(eng, "tensor_copy", out, in_, reads=reads, writes=writes, partial=partial)

    P.dma("sp", ident[:], ident_d, writes=[b_ident], sem=b_ident)
    P.dma("sp", ones_r[:], R(ones_d), writes=[b_ones], sem=b_ones)
    P.op("dve", "memset", zeros[:], 0.0, writes=[b_zero])
    P.dma("sp", d0r[:], R(d0_d), writes=[b_sid], sem=b_sid)
    d0tab = d0r[:].bitcast(F32)
    for dst, src in [(gains, gains_d), (cw, cw_d), (cvec, cvec_d), (subg, subg_d), (stab, stab_d),
                     (slopecol, slopecol_d), (dfm, dfm_d)]:
        P.dma("sp", dst[:], src, writes=[b_par], sem=b_par, partial=True)
    lq = sm(0, 256)
    P.dma("pool", lq, lqk_d.partition_broadcast(128), writes=[b_small[0]], sem=b_small[0])
    P.op("dve", "tensor_tensor", sm(1, 128), lq[:, 0:128], lq[:, 128:256], op=ALU.mult, reads=[b_small[0]], writes=[b_small[1]])
    P.op("dve", "tensor_reduce", lamt[:, 0:2], sm(1, 128).rearrange("p (a b) -> p a b", a=2), axis=AX.X, op=ALU.add,
         reads=[b_small[1]], writes=[b_par], partial=True)
    P.op("act", "activation", lamt[:, 2:4], lamt[:, 0:2], AF.Exp, reads=[b_par], writes=[b_par], partial=True)
    P.op("dve", "scalar_tensor_tensor", lamt[:, 4:5], lamt[:, 3:4], -LAM_INIT, lamt[:, 2:3], op0=ALU.add, op1=ALU.subtract,
         reads=[b_par], writes=[b_par], partial=True)
    P.op("dve", "tensor_scalar", lamt[:, 5:6], subg[:], 1.0 - LAM_INIT, None, op0=ALU.mult, reads=[b_par], writes=[b_par], partial=True)
    neglam = lamt[:, 4:5]; subg2 = lamt[:, 5:6]

    def gain(kind, layer, ft):
        return gains[:, kind * 2 + layer, ft:ft + 1]

    def ssm_prep():
        o = 0
        def T(n):
            nonlocal o
            v = arenaF[:, o:o + n]; o += n
            return v
        b_t = Buf("ssmprep")
        lr, li, dt_, zr, zi, er, cs, sn, mk, t1, t2, cr, ci, den = [T(64) for _ in range(14)]
        bre = T(1024); bim = T(1024); bbT = T(2048); zero = T(1024)
        dtc = T(1)
        rw = dict(reads=[b_t], writes=[b_t], partial=True)
        P.dma("sp", lr, a_re_d, writes=[b_t], sem=b_t, partial=True)
        P.dma("sp", li, a_im_d, writes=[b_t], sem=b_t, partial=True)
        P.dma("sp", dtc, ldt_d, writes=[b_t], sem=b_t, partial=True)
        P.dma("sp", bre, b_re_d.rearrange("g p c -> g (p c)"), writes=[b_t], sem=b_t, partial=True)
        P.dma("sp", bim, b_im_d.rearrange("g p c -> g (p c)"), writes=[b_t], sem=b_t, partial=True)
        P.dma("sp", ccat[:], R(ccat_d), writes=[b_ccat], sem=b_ccat)
        P.op("act", "activation", ccat[64:128], ccat[64:128].bitcast(F32), AF.Copy, scale=-1.0, reads=[b_ccat], writes=[b_ccat])
        P.op("act", "activation", dtc, dtc, AF.Exp, **rw)
        P.op("dve", "tensor_scalar", zr, lr, dtc, None, op0=ALU.mult, **rw)
        P.op("dve", "tensor_scalar", zi, li, dtc, None, op0=ALU.mult, **rw)
        P.op("act", "activation", er, zr, AF.Exp, **rw)
        P.op("dve", "tensor_copy", sn, zi, **rw)
        P.op("dve", "tensor_scalar", cs, zi, math.pi / 2, None, op0=ALU.add, **rw)
        for tgt in (sn, cs):
            for _ in range(7):
                P.op("dve", "tensor_scalar", mk, tgt, math.pi, None, op0=ALU.is_gt, **rw)
                P.op("dve", "scalar_tensor_tensor", tgt, mk, -2 * math.pi, tgt, op0=ALU.mult, op1=ALU.add, **rw)
            P.op("act", "activation", tgt, tgt, AF.Sin, **rw)
        P.op("dve", "tensor_tensor", XY[:, 0:64], er, cs, op=ALU.mult, reads=[b_t], writes=[b_xy], partial=True)
        P.op("dve", "tensor_copy", XY[:, 64:128], XY[:, 0:64], reads=[b_xy], writes=[b_xy], partial=True)
        P.op("dve", "tensor_tensor", XY[:, 192:256], er, sn, op=ALU.mult, reads=[b_t], writes=[b_xy], partial=True)
        P.op("dve", "tensor_scalar", XY[:, 128:192], XY[:, 192:256], -1.0, None, op0=ALU.mult, reads=[b_xy], writes=[b_xy], partial=True)
        ar = XY[:, 0:64]; ai = XY[:, 192:256]
        rx = dict(reads=[b_t, b_xy], writes=[b_t], partial=True)
        P.op("dve", "tensor_scalar", t1, ar, -1.0, None, op0=ALU.add, **rx)
        P.op("dve", "tensor_tensor", den, lr, lr, op=ALU.mult, **rw)
        P.op("dve", "tensor_tensor", t2, li, li, op=ALU.mult, **rw)
        P.op("dve", "tensor_tensor", den, den, t2, op=ALU.add, **rw)
        P.op("dve", "reciprocal", den, den, **rw)
        P.op("dve", "tensor_tensor", cr, t1, lr, op=ALU.mult, **rw)
        P.op("dve", "tensor_tensor", t2, ai, li, op=ALU.mult, **rx)
        P.op("dve", "tensor_tensor", cr, cr, t2, op=ALU.add, **rw)
        P.op("dve", "tensor_tensor", cr, cr, den, op=ALU.mult, **rw)
        P.op("dve", "tensor_tensor", ci, ai, lr, op=ALU.mult, **rx)
        P.op("dve", "tensor_tensor", t2, t1, li, op=ALU.mult, **rw)
        P.op("dve", "tensor_tensor", ci, ci, t2, op=ALU.subtract, **rw)
        P.op("dve", "tensor_tensor", ci, ci, den, op=ALU.mult, **rw)
        bre3 = bre.rearrange("g (p c) -> g p c", c=16); bim3 = bim.rearrange("g (p c) -> g p c", c=16)
        bb3 = bbT.rearrange("g (c s) -> g c s", c=16)
        for c in range(16):
            br = bre3[:, :, c]; bi = bim3[:, :, c]
            o_re = bb3[:, c, 0:64]; o_im = bb3[:, c, 64:128]
            P.op("dve", "tensor_tensor", o_re, cr, br, op=ALU.mult, **rw)
            P.op("dve", "tensor_tensor", t2, ci, bi, op=ALU.mult, **rw)
            P.op("dve", "tensor_tensor", o_re, o_re, t2, op=ALU.subtract, **rw)
            P.op("dve", "tensor_tensor", o_im, cr, bi, op=ALU.mult, **rw)
            P.op("dve", "tensor_tensor", t2, ci, br, op=ALU.mult, **rw)
            P.op("dve", "tensor_tensor", o_im, o_im, t2, op=ALU.add, **rw)
        for tt in range(2):
            P.op("dve", "tensor_tensor", XY2[:, tt, 0:64], ar, ar, op=ALU.mult, reads=[b_xy], writes=[b_xy2], partial=True)
            P.op("dve", "tensor_tensor", t2, ai, ai, op=ALU.mult, **rx)
            P.op("dve", "tensor_tensor", XY2[:, tt, 0:64], XY2[:, tt, 0:64], t2, op=ALU.subtract, reads=[b_xy2, b_t], writes=[b_xy2], partial=True)
            P.op("dve", "tensor_copy", XY2[:, tt, 64:128], XY2[:, tt, 0:64], reads=[b_xy2], writes=[b_xy2], partial=True)
            P.op("dve", "tensor_tensor", t2, ar, ai, op=ALU.mult, **rx)
            P.op("dve", "tensor_scalar", XY2[:, tt, 192:256], t2, 2.0, None, op0=ALU.mult, reads=[b_t], writes=[b_xy2], partial=True)
            P.op("dve", "tensor_scalar", XY2[:, tt, 128:192], t2, -2.0, None, op0=ALU.mult, reads=[b_t], writes=[b_xy2], partial=True)
        bbA3 = arenaF[:, 14 * 64: 14 * 64 + 2048].rearrange("g (c s) -> g c s", c=16)
        for c in range(16):
            s_re = bb3[:, c, 0:64]; s_im = bb3[:, c, 64:128]
            d_re = bbA3[:, c, 0:64]; d_im = bbA3[:, c, 64:128]
            P.op("dve", "tensor_tensor", d_re, ar, s_re, op=ALU.mult, **rx)
            P.op("dve", "tensor_tensor", t2, ai, s_im, op=ALU.mult, **rx)
            P.op("dve", "tensor_tensor", d_re, d_re, t2, op=ALU.subtract, **rw)
            P.op("dve", "tensor_tensor", d_im, ar, s_im, op=ALU.mult, **rx)
            P.op("dve", "tensor_tensor", t2, ai, s_re, op=ALU.mult, **rx)
            P.op("dve", "tensor_tensor", d_im, d_im, t2, op=ALU.add, **rw)
        P.op("dve", "memset", zero, 0.0, **rw)
        b_bd = Buf("bdscr")
        for f in range(16):
            P.dma("pool", BDscr[f], zero, reads=[b_t], writes=[b_bd], sem=b_t, partial=True)
            P.dma("sp", BDscr2[f], zero, reads=[b_t], writes=[b_bd], sem=b_t, partial=True)
        P.barrier()
        for g in range(128):
            f, gl = divmod(g, 8)
            P.dma("pool", BDscr[f, gl * 16:(gl + 1) * 16, gl * 128:(gl + 1) * 128].unsqueeze(0),
                  bb3[g:g + 1, :, :], reads=[b_t], writes=[b_bd], sem=b_t, partial=True)
            P.dma("sp", BDscr2[f, gl * 16:(gl + 1) * 16, gl * 128:(gl + 1) * 128].unsqueeze(0),
                  bbA3[g:g + 1, :, :], reads=[b_t], writes=[b_bd], sem=b_t, partial=True)
        P.op("dve", "memset", S_p[:], 0.0, writes=[b_Sp])
        P.op("dve", "memset", u_last[:], 0.0, writes=[b_ul])
        P.dma("sp", S_s[:], sssm_d, writes=[b_Ss], sem=b_Ss)
        P.barrier()
        return b_bd

    b_bd = ssm_prep()
    b_ktscr = Buf("ktscr"); b_vscr = Buf("vscr"); b_buscr = Buf("buscr"); b_hscr = Buf("hscr")
    b_out = Buf("outs")

    def rmsnorm_stats(src_fn, nt_tiles, n, denom, srcbufs):
        ssb = b_ps[4]
        for i in range(nt_tiles):
            sl = i % 2
            P.op("act", "activation", smr(sl, n), src_fn(i), AF.Square, reads=[srcbufs[i]], writes=[b_smr[sl]])
            P.op("pe", "matmul", PS[4][:, :n], lhsT=ones_r[:], rhs=smr(sl, n), start=(i == 0), stop=(i == nt_tiles - 1),
                 reads=[b_smr[sl], b_ones], writes=[ssb], partial=(i > 0))
        P.op("act", "activation", sm(6, n), PS[4][:, :n], AF.Sqrt, bias=EPS, scale=1.0 / denom, reads=[ssb], writes=[b_small[6]])
        P.op("dve", "reciprocal", sm(6, n), sm(6, n), reads=[b_small[6]], writes=[b_small[6]])
        return sm(6, n), b_small[6]

    def linear(Wap, KT, blocks, rhs_fn, rhs_bufs, n, evac):
        W3 = Wap.rearrange("(kt p) c -> p kt c", p=128)
        for c0 in blocks:
            wi = state["wi"] % NWB; state["wi"] += 1
            wb = wbuf[wi]
            P.dma("sp", wb[:, 0:KT, :], R(W3[:, :, c0:c0 + 128]), writes=[b_w[wi]], sem=b_w[wi])
            pi = state["mm"] % 2; state["mm"] += 1
            for kt in range(KT):
                P.op("pe", "matmul", PS[pi][:, :n], lhsT=wb[:, kt, :], rhs=rhs_fn(kt), start=(kt == 0), stop=(kt == KT - 1),
                     reads=[b_w[wi], rhs_bufs[kt]], writes=[b_ps[pi]], partial=(kt > 0))
            evac(c0, PS[pi][:, :n], b_ps[pi])

    def blocks_range(c_lo, c_hi):
        return list(range(c_lo, c_hi, 128))

    def post_norm_residual(m3, b_m, kind, layer, n):
        rstd, b_r = rmsnorm_stats(lambda i: m3[:, i, :n], 16, n, float(D), b_m)
        for ft in range(16):
            eng = "dve"
            P.op(eng, "scalar_tensor_tensor", m3[:, ft, :n], m3[:, ft, :n], gain(kind, layer, ft), rstd, op0=ALU.mult, op1=ALU.mult,
                 reads=[b_m[ft], b_r, b_par], writes=[b_m[ft]])
            P.op(eng, "tensor_tensor", x_fm[:, ft, :n], x_fm[:, ft, :n], m3[:, ft, :n], op=ALU.add,
                 reads=[b_m[ft], b_x[ft]], writes=[b_x[ft]])

    def pre_norm(hn3, b_hn, kind, layer, n):
        rstd, b_r = rmsnorm_stats(lambda i: x_fm[:, i, :n], 16, n, float(D), b_x)
        for ft in range(16):
            eng = "dve"
            P.op(eng, "scalar_tensor_tensor", R(hn3[:, ft, :n]), x_fm[:, ft, :n], gain(kind, layer, ft), rstd, op0=ALU.mult, op1=ALU.mult,
                 reads=[b_x[ft], b_r, b_par], writes=[b_hn[ft]])

    A0 = 0
    A1 = 16 * W
    SZ16 = 16 * W
    b_A0 = [Buf(f"A0_{i}") for i in range(16)]
    b_A1 = [Buf(f"A1_{i}") for i in range(40)]


    def sample_attention(q3, b_q, ksamp, b_ks, oc3, b_oc, npr, AT):
        P.barrier()
        o = AT
        def T(nel):
            nonlocal o
            v = arena[:, o:o + nel]; o += nel
            return v
        kpg = [T(1024) for _ in range(2)]; vpg = [R(T(1024)) for _ in range(2)]
        ktp = [R(T(1024)).rearrange("p (a b) -> p a b", a=8) for _ in range(2)]
        qblk = T(512); ptb = ptb_t[:]; idx = idx_t[:]; iot = iot_t[:]
        ptp = [R(T(64)) for _ in range(2)]; vst = R(T(1024))
        psb = arenaF[:, 0:128]; tms = arenaF[:, 128:256]; oms = arenaF[:, 256:320]; ods = arenaF[:, 320:352]; rss = arenaF[:, 352:384]
        sqs = smr(1, 32)
        assert o <= ARENA
        bk = [Buf("kpg0"), Buf("kpg1")]; bv = [Buf("vpg0"), Buf("vpg1")]; bkt = [Buf("ktp0"), Buf("ktp1")]
        bq = Buf("qblk"); bi = Buf("idx"); bp = Buf("psb"); bpt = [Buf("ptp0"), Buf("ptp1")]; bt = Buf("tms"); bvs = Buf("vst"); bm = Buf("misc")
        P.dma("pool", ptb, pt_d.partition_broadcast(128), writes=[bi], sem=bi)
        P.dma("sp", iot, iota_d, writes=[bi], sem=bi, partial=True)
        P.op("dve", "tensor_scalar", idx, ptb, 128, iot[:, 0:1], op0=ALU.mult, op1=ALU.add, reads=[bi], writes=[bi], partial=True)
        P.dma("sp", vst[:4, :], R(Vscr[NPR:NPR + 4, :]), reads=[b_vscr], writes=[bvs], sem=bvs)
        q4 = qblk.rearrange("p (h c) -> p h c", h=8)
        for qi in range(4):
            P.op("dve", "tensor_copy", R(qblk[:, qi * 128:(qi + 1) * 128]), zeros[:, 0:128], reads=[b_zero], writes=[bq], partial=(qi > 0))
        for h in range(8):
            for m in range(2):
                P.op("dve", "tensor_copy", R(q4[64 * m:64 * m + 64, h, h * 8 + m * 4:h * 8 + m * 4 + 4]), q3[64 * m:64 * m + 64, h, npr:npr + 4],
                     reads=[b_q[h]], writes=[bq], partial=True)
        qb = R(q4)
        o_ps, d_ps, s_ps, t_ps = PS[2], PS[5], PS[7], PS[4]
        psb2 = [arenaF[:, 384:512], arenaF[:, 512:640]]; tms2 = [arenaF[:, 640:768], arenaF[:, 768:896]]
        bp2 = [Buf("psb0"), Buf("psb1")]; bt2 = [Buf("tms0"), Buf("tms1")]
        SPS = [7, 3]

        def stage_a1(i):
            s = i % 2
            sp_ = PS[SPS[s]]; bsp = b_ps[SPS[s]]
            P.dma_custom("pool", "indirect_dma_start", reads=[bi], writes=[bk[s]], sem=bk[s],
                         out=R(kpg[s]), out_offset=None, in_=R(ck_d), in_offset=bass.IndirectOffsetOnAxis(ap=idx[:, i:i + 1], axis=0))
            P.dma_custom("pool", "indirect_dma_start", reads=[bi], writes=[bv[s]], sem=bv[s],
                         out=vpg[s], out_offset=None, in_=R(cv_d), in_offset=bass.IndirectOffsetOnAxis(ap=idx[:, i:i + 1], axis=0))
            for h in range(8):
                P.op("pe", "transpose", PS[h // 4][:, (h % 4) * 128:(h % 4 + 1) * 128], kpg[s][:, h * 128:(h + 1) * 128], ident[:, :],
                     reads=[bk[s], b_ident], writes=[b_ps[h // 4]], partial=(h % 4 > 0))
            copy_op("act", ktp[s][:, 0:4, :], PS[0][:, :].rearrange("p (a b) -> p a b", a=4), reads=[b_ps[0]], writes=[bkt[s]])
            copy_op("dve", ktp[s][:, 4:8, :], PS[1][:, :].rearrange("p (a b) -> p a b", a=4), reads=[b_ps[1]], writes=[bkt[s]], partial=True)
            for h in range(8):
                P.op("pe", "matmul", sp_[:64, 0:128], lhsT=qb[:, h, :], rhs=ktp[s][:, h, :], start=(h == 0), stop=(h == 7),
                     reads=[bq, bkt[s]], writes=[bsp], partial=(h > 0))
            P.op("pool", "tensor_scalar", tms2[s][:64, :], stab[:, 8:136], float(128 * i), slopecol[:, 0:1], op0=ALU.add, op1=ALU.mult,
                 reads=[b_par], writes=[bt2[s]])
            P.op("dve", "tensor_tensor", tms2[s][:64, :], tms2[s][:64, :], sp_[:64, 0:128], op=ALU.add, reads=[bt2[s], bsp], writes=[bt2[s]])
            P.op("act", "activation", psb2[s][:64, :], tms2[s][:64, :], AF.Exp, reads=[bt2[s]], writes=[bp2[s]])

        def stage_a2(i):
            s = i % 2
            P.op("pe", "transpose", t_ps[:, 0:64], psb2[s][:64, :], ident[:64, :64], reads=[bp2[s], b_ident], writes=[b_ps[4]])
            copy_op("dve", ptp[s], t_ps[:, 0:64], reads=[b_ps[4]], writes=[bpt[s]])

        def stage_b(i):
            s = i % 2
            for h in range(8):
                P.op("pe", "matmul", o_ps[:, h * 8:(h + 1) * 8], lhsT=vpg[s][:, h * 128:(h + 1) * 128], rhs=ptp[s][:, h * 8:(h + 1) * 8],
                     start=(i == 0), stop=False, reads=[bv[s], bpt[s]], writes=[b_ps[2]], partial=True)
            P.op("pe", "matmul", d_ps[:, 0:64], lhsT=ones_r[:], rhs=ptp[s], start=(i == 0), stop=False,
                 reads=[b_ones, bpt[s]], writes=[b_ps[5]], partial=True)

        stage_a1(0)
        for i in range(128):
            if i + 1 < 128:
                stage_a1(i + 1)
            stage_a2(i)
            stage_b(i)
        for h in range(8):
            P.op("pe", "matmul", s_ps[:64, 0:4], lhsT=qb[:, h, :], rhs=R(ksamp[:, h, :]), start=(h == 0), stop=(h == 7),
                 reads=[bq, b_ks], writes=[b_ps[7]], partial=(h > 0))
        P.op("dve", "scalar_tensor_tensor", tms[:64, 0:4], stab[:, 0:4], slopecol[:, 0:1], stab[:, 4:8], op0=ALU.mult, op1=ALU.add,
             reads=[b_par], writes=[bt])
        P.op("dve", "tensor_tensor", tms[:64, 0:4], tms[:64, 0:4], s_ps[:64, 0:4], op=ALU.add, reads=[bt, b_ps[7]], writes=[bt])
        P.op("act", "activation", psb[:64, 0:4], tms[:64, 0:4], AF.Exp, reads=[bt], writes=[bp])
        P.op("pe", "transpose", t_ps[:4, 0:64], psb[:64, 0:4], ident[:64, :64], reads=[bp, b_ident], writes=[b_ps[4]])
        copy_op("dve", ptp[0][:4, :], t_ps[:4, 0:64], reads=[b_ps[4]], writes=[bpt[0]])
        for h in range(8):
            P.op("pe", "matmul", o_ps[:, h * 8:(h + 1) * 8], lhsT=vst[:4, h * 128:(h + 1) * 128], rhs=ptp[0][:4, h * 8:(h + 1) * 8],
                 start=False, stop=True, reads=[bvs, bpt[0]], writes=[b_ps[2]], partial=True)
        P.op("pe", "matmul", d_ps[:, 0:64], lhsT=ones_r[:4, :], rhs=ptp[0][:4, :], start=False, stop=True,
             reads=[b_ones, bpt[0]], writes=[b_ps[5]], partial=True)
        P.op("dve", "reciprocal", oms, d_ps[:, 0:64], reads=[b_ps[5]], writes=[bm])
        P.op("dve", "tensor_tensor", oms, oms, o_ps[:, 0:64], op=ALU.mult, reads=[bm, b_ps[2]], writes=[bm])
        om4 = oms.rearrange("p (h m q) -> p h m q", h=8, m=2)
        od3 = ods.rearrange("p (h q) -> p h q", h=8)
        P.op("dve", "scalar_tensor_tensor", od3, om4[:, :, 1, :], neglam, om4[:, :, 0, :], op0=ALU.mult, op1=ALU.add, reads=[bm, b_par], writes=[bm], partial=True)
        P.op("act", "activation", sqs, ods, AF.Square, reads=[bm], writes=[b_smr[1]])
        P.op("pe", "matmul", PS[4][:, 0:32], lhsT=ones_r[:], rhs=sqs, start=True, stop=True, reads=[b_smr[1], b_ones], writes=[b_ps[4]])
        P.op("act", "activation", rss, PS[4][:, 0:32], AF.Sqrt, bias=EPS, scale=1.0 / 128, reads=[b_ps[4]], writes=[bm], partial=True)
        P.op("dve", "reciprocal", rss, rss, reads=[bm], writes=[bm], partial=True)
        P.op("dve", "scalar_tensor_tensor", R(oc3[:, 0:8, npr:npr + 4]), od3, subg2, rss.rearrange("p (h q) -> p h q", h=8), op0=ALU.mult, op1=ALU.mult,
             reads=[bm, b_par], writes=list(b_oc[0:8]), partial=True)
        P.barrier()

    rgn = [Buf(f"ssm_r{i}") for i in range(4)]

    def ssm(u3, u3e, b_u, y3, b_y, npr, ns):
        SB = A1 + 16 * (W + 1)
        o = SB
        def T(nel):
            nonlocal o
            v = arena[:, o:o + nel]; o += nel
            return v
        TSZ = npr // 4
        TS = TSZ // 4
        scanb = [arenaF[:, i * 1792:(i + 1) * 1792].rearrange("g (t s) -> g t s", s=128) for i in range(2)]
        scan = arenaF[:, 0:3328].rearrange("g (t s) -> g t s", s=128)
        bd = [R(T(1024)) for _ in range(2)]; bd2 = [R(T(1024)) for _ in range(2)]
        bus = [arenaF[:, 3584 + i * 1024: 4608 + i * 1024] for i in range(3)]; b_bus = rgn[0:3]
        htm = [arenaF[:, 3584 + i * 1024: 4608 + i * 1024] for i in range(2)]; b_htm = rgn[0:2]
        ytm = arenaF[:, 5632:7680]; b_ytm = [rgn[2], rgn[3]]
        hT = [R(T(1024)).rearrange("p (a b) -> p a b", a=8) for _ in range(2)]
        assert o <= ARENA, o
        b_scan = Buf("scan"); b_scb = [b_scan, Buf("scan1")]; b_bdb = [Buf("bd0"), Buf("bd1")]; b_bdb2 = [Buf("bd20"), Buf("bd21")]
        b_hT = [Buf("hT0"), Buf("hT1")]
        b_t1 = [Buf("t1a"), Buf("t1b")]; b_t2 = [Buf("t2a"), Buf("t2b")]; b_s1 = [Buf("s1a"), Buf("s1b")]
        tiles = [(i * TSZ, TSZ, False) for i in range(4)] + ([(npr, ns, True)] if ns else [])
        P.stage = "ssm_A"
        k = 0
        for f in range(16):
            sb_ = f % 2
            P.dma("sp", bd[sb_], R(BDscr[f]), reads=[b_bd], writes=[b_bdb[sb_]], sem=b_bdb[sb_])
            P.dma("sp", bd2[sb_], R(BDscr2[f]), reads=[b_bd], writes=[b_bdb2[sb_]], sem=b_bdb2[sb_])
            for (t0, nt, is_s) in tiles:
                s = k % 3; k += 1
                for hf in range(2):
                    pb = (k % 2) * 2 + hf
                    P.op("pe", "matmul", PS[pb][:nt, :], lhsT=R(u3e[:, f, 1 + t0:1 + t0 + nt]), rhs=bd[sb_][:, hf * 512:(hf + 1) * 512], start=True, stop=is_s,
                         reads=[b_u[f], b_bdb[sb_]], writes=[b_ps[pb]])
                    if not is_s:
                        P.op("pe", "matmul", PS[pb][:nt, :], lhsT=R(u3e[:, f, t0:t0 + nt]), rhs=bd2[sb_][:, hf * 512:(hf + 1) * 512], start=False, stop=True,
                             reads=[b_u[f], b_bdb2[sb_]], writes=[b_ps[pb]], partial=True)
                    copy_op("act" if hf == 0 else "dve", bus[s][:nt, hf * 512:(hf + 1) * 512], PS[pb][:nt, :], reads=[b_ps[pb]], writes=[b_bus[s]], partial=(hf > 0))
                P.dma("sp", BuScr[t0:t0 + nt, f * 8:(f + 1) * 8, :].rearrange("t g s -> t (g s)"), bus[s][:nt, :],
                      reads=[b_bus[s]], writes=[b_buscr], sem=b_bus[s], partial=True)
        X3 = XY[:, 0:128].rearrange("g (r p) -> g r p", r=2); Yn = XY[:, 128:192]; Yp = XY[:, 192:256]
        t1s = sm(0, 128).rearrange("g (r p) -> g r p", r=2); t2s = sm(1, 128).rearrange("g (r p) -> g r p", r=2)
        s1s = sm(2, 128).rearrange("g (r p) -> g r p", r=2)

        X2 = XY2[:, :, 0:128]; Y2n = XY2[:, :, 128:192]; Y2p = XY2[:, :, 192:256]
        p1 = sm(0, 256).rearrange("g (t s) -> g t s", t=2); p2 = sm(1, 256).rearrange("g (t r p) -> g t r p", t=2, r=2)
        p3 = sm(2, 256).rearrange("g (t s) -> g t s", t=2)
        b_p1 = Buf("p1"); b_p2 = Buf("p2"); b_p3 = Buf("p3")
        spc = sm(3, 256); b_spc = Buf("spc"); b_spc2 = Buf("spc2")

        sub_i = {"k": 0}

        def scan_pairs(t0, nt):
            P.stage = "ssm_scan"
            sizes = [14, 12] * 4 if nt == 104 else [14, 12, 14, 12, 12, 12, 12, 12]
            assert sum(sizes) == nt
            starts = [t0 + sum(sizes[:i]) for i in range(len(sizes))]
            base = sub_i["k"]; sub_i["k"] += len(sizes)

            def load(j):
                bi = (base + j) % 2
                P.dma("sp", scanb[bi][:, 0:sizes[j], :], BuScr[starts[j]:starts[j] + sizes[j]].rearrange("t g s -> g t s"),
                      reads=[b_buscr], writes=[b_scb[bi]], sem=b_scb[bi])
            load(0)
            for j, tn in enumerate(sizes):
                if j + 1 < len(sizes):
                    load(j + 1)
                bi = (base + j) % 2
                sc_ = scanb[bi]; bsc = b_scb[bi]
                ts0 = starts[j]
                for i in range(tn // 2):
                    prev = S_p[:, :, :] if i == 0 else sc_[:, 2 * i - 2:2 * i, :]
                    pb = [b_Sp] if i == 0 else [bsc]
                    cur = sc_[:, 2 * i:2 * i + 2, :]
                    pv4 = prev.rearrange("g t (r p) -> g t r p", r=2)
                    P.op("dve", "tensor_tensor", p1, prev, X2, op=ALU.mult, reads=pb + [b_xy2], writes=[b_p1])
                    P.op("dve", "tensor_tensor", p2[:, :, 0, :], pv4[:, :, 1, :], Y2n, op=ALU.mult, reads=pb + [b_xy2], writes=[b_p2])
                    P.op("dve", "tensor_tensor", p2[:, :, 1, :], pv4[:, :, 0, :], Y2p, op=ALU.mult, reads=pb + [b_xy2], writes=[b_p2], partial=True)
                    P.op("dve", "tensor_tensor", p3, cur, p1, op=ALU.add, reads=[b_p1, bsc], writes=[b_p3])
                    P.op("dve", "tensor_copy", spc[:, 0:128], zeros[:, 0:128], reads=[b_zero], writes=[b_spc])
                    P.op("dve", "tensor_tensor", cur, p3, p2.rearrange("g t r p -> g t (r p)"), op=ALU.add, reads=[b_p2, b_p3], writes=[bsc], partial=True)
                    P.op("dve", "tensor_copy", spc[:, 128:256], zeros[:, 0:128], reads=[b_zero], writes=[b_spc2])
                P.op("dve", "tensor_copy", S_p[:, :, :], sc_[:, tn - 2:tn, :], reads=[bsc], writes=[b_Sp])
                P.dma("pool", HScr[ts0:ts0 + tn].rearrange("t g s -> g t s"), sc_[:, 0:tn, :], reads=[bsc], writes=[b_hscr], sem=bsc, partial=True)

        def scan_tile(t0, nt, is_s):
            if not is_s:
                return scan_pairs(t0, nt)
            P.stage = "ssm_scan"
            St, b_St = (S_s, b_Ss)
            subs = [(t0, nt)]
            for (ts0, tn) in subs:
                P.dma("pool", scan[:, 0:tn, :], BuScr[ts0:ts0 + tn].rearrange("t g s -> g t s"), reads=[b_buscr], writes=[b_scan], sem=b_scan)
                for t in range(tn):
                    prev = St[:] if t == 0 else scan[:, t - 1, :]
                    pb = [b_St] if t == 0 else [b_scan]
                    pv = prev.rearrange("g (r p) -> g r p", r=2)
                    cv_ = scan[:, t, :].rearrange("g (r p) -> g r p", r=2)
                    hs = [(0, 32), (32, 64)]
                    for si, (a, b_) in enumerate(hs):
                        P.op("dve", "tensor_tensor", t1s[:, :, a:b_], pv[:, :, a:b_], X3[:, :, a:b_], op=ALU.mult, reads=pb + [b_xy], writes=[b_t1[si]])
                    for si, (a, b_) in enumerate(hs):
                        P.op("dve", "tensor_tensor", t2s[:, 0, a:b_], pv[:, 1, a:b_], Yn[:, a:b_], op=ALU.mult, reads=pb + [b_xy], writes=[b_t2[si]])
                    for si, (a, b_) in enumerate(hs):
                        P.op("dve", "tensor_tensor", t2s[:, 1, a:b_], pv[:, 0, a:b_], Yp[:, a:b_], op=ALU.mult, reads=pb + [b_xy], writes=[b_t2[si]], partial=True)
                    for si, (a, b_) in enumerate(hs):
                        P.op("dve", "tensor_tensor", s1s[:, :, a:b_], cv_[:, :, a:b_], t1s[:, :, a:b_], op=ALU.add, reads=[b_t1[si], b_scan], writes=[b_s1[si]])
                    for si, (a, b_) in enumerate(hs):
                        P.op("dve", "tensor_tensor", cv_[:, :, a:b_], s1s[:, :, a:b_], t2s[:, :, a:b_], op=ALU.add, reads=[b_t2[si], b_s1[si]], writes=[b_scan], partial=True)
                P.op("dve", "tensor_copy", St[:], scan[:, tn - 1, :], reads=[b_scan], writes=[b_St])
                P.dma("pool", HScr[ts0:ts0 + tn].rearrange("t g s -> g t s"), scan[:, 0:tn, :], reads=[b_scan], writes=[b_hscr], sem=b_scan, partial=True)

        kc = {"k": 0}

        def c_tile(t0, nt):
            P.stage = "ssm_C"
            for f in range(16):
                s = kc["k"] % 2; kc["k"] += 1
                P.dma("sp", htm[s][:nt, :], HScr[t0:t0 + nt, f * 8:(f + 1) * 8, :].rearrange("t g s -> t (g s)"), reads=[b_hscr], writes=[b_htm[s]], sem=b_htm[s])
                for gl in range(8):
                    P.op("pe", "transpose", PS[2 + gl // 4][:, (gl % 4) * 128:(gl % 4) * 128 + nt], htm[s][:nt, gl * 128:(gl + 1) * 128], ident[:nt, :nt],
                         reads=[b_htm[s], b_ident], writes=[b_ps[2 + gl // 4]], partial=(gl % 4 > 0))
                copy_op("act", hT[s][:, 0:4, :nt], PS[2][:, :].rearrange("p (a b) -> p a b", a=4)[:, :, :nt], reads=[b_ps[2]], writes=[b_hT[s]])
                copy_op("act", hT[s][:, 4:8, :nt], PS[3][:, :].rearrange("p (a b) -> p a b", a=4)[:, :, :nt], reads=[b_ps[3]], writes=[b_hT[s]], partial=True)
                for gl in range(8):
                    P.op("pe", "matmul", PS[4 + f // 4][:nt, (f % 4) * 128 + gl * 16:(f % 4) * 128 + gl * 16 + 16], lhsT=hT[s][:, gl, :nt], rhs=ccat[:, f * 8 + gl, :],
                         start=True, stop=True, reads=[b_hT[s], b_ccat], writes=[b_ps[4 + f // 4]], partial=True)
            for bq_ in range(4):
                copy_op("act", ytm[:nt, bq_ * 512:(bq_ + 1) * 512], PS[4 + bq_][:nt, :], reads=[b_ps[4 + bq_]], writes=b_ytm, partial=(bq_ > 0))
            for fg in range(4):
                pb_ = fg % 2
                for kk in range(4):
                    f = fg * 4 + kk
                    P.op("pe", "transpose", PS[pb_][:, kk * 128:kk * 128 + nt], ytm[:nt, f * 128:(f + 1) * 128], ident[:nt, :nt],
                         reads=b_ytm + [b_ident], writes=[b_ps[pb_]], partial=(kk > 0))
                for kk in range(4):
                    f = fg * 4 + kk
                    P.op("dve", "scalar_tensor_tensor", y3[:, f, t0:t0 + nt], u3[:, f, t0:t0 + nt], dfm[:, f:f + 1], PS[pb_][:, kk * 128:kk * 128 + nt],
                         op0=ALU.mult, op1=ALU.add, reads=[b_u[f], b_ps[pb_], b_par], writes=[b_y[f]], partial=True)

        P.op("dve", "tensor_copy", u_last[:].unsqueeze(2), u3e[:, :, npr:npr + 1], reads=list(b_u), writes=[b_ul])
        prev_tile = None
        for (t0, nt, is_s) in tiles:
            scan_tile(t0, nt, is_s)
            if prev_tile is not None:
                c_tile(*prev_tile)
            prev_tile = (t0, nt)
        c_tile(*prev_tile)

    for ci, (c0, c1) in enumerate(CHUNKS):
        n = c1 - c0
        npr = min(c1, NPR) - c0
        ns = n - npr
        ttiles = [(t0, min(128, n - t0)) for t0 in range(0, n, 128)]

        P.stage = "xload"
        xst = [arenaF[:, 0:2048], arenaF[:, 0:2048]]
        b_xst = [b_AF[0], b_AF[0]]
        for ti, (t0, nt) in enumerate(ttiles):
            s = ti % 2
            P.dma("sp", xst[s][:nt, :], xin[c0 + t0:c0 + t0 + nt, :], writes=[b_xst[s]], sem=b_xst[s])
            for fg in range(4):
                pb = 2 + (fg % 2)
                for k in range(4):
                    ft = fg * 4 + k
                    P.op("pe", "transpose", PS[pb][:, k * 128:k * 128 + nt], xst[s][:nt, ft * 128:(ft + 1) * 128], ident[:nt, :nt],
                         reads=[b_xst[s], b_ident], writes=[b_ps[pb]], partial=(k > 0))
                eng = ev_eng()
                copy_op(eng, x_fm[:, fg * 4:fg * 4 + 4, t0:t0 + nt], PS[pb][:].rearrange("p (a b) -> p a b", a=4)[:, :, :nt],
                        reads=[b_ps[pb]], writes=[b_x[fg * 4 + k] for k in range(4)], partial=True)
        P.barrier()

        P.stage = "l0_inproj"
        hn3 = av(A0, 16, W); b_hn = b_A0
        pre_norm(hn3, b_hn, 0, 0, n)
        q3 = av(A1, 8, W); b_q = b_A1[0:8]
        GW = 30 + W
        gext = av(A1 + 8 * W, 8, GW); b_g = b_A1[8:16]
        conv3 = arenaF[:, 3072:3072 + 8 * W].rearrange("p (a b) -> p a b", a=8); b_cv = b_A1[16:24]
        OFFX = A1 + 8 * W + 8 * GW
        gs_ext = av(OFFX, 8, 34); b_gs = b_A1[24]
        ksamp = av(OFFX + 272, 8, 4); b_ks = b_A1[25]
        OFFX2 = OFFX + 272 + 32
        for j in range(8):
            if ci == 0:
                P.op("dve", "tensor_copy", gext[:, j, 0:30], zeros[:, 0:30], reads=[b_zero], writes=[b_g[j]])
            else:
                P.op("dve", "tensor_copy", gext[:, j, 0:30], gtail[:, j, :], reads=[b_gt], writes=[b_g[j]])
        if ns:
            scst = arenaF[:, 2048:3072]; b_sc = b_AF[1]
            P.dma("sp", scst[:30, :], sconv_d, writes=[b_sc], sem=b_sc)
            for j in range(8):
                P.op("pe", "transpose", PS[2][:, j * 32:j * 32 + 30], scst[:30, j * 128:(j + 1) * 128], ident[:30, :30],
                     reads=[b_sc, b_ident], writes=[b_ps[2]], partial=(j > 0))
            copy_op("dve", gs_ext[:, :, 0:30], PS[2][:, 0:256].rearrange("p (a b) -> p a b", a=8)[:, :, 0:30], reads=[b_ps[2]], writes=[b_gs], partial=True)

        def kv_out(which, h, src, b_src):
            dst = kout if which == "k" else vout
            sl = 2 + (state["evac"] % 2)
            for ti, (t0, nt) in enumerate(ttiles):
                P.op("pe", "transpose", PS[sl][:nt, ti * 128:(ti + 1) * 128], src[:, t0:t0 + nt], ident[:, :],
                     reads=[b_src, b_ident], writes=[b_ps[sl]], partial=(ti > 0))
            stg = arenaF[:, 3072 + (sl - 2) * 512: 3072 + (sl - 1) * 512]
            copy_op(ev_eng(), stg, PS[sl][:, :], reads=[b_ps[sl]], writes=[b_AF[sl]])
            for ti, (t0, nt) in enumerate(ttiles):
                P.dma("pool", dst[c0 + t0:c0 + t0 + nt, h * 128:(h + 1) * 128], stg[:nt, ti * 128:(ti + 1) * 128],
                      reads=[b_AF[sl]], writes=[b_out], sem=b_AF[sl], partial=True)
                if which == "v":
                    P.dma("pool", Vscr[c0 + t0:c0 + t0 + nt, h * 128:(h + 1) * 128], stg[:nt, ti * 128:(ti + 1) * 128],
                          reads=[b_AF[sl]], writes=[b_vscr], sem=b_AF[sl], partial=True)

        pending_kv = []

        def flush_kv():
            while pending_kv:
                kv_out(*pending_kv.pop(0))

        def evac_in_even(col0, ps, b_p):
            t = col0 // 128
            if t >= 24:
                flush_kv()
            if t < 8:
                P.op("act", "activation", R(q3[:, t, :n]), ps, AF.Copy, scale=0.125, reads=[b_p], writes=[b_q[t]])
            elif t < 24:
                which = "k" if t < 16 else "v"
                h = t % 8
                sl = t % 2
                tmp = sm(sl, n)
                copy_op(ev_eng(), tmp, ps, reads=[b_p], writes=[b_small[sl]])
                if which == "k":
                    P.dma("pool", KTscr[h, :, c0:c0 + npr], tmp[:, :npr], reads=[b_small[sl]], writes=[b_ktscr], sem=b_small[sl], partial=True)
                    if ns:
                        P.op("dve", "tensor_copy", R(ksamp[:, h, :]), tmp[:, npr:n], reads=[b_small[sl]], writes=[b_ks], partial=True)
                flush_kv()
                pending_kv.append((which, h, tmp, b_small[sl]))
                return
            elif t < 32:
                P.op("act", "copy", sm(4, n), ps, reads=[b_p], writes=[b_small[4]])
            else:
                j = t - 32
                P.op("act", "activation", sm(5, n), ps, AF.Sigmoid, reads=[b_p], writes=[b_small[5]])
                P.op("dve", "tensor_tensor", gext[:, j, 30:30 + npr], sm(4, n)[:, :npr], sm(5, n)[:, :npr], op=ALU.mult,
                     reads=[b_small[4], b_small[5]], writes=[b_g[j]], partial=True)
                if ns:
                    P.op("dve", "tensor_tensor", gs_ext[:, j, 30:34], sm(4, n)[:, npr:n], sm(5, n)[:, npr:n], op=ALU.mult,
                         reads=[b_small[4], b_small[5]], writes=[b_gs], partial=True)

        blocks = blocks_range(0, 3072) + [c for j in range(8) for c in (3072 + 128 * j, 4096 + 128 * j)]
        linear(w_in_even, 16, blocks, lambda kt: R(hn3[:, kt, :n]), b_hn, n, evac_in_even)
        P.barrier()
        if stop_after == "inproj":
            break

        P.stage = "conv"
        oc3 = av(A0, 16, W); b_oc = b_A0
        for w in range(31):
            for j in range(8):
                segs = [(gext, b_g[j], 0, npr)] + ([(gs_ext, b_gs, npr, ns)] if ns else [])
                for (gsrc, b_src, o0, ln) in segs:
                    if w == 0:
                        P.op("dve", "tensor_scalar", conv3[:, j, o0:o0 + ln], gsrc[:, j, 0:ln], cw[:, j, 0:1], cvec[:, 0, j:j + 1], op0=ALU.mult, op1=ALU.add,
                             reads=[b_src, b_par], writes=[b_cv[j]], partial=True)
                    else:
                        P.op("dve", "scalar_tensor_tensor", conv3[:, j, o0:o0 + ln], gsrc[:, j, w:w + ln], cw[:, j, w:w + 1], conv3[:, j, o0:o0 + ln], op0=ALU.mult, op1=ALU.add,
                             reads=[b_src, b_par, b_cv[j]], writes=[b_cv[j]], partial=True)
        if ns:
            for (gsrc, b_src, lo, dsto) in [(gext, None, npr, convp_o), (gs_ext, b_gs, 4, convs_o)]:
                for j in range(8):
                    bs = b_g[j] if b_src is None else b_src
                    P.op("pe", "transpose", PS[2 + j // 4][:30, (j % 4) * 128:(j % 4) * 128 + 128], gsrc[:, j, lo:lo + 30], ident[:, :],
                         reads=[bs, b_ident], writes=[b_ps[2 + j // 4]], partial=(j % 4 > 0))
                stg = arenaF[:, 2048:3072]
                copy_op("act", stg[:30, 0:512], PS[2][:30, :], reads=[b_ps[2]], writes=[b_AF[1]])
                copy_op("dve", stg[:30, 512:1024], PS[3][:30, :], reads=[b_ps[3]], writes=[b_AF[1]], partial=True)
                P.dma("pool", dsto, stg[:30, :], reads=[b_AF[1]], writes=[b_out], sem=b_AF[1], partial=True)
        else:
            pass
        for j in range(8):
            sl = j % 2
            P.op("act", "copy", smr(sl, n), conv3[:, j, :n], reads=[b_cv[j]], writes=[b_smr[sl]])
            P.op("pe", "matmul", PS[4][:, :n], lhsT=ones_r[:], rhs=smr(sl, n), start=(j == 0), stop=(j == 7),
                 reads=[b_smr[sl], b_ones], writes=[b_ps[4]], partial=(j > 0))
        for j in range(8):
            sl = j % 2
            P.op("act", "activation", smr(sl, n), conv3[:, j, :n], AF.Square, reads=[b_cv[j]], writes=[b_smr[sl]])
            P.op("pe", "matmul", PS[5][:, :n], lhsT=ones_r[:], rhs=smr(sl, n), start=(j == 0), stop=(j == 7),
                 reads=[b_smr[sl], b_ones], writes=[b_ps[5]], partial=(j > 0))
        mean = sm(0, n); rstd = sm(1, n); msq = sm(2, n)
        P.op("dve", "tensor_scalar", mean, PS[4][:, :n], 1.0 / 1024, None, op0=ALU.mult, reads=[b_ps[4]], writes=[b_small[0]])
        P.op("dve", "tensor_tensor", msq, mean, mean, op=ALU.mult, reads=[b_small[0]], writes=[b_small[2]])
        P.op("dve", "scalar_tensor_tensor", rstd, PS[5][:, :n], 1.0 / 1024, msq, op0=ALU.mult, op1=ALU.subtract, reads=[b_ps[5], b_small[2]], writes=[b_small[1]])
        P.op("act", "activation", rstd, rstd, AF.Sqrt, bias=EPS, scale=1.0, reads=[b_small[1]], writes=[b_small[1]])
        P.op("dve", "reciprocal", rstd, rstd, reads=[b_small[1]], writes=[b_small[1]])
        for j in range(8):
            eng = "dve" if j % 2 == 0 else "pool"
            P.op(eng, "tensor_tensor", conv3[:, j, :n], conv3[:, j, :n], mean, op=ALU.subtract, reads=[b_cv[j], b_small[0]], writes=[b_cv[j]])
            P.op(eng, "tensor_tensor", conv3[:, j, :n], conv3[:, j, :n], rstd, op=ALU.mult, reads=[b_cv[j], b_small[1]], writes=[b_cv[j]])
            P.op("act", "activation", R(oc3[:, 8 + j, :n]), conv3[:, j, :n], AF.Silu, bias=cvec[:, 2, j:j + 1], scale=cvec[:, 1, j:j + 1],
                 reads=[b_cv[j], b_par], writes=[b_oc[8 + j]])
        if not ns:
            for j in range(8):
                P.op("pool", "tensor_copy", gtail[:, j, :], gext[:, j, npr:npr + 30], reads=[b_g[j]], writes=[b_gt], partial=(j > 0))
        if stop_after == "conv":
            break

        P.stage = "attn"
        AT = OFFX2
        kend = c0 + npr
        nkt = (kend + 127) // 128
        ktb = [arena[:, AT + i * NPR: AT + (i + 1) * NPR].bitcast(F32R) for i in range(2)]; b_ktb = b_A1[27:29]
        VB0 = AT + 2 * NPR
        vb = [arena[:, VB0 + i * 17 * 128: VB0 + (i + 1) * 17 * 128].bitcast(F32R).rearrange("p (a b) -> p a b", a=17) for i in range(2)]; b_vb = b_A1[29:31]
        PB0 = VB0 + 2 * 17 * 128
        pex = [arena[:, PB0 + i * W: PB0 + (i + 1) * W] for i in range(3)]; b_pex = b_A1[31:34]
        tmpb = [arenaF[:, i * W: (i + 1) * W] for i in range(3)]; b_tmp = b_A1[34:37]
        mtmp = arenaF[:, 3 * W: 4 * W]; b_mt = b_A1[37]
        ENDAT = PB0 + 3 * W
        assert ENDAT <= ARENA, ENDAT
        SCB = [0, 1, 7]
        for h in range(8):
            s = h % 2
            P.dma("sp", ktb[s][:, 0:kend], R(KTscr[h, :, 0:kend]), reads=[b_ktscr], writes=[b_ktb[s]], sem=b_ktb[s])
            nfull = kend // 128
            if nfull:
                P.dma("sp", vb[s][:, 0:nfull, :], R(Vscr[0:nfull * 128, h * 128:(h + 1) * 128].rearrange("(a p) d -> p a d", p=128)),
                      reads=[b_vscr], writes=[b_vb[s]], sem=b_vb[s])
            if kend % 128:
                P.dma("sp", vb[s][:kend % 128, nfull, :], R(Vscr[nfull * 128:kend, h * 128:(h + 1) * 128]),
                      reads=[b_vscr], writes=[b_vb[s]], sem=b_vb[s], partial=True)
            work = [(m, kt) for m in range(2) for kt in range(nkt)]

            def front(i):
                m, kt = work[i]
                k0 = kt * 128; kn = min(128, kend - k0)
                sc = SCB[i % 3]; tb = i % 3; pe_i = i % 3
                P.op("pe", "matmul", PS[sc][:kn, :npr], lhsT=ktb[s][64 * m:64 * m + 64, k0:k0 + kn],
                     rhs=R(q3[64 * m:64 * m + 64, h, :npr]), start=True, stop=True,
                     reads=[b_ktb[s], b_q[h]], writes=[b_ps[sc]])
                P.op("dve", "scalar_tensor_tensor", tmpb[tb][:kn, :npr], d0tab[:kn, :npr], SLOPES[h], PS[sc][:kn, :npr], op0=ALU.mult, op1=ALU.add,
                     reads=[b_ps[sc], b_sid], writes=[b_tmp[tb]])
                if k0 + kn - 1 > c0:
                    P.op("dve", "tensor_scalar", mtmp[:kn, :npr], d0tab[:kn, :npr], float(k0 - c0), 0.0, op0=ALU.add, op1=ALU.is_gt,
                         reads=[b_sid], writes=[b_mt])
                    P.op("dve", "scalar_tensor_tensor", tmpb[tb][:kn, :npr], mtmp[:kn, :npr], NEG, tmpb[tb][:kn, :npr], op0=ALU.mult, op1=ALU.add,
                         reads=[b_mt, b_tmp[tb]], writes=[b_tmp[tb]])
                P.op("act", "activation", R(pex[pe_i][:kn, :npr]), tmpb[tb][:kn, :npr], AF.Exp, bias=float(SLOPES[h] * (k0 - c0)), scale=1.0,
                     reads=[b_tmp[tb]], writes=[b_pex[pe_i]])

            def back(i):
                m, kt = work[i]
                k0 = kt * 128; kn = min(128, kend - k0)
                pe_i = i % 3
                o_ps, d_ps = PS[2 + m], PS[5 + m]
                b_o, b_d = b_ps[2 + m], b_ps[5 + m]
                P.op("pe", "matmul", o_ps[:, :npr], lhsT=vb[s][:kn, kt, :], rhs=R(pex[pe_i][:kn, :npr]), start=(kt == 0), stop=(kt == nkt - 1),
                     reads=[b_vb[s], b_pex[pe_i]], writes=[b_o], partial=(kt > 0))
                P.op("pe", "matmul", d_ps[:, :npr], lhsT=ones_r[:kn, :], rhs=R(pex[pe_i][:kn, :npr]), start=(kt == 0), stop=(kt == nkt - 1),
                     reads=[b_ones, b_pex[pe_i]], writes=[b_d], partial=(kt > 0))
                if kt == nkt - 1:
                    P.op("dve", "reciprocal", sm(6, npr), d_ps[:, :npr], reads=[b_d], writes=[b_small[6]])
                    P.op("dve", "tensor_tensor", sm(2 + m, npr), o_ps[:, :npr], sm(6, npr), op=ALU.mult, reads=[b_o, b_small[6]], writes=[b_small[2 + m]])

            LA = 2
            for i in range(min(LA, len(work))):
                front(i)
            for i in range(len(work)):
                if i + LA < len(work):
                    front(i + LA)
                back(i)
            if True:
                if True:
                    pass
            P.op("dve", "scalar_tensor_tensor", sm(2, npr), sm(3, npr), neglam, sm(2, npr), op0=ALU.mult, op1=ALU.add,
                 reads=[b_small[2], b_small[3], b_par], writes=[b_small[2]])
            P.op("act", "activation", smr(0, npr), sm(2, npr), AF.Square, reads=[b_small[2]], writes=[b_smr[0]])
            P.op("pe", "matmul", PS[4][:, :npr], lhsT=ones_r[:], rhs=smr(0, npr), start=True, stop=True, reads=[b_smr[0], b_ones], writes=[b_ps[4]])
            P.op("act", "activation", sm(6, npr), PS[4][:, :npr], AF.Sqrt, bias=EPS, scale=1.0 / 128, reads=[b_ps[4]], writes=[b_small[6]])
            P.op("dve", "reciprocal", sm(6, npr), sm(6, npr), reads=[b_small[6]], writes=[b_small[6]])
            P.op("dve", "scalar_tensor_tensor", R(oc3[:, h, :npr]), sm(2, npr), subg2, sm(6, npr), op0=ALU.mult, op1=ALU.mult,
                 reads=[b_small[2], b_small[6], b_par], writes=[b_oc[h]], partial=True)

        if ns:
            P.stage = "sattn"
            sample_attention(q3, b_q, ksamp, b_ks, oc3, b_oc, npr, AT)
        if stop_after == "attn":
            dump(0, oc3[:, :, :], b_oc, ci)
            break
        P.barrier()
        dump(0, oc3[:, :, :], b_oc, ci)

        P.stage = "outproj0"
        m3 = av(A1, 16, W); b_m = b_A1[0:16]

        def evac_m(col0, ps, b_p, m3=m3, b_m=b_m):
            t = col0 // 128
            copy_op(ev_eng(), m3[:, t, :n], ps, reads=[b_p], writes=[b_m[t]])
        linear(w_out_even, 16, blocks_range(0, D), lambda kt: R(oc3[:, kt, :n]), b_oc, n, evac_m)
        post_norm_residual(m3, b_m, 1, 0, n)
        P.barrier()
        dump(1, x_fm[:, :, :], b_x, ci)
        if stop_after == "mix0":
            break

        for layer in range(2):
            if layer == 1:
                P.stage = "l1_inproj"
                pre_norm(hn3, b_hn, 0, 1, n)
                u3e = av(A1, 16, W + 1); b_u = b_A1[0:16]
                u3 = u3e[:, :, 1:1 + W]
                P.op("dve", "tensor_copy", R(u3e[:, :, 0:1]), u_last[:].unsqueeze(2), reads=[b_ul], writes=list(b_u), partial=True)

                def evac_u(col0, ps, b_p):
                    t = col0 // 128
                    copy_op(ev_eng(), R(u3[:, t, :n]), ps, reads=[b_p], writes=[b_u[t]], partial=True)
                linear(w_in_odd, 16, blocks_range(0, D), lambda kt: R(hn3[:, kt, :n]), b_hn, n, evac_u)
                P.barrier()
                y3 = av(A0, 16, W); b_y = b_A0
                P.stage = "ssm"
                ssm(u3, u3e, b_u, y3, b_y, npr, ns)
                P.stage = "l1_glu_out"
                P.barrier()
                dump(3, y3[:, :, :], b_y, ci)
                for ft in range(16):
                    eng = "dve" if ft % 2 == 0 else "pool"
                    sl = ft % 2
                    yv = y3[:, ft, :n]
                    P.op(eng, "tensor_tensor", sm(sl, n), yv, yv, op=ALU.mult, reads=[b_y[ft]], writes=[b_small[sl]])
                    P.op(eng, "tensor_scalar", sm(sl, n), sm(sl, n), 0.044715, 1.0, op0=ALU.mult, op1=ALU.add, reads=[b_small[sl]], writes=[b_small[sl]])
                    P.op(eng, "tensor_tensor", sm(sl, n), sm(sl, n), yv, op=ALU.mult, reads=[b_small[sl], b_y[ft]], writes=[b_small[sl]])
                    P.op("act", "activation", sm(sl, n), sm(sl, n), AF.Sigmoid, scale=1.5957691216057308, reads=[b_small[sl]], writes=[b_small[sl]])
                    P.op(eng, "tensor_tensor", R(yv), yv, sm(sl, n), op=ALU.mult, reads=[b_small[sl], b_y[ft]], writes=[b_y[ft]])
                z3 = av(A1, 16, W); b_z = b_A1[0:16]

                def evac_z(col0, ps, b_p):
                    t = col0 // 128
                    sl = 2 + t % 2
                    P.op("act", "activation", sm(sl, n), ps, AF.Sigmoid, reads=[b_p], writes=[b_small[sl]])
                    P.op("dve", "tensor_tensor", R(z3[:, t, :n]), y3[:, t, :n], sm(sl, n), op=ALU.mult, reads=[b_small[sl], b_y[t]], writes=[b_z[t]])
                linear(w_glu, 16, blocks_range(0, D), lambda kt: R(y3[:, kt, :n]), b_y, n, evac_z)
                P.barrier()
                m3 = av(A0, 16, W); b_m = b_A0

                def evac_m1(col0, ps, b_p):
                    t = col0 // 128
                    copy_op(ev_eng(), m3[:, t, :n], ps, reads=[b_p], writes=[b_m[t]])
                linear(w_out_odd, 16, blocks_range(0, D), lambda kt: R(z3[:, kt, :n]), b_z, n, evac_m1)
                post_norm_residual(m3, b_m, 1, 1, n)
                P.barrier()
                dump(4, x_fm[:, :, :], b_x, ci)
                if stop_after == "mix1":
                    break
            P.stage = "ffn"
            pre_norm(hn3, b_hn, 2, layer, n)
            act3 = av(A1, 11, W); b_act = b_A1[0:11]
            yacc = av(A1 + 11 * W, 16, W); b_ya = b_A1[11:27]
            for qd in range(4):
                lo = 1408 * qd

                def evac_gate(col0, ps, b_p, lo=lo):
                    jj = (col0 - lo) // 128
                    P.op("act", "activation", act3[:, jj, :n], ps, AF.Silu, reads=[b_p], writes=[b_act[jj]])

                def evac_up(col0, ps, b_p, lo=lo):
                    jj = (col0 - lo) // 128
                    P.op("dve", "tensor_tensor", R(act3[:, jj, :n]), act3[:, jj, :n], ps, op=ALU.mult, reads=[b_p, b_act[jj]], writes=[b_act[jj]])
                linear(w_gate[layer], 16, blocks_range(lo, lo + 1408), lambda kt: R(hn3[:, kt, :n]), b_hn, n, evac_gate)
                linear(w_up[layer], 16, blocks_range(lo, lo + 1408), lambda kt: R(hn3[:, kt, :n]), b_hn, n, evac_up)

                def evac_down(col0, ps, b_p, qd=qd):
                    t = col0 // 128
                    if qd == 0:
                        copy_op(ev_eng(), yacc[:, t, :n], ps, reads=[b_p], writes=[b_ya[t]])
                    else:
                        P.op("dve", "tensor_tensor", yacc[:, t, :n], yacc[:, t, :n], ps, op=ALU.add, reads=[b_p, b_ya[t]], writes=[b_ya[t]])
                linear(w_down[layer][lo:lo + 1408, :], 11, blocks_range(0, D), lambda kt: R(act3[:, kt, :n]), b_act, n, evac_down)
            post_norm_residual(yacc, b_ya, 3, layer, n)
            P.barrier()
            dump(2 if layer == 0 else 5, x_fm[:, :, :], b_x, ci)
            if stop_after == f"ffn{layer}":
                break
        if stop_after is not None and stop_after != "chunk0":
            break

        P.stage = "ystore"
        yst = [arenaF[:, 0:2048], arenaF[:, 0:2048]]
        b_yst = [b_AF[0], b_AF[0]]
        for ti, (t0, nt) in enumerate(ttiles):
            s = ti % 2
            for fg in range(4):
                pb = 2 + (fg % 2)
                for k in range(4):
                    ft = fg * 4 + k
                    P.op("pe", "transpose", PS[pb][:nt, k * 128:(k + 1) * 128], x_fm[:, ft, t0:t0 + nt], ident[:, :],
                         reads=[b_x[ft], b_ident], writes=[b_ps[pb]], partial=(k > 0))
                copy_op(ev_eng(), yst[s][:nt, fg * 512:(fg + 1) * 512], PS[pb][:nt, :], reads=[b_ps[pb]], writes=[b_yst[s]], partial=(fg > 0))
            P.dma("pool", yout[c0 + t0:c0 + t0 + nt, :], yst[s][:nt, :], reads=[b_yst[s]], writes=[b_out], sem=b_yst[s], partial=True)
        P.barrier()
        if stop_after == "chunk0":
            break

    if stop_after is None:
        P.dma("pool", ssmp_o, S_p[:, 1, :], reads=[b_Sp], writes=[b_out], sem=b_Sp, partial=True)
        P.dma("pool", ssms_o, S_s[:], reads=[b_Ss], writes=[b_out], sem=b_Ss, partial=True)
    P.barrier()
    P.emit()
    return nc


_PROG_CACHE = {}


def _host_inputs(inp):
    f32 = np.float32
    c = lambda a: np.ascontiguousarray(a, dtype=a.dtype)
    shared = {}
    shared["w_in_even"] = c(inp["w_in_even"][0]); shared["w_out_even"] = c(inp["w_out_even"][0])
    shared["w_in_odd"] = c(inp["w_in_odd"][0]); shared["w_glu"] = c(inp["w_glu"][0]); shared["w_out_odd"] = c(inp["w_out_odd"][0])
    shared["w_gate"] = c(inp["w_ffn_gate"]); shared["w_up"] = c(inp["w_ffn_up"]); shared["w_down"] = c(inp["w_ffn_down"])
    g = np.zeros((8, 16, 128), f32)
    for kind, nm in enumerate(["norm_mix_pre", "norm_mix_post", "norm_ffn_pre", "norm_ffn_post"]):
        for layer in range(2):
            g[kind * 2 + layer] = np.asarray(inp[nm][layer]).reshape(16, 128)
    shared["gains"] = c(g.transpose(2, 0, 1))
    shared["cw"] = c(np.asarray(inp["conv_w"][0]).reshape(31, 8, 128).transpose(2, 1, 0))
    cv = np.stack([np.asarray(inp[k][0]).reshape(8, 128) for k in ("conv_b", "conv_ln_g", "conv_ln_b")])
    shared["cvec"] = c(cv.transpose(2, 0, 1))
    shared["subg"] = c(np.asarray(inp["subln_g"][0]).reshape(128, 1))
    shared["lqk"] = c(np.concatenate([np.asarray(inp["lambda_q"][0]).ravel(), np.asarray(inp["lambda_k"][0]).ravel()]).reshape(1, 256))
    shared["a_re"] = c(inp["ssm_a_re"][0]); shared["a_im"] = c(inp["ssm_a_im"][0])
    shared["ldt"] = c(np.asarray(inp["ssm_log_dt"][0]).reshape(128, 1))
    shared["b_re"] = c(inp["ssm_b_re"][0]); shared["b_im"] = c(inp["ssm_b_im"][0])
    cre = np.asarray(inp["ssm_c_re"][0]).transpose(2, 0, 1); cim = np.asarray(inp["ssm_c_im"][0]).transpose(2, 0, 1)
    shared["ccat"] = c(np.concatenate([cre, cim], axis=0))
    shared["dfm"] = c(np.asarray(inp["ssm_d"][0]).reshape(16, 128).T)
    shared["cache_k"] = c(np.asarray(inp["cache_k"][0]).reshape(1280 * 128, 1024))
    shared["cache_v"] = c(np.asarray(inp["cache_v"][0]).reshape(1280 * 128, 1024))
    shared["iota"] = np.arange(128, dtype=np.int32).reshape(128, 1)
    shared["ident"] = np.eye(128, dtype=f32)
    shared["ones"] = np.ones((128, 128), f32)
    shared["d0tab"] = (np.arange(128, dtype=f32)[:, None] - np.arange(W, dtype=f32)[None, :]).astype(f32)
    stab = np.zeros((64, 136), f32); slopecol = np.zeros((64, 1), f32)
    for h in range(8):
        for m in range(2):
            for q in range(4):
                r = h * 8 + m * 4 + q
                stab[r, 0:4] = np.arange(4)
                stab[r, 4:8] = np.where(np.arange(4) > q, NEG, 0.0)
                stab[r, 8:136] = np.arange(128) - PAST
                slopecol[r, 0] = SLOPES[h]
    shared["stab"] = stab; shared["slopecol"] = slopecol
    maps = []
    meta = np.asarray(inp["meta_tokens"], f32)
    for i in range(8):
        b = i % 4
        d = dict(shared)
        d["xin"] = c(np.concatenate([meta, np.asarray(inp["x_prompt"][b]), np.asarray(inp["x_sample"][i])], axis=0))
        d["ptab"] = c(np.asarray(inp["page_table"][i], dtype=np.int32).reshape(1, 128))
        d["sconv"] = c(inp["state_conv"][0, i])
        d["sssm"] = c(np.concatenate([np.asarray(inp["state_ssm_re"][0, i]), np.asarray(inp["state_ssm_im"][0, i])], axis=1))
        maps.append(d)
    return maps


def _assemble(res):
    f32 = np.float32
    R_ = [r for r in res]
    y_prompt = np.stack([R_[b]["yout"][16:NPR] for b in range(4)]).astype(f32)
    y_sample = np.stack([R_[i]["yout"][NPR:NT] for i in range(8)]).astype(f32)
    k_prompt = np.stack([R_[b]["kout"][0:NPR].reshape(NPR, 8, 2, 64) for b in range(4)])[None].astype(f32)
    v_prompt = np.stack([R_[b]["vout"][0:NPR].reshape(NPR, 8, 128) for b in range(4)])[None].astype(f32)
    k_sample = np.stack([R_[i]["kout"][NPR:NT].reshape(4, 8, 2, 64) for i in range(8)])[None].astype(f32)
    v_sample = np.stack([R_[i]["vout"][NPR:NT].reshape(4, 8, 128) for i in range(8)])[None].astype(f32)
    conv_prompt = np.stack([R_[b]["convp"] for b in range(4)])[None].astype(f32)
    conv_sample = np.stack([R_[i]["convs"] for i in range(8)])[None].astype(f32)
    srp = np.stack([R_[b]["ssmp"][:, 0:64] for b in range(4)])[None].astype(f32)
    sip = np.stack([R_[b]["ssmp"][:, 64:128] for b in range(4)])[None].astype(f32)
    srs = np.stack([R_[i]["ssms"][:, 0:64] for i in range(8)])[None].astype(f32)
    sis = np.stack([R_[i]["ssms"][:, 64:128] for i in range(8)])[None].astype(f32)
    return (y_prompt, y_sample, k_prompt, v_prompt, k_sample, v_sample, conv_prompt, conv_sample, srp, sip, srs, sis)


def kernel(_stop_after=None, **inputs):
    maps = _host_inputs(inputs)
    nc = build_program(_stop_after)
    res = run_bass_kernel_spmd(nc, maps, core_ids=list(range(8)))
    return _assemble(res.results)
```
